# Optimizing a Trainium2 kernel written in Bass

```python
import math
import jax, jax.numpy as jnp
from jax import lax
import numpy as np

D_MODEL = 1024
BATCH = 8
SEQ = 4096
DEPTH = 1

CHUNK = 64
D_FF = 2816
POOL_WIDTH = D_MODEL // 2
POOL_WINDOWS = (2, 4, 8, 16)
N_POOL_GROUPS = len(POOL_WINDOWS)
POOL_GROUP = POOL_WIDTH // N_POOL_GROUPS
SSM_WIDTH = D_MODEL // 2
SSM_GROUP = 16
N_SSM_GROUPS = SSM_WIDTH // SSM_GROUP
SSM_STATE = 64
N_SUBLAYERS = 3
N_MOD = 3
IN_WIDTH = POOL_WIDTH + SSM_WIDTH + 2 * D_MODEL
EPS = 1e-6
DT_MIN = 1e-3
DT_MAX = 1e-1

kernel_name = "hybrid_pool_s5_macaron_adaln"


def rms_norm(x, g):
    xf = x.astype(jnp.float32)
    y = xf * lax.rsqrt(jnp.mean(xf * xf, axis=-1, keepdims=True) + EPS)
    return (y * g.astype(jnp.float32)).astype(x.dtype)


def modulate(h, shift, scale):
    return h * (1 + scale[:, None, :]) + shift[:, None, :]


def swiglu(h, w_in, w_out):
    a, b = jnp.split(h @ w_in, 2, axis=-1)
    return (jax.nn.silu(a) * b) @ w_out


def multiscale_pool(u, pool_w, pool_b, pool_scale):
    B, T, _ = u.shape
    ug = u.reshape(B, T, N_POOL_GROUPS, POOL_GROUP)
    cs = jnp.cumsum(ug.astype(jnp.float32), axis=1)
    pos = jnp.arange(T)
    means = []
    for k, w in enumerate(POOL_WINDOWS):
        csk = cs[:, :, k]
        prev = jnp.pad(csk, ((0, 0), (w, 0), (0, 0)))[:, :T]
        cnt = jnp.minimum(pos + 1, w).astype(jnp.float32)[None, :, None]
        means.append((csk - prev) / cnt)
    mean = jnp.stack(means, axis=2).astype(u.dtype)
    z = mean - ug
    z = jnp.einsum('btgc,gcd->btgd', z, pool_w) + pool_b.reshape(N_POOL_GROUPS, POOL_GROUP)
    return z.reshape(B, T, POOL_WIDTH) * pool_scale


def _ssm_combine(left, right):
    ar_l, ai_l, br_l, bi_l = left
    ar_r, ai_r, br_r, bi_r = right
    ar = ar_r * ar_l - ai_r * ai_l
    ai = ar_r * ai_l + ai_r * ar_l
    br = ar_r * br_l - ai_r * bi_l + br_r
    bi = ar_r * bi_l + ai_r * br_l + bi_r
    return (ar, ai, br, bi)


def s5_mixer(u, lam_re_log, lam_im, log_dt, b_re, b_im, c_re, c_im, d_skip, w_glu, b_glu):
    B, T, _ = u.shape
    f32 = jnp.float32
    uf = u.astype(f32).reshape(B, T, N_SSM_GROUPS, SSM_GROUP)
    lr = -jnp.exp(lam_re_log.astype(f32))
    li = lam_im.astype(f32)
    dt = jnp.exp(log_dt.astype(f32))[:, None]
    mag = jnp.exp(lr * dt)
    ang = li * dt
    ab_re = mag * jnp.cos(ang)
    ab_im = mag * jnp.sin(ang)
    num_re = ab_re - 1.0
    num_im = ab_im
    den = lr * lr + li * li
    f_re = (num_re * lr + num_im * li) / den
    f_im = (num_im * lr - num_re * li) / den
    br = b_re.astype(f32)
    bi = b_im.astype(f32)
    bb_re = f_re[..., None] * br - f_im[..., None] * bi
    bb_im = f_re[..., None] * bi + f_im[..., None] * br
    bu_re = jnp.einsum('btgh,gnh->btgn', uf, bb_re)
    bu_im = jnp.einsum('btgh,gnh->btgn', uf, bb_im)
    a_re = jnp.broadcast_to(ab_re[None, None], (1, T, N_SSM_GROUPS, SSM_STATE))
    a_im = jnp.broadcast_to(ab_im[None, None], (1, T, N_SSM_GROUPS, SSM_STATE))
    _, _, s_re, s_im = lax.associative_scan(_ssm_combine, (a_re, a_im, bu_re, bu_im), axis=1)
    y = (jnp.einsum('btgn,ghn->btgh', s_re, c_re.astype(f32))
         - jnp.einsum('btgn,ghn->btgh', s_im, c_im.astype(f32)))
    y = y.reshape(B, T, SSM_WIDTH) + d_skip.astype(f32) * uf.reshape(B, T, SSM_WIDTH)
    y = jax.nn.gelu(y.astype(u.dtype), approximate=False)
    val, gate = jnp.split(y @ w_glu + b_glu, 2, axis=-1)
    return val * jax.nn.sigmoid(gate)


def setup_inputs(seed: int = 0) -> dict:
    key = jax.random.key(seed)
    ks = jax.random.split(key, 32)
    L, D, F = DEPTH, D_MODEL, D_FF
    G, H, N = N_SSM_GROUPS, SSM_GROUP, SSM_STATE
    nrm = lambda k, shape, s: jax.random.normal(k, shape, jnp.float32) * s
    n_idx = jnp.arange(N, dtype=jnp.float32)[None, :]
    return {
        "x": nrm(ks[0], (BATCH, SEQ, D), 1.0),
        "c": nrm(ks[1], (BATCH, D), 1.0),
        "w_ada": nrm(ks[2], (L, D, N_SUBLAYERS * N_MOD * D), 0.5 * D ** -0.5),
        "b_ada": nrm(ks[3], (L, N_SUBLAYERS * N_MOD * D), 0.01),
        "g_ffn1": 1.0 + nrm(ks[4], (L, D), 0.01),
        "w_ffn1_in": nrm(ks[5], (L, D, 2 * F), D ** -0.5),
        "w_ffn1_out": nrm(ks[6], (L, F, D), F ** -0.5),
        "g_mix": 1.0 + nrm(ks[7], (L, D), 0.01),
        "w_in": nrm(ks[8], (L, D, IN_WIDTH), D ** -0.5),
        "pool_w": nrm(ks[9], (L, N_POOL_GROUPS, POOL_GROUP, POOL_GROUP), POOL_GROUP ** -0.5),
        "pool_b": nrm(ks[10], (L, POOL_WIDTH), 0.01),
        "pool_scale": 1.0 + nrm(ks[11], (L, POOL_WIDTH), 0.05),
        "w_pool_up": nrm(ks[12], (L, POOL_WIDTH, D), POOL_WIDTH ** -0.5),
        "ssm_lam_re_log": jnp.log(0.5) + nrm(ks[13], (L, G, N), 0.01),
        "ssm_lam_im": math.pi * n_idx + nrm(ks[14], (L, G, N), 0.01),
        "ssm_log_dt": jax.random.uniform(ks[15], (L, G), jnp.float32, math.log(DT_MIN), math.log(DT_MAX)),
        "ssm_b_re": nrm(ks[16], (L, G, N, H), (2 * H) ** -0.5),
        "ssm_b_im": nrm(ks[17], (L, G, N, H), (2 * H) ** -0.5),
        "ssm_c_re": nrm(ks[18], (L, G, H, N), N ** -0.5),
        "ssm_c_im": nrm(ks[19], (L, G, H, N), N ** -0.5),
        "ssm_d": nrm(ks[20], (L, SSM_WIDTH), 1.0),
        "w_glu": nrm(ks[21], (L, SSM_WIDTH, 2 * SSM_WIDTH), SSM_WIDTH ** -0.5),
        "b_glu": nrm(ks[22], (L, 2 * SSM_WIDTH), 0.01),
        "w_ssm_up": nrm(ks[23], (L, SSM_WIDTH, D), SSM_WIDTH ** -0.5),
        "w_out": nrm(ks[24], (L, D, D), D ** -0.5),
        "g_ffn2": 1.0 + nrm(ks[25], (L, D), 0.01),
        "w_ffn2_in": nrm(ks[26], (L, D, 2 * F), D ** -0.5),
        "w_ffn2_out": nrm(ks[27], (L, F, D), F ** -0.5),
        "g_final": 1.0 + nrm(ks[28], (D,), 0.01),
    }


def reference(x, c, w_ada, b_ada, g_ffn1, w_ffn1_in, w_ffn1_out, g_mix, w_in,
              pool_w, pool_b, pool_scale, w_pool_up,
              ssm_lam_re_log, ssm_lam_im, ssm_log_dt, ssm_b_re, ssm_b_im, ssm_c_re, ssm_c_im, ssm_d,
              w_glu, b_glu, w_ssm_up, w_out, g_ffn2, w_ffn2_in, w_ffn2_out, g_final):
    B = x.shape[0]
    split_pts = (POOL_WIDTH, POOL_WIDTH + SSM_WIDTH, POOL_WIDTH + SSM_WIDTH + D_MODEL)
    for l in range(DEPTH):
        mod = (jax.nn.silu(c) @ w_ada[l] + b_ada[l]).reshape(B, N_SUBLAYERS, N_MOD, D_MODEL)

        h = modulate(rms_norm(x, g_ffn1[l]), mod[:, 0, 0], mod[:, 0, 1])
        x = x + 0.5 * mod[:, 0, 2][:, None, :] * swiglu(h, w_ffn1_in[l], w_ffn1_out[l])

        h = modulate(rms_norm(x, g_mix[l]), mod[:, 1, 0], mod[:, 1, 1])
        u_pool, u_ssm, gl_pool, gl_ssm = jnp.split(h @ w_in[l], split_pts, axis=-1)
        y_pool = multiscale_pool(u_pool, pool_w[l], pool_b[l], pool_scale[l]) @ w_pool_up[l]
        y_ssm = s5_mixer(u_ssm, ssm_lam_re_log[l], ssm_lam_im[l], ssm_log_dt[l],
                         ssm_b_re[l], ssm_b_im[l], ssm_c_re[l], ssm_c_im[l], ssm_d[l],
                         w_glu[l], b_glu[l]) @ w_ssm_up[l]
        merged = jax.nn.sigmoid(gl_pool) * y_pool + jax.nn.sigmoid(gl_ssm) * y_ssm
        x = x + mod[:, 1, 2][:, None, :] * (merged @ w_out[l])

        h = modulate(rms_norm(x, g_ffn2[l]), mod[:, 2, 0], mod[:, 2, 1])
        x = x + 0.5 * mod[:, 2, 2][:, None, :] * swiglu(h, w_ffn2_in[l], w_ffn2_out[l])
    return rms_norm(x, g_final)
```

```python
import numpy as np
from contextlib import ExitStack
import concourse.bass as bass
import concourse.mybir as mybir
from concourse.bass_utils import run_bass_kernel_spmd

F32 = mybir.dt.float32
BF16 = mybir.dt.bfloat16
AF = mybir.ActivationFunctionType
ALU = mybir.AluOpType

D = 1024
T = 4096
FF = 2816
NT = 1024
import os
NTILES = int(os.environ.get('KTILES', str(T // NT)))
NH = NT // 512
KT = D // 128
FT = FF // 128
EPS = 1e-6
LCH = 8
NC_T = NT // LCH
NPAIR = 16

import os
MIXER = True
STAGE = int(os.environ.get('KSTAGE', '9'))
SETUP = int(os.environ.get('KSETUP', '9'))
SUB = int(os.environ.get('KSUB', '9'))


class Prog:
    def __init__(self, nc, es):
        self.nc = nc
        self.es = es
        self.ops = []
        self.last_w = {}
        self.readers = {}
        self.dma_keys = {}

    skip = False

    def add(self, eng, fn, reads=(), writes=(), dma=None, sig=True):
        if self.skip:
            return None
        idx = len(self.ops)
        deps = set()
        for b in reads:
            if b in self.last_w:
                deps.add(self.last_w[b])
        for b in writes:
            if b in self.last_w:
                deps.add(self.last_w[b])
            for r in self.readers.get(b, ()):
                deps.add(r)
        for b in writes:
            self.last_w[b] = idx
            self.readers[b] = []
        for b in reads:
            self.readers.setdefault(b, []).append(idx)
        deps.discard(idx)
        self.ops.append(dict(eng=eng, fn=fn, deps=deps, dma=dma, sig=sig))
        return idx

    def emit(self, block):
        nc = self.nc
        ops = self.ops
        engs = ['pe', 'act', 'dve', 'pool', 'sp']
        sems = {e: self.es.enter_context(nc.semaphore(f"s_{e}")) for e in ['pe', 'act', 'dve', 'pool']}
        keysem = {}
        for o in ops:
            if o['dma'] is not None and o['dma'] not in keysem:
                keysem[o['dma']] = self.es.enter_context(nc.semaphore(f"d_{len(keysem)}"))
        cnt = {e: 0 for e in sems}
        for o in ops:
            if o['dma'] is None:
                if o['sig']:
                    cnt[o['eng']] += 1
                    o['cnt'] = cnt[o['eng']]
                else:
                    o['cnt'] = None
        nxt = {e: None for e in sems}
        for o in reversed(ops):
            if o['dma'] is None:
                if o['cnt'] is None:
                    o['cnt'] = nxt[o['eng']]
                    assert o['cnt'] is not None
                else:
                    nxt[o['eng']] = o['cnt']
        cum = {k: 0 for k in keysem}
        waited = {}
        per_eng = {e: [] for e in engs}
        for i, o in enumerate(ops):
            waits = {}
            for j in o['deps']:
                d = ops[j]
                if d['dma'] is not None:
                    s = keysem[d['dma']]
                    v = cum[d['dma']]
                else:
                    if d['eng'] == 'pe' and o['eng'] == 'pe' and o['dma'] is None:
                        continue
                    s = sems[d['eng']]
                    v = d['cnt']
                key = (o['eng'], id(s))
                if waited.get(key, 0) >= v:
                    continue
                if key not in waits or waits[key][1] < v:
                    waits[key] = (s, v)
            for key, (s, v) in waits.items():
                waited[key] = v
            if o['dma'] is not None:
                cum[o['dma']] += 16
            per_eng[o['eng']].append((o, list(waits.values())))
        self.n_ops = {e: len(per_eng[e]) for e in engs}

        def run(e, lst):
            for o, waits in lst:
                for (s, v) in waits:
                    e.wait_ge(s, v)
                ins = o['fn'](e)
                if o['dma'] is not None:
                    ins.then_inc(keysem[o['dma']], 16)
                elif o['sig']:
                    ins.then_inc(sems[o['eng']], 1)

        @block.tensor
        def _(e):
            run(e, per_eng['pe'])

        @block.scalar
        def _(e):
            run(e, per_eng['act'])

        @block.vector
        def _(e):
            run(e, per_eng['dve'])

        @block.gpsimd
        def _(e):
            run(e, per_eng['pool'])

        @block.sync
        def _(e):
            run(e, per_eng['sp'])
            for k, s in keysem.items():
                if isinstance(k, str) and k.startswith('out'):
                    e.wait_ge(s, cum[k])


def build_nc(debug=False):
    nc = bass.Bass("TRN2", target_bir_lowering=False)
    es = ExitStack()

    def din(name, shape, dt=F32):
        return nc.dram_tensor(name, list(shape), dt, kind="ExternalInput").ap()

    xT = din("xT", [D, T])
    c_l = din("c_l", [128, KT])
    w_ada = din("w_ada", [D, 9 * D])
    b_ada = din("b_ada", [128, 72])
    gvec = din("gvec", [128, 4, KT])
    w1i = din("w1i", [D, 2 * FF])
    w1o = din("w1o", [FF, D])
    w2i = din("w2i", [D, 2 * FF])
    w2o = din("w2o", [FF, D])
    w_in = din("w_in", [D, 3 * D])
    pool_w = din("pool_w", [128, 4, 128])
    pool_bs = din("pool_bs", [128, 2, 4])
    w_pu = din("w_pu", [512, D])
    w_glu = din("w_glu", [512, D])
    b_glu = din("b_glu", [128, 8])
    w_su = din("w_su", [512, D])
    w_o = din("w_o", [D, D])
    ssm_sc = din("ssm_sc", [128, 3, NPAIR])
    ssm_b = din("ssm_b", [128, 2, NPAIR, 16])
    ssm_cc = din("ssm_cc", [128, 2, NPAIR, 16])
    ssm_dd = din("ssm_dd", [128, 32])
    cst_ident = din("cst_ident", [128, 128])
    cst_sel = din("cst_sel", [128, 1920])
    cst_mask = din("cst_mask", [128, 128])
    outT = nc.dram_tensor("outT", [D, T], F32, kind="ExternalOutput").ap()
    DBG = os.environ.get('KDBG', '0') == '1'
    dbgt = {}
    if DBG:
        for nm in ('s', 'v', 'z', 'm'):
            dbgt[nm] = nc.dram_tensor("dbg_" + nm, [128, 8, NT], BF16, kind="ExternalOutput").ap()

    def tap(nm, j0, ti):
        if DBG and ti == 0:
            P.add('sp', lambda e: e.dma_start(out=dbgt[nm], in_=cp[:, j0:j0 + 8, :]), reads=[('cp', j) for j in range(j0, j0 + 8)], dma='out_' + nm)

    def sb(name, shape, dt=F32):
        return es.enter_context(nc.sbuf_tensor(name, list(shape), dt))

    def ps(name, shape, dt=F32):
        return es.enter_context(nc.psum_tensor(name, list(shape), dt))

    P = Prog(nc, es)

    xs = sb("xs", [128, KT, NT])
    hb = sb("hb", [128, KT, NT], BF16)
    NCHK = 24
    cp = sb("cp", [128, NCHK, NT], BF16)
    tmpf = sb("tmpf", [128, 2, NT + 16])
    rstd = sb("rstd", [128, NT])
    NSLOT = 4
    wsl = sb("wsl", [128, NSLOT, 4096], BF16)
    ones_bf = sb("ones_bf", [128, 128], BF16)
    ident = sb("ident", [128, 128])
    eps_sb = sb("eps_sb", [128, 1])
    modv = sb("modv", [128, 72])
    cl_sb = sb("cl_sb", [128, KT])
    sc_bf = sb("sc_bf", [128, KT], BF16)
    bada_sb = sb("bada_sb", [128, 72])
    gv_sb = sb("gv_sb", [128, 4, KT])
    Avec = sb("Avec", [128, 3, KT])
    Gvec = sb("Gvec", [128, 3, KT])
    banks = [ps(f"bank{i}", [128, 512]) for i in range(8)]

    def bank_b(i):
        return ('bank', i)

    slot_use = [0]

    def load_w(src_ap_list, shape_views):
        s = slot_use[0] % NSLOT
        slot_use[0] += 1
        for dst_fn, src in src_ap_list:
            dst = dst_fn(wsl[:, s, :])
            P.add('pool', lambda e, dst=dst, src=src: e.dma_start(out=dst, in_=src),
                  writes=[('slot', s)], dma=('slot', s))
        return s

    bank_rr = [0]

    reserved = [None]

    def next_bank():
        b = bank_rr[0] % 8
        bank_rr[0] += 1
        if b == reserved[0] or b in reserved_set:
            return next_bank()
        return b

    reserved_set = set()

    def sq_begin():
        bs = [next_bank() for _ in range(NH)]
        reserved_set.update(bs)
        return bs

    def sq_step(bs, k):
        P.add('act', lambda e, k=k: e.activation(out=hb[:, k, :], in_=xs[:, k, :], func=AF.Square),
              reads=[('xs', k)], writes=[('hb', k)])
        for hf in range(NH):
            P.add('pe', lambda e, k=k, hf=hf: e.matmul(
                banks[bs[hf]][:], lhsT=ones_bf[:], rhs=hb[:, k, hf * 512:(hf + 1) * 512],
                start=(k == 0), stop=(k == KT - 1)),
                reads=['ones', ('hb', k)], writes=[bank_b(bs[hf])], sig=(k == KT - 1))

    def small_load(dst, src, key):
        P.add('sp', lambda e: e.dma_start(out=dst, in_=src), writes=[key], dma=key)

    small_load(cl_sb[:], c_l, 'cl')
    small_load(bada_sb[:], b_ada, 'bada')
    small_load(gv_sb[:], gvec, 'gv')
    small_load(ident[:], cst_ident, 'ident')
    P.add('dve', lambda e: e.memset(ones_bf[:], 1.0 / D), writes=['ones'])
    P.add('dve', lambda e: e.memset(eps_sb[:], EPS), writes=['eps'])

    P.add('act', lambda e: e.activation(out=sc_bf[:], in_=cl_sb[:], func=AF.Silu),
          reads=['cl'], writes=['sc'])
    mod_bank = next_bank()
    reserved[0] = mod_bank
    for sl in range(18):
        c0 = sl * 512
        s = load_w([(lambda sa: sa.rearrange("p (k c) -> p k c", k=KT),
                     w_ada[:, c0:c0 + 512].rearrange("(k p) c -> p k c", p=128))], None)
        wv = wsl[:, s, :].rearrange("p (k c) -> p k c", k=KT)
        for mm in range(4):
            m = sl * 4 + mm
            for k in range(KT):
                P.add('pe', lambda e, m=m, k=k, mm=mm, wv=wv: e.matmul(
                    banks[mod_bank][:, m:m + 1], lhsT=wv[:, k, mm * 128:(mm + 1) * 128],
                    rhs=sc_bf[:, k:k + 1], start=(k == 0), stop=(k == KT - 1)),
                    reads=[('slot', s), 'sc'], writes=[bank_b(mod_bank)], sig=(k == KT - 1))
    def mod_finalize():
        P.add('dve', lambda e: e.tensor_tensor(out=modv[:], in0=banks[mod_bank][:, 0:72], in1=bada_sb[:], op=ALU.add),
              reads=[bank_b(mod_bank), 'bada'], writes=['modv'])
        for sub in range(3):
            sc_ap = modv[:, (sub * 3 + 1) * 8:(sub * 3 + 1) * 8 + 8]
            gt_ap = modv[:, (sub * 3 + 2) * 8:(sub * 3 + 2) * 8 + 8]
            P.add('dve', lambda e, sub=sub, sc_ap=sc_ap: e.scalar_tensor_tensor(
                out=Avec[:, sub, :], in0=sc_ap, scalar=1.0, in1=gv_sb[:, sub, :], op0=ALU.add, op1=ALU.mult),
                reads=['modv', 'gv'], writes=['Avec'])
            P.add('dve', lambda e, sub=sub, gt_ap=gt_ap: e.tensor_scalar(
                out=Gvec[:, sub, :], in0=gt_ap, scalar1=0.5, scalar2=None, op0=ALU.mult),
                reads=['modv'], writes=['Gvec'])


    def Bvec(sub, k):
        return modv[:, (sub * 3) * 8 + k:(sub * 3) * 8 + k + 1]

    def norm_mod(sub, t0, pre=None):
        if pre is None:
            bs = sq_begin()
            for k in range(KT):
                sq_step(bs, k)
        else:
            bs = pre
        for hf in range(NH):
            P.add('act', lambda e, hf=hf: e.activation(
                out=rstd[:, hf * 512:(hf + 1) * 512], in_=banks[bs[hf]][:], func=AF.Sqrt,
                bias=eps_sb[:, 0:1], scale=1.0),
                reads=[bank_b(bs[hf]), 'eps'], writes=[('rstd', hf)])
            P.add('dve', lambda e, hf=hf: e.reciprocal(
                out=rstd[:, hf * 512:(hf + 1) * 512], in_=rstd[:, hf * 512:(hf + 1) * 512]),
                reads=[('rstd', hf)], writes=[('rstd', hf)])
        for b_ in bs:
            reserved_set.discard(b_)
        for k in range(KT):
            tb = k % 2
            if sub < 3:
                a_ap = Avec[:, sub, k:k + 1]
            else:
                a_ap = gv_sb[:, 3, k:k + 1]
            P.add('dve', lambda e, k=k, tb=tb, a_ap=a_ap: e.scalar_tensor_tensor(
                out=tmpf[:, tb, 0:NT], in0=xs[:, k, :], scalar=a_ap, in1=rstd[:], op0=ALU.mult, op1=ALU.mult),
                reads=[('xs', k), ('rstd', 0), ('rstd', 1), 'Avec', 'gv'], writes=[('tmpf', tb)])
            if sub < 3:
                P.add('act', lambda e, k=k, tb=tb: e.activation(
                    out=hb[:, k, :], in_=tmpf[:, tb, 0:NT], func=AF.Identity, bias=Bvec(sub, k), scale=1.0),
                    reads=[('tmpf', tb), 'modv'], writes=[('hb', k)])
            else:
                P.add('sp', lambda e, k=k, tb=tb: e.dma_start(
                    out=outT[k * 128:(k + 1) * 128, t0:t0 + NT], in_=tmpf[:, tb, 0:NT]),
                    reads=[('tmpf', tb)], dma='out')

    def ffn(sub, w_i, w_o_):
        for sl in range(FT // 2):
            f0 = sl * 2
            s = load_w([
                (lambda sa: sa.rearrange("p (k a c) -> p k a c", k=KT, a=2)[:, :, 0, :],
                 w_i[:, f0 * 128:f0 * 128 + 256].rearrange("(k p) c -> p k c", p=128)),
                (lambda sa: sa.rearrange("p (k a c) -> p k a c", k=KT, a=2)[:, :, 1, :],
                 w_i[:, FF + f0 * 128:FF + f0 * 128 + 256].rearrange("(k p) c -> p k c", p=128)),
            ], None)
            wv = wsl[:, s, :].rearrange("p (k a c) -> p k a c", k=KT, a=2)
            for ff in range(2):
                f = f0 + ff
                ba = [[next_bank() for _ in range(NH)] for _ in range(2)]
                for part in range(2):
                    for hf in range(NH):
                        b = ba[part][hf]
                        for k in range(KT):
                            P.add('pe', lambda e, k=k, hf=hf, part=part, ff=ff, b=b, wv=wv: e.matmul(
                                banks[b][:], lhsT=wv[:, k, part, ff * 128:(ff + 1) * 128],
                                rhs=hb[:, k, hf * 512:(hf + 1) * 512], start=(k == 0), stop=(k == KT - 1)),
                                reads=[('slot', s), ('hb', k)], writes=[bank_b(b)], sig=(k == KT - 1))
                tb = f % 2
                for hf in range(NH):
                    P.add('act', lambda e, hf=hf, tb=tb, b=ba[0][hf]: e.activation(
                        out=cp[:, 22 + tb, hf * 512:(hf + 1) * 512], in_=banks[b][:], func=AF.Silu),
                        reads=[bank_b(ba[0][hf])], writes=[('cp', 22 + tb)])
                    P.add('dve', lambda e, hf=hf, tb=tb, f=f, b=ba[1][hf]: e.tensor_tensor(
                        out=cp[:, f, hf * 512:(hf + 1) * 512], in0=banks[b][:],
                        in1=cp[:, 22 + tb, hf * 512:(hf + 1) * 512], op=ALU.mult),
                        reads=[bank_b(ba[1][hf]), ('cp', 22 + tb)], writes=[('cp', f)])
        sqb = sq_begin()
        pend = []
        for m2 in range(4):
            ss = []
            for fh in range(2):
                s = load_w([(lambda sa: sa[:, 0:11 * 256].rearrange("p (f c) -> p f c", f=11),
                             w_o_[fh * 1408:(fh + 1) * 1408, m2 * 256:(m2 + 1) * 256].rearrange(
                                 "(f p) c -> p f c", p=128))], None)
                ss.append(s)
            for mm in range(2):
                m = m2 * 2 + mm
                for hf in range(NH):
                    b = next_bank()
                    for f in range(FT):
                        s = ss[f // 11]
                        wv = wsl[:, s, 0:11 * 256].rearrange("p (f c) -> p f c", f=11)
                        P.add('pe', lambda e, f=f, hf=hf, mm=mm, b=b, wv=wv: e.matmul(
                            banks[b][:], lhsT=wv[:, f % 11, mm * 128:(mm + 1) * 128],
                            rhs=cp[:, f, hf * 512:(hf + 1) * 512], start=(f == 0), stop=(f == FT - 1)),
                            reads=[('slot', s), ('cp', f)], writes=[bank_b(b)], sig=(f == FT - 1))
                    P.add('dve', lambda e, m=m, hf=hf, b=b: e.scalar_tensor_tensor(
                        out=xs[:, m, hf * 512:(hf + 1) * 512], in0=banks[b][:], scalar=Gvec[:, sub, m:m + 1],
                        in1=xs[:, m, hf * 512:(hf + 1) * 512], op0=ALU.mult, op1=ALU.add),
                        reads=[bank_b(b), 'Gvec', ('xs', m)], writes=[('xs', m)])
                pend.append(m)
                if len(pend) > 1:
                    sq_step(sqb, pend.pop(0))
        while pend:
            sq_step(sqb, pend.pop(0))
        return sqb

    PI = float(np.pi)
    cst_rc = din("cst_rc", [128, 4, 16])
    lhsTX = sb("lhsTX", [128, 32, 2, 128], BF16)
    Wintra = sb("Wintra", [128, 32, 128], BF16)
    WinterZ = sb("WinterZ", [128, 2, 2, NPAIR, 128], BF16)
    ident_bf = sb("ident_bf", [128, 128], BF16)
    sel_bf = sb("sel_bf", [128, 1920], BF16)
    Tc = sb("Tc", [128, NPAIR, 128])
    Ts = sb("Ts", [128, NPAIR, 128])
    smalls = sb("smalls", [128, 30, NPAIR])
    PW = sb("PW", [128, 9, 2, NPAIR])
    carry = sb("carry", [128, 2, NPAIR])
    halo = sb("halo", [128, 4, 16])
    maskf = sb("maskf", [128, 128])
    dd_sb = sb("dd_sb", [128, 32])
    rc_sb = sb("rc_sb", [128, 4, 16])
    poolw_bf = sb("poolw_bf", [128, 4, 128], BF16)
    pbs_sb = sb("pbs_sb", [128, 2, 4])
    pbsc = sb("pbsc", [128, 4])
    bglu_sb = sb("bglu_sb", [128, 8])
    bglu_h = sb("bglu_h", [128, 8])
    scl = sb("scl", [128, 3, NPAIR])
    ki_sb = sb("ki_sb", [128, NPAIR], mybir.dt.int32)

    cpf = cp[:].rearrange("p a b -> p (a b)").bitcast(F32)
    hbf = hb[:].rearrange("p a b -> p (a b)").bitcast(F32)
    WxT = cpf[:, 0:4096].rearrange("p (r q c) -> p r q c", r=2, q=NPAIR)
    WinT = cpf[:, 4096:8192].rearrange("p (r q c) -> p r q c", r=2, q=NPAIR)
    bb = cpf[:, 8192:8704].rearrange("p (r q c) -> p r q c", r=2, q=NPAIR)
    cc = cpf[:, 8704:9216].rearrange("p (r q c) -> p r q c", r=2, q=NPAIR)
    Bbar = cpf[:, 9216:9728].rearrange("p (r q c) -> p r q c", r=2, q=NPAIR)
    tS = cpf[:, 9728:10752].rearrange("p (r q c) -> p r q c", r=4, q=NPAIR)
    Kpp = hbf.rearrange("p (r q c) -> p r q c", r=2, q=NPAIR)
    ID_WXT = [('cp', j) for j in range(0, 8)]
    ID_WIN = [('cp', j) for j in range(8, 16)]
    ID_SM = [('cp', j) for j in range(16, 24)]
    ID_KPP = [('hb', j) for j in range(KT)]

    def dv(fn, reads, writes):
        P.add('dve', fn, reads=reads, writes=writes)

    def tt(out, a, b, op, reads, writes):
        dv(lambda e: e.tensor_tensor(out=out, in0=a, in1=b, op=op), reads, writes)

    def sm(i):
        return smalls[:, i, :]

    if MIXER:
        small_load(scl[:], ssm_sc, 'scl')
        small_load(dd_sb[:], ssm_dd, 'dd')
        small_load(maskf[:], cst_mask, 'maskf')
        small_load(rc_sb[:], cst_rc, 'rc')
        small_load(pbs_sb[:], pool_bs, 'pbs')
        small_load(bglu_sb[:], b_glu, 'bglu')
        P.add('sp', lambda e: e.dma_start(out=bb, in_=ssm_b), writes=ID_SM, dma='bbcc')
        P.add('sp', lambda e: e.dma_start(out=cc, in_=ssm_cc), writes=ID_SM, dma='bbcc')
        P.add('pool', lambda e: e.dma_start(out=sel_bf[:], in_=cst_sel), writes=['sel'], dma='sel')
        P.add('pool', lambda e: e.dma_start(out=poolw_bf[:], in_=pool_w), writes=['poolw'], dma='poolw')
        dv(lambda e: e.memset(carry[:], 0.0), [], ['carry'])
        dv(lambda e: e.memset(halo[:], 0.0), [], ['halo'])
        dv(lambda e: e.memset(lhsTX[:], 0.0), [], ['lhsTX'])
        dv(lambda e: e.tensor_tensor(out=pbsc[:], in0=pbs_sb[:, 0, :], in1=pbs_sb[:, 1, :], op=ALU.mult), ['pbs'], ['pbsc'])
        dv(lambda e: e.tensor_scalar(out=bglu_h[:], in0=bglu_sb[:], scalar1=0.5, scalar2=None, op0=ALU.mult), ['bglu'], ['bgluh'])

        P.skip = SUB < 1
        S = 'smalls'
        LR, DT, XM, MAG, R8, ANG, YS, YC, SN, CS, AR, AI, NR, DEN, RDEN, FRE, FIM, T1, T2, T3, T4, IR, II, RR8 = range(24)
        P.add('act', lambda e: e.activation(out=sm(LR), in_=scl[:, 0, :], func=AF.Exp), reads=['scl'], writes=[S])
        P.add('act', lambda e: e.activation(out=sm(DT), in_=scl[:, 2, :], func=AF.Exp), reads=['scl'], writes=[S])
        dv(lambda e: e.tensor_scalar(out=sm(LR), in0=sm(LR), scalar1=-1.0, scalar2=None, op0=ALU.mult), [S], [S])
        tt(sm(XM), sm(LR), sm(DT), ALU.mult, [S], [S])
        P.add('act', lambda e: e.activation(out=sm(MAG), in_=sm(XM), func=AF.Exp), reads=[S], writes=[S])
        P.add('act', lambda e: e.activation(out=sm(R8), in_=sm(XM), func=AF.Exp, scale=8.0), reads=[S], writes=[S])
        tt(sm(ANG), scl[:, 1, :], sm(DT), ALU.mult, [S, 'scl'], [S])
        P.skip = SUB < 2

        def range_reduce(dst, off):
            dv(lambda e: e.tensor_scalar(out=sm(T4), in0=sm(ANG), scalar1=off, scalar2=None, op0=ALU.add), [S], [S])
            dv(lambda e: e.tensor_scalar(out=sm(T3), in0=sm(T4), scalar1=1.0 / (2 * PI), scalar2=None, op0=ALU.mult), [S], [S])
            dv(lambda e: e.tensor_copy(out=ki_sb[:], in_=sm(T3)), [S], ['ki'])
            dv(lambda e: e.tensor_copy(out=sm(T3), in_=ki_sb[:]), ['ki'], [S])
            dv(lambda e: e.scalar_tensor_tensor(out=sm(dst), in0=sm(T3), scalar=-2 * PI, in1=sm(T4), op0=ALU.mult, op1=ALU.add), [S], [S])
            dv(lambda e: e.tensor_scalar(out=sm(T3), in0=sm(dst), scalar1=PI, scalar2=None, op0=ALU.is_gt), [S], [S])
            dv(lambda e: e.scalar_tensor_tensor(out=sm(dst), in0=sm(T3), scalar=-2 * PI, in1=sm(dst), op0=ALU.mult, op1=ALU.add), [S], [S])
            dv(lambda e: e.tensor_scalar(out=sm(T3), in0=sm(dst), scalar1=-PI, scalar2=None, op0=ALU.is_lt), [S], [S])
            dv(lambda e: e.scalar_tensor_tensor(out=sm(dst), in0=sm(T3), scalar=2 * PI, in1=sm(dst), op0=ALU.mult, op1=ALU.add), [S], [S])
        range_reduce(YS, 0.0)
        P.skip = SUB < 3
        TH, ZZ, SS, CC, U1, U2 = 24, 25, 26, 27, 28, 29
        dv(lambda e: e.tensor_scalar(out=sm(TH), in0=sm(YS), scalar1=0.25, scalar2=None, op0=ALU.mult), [S], [S])
        tt(sm(ZZ), sm(TH), sm(TH), ALU.mult, [S], [S])
        dv(lambda e: e.memset(sm(SS), 1.0), [S], [S])
        dv(lambda e: e.memset(sm(CC), 1.0), [S], [S])
        for kk in (156.0, 110.0, 72.0, 42.0, 20.0, 6.0):
            tt(sm(SS), sm(SS), sm(ZZ), ALU.mult, [S], [S])
            dv(lambda e, kk=kk: e.tensor_scalar(out=sm(SS), in0=sm(SS), scalar1=-1.0 / kk, scalar2=1.0, op0=ALU.mult, op1=ALU.add), [S], [S])
        tt(sm(SS), sm(SS), sm(TH), ALU.mult, [S], [S])
        for kk in (182.0, 132.0, 90.0, 56.0, 30.0, 12.0, 2.0):
            tt(sm(CC), sm(CC), sm(ZZ), ALU.mult, [S], [S])
            dv(lambda e, kk=kk: e.tensor_scalar(out=sm(CC), in0=sm(CC), scalar1=-1.0 / kk, scalar2=1.0, op0=ALU.mult, op1=ALU.add), [S], [S])
        for _ in range(2):
            tt(sm(U1), sm(SS), sm(CC), ALU.mult, [S], [S])
            tt(sm(U2), sm(SS), sm(SS), ALU.mult, [S], [S])
            tt(sm(CC), sm(CC), sm(CC), ALU.mult, [S], [S])
            tt(sm(CC), sm(CC), sm(U2), ALU.subtract, [S], [S])
            dv(lambda e: e.tensor_scalar(out=sm(SS), in0=sm(U1), scalar1=2.0, scalar2=None, op0=ALU.mult), [S], [S])
        dv(lambda e: e.tensor_copy(out=sm(SN), in_=sm(SS)), [S], [S])
        dv(lambda e: e.tensor_copy(out=sm(CS), in_=sm(CC)), [S], [S])
        tt(sm(AR), sm(MAG), sm(CS), ALU.mult, [S], [S])
        tt(sm(AI), sm(MAG), sm(SN), ALU.mult, [S], [S])
        P.skip = SETUP < 2
        dv(lambda e: e.tensor_scalar(out=sm(NR), in0=sm(AR), scalar1=-1.0, scalar2=None, op0=ALU.add), [S], [S])
        tt(sm(T1), sm(LR), sm(LR), ALU.mult, [S], [S])
        tt(sm(T2), scl[:, 1, :], scl[:, 1, :], ALU.mult, [S, 'scl'], [S])
        tt(sm(DEN), sm(T1), sm(T2), ALU.add, [S], [S])
        dv(lambda e: e.reciprocal(out=sm(RDEN), in_=sm(DEN)), [S], [S])
        tt(sm(T1), sm(NR), sm(LR), ALU.mult, [S], [S])
        tt(sm(T2), sm(AI), scl[:, 1, :], ALU.mult, [S, 'scl'], [S])
        tt(sm(T1), sm(T1), sm(T2), ALU.add, [S], [S])
        tt(sm(FRE), sm(T1), sm(RDEN), ALU.mult, [S], [S])
        tt(sm(T1), sm(AI), sm(LR), ALU.mult, [S], [S])
        tt(sm(T2), sm(NR), scl[:, 1, :], ALU.mult, [S, 'scl'], [S])
        tt(sm(T1), sm(T1), sm(T2), ALU.subtract, [S], [S])
        tt(sm(FIM), sm(T1), sm(RDEN), ALU.mult, [S], [S])

        def bc16(ap2):
            return ap2.unsqueeze(2).to_broadcast([128, NPAIR, 16])

        def cmul(o_re, o_im, xr, xi, yr, yi, reads, writes, neg_im=False):
            t0, t1 = tS[:, 0], tS[:, 1]
            rd = reads + ID_SM
            wr = writes + ID_SM
            tt(t0, xr, yr, ALU.mult, rd, ID_SM)
            tt(t1, xi, yi, ALU.mult, rd, ID_SM)
            tt(o_re, t0, t1, ALU.subtract, rd, wr)
            tt(t0, xr, yi, ALU.mult, rd, ID_SM)
            tt(t1, xi, yr, ALU.mult, rd, ID_SM)
            if not neg_im:
                tt(o_im, t0, t1, ALU.add, rd, wr)
            else:
                dv(lambda e: e.scalar_tensor_tensor(out=o_im, in0=t0, scalar=-1.0, in1=t1, op0=ALU.mult, op1=ALU.subtract), rd, wr)

        cmul(Bbar[:, 0], Bbar[:, 1], bc16(sm(FRE)), bc16(sm(FIM)), bb[:, 0], bb[:, 1], [S, 'bbcc'], [])
        dv(lambda e: e.tensor_copy(out=PW[:, 1, 0, :], in_=sm(AR)), [S], ['PW'])
        dv(lambda e: e.tensor_copy(out=PW[:, 1, 1, :], in_=sm(AI)), [S], ['PW'])
        for k in range(1, 8):
            t0, t1 = sm(T1), sm(T2)
            tt(t0, PW[:, k, 0, :], sm(AR), ALU.mult, [S, 'PW'], [S])
            tt(t1, PW[:, k, 1, :], sm(AI), ALU.mult, [S, 'PW'], [S])
            tt(PW[:, k + 1, 0, :], t0, t1, ALU.subtract, [S], ['PW'])
            tt(t0, PW[:, k, 0, :], sm(AI), ALU.mult, [S, 'PW'], [S])
            tt(t1, PW[:, k, 1, :], sm(AR), ALU.mult, [S, 'PW'], [S])
            tt(PW[:, k + 1, 1, :], t0, t1, ALU.add, [S], ['PW'])
        WxT5 = [WxT[:, r].rearrange("p q (t h) -> p q t h", t=8) for r in range(2)]
        WinT5 = [WinT[:, r].rearrange("p q (t h) -> p q t h", t=8) for r in range(2)]
        for tau in range(8):
            k = 7 - tau
            if k == 0:
                dv(lambda e, tau=tau: e.tensor_copy(out=WxT5[0][:, :, tau, :], in_=Bbar[:, 0]), ID_SM, ID_WXT)
                dv(lambda e, tau=tau: e.tensor_copy(out=WxT5[1][:, :, tau, :], in_=Bbar[:, 1]), ID_SM, ID_WXT)
            else:
                cmul(WxT5[0][:, :, tau, :], WxT5[1][:, :, tau, :], bc16(PW[:, k, 0, :]), bc16(PW[:, k, 1, :]),
                     Bbar[:, 0], Bbar[:, 1], ['PW'], ID_WXT)
        for t in range(8):
            cmul(WinT5[0][:, :, t, :], WinT5[1][:, :, t, :], cc[:, 0], cc[:, 1],
                 bc16(PW[:, t + 1, 0, :]), bc16(PW[:, t + 1, 1, :]), ['PW', 'bbcc'], ID_WIN, neg_im=True)
        tt(sm(T1), PW[:, 8, 0, :], PW[:, 8, 0, :], ALU.mult, ['PW'], [S])
        tt(sm(T2), PW[:, 8, 1, :], PW[:, 8, 1, :], ALU.mult, ['PW'], [S])
        tt(sm(T1), sm(T1), sm(T2), ALU.add, [S], [S])
        dv(lambda e: e.reciprocal(out=sm(T3), in_=sm(T1)), [S], [S])
        tt(sm(IR), PW[:, 8, 0, :], sm(T3), ALU.mult, [S, 'PW'], [S])
        dv(lambda e: e.scalar_tensor_tensor(out=sm(II), in0=PW[:, 8, 1, :], scalar=-1.0, in1=sm(T3), op0=ALU.mult, op1=ALU.mult), [S, 'PW'], [S])
        for hq in range(2):
            qs = slice(hq * 8, hq * 8 + 8)
            tA = tmpf[:, 0, 0:1024].rearrange("p (q c) -> p q c", q=8)
            tB = tmpf[:, 1, 0:1024].rearrange("p (q c) -> p q c", q=8)
            irb = sm(IR)[:, qs].unsqueeze(2).to_broadcast([128, 8, 128])
            iib = sm(II)[:, qs].unsqueeze(2).to_broadcast([128, 8, 128])
            TT = [('tmpf', 0), ('tmpf', 1)]
            tt(tA, WxT[:, 0, qs, :], irb, ALU.mult, [S] + ID_WXT, TT)
            tt(tB, WxT[:, 1, qs, :], iib, ALU.mult, [S] + ID_WXT, TT)
            tt(Kpp[:, 0, qs, :], tA, tB, ALU.subtract, TT, ID_KPP)
            tt(tA, WxT[:, 1, qs, :], irb, ALU.mult, [S] + ID_WXT, TT)
            tt(tB, WxT[:, 0, qs, :], iib, ALU.mult, [S] + ID_WXT, TT)
            tt(Kpp[:, 1, qs, :], tA, tB, ALU.add, TT, ID_KPP)
        xsb = xs[:].rearrange("p a b -> p (a b)").bitcast(BF16)
        Kpp_bf = xsb[:, 0:4096].rearrange("p (r q c) -> p r q c", r=2, q=NPAIR)
        WxT_bf = xsb[:, 4096:8192].rearrange("p (r q c) -> p r q c", r=2, q=NPAIR)
        ID_XB = [('xs', j) for j in range(4)]
        dv(lambda e: e.memset(WinterZ[:], 0.0), [], ['Winter'])
        dv(lambda e: e.tensor_copy(out=ident_bf[:], in_=ident[:]), ['ident'], ['identbf'])
        for r in range(2):
            for g2 in range(2):
                rows = slice(g2 * 64, g2 * 64 + 64)
                dv(lambda e, r=r, g2=g2, rows=rows: e.tensor_copy(out=WinterZ[rows, g2, r], in_=WinT[rows, r]), ID_WIN, ['Winter'])
            dv(lambda e, r=r: e.tensor_copy(out=Kpp_bf[:, r], in_=Kpp[:, r]), ID_KPP, ID_XB)
            dv(lambda e, r=r: e.tensor_copy(out=WxT_bf[:, r], in_=WxT[:, r]), ID_WXT, ID_XB)
        P.skip = SETUP < 3
        for gb in range(8):
            b = next_bank()
            for j in range(4):
                g = gb * 4 + j
                q, g2 = g // 2, g % 2
                rows = slice(g2 * 64, g2 * 64 + 64)
                P.add('pe', lambda e, b=b, j=j, q=q, g2=g2: e.matmul(
                    banks[b][:, j * 128:(j + 1) * 128], lhsT=Kpp_bf[:, 0, q, :], rhs=WinterZ[:, g2, 0, q, :], start=True, stop=False),
                    reads=ID_XB + ['Winter'], writes=[bank_b(b)], sig=False)
                P.add('pe', lambda e, b=b, j=j, q=q, g2=g2: e.matmul(
                    banks[b][:, j * 128:(j + 1) * 128], lhsT=Kpp_bf[:, 1, q, :], rhs=WinterZ[:, g2, 1, q, :], start=False, stop=True),
                    reads=ID_XB + ['Winter'], writes=[bank_b(b)], sig=True)
            for j in range(4):
                g = gb * 4 + j
                tA = tmpf[:, j % 2, 0:128]
                dv(lambda e, b=b, j=j, tA=tA: e.tensor_tensor(out=tA, in0=banks[b][:, j * 128:(j + 1) * 128], in1=maskf[:], op=ALU.mult),
                   [bank_b(b), 'maskf'], [('tmpf', j % 2)])
                dv(lambda e, g=g, tA=tA: e.scalar_tensor_tensor(out=Wintra[:, g, :], in0=ident[:], scalar=dd_sb[:, g:g + 1], in1=tA,
                                                              op0=ALU.mult, op1=ALU.add),
                   [('tmpf', j % 2), 'ident', 'dd'], ['Wintra'])
        P.skip = SETUP < 4
        for qb in range(8):
            b = next_bank()
            for j in range(4):
                q, r = qb * 2 + j // 2, j % 2
                P.add('pe', lambda e, b=b, j=j, q=q, r=r: e.matmul(banks[b][:, j * 128:(j + 1) * 128], lhsT=WxT_bf[:, r, q, :], rhs=ident_bf[:],
                                                              start=True, stop=True),
                      reads=ID_XB + ['identbf'], writes=[bank_b(b)])
            for j in range(4):
                q, r = qb * 2 + j // 2, j % 2
                dv(lambda e, b=b, j=j, q=q, r=r: e.tensor_copy(out=lhsTX[:, 2 * q, r, 0:64], in_=banks[b][:, j * 128:j * 128 + 64]),
                   [bank_b(b)], ['lhsTX'])
                dv(lambda e, b=b, j=j, q=q, r=r: e.tensor_copy(out=lhsTX[:, 2 * q + 1, r, 64:128], in_=banks[b][:, j * 128 + 64:j * 128 + 128]),
                   [bank_b(b)], ['lhsTX'])
        P.skip = SETUP < 5
        dv(lambda e: e.reciprocal(out=sm(RR8), in_=sm(R8)), [S], [S])
        tt(Tc[:, :, 0], PW[:, 8, 0, :], sm(RR8), ALU.mult, [S, 'PW'], ['Tc'])
        tt(Ts[:, :, 0], PW[:, 8, 1, :], sm(RR8), ALU.mult, [S, 'PW'], ['Ts'])
        for mlev in range(7):
            n = 1 << mlev
            cb = Tc[:, :, n - 1:n].to_broadcast([128, NPAIR, n])
            sbb = Ts[:, :, n - 1:n].to_broadcast([128, NPAIR, n])
            tA = tmpf[:, 0, 0:NPAIR * n].rearrange("p (q c) -> p q c", q=NPAIR)
            tB = tmpf[:, 1, 0:NPAIR * n].rearrange("p (q c) -> p q c", q=NPAIR)
            TT = [('tmpf', 0), ('tmpf', 1)]
            tt(tA, Tc[:, :, 0:n], cb, ALU.mult, ['Tc'], TT)
            tt(tB, Ts[:, :, 0:n], sbb, ALU.mult, ['Ts'], TT)
            tt(Tc[:, :, n:2 * n], tA, tB, ALU.subtract, TT, ['Tc'])
            tt(tA, Tc[:, :, 0:n], sbb, ALU.mult, ['Tc', 'Ts'], TT)
            tt(tB, Ts[:, :, 0:n], cb, ALU.mult, ['Tc', 'Ts'], TT)
            tt(Ts[:, :, n:2 * n], tA, tB, ALU.add, TT, ['Ts'])

    P.skip = False
    mod_finalize()
    reserved[0] = None
    Ush = cp[:, 4:8, :].rearrange("p a (g c) -> p (a g) c", g=8)
    Ssh = [cp[:, 8:10, :].rearrange("p a (q c) -> p (a q) c", q=8),
           cp[:, 10:12, :].rearrange("p a (q c) -> p (a q) c", q=8)]
    Gsh = cp[:, 12:16, :].rearrange("p a (g c) -> p (a g) c", g=8)
    swf = cp[:, 12:20, :].rearrange("p a b -> p (a b)").bitcast(F32).rearrange("p (w q c) -> p w q c", w=4, q=8)
    ID_SW = [('cp', j) for j in range(12, 20)]
    upf = cp[:, 8:17, :].rearrange("p a b -> p (a b)").bitcast(F32)[:, 0:4 * (NT + 16)].rearrange("p (g c) -> p g c", g=4)
    ID_UPF = [('cp', j) for j in range(8, 17)]
    TGS = [16, 17, 18, 19, 0, 1, 2, 3]

    def proj_slab(w_src, col0, ncols_tiles, consume):
        s = load_w([(lambda sa: sa.rearrange("p (k c) -> p k c", k=KT),
                     w_src[:, col0:col0 + 512].rearrange("(k p) c -> p k c", p=128))], None)
        wv = wsl[:, s, :].rearrange("p (k c) -> p k c", k=KT)
        for mi in range(4):
            for hf in range(NH):
                b = next_bank()
                for k in range(KT):
                    P.add('pe', lambda e, k=k, hf=hf, mi=mi, b=b, wv=wv: e.matmul(
                        banks[b][:], lhsT=wv[:, k, mi * 128:(mi + 1) * 128], rhs=hb[:, k, hf * 512:(hf + 1) * 512],
                        start=(k == 0), stop=(k == KT - 1)),
                        reads=[('slot', s), ('hb', k)], writes=[bank_b(b)], sig=(k == KT - 1))
                consume(mi, hf, b)

    def hs(hf):
        return slice(hf * 512, (hf + 1) * 512)

    def mixer(ti):
        if STAGE < 1:
            return
        def c_ussm(mi, hf, b):
            P.add('act', lambda e: e.activation(out=cp[:, mi, hs(hf)], in_=banks[b][:], func=AF.Identity),
                  reads=[bank_b(b)], writes=[('cp', mi)])
        proj_slab(w_in, 512, 4, c_ussm)
        if STAGE < 2:
            return
        for gb in range(8):
            b = next_bank()
            for j in range(4):
                g = gb * 4 + j
                blk, gi = g // 8, g % 8
                uv = cp[:, blk, :].rearrange("p (c t) -> p c t", t=8)
                for tau in range(8):
                    o = 240 * gi + 112 - 16 * tau
                    P.add('pe', lambda e, b=b, j=j, o=o, tau=tau, uv=uv: e.matmul(
                        banks[b][:, j * 128:(j + 1) * 128], lhsT=sel_bf[:, o:o + 128], rhs=uv[:, :, tau],
                        start=(tau == 0), stop=(tau == 7)),
                        reads=['sel', ('cp', blk)], writes=[bank_b(b)], sig=(tau == 7))
            P.add('act', lambda e, b=b, gb=gb: e.activation(
                out=Ush[:, gb * 4:gb * 4 + 4, :], in_=banks[b][:].rearrange("p (g c) -> p g c", g=4), func=AF.Identity),
                reads=[bank_b(b)], writes=[('cp', 4 + gb // 2)])
        if STAGE < 3:
            return
        A_, B_, C_, D_ = swf[:, 0], swf[:, 1], swf[:, 2], swf[:, 3]
        for hq in range(2):
            xb = [[next_bank(), next_bank()], [next_bank(), next_bank()]]
            for qq in range(8):
                q = hq * 8 + qq
                for r in range(2):
                    b = xb[r][qq // 4]
                    reg = banks[b][:, (qq % 4) * 128:(qq % 4 + 1) * 128]
                    for g2 in range(2):
                        g = 2 * q + g2
                        P.add('pe', lambda e, reg=reg, g=g, r=r, g2=g2: e.matmul(
                            reg, lhsT=lhsTX[:, g, r, :], rhs=Ush[:, g, :], start=(g2 == 0), stop=(g2 == 1)),
                            reads=['lhsTX', ('cp', 4 + g // 8)], writes=[bank_b(b)], sig=(g2 == 1))
            qs = slice(hq * 8, hq * 8 + 8)
            for bh in range(2):
                q4 = slice(bh * 4, bh * 4 + 4)
                qg = slice(hq * 8 + bh * 4, hq * 8 + bh * 4 + 4)
                Xr = banks[xb[0][bh]][:].rearrange("p (q c) -> p q c", q=4)
                Xi = banks[xb[1][bh]][:].rearrange("p (q c) -> p q c", q=4)
                rdb = [bank_b(xb[0][bh]), bank_b(xb[1][bh]), 'Tc', 'Ts'] + ID_SW
                tt(A_[:, q4, :], Xr, Tc[:, qg, :], ALU.mult, rdb, ID_SW)
                tt(B_[:, q4, :], Xi, Ts[:, qg, :], ALU.mult, rdb, ID_SW)
                tt(A_[:, q4, :], A_[:, q4, :], B_[:, q4, :], ALU.add, rdb, ID_SW)
                tt(C_[:, q4, :], Xi, Tc[:, qg, :], ALU.mult, rdb, ID_SW)
                tt(B_[:, q4, :], Xr, Ts[:, qg, :], ALU.mult, rdb, ID_SW)
                tt(C_[:, q4, :], C_[:, q4, :], B_[:, q4, :], ALU.subtract, rdb, ID_SW)
            for qq in range(8):
                q = hq * 8 + qq
                r8b = sm(R8)[:, q:q + 1].to_broadcast([128, 128])
                dv(lambda e, qq=qq, q=q, r8b=r8b: e.tensor_tensor_scan(
                    out=B_[:, qq, :], data0=r8b, data1=A_[:, qq, :], initial=carry[:, 0, q:q + 1], op0=ALU.mult, op1=ALU.add),
                    ID_SW + ['carry', 'smalls'], ID_SW)
                dv(lambda e, qq=qq, q=q, r8b=r8b: e.tensor_tensor_scan(
                    out=D_[:, qq, :], data0=r8b, data1=C_[:, qq, :], initial=carry[:, 1, q:q + 1], op0=ALU.mult, op1=ALU.add),
                    ID_SW + ['carry', 'smalls'], ID_SW)
            rd = ID_SW + ['Tc', 'Ts']
            tt(A_, B_, Tc[:, qs, :], ALU.mult, rd, ID_SW)
            tt(C_, D_, Ts[:, qs, :], ALU.mult, rd, ID_SW)
            tt(A_, A_, C_, ALU.subtract, rd, ID_SW)
            tt(C_, B_, Ts[:, qs, :], ALU.mult, rd, ID_SW)
            tt(B_, D_, Tc[:, qs, :], ALU.mult, rd, ID_SW)
            tt(C_, C_, B_, ALU.add, rd, ID_SW)
            for r, src in ((0, A_), (1, C_)):
                sid = [('cp', 8 + 2 * r), ('cp', 9 + 2 * r)]
                dv(lambda e, r=r, src=src, qs=qs: e.tensor_copy(out=Ssh[r][:, qs, 1:128], in_=src[:, :, 0:127]), ID_SW, sid)
                dv(lambda e, r=r, qs=qs: e.tensor_copy(out=Ssh[r][:, qs, 0:1], in_=carry[:, r, qs].unsqueeze(2)), ['carry'], sid)
                dv(lambda e, r=r, src=src, qs=qs: e.tensor_copy(out=carry[:, r, qs].unsqueeze(2), in_=src[:, :, 127:128]), ID_SW, ['carry'])
        tap('s', 4, ti)
        if STAGE < 4:
            return
        for gb in range(8):
            b = next_bank()
            for j in range(4):
                g = gb * 4 + j
                q, g2 = g // 2, g % 2
                rows = slice(g2 * 64, g2 * 64 + 64)
                reg = banks[b][:, j * 128:(j + 1) * 128]
                rdl = ['Wintra', 'Winter', ('cp', 4 + g // 8), ('cp', 8), ('cp', 9), ('cp', 10), ('cp', 11)]
                P.add('pe', lambda e, reg=reg, g=g: e.matmul(reg, lhsT=Wintra[:, g, :], rhs=Ush[:, g, :], start=True, stop=False),
                      reads=rdl, writes=[bank_b(b)], sig=False)
                P.add('pe', lambda e, reg=reg, q=q, g2=g2: e.matmul(reg, lhsT=WinterZ[:, g2, 0, q, :], rhs=Ssh[0][:, q, :], start=False, stop=False),
                      reads=rdl, writes=[bank_b(b)], sig=False)
                P.add('pe', lambda e, reg=reg, q=q, g2=g2: e.matmul(reg, lhsT=WinterZ[:, g2, 1, q, :], rhs=Ssh[1][:, q, :], start=False, stop=True),
                      reads=rdl, writes=[bank_b(b)], sig=True)
            P.add('act', lambda e, b=b, gb=gb: e.activation(
                out=Gsh[:, gb * 4:gb * 4 + 4, :], in_=banks[b][:].rearrange("p (g c) -> p g c", g=4), func=AF.Gelu),
                reads=[bank_b(b)], writes=[('cp', 12 + gb // 2)])
        for blk in range(4):
            for th in range(2):
                b = next_bank()
                for j in range(4):
                    t = th * 4 + j
                    for gi in range(8):
                        o = 240 * t + 112 - 16 * gi
                        P.add('pe', lambda e, b=b, j=j, o=o, gi=gi, blk=blk: e.matmul(
                            banks[b][:, j * 128:(j + 1) * 128], lhsT=sel_bf[:, o:o + 128], rhs=Gsh[:, blk * 8 + gi, :],
                            start=(gi == 0), stop=(gi == 7)),
                            reads=['sel', ('cp', 12 + blk)], writes=[bank_b(b)], sig=(gi == 7))
                gv = cp[:, 16 + blk, :].rearrange("p (c t) -> p c t", t=8)[:, :, th * 4:th * 4 + 4]
                dv(lambda e, b=b, gv=gv: e.tensor_copy(out=gv, in_=banks[b][:].rearrange("p (t c) -> p c t", t=4)),
                   [bank_b(b)], [('cp', 16 + blk)])
        s = load_w([(lambda sa: sa.rearrange("p (k c) -> p k c", k=4), w_glu.rearrange("(k p) c -> p k c", p=128))], None)
        wv = wsl[:, s, :].rearrange("p (k c) -> p k c", k=4)
        for mi in range(4):
            for hf in range(NH):
                bv_, bg_ = next_bank(), next_bank()
                for part, b in ((0, bv_), (1, bg_)):
                    for k in range(4):
                        P.add('pe', lambda e, k=k, b=b, part=part, mi=mi, hf=hf, wv=wv: e.matmul(
                            banks[b][:], lhsT=wv[:, k, part * 512 + mi * 128: part * 512 + (mi + 1) * 128],
                            rhs=cp[:, 16 + k, hs(hf)], start=(k == 0), stop=(k == 3)),
                            reads=[('slot', s), ('cp', 16 + k)], writes=[bank_b(b)], sig=(k == 3))
                tb = (mi * NH + hf) % 2
                P.add('act', lambda e, b=bg_, mi=mi, tb=tb: e.activation(
                    out=cp[:, tb, 0:512], in_=banks[b][:], func=AF.Tanh, bias=bglu_h[:, 4 + mi:5 + mi], scale=0.5),
                    reads=[bank_b(bg_), 'bgluh'], writes=[('cp', tb)])
                dv(lambda e, b=bv_, mi=mi, tb=tb: e.tensor_scalar(
                    out=tmpf[:, tb, 0:512], in0=banks[b][:], scalar1=bglu_sb[:, mi:mi + 1], scalar2=0.5, op0=ALU.add, op1=ALU.mult),
                    [bank_b(bv_), 'bglu'], [('tmpf', tb)])
                dv(lambda e, mi=mi, hf=hf, tb=tb: e.scalar_tensor_tensor(
                    out=cp[:, 20 + mi, hs(hf)], in0=cp[:, tb, 0:512], scalar=1.0, in1=tmpf[:, tb, 0:512], op0=ALU.add, op1=ALU.mult),
                    [('cp', tb), ('tmpf', tb)], [('cp', 20 + mi)])
        tap('v', 16, ti)
        if STAGE < 5:
            return
        W16 = NT + 16
        dv(lambda e: e.tensor_copy(out=upf[:, :, 0:16], in_=halo[:]), ['halo'], ID_UPF)

        def c_upool(mi, hf, b):
            P.add('act', lambda e: e.activation(out=upf[:, mi, 16 + hf * 512:16 + (hf + 1) * 512], in_=banks[b][:], func=AF.Identity),
                  reads=[bank_b(b)], writes=ID_UPF)
        proj_slab(w_in, 0, 4, c_upool)
        dv(lambda e: e.tensor_copy(out=halo[:], in_=upf[:, :, NT:NT + 16]), ID_UPF, ['halo'])
        for kg in range(4):
            w = 2 << kg
            src = upf[:, kg, :]
            TT = [('tmpf', 0), ('tmpf', 1)]
            for j in range(kg + 1):
                d = 1 << j
                lo = 2 * d - 1
                dst = tmpf[:, j % 2, :]
                P.add('dve', lambda e, dst=dst, src=src, d=d, lo=lo: e.tensor_tensor(
                    out=dst[:, lo:W16], in0=src[:, lo:W16], in1=src[:, lo - d:W16 - d], op=ALU.add),
                    reads=ID_UPF + TT, writes=[('tmpf', j % 2)])
                src = dst
            ssum = src
            dv(lambda e, kg=kg, w=w, ssum=ssum: e.scalar_tensor_tensor(
                out=cp[:, kg, :], in0=ssum[:, 16:W16], scalar=1.0 / w, in1=upf[:, kg, 16:W16], op0=ALU.mult, op1=ALU.subtract),
                ID_UPF + TT, [('cp', kg)])
            if ti == 0:
                dv(lambda e, kg=kg, ssum=ssum: e.tensor_tensor(out=ssum[:, 16:32], in0=ssum[:, 16:32], in1=rc_sb[:, kg, :], op=ALU.mult),
                   TT + ['rc'], TT)
                dv(lambda e, kg=kg, ssum=ssum: e.tensor_tensor(out=cp[:, kg, 0:16], in0=ssum[:, 16:32], in1=upf[:, kg, 16:32], op=ALU.subtract),
                   TT + ID_UPF, [('cp', kg)])
        for kg in range(4):
            for hf in range(NH):
                b = next_bank()
                P.add('pe', lambda e, kg=kg, hf=hf, b=b: e.matmul(banks[b][:], lhsT=poolw_bf[:, kg, :], rhs=cp[:, kg, hs(hf)], start=True, stop=True),
                      reads=['poolw', ('cp', kg)], writes=[bank_b(b)])
                P.add('act', lambda e, kg=kg, hf=hf, b=b: e.activation(
                    out=cp[:, 4 + kg, hs(hf)], in_=banks[b][:], func=AF.Identity, bias=pbsc[:, kg:kg + 1], scale=pbs_sb[:, 1, kg:kg + 1]),
                    reads=[bank_b(b), 'pbsc', 'pbs'], writes=[('cp', 4 + kg)])
        tap('z', 0, ti)
        if STAGE < 6:
            return
        for gi in range(2):
            def c_gp(mi, hf, b, gi=gi):
                P.add('act', lambda e: e.activation(out=cp[:, 8 + gi * 4 + mi, hs(hf)], in_=banks[b][:], func=AF.Tanh, scale=0.5),
                      reads=[bank_b(b)], writes=[('cp', 8 + gi * 4 + mi)])
            proj_slab(w_in, 1024 + gi * 512, 4, c_gp)
        for gi in range(2):
            def c_gs(mi, hf, b, gi=gi):
                ch = TGS[gi * 4 + mi]
                P.add('act', lambda e: e.activation(out=cp[:, ch, hs(hf)], in_=banks[b][:], func=AF.Tanh, scale=0.5),
                      reads=[bank_b(b)], writes=[('cp', ch)])
            proj_slab(w_in, 2048 + gi * 512, 4, c_gs)
        s_pu = load_w([(lambda sa: sa.rearrange("p (k c) -> p k c", k=4), w_pu.rearrange("(k p) c -> p k c", p=128))], None)
        s_su = load_w([(lambda sa: sa.rearrange("p (k c) -> p k c", k=4), w_su.rearrange("(k p) c -> p k c", p=128))], None)
        wpu = wsl[:, s_pu, :].rearrange("p (k c) -> p k c", k=4)
        wsu = wsl[:, s_su, :].rearrange("p (k c) -> p k c", k=4)
        for m in range(8):
            for hf in range(NH):
                bp, bs_ = next_bank(), next_bank()
                for k in range(4):
                    P.add('pe', lambda e, k=k, m=m, hf=hf, b=bp: e.matmul(
                        banks[b][:], lhsT=wpu[:, k, m * 128:(m + 1) * 128], rhs=cp[:, 4 + k, hs(hf)], start=(k == 0), stop=(k == 3)),
                        reads=[('slot', s_pu), ('cp', 4 + k)], writes=[bank_b(bp)], sig=(k == 3))
                for k in range(4):
                    P.add('pe', lambda e, k=k, m=m, hf=hf, b=bs_: e.matmul(
                        banks[b][:], lhsT=wsu[:, k, m * 128:(m + 1) * 128], rhs=cp[:, 20 + k, hs(hf)], start=(k == 0), stop=(k == 3)),
                        reads=[('slot', s_su), ('cp', 20 + k)], writes=[bank_b(bs_)], sig=(k == 3))
                chs = TGS[m]
                dv(lambda e, m=m, hf=hf, b=bp: e.scalar_tensor_tensor(
                    out=tmpf[:, 0, 0:512], in0=cp[:, 8 + m, hs(hf)], scalar=1.0, in1=banks[b][:], op0=ALU.add, op1=ALU.mult),
                    [bank_b(bp), ('cp', 8 + m)], [('tmpf', 0)])
                dv(lambda e, chs=chs, hf=hf, b=bs_: e.scalar_tensor_tensor(
                    out=tmpf[:, 1, 0:512], in0=cp[:, chs, hs(hf)], scalar=1.0, in1=banks[b][:], op0=ALU.add, op1=ALU.mult),
                    [bank_b(bs_), ('cp', chs)], [('tmpf', 1)])
                P.add('dve', lambda e, m=m, hf=hf: e.tensor_tensor(
                    out=cp[:, 8 + m, hs(hf)], in0=tmpf[:, 0, 0:512], in1=tmpf[:, 1, 0:512], op=ALU.add),
                    reads=[('tmpf', 0), ('tmpf', 1)], writes=[('cp', 8 + m)])
        tap('m', 8, ti)
        sqb = sq_begin()
        pend = []
        for m2 in range(2):
            s = load_w([(lambda sa: sa.rearrange("p (k c) -> p k c", k=KT),
                         w_o[:, m2 * 512:(m2 + 1) * 512].rearrange("(k p) c -> p k c", p=128))], None)
            wv = wsl[:, s, :].rearrange("p (k c) -> p k c", k=KT)
            for mm in range(4):
                m = m2 * 4 + mm
                for hf in range(NH):
                    b = next_bank()
                    for k in range(KT):
                        P.add('pe', lambda e, k=k, mm=mm, hf=hf, b=b, wv=wv: e.matmul(
                            banks[b][:], lhsT=wv[:, k, mm * 128:(mm + 1) * 128], rhs=cp[:, 8 + k, hs(hf)], start=(k == 0), stop=(k == KT - 1)),
                            reads=[('slot', s), ('cp', 8 + k)], writes=[bank_b(b)], sig=(k == KT - 1))
                    dv(lambda e, m=m, hf=hf, b=b: e.scalar_tensor_tensor(
                        out=xs[:, m, hs(hf)], in0=banks[b][:], scalar=Gvec[:, 1, m:m + 1], in1=xs[:, m, hs(hf)], op0=ALU.mult, op1=ALU.add),
                        [bank_b(b), 'Gvec', ('xs', m)], [('xs', m)])
                pend.append(m)
                if len(pend) > 1:
                    sq_step(sqb, pend.pop(0))
        while pend:
            sq_step(sqb, pend.pop(0))
        return sqb

    for ti in range(NTILES):
        t0 = ti * NT
        for k in range(KT):
            P.add('sp', lambda e, k=k, t0=t0: e.dma_start(out=xs[:, k, :], in_=xT[k * 128:(k + 1) * 128, t0:t0 + NT]),
                  writes=[('xs', k)], dma=('xs', k))
        norm_mod(0, t0)
        pre = ffn(0, w1i, w1o)
        norm_mod(1, t0, pre)
        pre = mixer(ti)
        norm_mod(2, t0, pre)
        pre = ffn(2, w2i, w2o)
        norm_mod(3, t0, pre)

    with nc.Block() as block:
        P.emit(block)
    return nc, es, P


_CACHE = {}


def _consts():
    ident = np.eye(128, dtype=np.float32)
    sel = np.zeros((128, 1920), np.float32)
    for p in range(128):
        sel[p, 240 * (p // 16) + 112 + (p % 16)] = 1.0
    mask = np.zeros((128, 128), np.float32)
    for a in range(8):
        for b in range(8):
            if b >= a:
                mask[a * 16:(a + 1) * 16, b * 16:(b + 1) * 16] = 1.0
    rc = np.zeros((128, 4, 16), np.float32)
    for k in range(4):
        w = 2 << k
        for tt_ in range(16):
            rc[:, k, tt_] = 1.0 / min(tt_ + 1, w)
    return ident, sel, mask, rc


def _pair_layout(a):
    g, n = a.shape[0], a.shape[1]
    rest = a.shape[2:]
    a = a.reshape((16, 2, n) + rest)
    a = np.moveaxis(a, 0, 2)
    return np.ascontiguousarray(a.reshape((128, 16) + rest))


def make_in_maps(inputs):
    f = lambda a: np.ascontiguousarray(np.asarray(a, dtype=np.float32))
    x = f(inputs["x"])
    c = f(inputs["c"])
    ident, sel, mask, rc = _consts()
    gvec = np.stack([f(inputs["g_ffn1"])[0], f(inputs["g_mix"])[0], f(inputs["g_ffn2"])[0], f(inputs["g_final"])], 0)
    gvec = np.ascontiguousarray(gvec.reshape(4, KT, 128).transpose(2, 0, 1))
    b_ada = np.ascontiguousarray(f(inputs["b_ada"])[0].reshape(72, 128).T)
    pool_w = np.ascontiguousarray(f(inputs["pool_w"])[0].transpose(1, 0, 2))
    pool_bs = np.ascontiguousarray(np.stack([f(inputs["pool_b"])[0].reshape(4, 128).T,
                                             f(inputs["pool_scale"])[0].reshape(4, 128).T], 1))
    b_glu = np.ascontiguousarray(f(inputs["b_glu"])[0].reshape(8, 128).T)
    lrl = f(inputs["ssm_lam_re_log"])[0]
    lim = f(inputs["ssm_lam_im"])[0]
    ldt = np.broadcast_to(f(inputs["ssm_log_dt"])[0][:, None], (32, 64))
    ssm_sc = np.ascontiguousarray(np.stack([_pair_layout(lrl), _pair_layout(lim), _pair_layout(ldt)], 1))
    ssm_b = np.ascontiguousarray(np.stack([_pair_layout(f(inputs["ssm_b_re"])[0]),
                                           _pair_layout(f(inputs["ssm_b_im"])[0])], 1))
    cre = f(inputs["ssm_c_re"])[0].transpose(0, 2, 1)
    cim = f(inputs["ssm_c_im"])[0].transpose(0, 2, 1)
    ssm_cc = np.ascontiguousarray(np.stack([_pair_layout(cre), _pair_layout(cim)], 1))
    dd = f(inputs["ssm_d"])[0].reshape(32, 16)
    ssm_dd = np.ascontiguousarray(np.tile(dd.T, (8, 1)))
    shared = dict(
        w_ada=f(inputs["w_ada"])[0], b_ada=b_ada, gvec=gvec,
        w1i=f(inputs["w_ffn1_in"])[0], w1o=f(inputs["w_ffn1_out"])[0],
        w2i=f(inputs["w_ffn2_in"])[0], w2o=f(inputs["w_ffn2_out"])[0],
        w_in=f(inputs["w_in"])[0], pool_w=pool_w, pool_bs=pool_bs,
        w_pu=f(inputs["w_pool_up"])[0], w_glu=f(inputs["w_glu"])[0], b_glu=b_glu,
        w_su=f(inputs["w_ssm_up"])[0], w_o=f(inputs["w_out"])[0],
        ssm_sc=ssm_sc, ssm_b=ssm_b, ssm_cc=ssm_cc, ssm_dd=ssm_dd,
        cst_ident=ident, cst_sel=sel, cst_mask=mask, cst_rc=rc,
    )
    maps = []
    for b in range(8):
        m = dict(shared)
        m["xT"] = np.ascontiguousarray(x[b].T)
        m["c_l"] = np.ascontiguousarray(c[b].reshape(KT, 128).T)
        maps.append(m)
    return maps


def kernel(**inputs):
    if "nc" not in _CACHE:
        _CACHE["nc"] = build_nc()
    nc, es, P = _CACHE["nc"]
    in_maps = make_in_maps(inputs)
    res = run_bass_kernel_spmd(nc, in_maps, core_ids=list(range(8)))
    out = np.stack([np.asarray(r["outT"], dtype=np.float32).T for r in res.results], 0)
    return np.ascontiguousarray(out)
```

```python
import numpy as np
from contextlib import ExitStack
import concourse.bass as bass
import concourse.mybir as mybir
from concourse.bass_utils import run_bass_kernel_spmd

F32 = mybir.dt.float32
BF16 = mybir.dt.bfloat16
AF = mybir.ActivationFunctionType
ALU = mybir.AluOpType

D = 1024
T = 4096
FF = 2816
NT = 1024
import os
NTILES = int(os.environ.get('KTILES', str(T // NT)))
NH = NT // 512
KT = D // 128
FT = FF // 128
EPS = 1e-6
LCH = 8
NC_T = NT // LCH
NPAIR = 16

import os
MIXER = True
STAGE = int(os.environ.get('KSTAGE', '9'))
SETUP = int(os.environ.get('KSETUP', '9'))
SUB = int(os.environ.get('KSUB', '9'))


class Prog:
    def __init__(self, nc, es):
        self.nc = nc
        self.es = es
        self.ops = []
        self.last_w = {}
        self.readers = {}
        self.dma_keys = {}

    skip = False

    def add(self, eng, fn, reads=(), writes=(), dma=None, sig=True):
        if self.skip:
            return None
        idx = len(self.ops)
        deps = set()
        for b in reads:
            if b in self.last_w:
                deps.add(self.last_w[b])
        for b in writes:
            if b in self.last_w:
                deps.add(self.last_w[b])
            for r in self.readers.get(b, ()):
                deps.add(r)
        for b in writes:
            self.last_w[b] = idx
            self.readers[b] = []
        for b in reads:
            self.readers.setdefault(b, []).append(idx)
        deps.discard(idx)
        self.ops.append(dict(eng=eng, fn=fn, deps=deps, dma=dma, sig=sig))
        return idx

    def emit(self, block):
        nc = self.nc
        ops = self.ops
        engs = ['pe', 'act', 'dve', 'pool', 'sp']
        sems = {e: self.es.enter_context(nc.semaphore(f"s_{e}")) for e in ['pe', 'act', 'dve', 'pool']}
        keysem = {}
        for o in ops:
            if o['dma'] is not None and o['dma'] not in keysem:
                keysem[o['dma']] = self.es.enter_context(nc.semaphore(f"d_{len(keysem)}"))
        cnt = {e: 0 for e in sems}
        for o in ops:
            if o['dma'] is None:
                if o['sig']:
                    cnt[o['eng']] += 1
                    o['cnt'] = cnt[o['eng']]
                else:
                    o['cnt'] = None
        nxt = {e: None for e in sems}
        for o in reversed(ops):
            if o['dma'] is None:
                if o['cnt'] is None:
                    o['cnt'] = nxt[o['eng']]
                    assert o['cnt'] is not None
                else:
                    nxt[o['eng']] = o['cnt']
        cum = {k: 0 for k in keysem}
        waited = {}
        per_eng = {e: [] for e in engs}
        for i, o in enumerate(ops):
            waits = {}
            for j in o['deps']:
                d = ops[j]
                if d['dma'] is not None:
                    s = keysem[d['dma']]
                    v = cum[d['dma']]
                else:
                    if d['eng'] == 'pe' and o['eng'] == 'pe' and o['dma'] is None:
                        continue
                    s = sems[d['eng']]
                    v = d['cnt']
                key = (o['eng'], id(s))
                if waited.get(key, 0) >= v:
                    continue
                if key not in waits or waits[key][1] < v:
                    waits[key] = (s, v)
            for key, (s, v) in waits.items():
                waited[key] = v
            if o['dma'] is not None:
                cum[o['dma']] += 16
            per_eng[o['eng']].append((o, list(waits.values())))
        self.n_ops = {e: len(per_eng[e]) for e in engs}

        def run(e, lst):
            for o, waits in lst:
                for (s, v) in waits:
                    e.wait_ge(s, v)
                ins = o['fn'](e)
                if o['dma'] is not None:
                    ins.then_inc(keysem[o['dma']], 16)
                elif o['sig']:
                    ins.then_inc(sems[o['eng']], 1)

        @block.tensor
        def _(e):
            run(e, per_eng['pe'])

        @block.scalar
        def _(e):
            run(e, per_eng['act'])

        @block.vector
        def _(e):
            run(e, per_eng['dve'])

        @block.gpsimd
        def _(e):
            run(e, per_eng['pool'])

        @block.sync
        def _(e):
            run(e, per_eng['sp'])
            for k, s in keysem.items():
                if isinstance(k, str) and k.startswith('out'):
                    e.wait_ge(s, cum[k])


def build_nc(debug=False):
    nc = bass.Bass("TRN2", target_bir_lowering=False)
    es = ExitStack()

    def din(name, shape, dt=F32):
        return nc.dram_tensor(name, list(shape), dt, kind="ExternalInput").ap()

    xT = din("xT", [D, T])
    c_l = din("c_l", [128, KT])
    w_ada = din("w_ada", [D, 9 * D])
    b_ada = din("b_ada", [128, 72])
    gvec = din("gvec", [128, 4, KT])
    w1i = din("w1i", [D, 2 * FF])
    w1o = din("w1o", [FF, D])
    w2i = din("w2i", [D, 2 * FF])
    w2o = din("w2o", [FF, D])
    w_in = din("w_in", [D, 3 * D])
    pool_w = din("pool_w", [128, 4, 128])
    pool_bs = din("pool_bs", [128, 2, 4])
    w_pu = din("w_pu", [512, D])
    w_glu = din("w_glu", [512, D])
    b_glu = din("b_glu", [128, 8])
    w_su = din("w_su", [512, D])
    w_o = din("w_o", [D, D])
    ssm_sc = din("ssm_sc", [128, 3, NPAIR])
    ssm_b = din("ssm_b", [128, 2, NPAIR, 16])
    ssm_cc = din("ssm_cc", [128, 2, NPAIR, 16])
    ssm_dd = din("ssm_dd", [128, 32])
    cst_ident = din("cst_ident", [128, 128])
    cst_sel = din("cst_sel", [128, 1920])
    cst_mask = din("cst_mask", [128, 128])
    outT = nc.dram_tensor("outT", [D, T], F32, kind="ExternalOutput").ap()
    DBG = os.environ.get('KDBG', '0') == '1'
    dbgt = {}
    if DBG:
        for nm in ('s', 'v', 'z', 'm'):
            dbgt[nm] = nc.dram_tensor("dbg_" + nm, [128, 8, NT], BF16, kind="ExternalOutput").ap()

    def tap(nm, j0, ti):
        if DBG and ti == 0:
            P.add('sp', lambda e: e.dma_start(out=dbgt[nm], in_=cp[:, j0:j0 + 8, :]), reads=[('cp', j) for j in range(j0, j0 + 8)], dma='out_' + nm)

    def sb(name, shape, dt=F32):
        return es.enter_context(nc.sbuf_tensor(name, list(shape), dt))

    def ps(name, shape, dt=F32):
        return es.enter_context(nc.psum_tensor(name, list(shape), dt))

    P = Prog(nc, es)

    xs = sb("xs", [128, KT, NT])
    hb = sb("hb", [128, KT, NT], BF16)
    NCHK = 24
    cp = sb("cp", [128, NCHK, NT], BF16)
    tmpf = sb("tmpf", [128, 2, NT + 16])
    rstd = sb("rstd", [128, NT])
    NSLOT = 4
    wsl = sb("wsl", [128, NSLOT, 4096], BF16)
    ones_bf = sb("ones_bf", [128, 128], BF16)
    ident = sb("ident", [128, 128])
    eps_sb = sb("eps_sb", [128, 1])
    modv = sb("modv", [128, 72])
    cl_sb = sb("cl_sb", [128, KT])
    sc_bf = sb("sc_bf", [128, KT], BF16)
    bada_sb = sb("bada_sb", [128, 72])
    gv_sb = sb("gv_sb", [128, 4, KT])
    Avec = sb("Avec", [128, 3, KT])
    Gvec = sb("Gvec", [128, 3, KT])
    banks = [ps(f"bank{i}", [128, 512]) for i in range(8)]
    ostage = cp[:].rearrange("p a b -> p (a b)").bitcast(F32)

    def bank_b(i):
        return ('bank', i)

    slot_use = [0]

    def load_w(src_ap_list, shape_views):
        s = slot_use[0] % NSLOT
        slot_use[0] += 1
        for dst_fn, src in src_ap_list:
            dst = dst_fn(wsl[:, s, :])
            P.add('pool', lambda e, dst=dst, src=src: e.dma_start(out=dst, in_=src),
                  writes=[('slot', s)], dma=('slot', s))
        return s

    bank_rr = [0]

    reserved = [None]

    def next_bank():
        b = bank_rr[0] % 8
        bank_rr[0] += 1
        if b == reserved[0]:
            return next_bank()
        return b

    def small_load(dst, src, key):
        P.add('sp', lambda e: e.dma_start(out=dst, in_=src), writes=[key], dma=key)

    small_load(cl_sb[:], c_l, 'cl')
    small_load(bada_sb[:], b_ada, 'bada')
    small_load(gv_sb[:], gvec, 'gv')
    small_load(ident[:], cst_ident, 'ident')
    P.add('dve', lambda e: e.memset(ones_bf[:], 1.0 / D), writes=['ones'])
    P.add('dve', lambda e: e.memset(eps_sb[:], EPS), writes=['eps'])

    P.add('act', lambda e: e.activation(out=sc_bf[:], in_=cl_sb[:], func=AF.Silu),
          reads=['cl'], writes=['sc'])
    mod_bank = next_bank()
    reserved[0] = mod_bank
    for sl in range(18):
        c0 = sl * 512
        s = load_w([(lambda sa: sa.rearrange("p (k c) -> p k c", k=KT),
                     w_ada[:, c0:c0 + 512].rearrange("(k p) c -> p k c", p=128))], None)
        wv = wsl[:, s, :].rearrange("p (k c) -> p k c", k=KT)
        for mm in range(4):
            m = sl * 4 + mm
            for k in range(KT):
                P.add('pe', lambda e, m=m, k=k, mm=mm, wv=wv: e.matmul(
                    banks[mod_bank][:, m:m + 1], lhsT=wv[:, k, mm * 128:(mm + 1) * 128],
                    rhs=sc_bf[:, k:k + 1], start=(k == 0), stop=(k == KT - 1)),
                    reads=[('slot', s), 'sc'], writes=[bank_b(mod_bank)], sig=(k == KT - 1))
    def mod_finalize():
        P.add('dve', lambda e: e.tensor_tensor(out=modv[:], in0=banks[mod_bank][:, 0:72], in1=bada_sb[:], op=ALU.add),
              reads=[bank_b(mod_bank), 'bada'], writes=['modv'])
        for sub in range(3):
            sc_ap = modv[:, (sub * 3 + 1) * 8:(sub * 3 + 1) * 8 + 8]
            gt_ap = modv[:, (sub * 3 + 2) * 8:(sub * 3 + 2) * 8 + 8]
            P.add('dve', lambda e, sub=sub, sc_ap=sc_ap: e.scalar_tensor_tensor(
                out=Avec[:, sub, :], in0=sc_ap, scalar=1.0, in1=gv_sb[:, sub, :], op0=ALU.add, op1=ALU.mult),
                reads=['modv', 'gv'], writes=['Avec'])
            P.add('dve', lambda e, sub=sub, gt_ap=gt_ap: e.tensor_scalar(
                out=Gvec[:, sub, :], in0=gt_ap, scalar1=0.5, scalar2=None, op0=ALU.mult),
                reads=['modv'], writes=['Gvec'])


    def Bvec(sub, k):
        return modv[:, (sub * 3) * 8 + k:(sub * 3) * 8 + k + 1]

    def norm_mod(sub, t0):
        for k in range(KT):
            P.add('act', lambda e, k=k: e.activation(out=hb[:, k, :], in_=xs[:, k, :], func=AF.Square),
                  reads=[('xs', k)], writes=[('hb', k)])
        bs = [next_bank() for _ in range(NH)]
        for hf in range(NH):
            for k in range(KT):
                P.add('pe', lambda e, k=k, hf=hf: e.matmul(
                    banks[bs[hf]][:], lhsT=ones_bf[:], rhs=hb[:, k, hf * 512:(hf + 1) * 512],
                    start=(k == 0), stop=(k == KT - 1)),
                    reads=['ones', ('hb', k)], writes=[bank_b(bs[hf])], sig=(k == KT - 1))
            P.add('act', lambda e, hf=hf: e.activation(
                out=rstd[:, hf * 512:(hf + 1) * 512], in_=banks[bs[hf]][:], func=AF.Sqrt,
                bias=eps_sb[:, 0:1], scale=1.0),
                reads=[bank_b(bs[hf]), 'eps'], writes=[('rstd', hf)])
            P.add('dve', lambda e, hf=hf: e.reciprocal(
                out=rstd[:, hf * 512:(hf + 1) * 512], in_=rstd[:, hf * 512:(hf + 1) * 512]),
                reads=[('rstd', hf)], writes=[('rstd', hf)])
        for k in range(KT):
            tb = k % 2
            if sub < 3:
                a_ap = Avec[:, sub, k:k + 1]
            else:
                a_ap = gv_sb[:, 3, k:k + 1]
            if sub == 3:
                P.add('dve', lambda e, k=k, a_ap=a_ap: e.scalar_tensor_tensor(
                    out=ostage[:, k * NT:(k + 1) * NT], in0=xs[:, k, :], scalar=a_ap, in1=rstd[:], op0=ALU.mult, op1=ALU.mult),
                    reads=[('xs', k), ('rstd', 0), ('rstd', 1), 'gv'], writes=[('cp', 2 * k), ('cp', 2 * k + 1)])
            else:
                P.add('dve', lambda e, k=k, tb=tb, a_ap=a_ap: e.scalar_tensor_tensor(
                    out=tmpf[:, tb, 0:NT], in0=xs[:, k, :], scalar=a_ap, in1=rstd[:], op0=ALU.mult, op1=ALU.mult),
                    reads=[('xs', k), ('rstd', 0), ('rstd', 1), 'Avec', 'gv'], writes=[('tmpf', tb)])
            if sub < 3:
                P.add('act', lambda e, k=k, tb=tb: e.activation(
                    out=hb[:, k, :], in_=tmpf[:, tb, 0:NT], func=AF.Identity, bias=Bvec(sub, k), scale=1.0),
                    reads=[('tmpf', tb), 'modv'], writes=[('hb', k)])
            else:
                P.add('sp', lambda e, k=k: e.dma_start(
                    out=outT[k * 128:(k + 1) * 128, t0:t0 + NT], in_=ostage[:, k * NT:(k + 1) * NT]),
                    reads=[('cp', 2 * k), ('cp', 2 * k + 1)], dma='out')

    def ffn(sub, w_i, w_o_):
        for sl in range(FT // 2):
            f0 = sl * 2
            s = load_w([
                (lambda sa: sa.rearrange("p (k a c) -> p k a c", k=KT, a=2)[:, :, 0, :],
                 w_i[:, f0 * 128:f0 * 128 + 256].rearrange("(k p) c -> p k c", p=128)),
                (lambda sa: sa.rearrange("p (k a c) -> p k a c", k=KT, a=2)[:, :, 1, :],
                 w_i[:, FF + f0 * 128:FF + f0 * 128 + 256].rearrange("(k p) c -> p k c", p=128)),
            ], None)
            wv = wsl[:, s, :].rearrange("p (k a c) -> p k a c", k=KT, a=2)
            for ff in range(2):
                f = f0 + ff
                ba = [[next_bank() for _ in range(NH)] for _ in range(2)]
                for part in range(2):
                    for hf in range(NH):
                        b = ba[part][hf]
                        for k in range(KT):
                            P.add('pe', lambda e, k=k, hf=hf, part=part, ff=ff, b=b, wv=wv: e.matmul(
                                banks[b][:], lhsT=wv[:, k, part, ff * 128:(ff + 1) * 128],
                                rhs=hb[:, k, hf * 512:(hf + 1) * 512], start=(k == 0), stop=(k == KT - 1)),
                                reads=[('slot', s), ('hb', k)], writes=[bank_b(b)], sig=(k == KT - 1))
                tb = f % 2
                for hf in range(NH):
                    P.add('act', lambda e, hf=hf, tb=tb, b=ba[0][hf]: e.activation(
                        out=cp[:, 22 + tb, hf * 512:(hf + 1) * 512], in_=banks[b][:], func=AF.Silu),
                        reads=[bank_b(ba[0][hf])], writes=[('cp', 22 + tb)])
                    P.add('dve', lambda e, hf=hf, tb=tb, f=f, b=ba[1][hf]: e.tensor_tensor(
                        out=cp[:, f, hf * 512:(hf + 1) * 512], in0=banks[b][:],
                        in1=cp[:, 22 + tb, hf * 512:(hf + 1) * 512], op=ALU.mult),
                        reads=[bank_b(ba[1][hf]), ('cp', 22 + tb)], writes=[('cp', f)])
        for m2 in range(4):
            ss = []
            for fh in range(2):
                s = load_w([(lambda sa: sa[:, 0:11 * 256].rearrange("p (f c) -> p f c", f=11),
                             w_o_[fh * 1408:(fh + 1) * 1408, m2 * 256:(m2 + 1) * 256].rearrange(
                                 "(f p) c -> p f c", p=128))], None)
                ss.append(s)
            for mm in range(2):
                m = m2 * 2 + mm
                for hf in range(NH):
                    b = next_bank()
                    for f in range(FT):
                        s = ss[f // 11]
                        wv = wsl[:, s, 0:11 * 256].rearrange("p (f c) -> p f c", f=11)
                        P.add('pe', lambda e, f=f, hf=hf, mm=mm, b=b, wv=wv: e.matmul(
                            banks[b][:], lhsT=wv[:, f % 11, mm * 128:(mm + 1) * 128],
                            rhs=cp[:, f, hf * 512:(hf + 1) * 512], start=(f == 0), stop=(f == FT - 1)),
                            reads=[('slot', s), ('cp', f)], writes=[bank_b(b)], sig=(f == FT - 1))
                    P.add('dve', lambda e, m=m, hf=hf, b=b: e.scalar_tensor_tensor(
                        out=xs[:, m, hf * 512:(hf + 1) * 512], in0=banks[b][:], scalar=Gvec[:, sub, m:m + 1],
                        in1=xs[:, m, hf * 512:(hf + 1) * 512], op0=ALU.mult, op1=ALU.add),
                        reads=[bank_b(b), 'Gvec', ('xs', m)], writes=[('xs', m)])

    PI = float(np.pi)
    cst_rc = din("cst_rc", [128, 4, 16])
    lhsTX = sb("lhsTX", [128, 32, 2, 128], BF16)
    Wintra = sb("Wintra", [128, 32, 128], BF16)
    WinterZ = sb("WinterZ", [128, 2, 2, NPAIR, 128], BF16)
    ident_bf = sb("ident_bf", [128, 128], BF16)
    sel_bf = sb("sel_bf", [128, 1920], BF16)
    Tc = sb("Tc", [128, NPAIR, 128])
    Ts = sb("Ts", [128, NPAIR, 128])
    smalls = sb("smalls", [128, 30, NPAIR])
    PW = sb("PW", [128, 9, 2, NPAIR])
    carry = sb("carry", [128, 2, NPAIR])
    halo = sb("halo", [128, 4, 16])
    maskf = sb("maskf", [128, 128])
    dd_sb = sb("dd_sb", [128, 32])
    rc_sb = sb("rc_sb", [128, 4, 16])
    poolw_bf = sb("poolw_bf", [128, 4, 128], BF16)
    pbs_sb = sb("pbs_sb", [128, 2, 4])
    pbsc = sb("pbsc", [128, 4])
    bglu_sb = sb("bglu_sb", [128, 8])
    bglu_h = sb("bglu_h", [128, 8])
    scl = sb("scl", [128, 3, NPAIR])
    ki_sb = sb("ki_sb", [128, NPAIR], mybir.dt.int32)

    cpf = cp[:].rearrange("p a b -> p (a b)").bitcast(F32)
    hbf = hb[:].rearrange("p a b -> p (a b)").bitcast(F32)
    WxT = cpf[:, 0:4096].rearrange("p (r q c) -> p r q c", r=2, q=NPAIR)
    WinT = cpf[:, 4096:8192].rearrange("p (r q c) -> p r q c", r=2, q=NPAIR)
    bb = cpf[:, 8192:8704].rearrange("p (r q c) -> p r q c", r=2, q=NPAIR)
    cc = cpf[:, 8704:9216].rearrange("p (r q c) -> p r q c", r=2, q=NPAIR)
    Bbar = cpf[:, 9216:9728].rearrange("p (r q c) -> p r q c", r=2, q=NPAIR)
    tS = cpf[:, 9728:10752].rearrange("p (r q c) -> p r q c", r=4, q=NPAIR)
    Kpp = hbf.rearrange("p (r q c) -> p r q c", r=2, q=NPAIR)
    ID_WXT = [('cp', j) for j in range(0, 8)]
    ID_WIN = [('cp', j) for j in range(8, 16)]
    ID_SM = [('cp', j) for j in range(16, 24)]
    ID_KPP = [('hb', j) for j in range(KT)]

    def dv(fn, reads, writes):
        P.add('dve', fn, reads=reads, writes=writes)

    def tt(out, a, b, op, reads, writes):
        dv(lambda e: e.tensor_tensor(out=out, in0=a, in1=b, op=op), reads, writes)

    def sm(i):
        return smalls[:, i, :]

    if MIXER:
        small_load(scl[:], ssm_sc, 'scl')
        small_load(dd_sb[:], ssm_dd, 'dd')
        small_load(maskf[:], cst_mask, 'maskf')
        small_load(rc_sb[:], cst_rc, 'rc')
        small_load(pbs_sb[:], pool_bs, 'pbs')
        small_load(bglu_sb[:], b_glu, 'bglu')
        P.add('sp', lambda e: e.dma_start(out=bb, in_=ssm_b), writes=ID_SM, dma='bbcc')
        P.add('sp', lambda e: e.dma_start(out=cc, in_=ssm_cc), writes=ID_SM, dma='bbcc')
        P.add('pool', lambda e: e.dma_start(out=sel_bf[:], in_=cst_sel), writes=['sel'], dma='sel')
        P.add('pool', lambda e: e.dma_start(out=poolw_bf[:], in_=pool_w), writes=['poolw'], dma='poolw')
        dv(lambda e: e.memset(carry[:], 0.0), [], ['carry'])
        dv(lambda e: e.memset(halo[:], 0.0), [], ['halo'])
        dv(lambda e: e.memset(lhsTX[:], 0.0), [], ['lhsTX'])
        dv(lambda e: e.tensor_tensor(out=pbsc[:], in0=pbs_sb[:, 0, :], in1=pbs_sb[:, 1, :], op=ALU.mult), ['pbs'], ['pbsc'])
        dv(lambda e: e.tensor_scalar(out=bglu_h[:], in0=bglu_sb[:], scalar1=0.5, scalar2=None, op0=ALU.mult), ['bglu'], ['bgluh'])

        P.skip = SUB < 1
        S = 'smalls'
        LR, DT, XM, MAG, R8, ANG, YS, YC, SN, CS, AR, AI, NR, DEN, RDEN, FRE, FIM, T1, T2, T3, T4, IR, II, RR8 = range(24)
        P.add('act', lambda e: e.activation(out=sm(LR), in_=scl[:, 0, :], func=AF.Exp), reads=['scl'], writes=[S])
        P.add('act', lambda e: e.activation(out=sm(DT), in_=scl[:, 2, :], func=AF.Exp), reads=['scl'], writes=[S])
        dv(lambda e: e.tensor_scalar(out=sm(LR), in0=sm(LR), scalar1=-1.0, scalar2=None, op0=ALU.mult), [S], [S])
        tt(sm(XM), sm(LR), sm(DT), ALU.mult, [S], [S])
        P.add('act', lambda e: e.activation(out=sm(MAG), in_=sm(XM), func=AF.Exp), reads=[S], writes=[S])
        P.add('act', lambda e: e.activation(out=sm(R8), in_=sm(XM), func=AF.Exp, scale=8.0), reads=[S], writes=[S])
        tt(sm(ANG), scl[:, 1, :], sm(DT), ALU.mult, [S, 'scl'], [S])
        P.skip = SUB < 2

        def range_reduce(dst, off):
            dv(lambda e: e.tensor_scalar(out=sm(T4), in0=sm(ANG), scalar1=off, scalar2=None, op0=ALU.add), [S], [S])
            dv(lambda e: e.tensor_scalar(out=sm(T3), in0=sm(T4), scalar1=1.0 / (2 * PI), scalar2=None, op0=ALU.mult), [S], [S])
            dv(lambda e: e.tensor_copy(out=ki_sb[:], in_=sm(T3)), [S], ['ki'])
            dv(lambda e: e.tensor_copy(out=sm(T3), in_=ki_sb[:]), ['ki'], [S])
            dv(lambda e: e.scalar_tensor_tensor(out=sm(dst), in0=sm(T3), scalar=-2 * PI, in1=sm(T4), op0=ALU.mult, op1=ALU.add), [S], [S])
            dv(lambda e: e.tensor_scalar(out=sm(T3), in0=sm(dst), scalar1=PI, scalar2=None, op0=ALU.is_gt), [S], [S])
            dv(lambda e: e.scalar_tensor_tensor(out=sm(dst), in0=sm(T3), scalar=-2 * PI, in1=sm(dst), op0=ALU.mult, op1=ALU.add), [S], [S])
            dv(lambda e: e.tensor_scalar(out=sm(T3), in0=sm(dst), scalar1=-PI, scalar2=None, op0=ALU.is_lt), [S], [S])
            dv(lambda e: e.scalar_tensor_tensor(out=sm(dst), in0=sm(T3), scalar=2 * PI, in1=sm(dst), op0=ALU.mult, op1=ALU.add), [S], [S])
        range_reduce(YS, 0.0)
        P.skip = SUB < 3
        TH, ZZ, SS, CC, U1, U2 = 24, 25, 26, 27, 28, 29
        dv(lambda e: e.tensor_scalar(out=sm(TH), in0=sm(YS), scalar1=0.25, scalar2=None, op0=ALU.mult), [S], [S])
        tt(sm(ZZ), sm(TH), sm(TH), ALU.mult, [S], [S])
        dv(lambda e: e.memset(sm(SS), 1.0), [S], [S])
        dv(lambda e: e.memset(sm(CC), 1.0), [S], [S])
        for kk in (156.0, 110.0, 72.0, 42.0, 20.0, 6.0):
            tt(sm(SS), sm(SS), sm(ZZ), ALU.mult, [S], [S])
            dv(lambda e, kk=kk: e.tensor_scalar(out=sm(SS), in0=sm(SS), scalar1=-1.0 / kk, scalar2=1.0, op0=ALU.mult, op1=ALU.add), [S], [S])
        tt(sm(SS), sm(SS), sm(TH), ALU.mult, [S], [S])
        for kk in (182.0, 132.0, 90.0, 56.0, 30.0, 12.0, 2.0):
            tt(sm(CC), sm(CC), sm(ZZ), ALU.mult, [S], [S])
            dv(lambda e, kk=kk: e.tensor_scalar(out=sm(CC), in0=sm(CC), scalar1=-1.0 / kk, scalar2=1.0, op0=ALU.mult, op1=ALU.add), [S], [S])
        for _ in range(2):
            tt(sm(U1), sm(SS), sm(CC), ALU.mult, [S], [S])
            tt(sm(U2), sm(SS), sm(SS), ALU.mult, [S], [S])
            tt(sm(CC), sm(CC), sm(CC), ALU.mult, [S], [S])
            tt(sm(CC), sm(CC), sm(U2), ALU.subtract, [S], [S])
            dv(lambda e: e.tensor_scalar(out=sm(SS), in0=sm(U1), scalar1=2.0, scalar2=None, op0=ALU.mult), [S], [S])
        dv(lambda e: e.tensor_copy(out=sm(SN), in_=sm(SS)), [S], [S])
        dv(lambda e: e.tensor_copy(out=sm(CS), in_=sm(CC)), [S], [S])
        tt(sm(AR), sm(MAG), sm(CS), ALU.mult, [S], [S])
        tt(sm(AI), sm(MAG), sm(SN), ALU.mult, [S], [S])
        P.skip = SETUP < 2
        dv(lambda e: e.tensor_scalar(out=sm(NR), in0=sm(AR), scalar1=-1.0, scalar2=None, op0=ALU.add), [S], [S])
        tt(sm(T1), sm(LR), sm(LR), ALU.mult, [S], [S])
        tt(sm(T2), scl[:, 1, :], scl[:, 1, :], ALU.mult, [S, 'scl'], [S])
        tt(sm(DEN), sm(T1), sm(T2), ALU.add, [S], [S])
        dv(lambda e: e.reciprocal(out=sm(RDEN), in_=sm(DEN)), [S], [S])
        tt(sm(T1), sm(NR), sm(LR), ALU.mult, [S], [S])
        tt(sm(T2), sm(AI), scl[:, 1, :], ALU.mult, [S, 'scl'], [S])
        tt(sm(T1), sm(T1), sm(T2), ALU.add, [S], [S])
        tt(sm(FRE), sm(T1), sm(RDEN), ALU.mult, [S], [S])
        tt(sm(T1), sm(AI), sm(LR), ALU.mult, [S], [S])
        tt(sm(T2), sm(NR), scl[:, 1, :], ALU.mult, [S, 'scl'], [S])
        tt(sm(T1), sm(T1), sm(T2), ALU.subtract, [S], [S])
        tt(sm(FIM), sm(T1), sm(RDEN), ALU.mult, [S], [S])

        def bc16(ap2):
            return ap2.unsqueeze(2).to_broadcast([128, NPAIR, 16])

        def cmul(o_re, o_im, xr, xi, yr, yi, reads, writes, neg_im=False):
            t0, t1 = tS[:, 0], tS[:, 1]
            rd = reads + ID_SM
            wr = writes + ID_SM
            tt(t0, xr, yr, ALU.mult, rd, ID_SM)
            tt(t1, xi, yi, ALU.mult, rd, ID_SM)
            tt(o_re, t0, t1, ALU.subtract, rd, wr)
            tt(t0, xr, yi, ALU.mult, rd, ID_SM)
            tt(t1, xi, yr, ALU.mult, rd, ID_SM)
            if not neg_im:
                tt(o_im, t0, t1, ALU.add, rd, wr)
            else:
                dv(lambda e: e.scalar_tensor_tensor(out=o_im, in0=t0, scalar=-1.0, in1=t1, op0=ALU.mult, op1=ALU.subtract), rd, wr)

        cmul(Bbar[:, 0], Bbar[:, 1], bc16(sm(FRE)), bc16(sm(FIM)), bb[:, 0], bb[:, 1], [S, 'bbcc'], [])
        dv(lambda e: e.tensor_copy(out=PW[:, 1, 0, :], in_=sm(AR)), [S], ['PW'])
        dv(lambda e: e.tensor_copy(out=PW[:, 1, 1, :], in_=sm(AI)), [S], ['PW'])
        for k in range(1, 8):
            t0, t1 = sm(T1), sm(T2)
            tt(t0, PW[:, k, 0, :], sm(AR), ALU.mult, [S, 'PW'], [S])
            tt(t1, PW[:, k, 1, :], sm(AI), ALU.mult, [S, 'PW'], [S])
            tt(PW[:, k + 1, 0, :], t0, t1, ALU.subtract, [S], ['PW'])
            tt(t0, PW[:, k, 0, :], sm(AI), ALU.mult, [S, 'PW'], [S])
            tt(t1, PW[:, k, 1, :], sm(AR), ALU.mult, [S, 'PW'], [S])
            tt(PW[:, k + 1, 1, :], t0, t1, ALU.add, [S], ['PW'])
        WxT5 = [WxT[:, r].rearrange("p q (t h) -> p q t h", t=8) for r in range(2)]
        WinT5 = [WinT[:, r].rearrange("p q (t h) -> p q t h", t=8) for r in range(2)]
        for tau in range(8):
            k = 7 - tau
            if k == 0:
                dv(lambda e, tau=tau: e.tensor_copy(out=WxT5[0][:, :, tau, :], in_=Bbar[:, 0]), ID_SM, ID_WXT)
                dv(lambda e, tau=tau: e.tensor_copy(out=WxT5[1][:, :, tau, :], in_=Bbar[:, 1]), ID_SM, ID_WXT)
            else:
                cmul(WxT5[0][:, :, tau, :], WxT5[1][:, :, tau, :], bc16(PW[:, k, 0, :]), bc16(PW[:, k, 1, :]),
                     Bbar[:, 0], Bbar[:, 1], ['PW'], ID_WXT)
        for t in range(8):
            cmul(WinT5[0][:, :, t, :], WinT5[1][:, :, t, :], cc[:, 0], cc[:, 1],
                 bc16(PW[:, t + 1, 0, :]), bc16(PW[:, t + 1, 1, :]), ['PW', 'bbcc'], ID_WIN, neg_im=True)
        tt(sm(T1), PW[:, 8, 0, :], PW[:, 8, 0, :], ALU.mult, ['PW'], [S])
        tt(sm(T2), PW[:, 8, 1, :], PW[:, 8, 1, :], ALU.mult, ['PW'], [S])
        tt(sm(T1), sm(T1), sm(T2), ALU.add, [S], [S])
        dv(lambda e: e.reciprocal(out=sm(T3), in_=sm(T1)), [S], [S])
        tt(sm(IR), PW[:, 8, 0, :], sm(T3), ALU.mult, [S, 'PW'], [S])
        dv(lambda e: e.scalar_tensor_tensor(out=sm(II), in0=PW[:, 8, 1, :], scalar=-1.0, in1=sm(T3), op0=ALU.mult, op1=ALU.mult), [S, 'PW'], [S])
        for hq in range(2):
            qs = slice(hq * 8, hq * 8 + 8)
            tA = tmpf[:, 0, 0:1024].rearrange("p (q c) -> p q c", q=8)
            tB = tmpf[:, 1, 0:1024].rearrange("p (q c) -> p q c", q=8)
            irb = sm(IR)[:, qs].unsqueeze(2).to_broadcast([128, 8, 128])
            iib = sm(II)[:, qs].unsqueeze(2).to_broadcast([128, 8, 128])
            TT = [('tmpf', 0), ('tmpf', 1)]
            tt(tA, WxT[:, 0, qs, :], irb, ALU.mult, [S] + ID_WXT, TT)
            tt(tB, WxT[:, 1, qs, :], iib, ALU.mult, [S] + ID_WXT, TT)
            tt(Kpp[:, 0, qs, :], tA, tB, ALU.subtract, TT, ID_KPP)
            tt(tA, WxT[:, 1, qs, :], irb, ALU.mult, [S] + ID_WXT, TT)
            tt(tB, WxT[:, 0, qs, :], iib, ALU.mult, [S] + ID_WXT, TT)
            tt(Kpp[:, 1, qs, :], tA, tB, ALU.add, TT, ID_KPP)
        xsb = xs[:].rearrange("p a b -> p (a b)").bitcast(BF16)
        Kpp_bf = xsb[:, 0:4096].rearrange("p (r q c) -> p r q c", r=2, q=NPAIR)
        WxT_bf = xsb[:, 4096:8192].rearrange("p (r q c) -> p r q c", r=2, q=NPAIR)
        ID_XB = [('xs', j) for j in range(4)]
        dv(lambda e: e.memset(WinterZ[:], 0.0), [], ['Winter'])
        dv(lambda e: e.tensor_copy(out=ident_bf[:], in_=ident[:]), ['ident'], ['identbf'])
        for r in range(2):
            for g2 in range(2):
                rows = slice(g2 * 64, g2 * 64 + 64)
                dv(lambda e, r=r, g2=g2, rows=rows: e.tensor_copy(out=WinterZ[rows, g2, r], in_=WinT[rows, r]), ID_WIN, ['Winter'])
            dv(lambda e, r=r: e.tensor_copy(out=Kpp_bf[:, r], in_=Kpp[:, r]), ID_KPP, ID_XB)
            dv(lambda e, r=r: e.tensor_copy(out=WxT_bf[:, r], in_=WxT[:, r]), ID_WXT, ID_XB)
        P.skip = SETUP < 3
        for gb in range(8):
            b = next_bank()
            for j in range(4):
                g = gb * 4 + j
                q, g2 = g // 2, g % 2
                rows = slice(g2 * 64, g2 * 64 + 64)
                P.add('pe', lambda e, b=b, j=j, q=q, g2=g2: e.matmul(
                    banks[b][:, j * 128:(j + 1) * 128], lhsT=Kpp_bf[:, 0, q, :], rhs=WinterZ[:, g2, 0, q, :], start=True, stop=False),
                    reads=ID_XB + ['Winter'], writes=[bank_b(b)], sig=False)
                P.add('pe', lambda e, b=b, j=j, q=q, g2=g2: e.matmul(
                    banks[b][:, j * 128:(j + 1) * 128], lhsT=Kpp_bf[:, 1, q, :], rhs=WinterZ[:, g2, 1, q, :], start=False, stop=True),
                    reads=ID_XB + ['Winter'], writes=[bank_b(b)], sig=True)
            for j in range(4):
                g = gb * 4 + j
                tA = tmpf[:, j % 2, 0:128]
                dv(lambda e, b=b, j=j, tA=tA: e.tensor_tensor(out=tA, in0=banks[b][:, j * 128:(j + 1) * 128], in1=maskf[:], op=ALU.mult),
                   [bank_b(b), 'maskf'], [('tmpf', j % 2)])
                dv(lambda e, g=g, tA=tA: e.scalar_tensor_tensor(out=Wintra[:, g, :], in0=ident[:], scalar=dd_sb[:, g:g + 1], in1=tA,
                                                              op0=ALU.mult, op1=ALU.add),
                   [('tmpf', j % 2), 'ident', 'dd'], ['Wintra'])
        P.skip = SETUP < 4
        for qb in range(8):
            b = next_bank()
            for j in range(4):
                q, r = qb * 2 + j // 2, j % 2
                P.add('pe', lambda e, b=b, j=j, q=q, r=r: e.matmul(banks[b][:, j * 128:(j + 1) * 128], lhsT=WxT_bf[:, r, q, :], rhs=ident_bf[:],
                                                              start=True, stop=True),
                      reads=ID_XB + ['identbf'], writes=[bank_b(b)])
            for j in range(4):
                q, r = qb * 2 + j // 2, j % 2
                dv(lambda e, b=b, j=j, q=q, r=r: e.tensor_copy(out=lhsTX[:, 2 * q, r, 0:64], in_=banks[b][:, j * 128:j * 128 + 64]),
                   [bank_b(b)], ['lhsTX'])
                dv(lambda e, b=b, j=j, q=q, r=r: e.tensor_copy(out=lhsTX[:, 2 * q + 1, r, 64:128], in_=banks[b][:, j * 128 + 64:j * 128 + 128]),
                   [bank_b(b)], ['lhsTX'])
        P.skip = SETUP < 5
        dv(lambda e: e.reciprocal(out=sm(RR8), in_=sm(R8)), [S], [S])
        tt(Tc[:, :, 0], PW[:, 8, 0, :], sm(RR8), ALU.mult, [S, 'PW'], ['Tc'])
        tt(Ts[:, :, 0], PW[:, 8, 1, :], sm(RR8), ALU.mult, [S, 'PW'], ['Ts'])
        for mlev in range(7):
            n = 1 << mlev
            cb = Tc[:, :, n - 1:n].to_broadcast([128, NPAIR, n])
            sbb = Ts[:, :, n - 1:n].to_broadcast([128, NPAIR, n])
            tA = tmpf[:, 0, 0:NPAIR * n].rearrange("p (q c) -> p q c", q=NPAIR)
            tB = tmpf[:, 1, 0:NPAIR * n].rearrange("p (q c) -> p q c", q=NPAIR)
            TT = [('tmpf', 0), ('tmpf', 1)]
            tt(tA, Tc[:, :, 0:n], cb, ALU.mult, ['Tc'], TT)
            tt(tB, Ts[:, :, 0:n], sbb, ALU.mult, ['Ts'], TT)
            tt(Tc[:, :, n:2 * n], tA, tB, ALU.subtract, TT, ['Tc'])
            tt(tA, Tc[:, :, 0:n], sbb, ALU.mult, ['Tc', 'Ts'], TT)
            tt(tB, Ts[:, :, 0:n], cb, ALU.mult, ['Tc', 'Ts'], TT)
            tt(Ts[:, :, n:2 * n], tA, tB, ALU.add, TT, ['Ts'])

    P.skip = False
    mod_finalize()
    reserved[0] = None
    Ush = cp[:, 4:8, :].rearrange("p a (g c) -> p (a g) c", g=8)
    Ssh = [cp[:, 8:10, :].rearrange("p a (q c) -> p (a q) c", q=8),
           cp[:, 10:12, :].rearrange("p a (q c) -> p (a q) c", q=8)]
    Gsh = cp[:, 12:16, :].rearrange("p a (g c) -> p (a g) c", g=8)
    swf = cp[:, 12:20, :].rearrange("p a b -> p (a b)").bitcast(F32).rearrange("p (w q c) -> p w q c", w=4, q=8)
    ID_SW = [('cp', j) for j in range(12, 20)]
    upf = cp[:, 4:9, :].rearrange("p a b -> p (a b)")[:, 0:4 * (NT + 16)].rearrange("p (g c) -> p g c", g=4)
    ID_UPF = [('cp', j) for j in range(4, 9)]
    TGP = [0, 1, 2, 3, 20, 21, 22, 23]
    TGS = [4, 5, 6, 7, 8, 17, 18, 19]
    VCH = [12, 13, 14, 15]
    GLT = [4, 5]
    ZP = [9, 10, 11, 16]
    u16 = sb("u16", [128, 4, 16])

    def proj_slab(w_src, col0, ncols_tiles, consume):
        s = load_w([(lambda sa: sa.rearrange("p (k c) -> p k c", k=KT),
                     w_src[:, col0:col0 + 512].rearrange("(k p) c -> p k c", p=128))], None)
        wv = wsl[:, s, :].rearrange("p (k c) -> p k c", k=KT)
        for mi in range(4):
            for hf in range(NH):
                b = next_bank()
                for k in range(KT):
                    P.add('pe', lambda e, k=k, hf=hf, mi=mi, b=b, wv=wv: e.matmul(
                        banks[b][:], lhsT=wv[:, k, mi * 128:(mi + 1) * 128], rhs=hb[:, k, hf * 512:(hf + 1) * 512],
                        start=(k == 0), stop=(k == KT - 1)),
                        reads=[('slot', s), ('hb', k)], writes=[bank_b(b)], sig=(k == KT - 1))
                consume(mi, hf, b)

    def hs(hf):
        return slice(hf * 512, (hf + 1) * 512)

    def mixer(ti):
        if STAGE < 1:
            return
        def c_ussm(mi, hf, b):
            P.add('act', lambda e: e.activation(out=cp[:, mi, hs(hf)], in_=banks[b][:], func=AF.Identity),
                  reads=[bank_b(b)], writes=[('cp', mi)])
        proj_slab(w_in, 512, 4, c_ussm)
        if STAGE < 2:
            return
        for gb in range(8):
            b = next_bank()
            for j in range(4):
                g = gb * 4 + j
                blk, gi = g // 8, g % 8
                uv = cp[:, blk, :].rearrange("p (c t) -> p c t", t=8)
                for tau in range(8):
                    o = 240 * gi + 112 - 16 * tau
                    P.add('pe', lambda e, b=b, j=j, o=o, tau=tau, uv=uv: e.matmul(
                        banks[b][:, j * 128:(j + 1) * 128], lhsT=sel_bf[:, o:o + 128], rhs=uv[:, :, tau],
                        start=(tau == 0), stop=(tau == 7)),
                        reads=['sel', ('cp', blk)], writes=[bank_b(b)], sig=(tau == 7))
            P.add('act', lambda e, b=b, gb=gb: e.activation(
                out=Ush[:, gb * 4:gb * 4 + 4, :], in_=banks[b][:].rearrange("p (g c) -> p g c", g=4), func=AF.Identity),
                reads=[bank_b(b)], writes=[('cp', 4 + gb // 2)])
        if STAGE < 3:
            return
        A_, B_, C_, D_ = swf[:, 0], swf[:, 1], swf[:, 2], swf[:, 3]
        for hq in range(2):
            xb = [[next_bank(), next_bank()], [next_bank(), next_bank()]]
            for qq in range(8):
                q = hq * 8 + qq
                for r in range(2):
                    b = xb[r][qq // 4]
                    reg = banks[b][:, (qq % 4) * 128:(qq % 4 + 1) * 128]
                    for g2 in range(2):
                        g = 2 * q + g2
                        P.add('pe', lambda e, reg=reg, g=g, r=r, g2=g2: e.matmul(
                            reg, lhsT=lhsTX[:, g, r, :], rhs=Ush[:, g, :], start=(g2 == 0), stop=(g2 == 1)),
                            reads=['lhsTX', ('cp', 4 + g // 8)], writes=[bank_b(b)], sig=(g2 == 1))
            qs = slice(hq * 8, hq * 8 + 8)
            for bh in range(2):
                q4 = slice(bh * 4, bh * 4 + 4)
                qg = slice(hq * 8 + bh * 4, hq * 8 + bh * 4 + 4)
                Xr = banks[xb[0][bh]][:].rearrange("p (q c) -> p q c", q=4)
                Xi = banks[xb[1][bh]][:].rearrange("p (q c) -> p q c", q=4)
                rdb = [bank_b(xb[0][bh]), bank_b(xb[1][bh]), 'Tc', 'Ts'] + ID_SW
                tt(A_[:, q4, :], Xr, Tc[:, qg, :], ALU.mult, rdb, ID_SW)
                tt(B_[:, q4, :], Xi, Ts[:, qg, :], ALU.mult, rdb, ID_SW)
                tt(A_[:, q4, :], A_[:, q4, :], B_[:, q4, :], ALU.add, rdb, ID_SW)
                tt(C_[:, q4, :], Xi, Tc[:, qg, :], ALU.mult, rdb, ID_SW)
                tt(B_[:, q4, :], Xr, Ts[:, qg, :], ALU.mult, rdb, ID_SW)
                tt(C_[:, q4, :], C_[:, q4, :], B_[:, q4, :], ALU.subtract, rdb, ID_SW)
            for qq in range(8):
                q = hq * 8 + qq
                r8b = sm(R8)[:, q:q + 1].to_broadcast([128, 128])
                dv(lambda e, qq=qq, q=q, r8b=r8b: e.tensor_tensor_scan(
                    out=B_[:, qq, :], data0=r8b, data1=A_[:, qq, :], initial=carry[:, 0, q:q + 1], op0=ALU.mult, op1=ALU.add),
                    ID_SW + ['carry', 'smalls'], ID_SW)
                dv(lambda e, qq=qq, q=q, r8b=r8b: e.tensor_tensor_scan(
                    out=D_[:, qq, :], data0=r8b, data1=C_[:, qq, :], initial=carry[:, 1, q:q + 1], op0=ALU.mult, op1=ALU.add),
                    ID_SW + ['carry', 'smalls'], ID_SW)
            rd = ID_SW + ['Tc', 'Ts']
            tt(A_, B_, Tc[:, qs, :], ALU.mult, rd, ID_SW)
            tt(C_, D_, Ts[:, qs, :], ALU.mult, rd, ID_SW)
            tt(A_, A_, C_, ALU.subtract, rd, ID_SW)
            tt(C_, B_, Ts[:, qs, :], ALU.mult, rd, ID_SW)
            tt(B_, D_, Tc[:, qs, :], ALU.mult, rd, ID_SW)
            tt(C_, C_, B_, ALU.add, rd, ID_SW)
            for r, src in ((0, A_), (1, C_)):
                sid = [('cp', 8 + 2 * r), ('cp', 9 + 2 * r)]
                dv(lambda e, r=r, src=src, qs=qs: e.tensor_copy(out=Ssh[r][:, qs, 1:128], in_=src[:, :, 0:127]), ID_SW, sid)
                dv(lambda e, r=r, qs=qs: e.tensor_copy(out=Ssh[r][:, qs, 0:1], in_=carry[:, r, qs].unsqueeze(2)), ['carry'], sid)
                dv(lambda e, r=r, src=src, qs=qs: e.tensor_copy(out=carry[:, r, qs].unsqueeze(2), in_=src[:, :, 127:128]), ID_SW, ['carry'])
        for gi in range(2):
            def c_gp(mi, hf, b, gi=gi):
                ch = TGP[gi * 4 + mi]
                P.add('act', lambda e: e.activation(out=cp[:, ch, hs(hf)], in_=banks[b][:], func=AF.Tanh, scale=0.5),
                      reads=[bank_b(b)], writes=[('cp', ch)])
            proj_slab(w_in, 1024 + gi * 512, 4, c_gp)
        for gb in range(8):
            b = next_bank()
            for j in range(4):
                g = gb * 4 + j
                q, g2 = g // 2, g % 2
                rows = slice(g2 * 64, g2 * 64 + 64)
                reg = banks[b][:, j * 128:(j + 1) * 128]
                rdl = ['Wintra', 'Winter', ('cp', 4 + g // 8), ('cp', 8), ('cp', 9), ('cp', 10), ('cp', 11)]
                P.add('pe', lambda e, reg=reg, g=g: e.matmul(reg, lhsT=Wintra[:, g, :], rhs=Ush[:, g, :], start=True, stop=False),
                      reads=rdl, writes=[bank_b(b)], sig=False)
                P.add('pe', lambda e, reg=reg, q=q, g2=g2: e.matmul(reg, lhsT=WinterZ[:, g2, 0, q, :], rhs=Ssh[0][:, q, :], start=False, stop=False),
                      reads=rdl, writes=[bank_b(b)], sig=False)
                P.add('pe', lambda e, reg=reg, q=q, g2=g2: e.matmul(reg, lhsT=WinterZ[:, g2, 1, q, :], rhs=Ssh[1][:, q, :], start=False, stop=True),
                      reads=rdl, writes=[bank_b(b)], sig=True)
            P.add('act', lambda e, b=b, gb=gb: e.activation(
                out=Gsh[:, gb * 4:gb * 4 + 4, :], in_=banks[b][:].rearrange("p (g c) -> p g c", g=4), func=AF.Gelu),
                reads=[bank_b(b)], writes=[('cp', 12 + gb // 2)])
        for blk in range(4):
            for th in range(2):
                b = next_bank()
                for j in range(4):
                    t = th * 4 + j
                    for gi in range(8):
                        o = 240 * t + 112 - 16 * gi
                        P.add('pe', lambda e, b=b, j=j, o=o, gi=gi, blk=blk: e.matmul(
                            banks[b][:, j * 128:(j + 1) * 128], lhsT=sel_bf[:, o:o + 128], rhs=Gsh[:, blk * 8 + gi, :],
                            start=(gi == 0), stop=(gi == 7)),
                            reads=['sel', ('cp', 12 + blk)], writes=[bank_b(b)], sig=(gi == 7))
                gv = cp[:, 16 + blk, :].rearrange("p (c t) -> p c t", t=8)[:, :, th * 4:th * 4 + 4]
                dv(lambda e, b=b, gv=gv: e.tensor_copy(out=gv, in_=banks[b][:].rearrange("p (t c) -> p c t", t=4)),
                   [bank_b(b)], [('cp', 16 + blk)])
        s = load_w([(lambda sa: sa.rearrange("p (k c) -> p k c", k=4), w_glu.rearrange("(k p) c -> p k c", p=128))], None)
        wv = wsl[:, s, :].rearrange("p (k c) -> p k c", k=4)
        for mi in range(4):
            for hf in range(NH):
                bv_, bg_ = next_bank(), next_bank()
                for part, b in ((0, bv_), (1, bg_)):
                    for k in range(4):
                        P.add('pe', lambda e, k=k, b=b, part=part, mi=mi, hf=hf, wv=wv: e.matmul(
                            banks[b][:], lhsT=wv[:, k, part * 512 + mi * 128: part * 512 + (mi + 1) * 128],
                            rhs=cp[:, 16 + k, hs(hf)], start=(k == 0), stop=(k == 3)),
                            reads=[('slot', s), ('cp', 16 + k)], writes=[bank_b(b)], sig=(k == 3))
                tb = (mi * NH + hf) % 2
                P.add('act', lambda e, b=bg_, mi=mi, tb=tb: e.activation(
                    out=cp[:, GLT[tb], 0:512], in_=banks[b][:], func=AF.Tanh, bias=bglu_h[:, 4 + mi:5 + mi], scale=0.5),
                    reads=[bank_b(bg_), 'bgluh'], writes=[('cp', GLT[tb])])
                dv(lambda e, b=bv_, mi=mi, tb=tb: e.tensor_scalar(
                    out=tmpf[:, tb, 0:512], in0=banks[b][:], scalar1=bglu_sb[:, mi:mi + 1], scalar2=0.5, op0=ALU.add, op1=ALU.mult),
                    [bank_b(bv_), 'bglu'], [('tmpf', tb)])
                dv(lambda e, mi=mi, hf=hf, tb=tb: e.scalar_tensor_tensor(
                    out=cp[:, VCH[mi], hs(hf)], in0=cp[:, GLT[tb], 0:512], scalar=1.0, in1=tmpf[:, tb, 0:512], op0=ALU.add, op1=ALU.mult),
                    [('cp', GLT[tb]), ('tmpf', tb)], [('cp', VCH[mi])])
        if STAGE < 5:
            return
        W16 = NT + 16
        dv(lambda e: e.tensor_copy(out=upf[:, :, 0:16], in_=halo[:]), ['halo'], ID_UPF)

        def c_upool(mi, hf, b):
            P.add('act', lambda e: e.activation(out=upf[:, mi, 16 + hf * 512:16 + (hf + 1) * 512], in_=banks[b][:], func=AF.Identity),
                  reads=[bank_b(b)], writes=ID_UPF)
        proj_slab(w_in, 0, 4, c_upool)
        dv(lambda e: e.tensor_copy(out=halo[:], in_=upf[:, :, NT:NT + 16]), ID_UPF, ['halo'])
        for kg in range(4):
            w = 2 << kg
            src = upf[:, kg, :]
            TT = [('tmpf', 0), ('tmpf', 1)]
            for j in range(kg + 1):
                d = 1 << j
                lo = 2 * d - 1
                dst = tmpf[:, j % 2, :]
                P.add('dve', lambda e, dst=dst, src=src, d=d, lo=lo: e.tensor_tensor(
                    out=dst[:, lo:W16], in0=src[:, lo:W16], in1=src[:, lo - d:W16 - d], op=ALU.add),
                    reads=ID_UPF + TT, writes=[('tmpf', j % 2)])
                src = dst
            ssum = src
            if ti == 0:
                dv(lambda e, kg=kg: e.tensor_copy(out=u16[:, kg, :], in_=upf[:, kg, 16:32]), ID_UPF, ['u16'])
            dv(lambda e, kg=kg, w=w, ssum=ssum: e.scalar_tensor_tensor(
                out=upf[:, kg, 16:W16], in0=ssum[:, 16:W16], scalar=1.0 / w, in1=upf[:, kg, 16:W16], op0=ALU.mult, op1=ALU.subtract),
                ID_UPF + TT, ID_UPF)
            if ti == 0:
                dv(lambda e, kg=kg, ssum=ssum: e.tensor_tensor(out=ssum[:, 16:32], in0=ssum[:, 16:32], in1=rc_sb[:, kg, :], op=ALU.mult),
                   TT + ['rc'], TT)
                dv(lambda e, kg=kg, ssum=ssum: e.tensor_tensor(out=upf[:, kg, 16:32], in0=ssum[:, 16:32], in1=u16[:, kg, :], op=ALU.subtract),
                   TT + ID_UPF + ['u16'], ID_UPF)
        for kg in range(4):
            for hf in range(NH):
                b = next_bank()
                P.add('pe', lambda e, kg=kg, hf=hf, b=b: e.matmul(banks[b][:], lhsT=poolw_bf[:, kg, :], rhs=upf[:, kg, 16 + hf * 512:16 + (hf + 1) * 512], start=True, stop=True),
                      reads=['poolw'] + ID_UPF, writes=[bank_b(b)])
                P.add('act', lambda e, kg=kg, hf=hf, b=b: e.activation(
                    out=cp[:, ZP[kg], hs(hf)], in_=banks[b][:], func=AF.Identity, bias=pbsc[:, kg:kg + 1], scale=pbs_sb[:, 1, kg:kg + 1]),
                    reads=[bank_b(b), 'pbsc', 'pbs'], writes=[('cp', ZP[kg])])
        if STAGE < 6:
            return
        for gi in range(2):
            def c_gs(mi, hf, b, gi=gi):
                ch = TGS[gi * 4 + mi]
                P.add('act', lambda e: e.activation(out=cp[:, ch, hs(hf)], in_=banks[b][:], func=AF.Tanh, scale=0.5),
                      reads=[bank_b(b)], writes=[('cp', ch)])
            proj_slab(w_in, 2048 + gi * 512, 4, c_gs)
        s_pu = load_w([(lambda sa: sa.rearrange("p (k c) -> p k c", k=4), w_pu.rearrange("(k p) c -> p k c", p=128))], None)
        s_su = load_w([(lambda sa: sa.rearrange("p (k c) -> p k c", k=4), w_su.rearrange("(k p) c -> p k c", p=128))], None)
        wpu = wsl[:, s_pu, :].rearrange("p (k c) -> p k c", k=4)
        wsu = wsl[:, s_su, :].rearrange("p (k c) -> p k c", k=4)
        for m in range(8):
            for hf in range(NH):
                bp, bs_ = next_bank(), next_bank()
                for k in range(4):
                    P.add('pe', lambda e, k=k, m=m, hf=hf, b=bp: e.matmul(
                        banks[b][:], lhsT=wpu[:, k, m * 128:(m + 1) * 128], rhs=cp[:, ZP[k], hs(hf)], start=(k == 0), stop=(k == 3)),
                        reads=[('slot', s_pu), ('cp', ZP[k])], writes=[bank_b(bp)], sig=(k == 3))
                for k in range(4):
                    P.add('pe', lambda e, k=k, m=m, hf=hf, b=bs_: e.matmul(
                        banks[b][:], lhsT=wsu[:, k, m * 128:(m + 1) * 128], rhs=cp[:, VCH[k], hs(hf)], start=(k == 0), stop=(k == 3)),
                        reads=[('slot', s_su), ('cp', VCH[k])], writes=[bank_b(bs_)], sig=(k == 3))
                chs = TGS[m]
                dv(lambda e, m=m, hf=hf, b=bp: e.scalar_tensor_tensor(
                    out=tmpf[:, 0, 0:512], in0=cp[:, TGP[m], hs(hf)], scalar=1.0, in1=banks[b][:], op0=ALU.add, op1=ALU.mult),
                    [bank_b(bp), ('cp', TGP[m])], [('tmpf', 0)])
                dv(lambda e, chs=chs, hf=hf, b=bs_: e.scalar_tensor_tensor(
                    out=tmpf[:, 1, 0:512], in0=cp[:, chs, hs(hf)], scalar=1.0, in1=banks[b][:], op0=ALU.add, op1=ALU.mult),
                    [bank_b(bs_), ('cp', chs)], [('tmpf', 1)])
                P.add('dve', lambda e, m=m, hf=hf: e.tensor_tensor(
                    out=cp[:, TGP[m], hs(hf)], in0=tmpf[:, 0, 0:512], in1=tmpf[:, 1, 0:512], op=ALU.add),
                    reads=[('tmpf', 0), ('tmpf', 1)], writes=[('cp', TGP[m])])
        for m2 in range(2):
            s = load_w([(lambda sa: sa.rearrange("p (k c) -> p k c", k=KT),
                         w_o[:, m2 * 512:(m2 + 1) * 512].rearrange("(k p) c -> p k c", p=128))], None)
            wv = wsl[:, s, :].rearrange("p (k c) -> p k c", k=KT)
            for mm in range(4):
                m = m2 * 4 + mm
                for hf in range(NH):
                    b = next_bank()
                    for k in range(KT):
                        P.add('pe', lambda e, k=k, mm=mm, hf=hf, b=b, wv=wv: e.matmul(
                            banks[b][:], lhsT=wv[:, k, mm * 128:(mm + 1) * 128], rhs=cp[:, TGP[k], hs(hf)], start=(k == 0), stop=(k == KT - 1)),
                            reads=[('slot', s), ('cp', TGP[k])], writes=[bank_b(b)], sig=(k == KT - 1))
                    dv(lambda e, m=m, hf=hf, b=b: e.scalar_tensor_tensor(
                        out=xs[:, m, hs(hf)], in0=banks[b][:], scalar=Gvec[:, 1, m:m + 1], in1=xs[:, m, hs(hf)], op0=ALU.mult, op1=ALU.add),
                        [bank_b(b), 'Gvec', ('xs', m)], [('xs', m)])

    for ti in range(NTILES):
        t0 = ti * NT
        for k in range(KT):
            P.add('sp', lambda e, k=k, t0=t0: e.dma_start(out=xs[:, k, :], in_=xT[k * 128:(k + 1) * 128, t0:t0 + NT]),
                  writes=[('xs', k)], dma=('xs', k))
        norm_mod(0, t0)
        ffn(0, w1i, w1o)
        if MIXER:
            norm_mod(1, t0)
            mixer(ti)
        norm_mod(2, t0)
        ffn(2, w2i, w2o)
        norm_mod(3, t0)

    with nc.Block() as block:
        P.emit(block)
    return nc, es, P


_CACHE = {}


def _consts():
    ident = np.eye(128, dtype=np.float32)
    sel = np.zeros((128, 1920), np.float32)
    for p in range(128):
        sel[p, 240 * (p // 16) + 112 + (p % 16)] = 1.0
    mask = np.zeros((128, 128), np.float32)
    for a in range(8):
        for b in range(8):
            if b >= a:
                mask[a * 16:(a + 1) * 16, b * 16:(b + 1) * 16] = 1.0
    rc = np.zeros((128, 4, 16), np.float32)
    for k in range(4):
        w = 2 << k
        for tt_ in range(16):
            rc[:, k, tt_] = 1.0 / min(tt_ + 1, w)
    return ident, sel, mask, rc


def _pair_layout(a):
    g, n = a.shape[0], a.shape[1]
    rest = a.shape[2:]
    a = a.reshape((16, 2, n) + rest)
    a = np.moveaxis(a, 0, 2)
    return np.ascontiguousarray(a.reshape((128, 16) + rest))


def make_in_maps(inputs):
    f = lambda a: np.ascontiguousarray(np.asarray(a, dtype=np.float32))
    x = f(inputs["x"])
    c = f(inputs["c"])
    ident, sel, mask, rc = _consts()
    gvec = np.stack([f(inputs["g_ffn1"])[0], f(inputs["g_mix"])[0], f(inputs["g_ffn2"])[0], f(inputs["g_final"])], 0)
    gvec = np.ascontiguousarray(gvec.reshape(4, KT, 128).transpose(2, 0, 1))
    b_ada = np.ascontiguousarray(f(inputs["b_ada"])[0].reshape(72, 128).T)
    pool_w = np.ascontiguousarray(f(inputs["pool_w"])[0].transpose(1, 0, 2))
    pool_bs = np.ascontiguousarray(np.stack([f(inputs["pool_b"])[0].reshape(4, 128).T,
                                             f(inputs["pool_scale"])[0].reshape(4, 128).T], 1))
    b_glu = np.ascontiguousarray(f(inputs["b_glu"])[0].reshape(8, 128).T)
    lrl = f(inputs["ssm_lam_re_log"])[0]
    lim = f(inputs["ssm_lam_im"])[0]
    ldt = np.broadcast_to(f(inputs["ssm_log_dt"])[0][:, None], (32, 64))
    ssm_sc = np.ascontiguousarray(np.stack([_pair_layout(lrl), _pair_layout(lim), _pair_layout(ldt)], 1))
    ssm_b = np.ascontiguousarray(np.stack([_pair_layout(f(inputs["ssm_b_re"])[0]),
                                           _pair_layout(f(inputs["ssm_b_im"])[0])], 1))
    cre = f(inputs["ssm_c_re"])[0].transpose(0, 2, 1)
    cim = f(inputs["ssm_c_im"])[0].transpose(0, 2, 1)
    ssm_cc = np.ascontiguousarray(np.stack([_pair_layout(cre), _pair_layout(cim)], 1))
    dd = f(inputs["ssm_d"])[0].reshape(32, 16)
    ssm_dd = np.ascontiguousarray(np.tile(dd.T, (8, 1)))
    shared = dict(
        w_ada=f(inputs["w_ada"])[0], b_ada=b_ada, gvec=gvec,
        w1i=f(inputs["w_ffn1_in"])[0], w1o=f(inputs["w_ffn1_out"])[0],
        w2i=f(inputs["w_ffn2_in"])[0], w2o=f(inputs["w_ffn2_out"])[0],
        w_in=f(inputs["w_in"])[0], pool_w=pool_w, pool_bs=pool_bs,
        w_pu=f(inputs["w_pool_up"])[0], w_glu=f(inputs["w_glu"])[0], b_glu=b_glu,
        w_su=f(inputs["w_ssm_up"])[0], w_o=f(inputs["w_out"])[0],
        ssm_sc=ssm_sc, ssm_b=ssm_b, ssm_cc=ssm_cc, ssm_dd=ssm_dd,
        cst_ident=ident, cst_sel=sel, cst_mask=mask, cst_rc=rc,
    )
    maps = []
    for b in range(8):
        m = dict(shared)
        m["xT"] = np.ascontiguousarray(x[b].T)
        m["c_l"] = np.ascontiguousarray(c[b].reshape(KT, 128).T)
        maps.append(m)
    return maps


def kernel(**inputs):
    if "nc" not in _CACHE:
        _CACHE["nc"] = build_nc()
    nc, es, P = _CACHE["nc"]
    in_maps = make_in_maps(inputs)
    res = run_bass_kernel_spmd(nc, in_maps, core_ids=list(range(8)))
    out = np.stack([np.asarray(r["outT"], dtype=np.float32).T for r in res.results], 0)
    return np.ascontiguousarray(out)
```

```python
import numpy as np
from contextlib import ExitStack
import concourse.bass as bass
import concourse.mybir as mybir
from concourse.bass_utils import run_bass_kernel_spmd

F32 = mybir.dt.float32
BF16 = mybir.dt.bfloat16
AF = mybir.ActivationFunctionType
ALU = mybir.AluOpType

D = 1024
T = 4096
FF = 2816
NT = 1024
import os
NTILES = int(os.environ.get('KTILES', str(T // NT)))
NH = NT // 512
KT = D // 128
FT = FF // 128
EPS = 1e-6
LCH = 8
NC_T = NT // LCH
NPAIR = 16

import os
MIXER = True
STAGE = int(os.environ.get('KSTAGE', '9'))
SETUP = int(os.environ.get('KSETUP', '9'))
SUB = int(os.environ.get('KSUB', '9'))


class Prog:
    def __init__(self, nc, es):
        self.nc = nc
        self.es = es
        self.ops = []
        self.last_w = {}
        self.readers = {}
        self.dma_keys = {}

    skip = False

    def add(self, eng, fn, reads=(), writes=(), dma=None, sig=True):
        if self.skip:
            return None
        idx = len(self.ops)
        deps = set()
        for b in reads:
            if b in self.last_w:
                deps.add(self.last_w[b])
        for b in writes:
            if b in self.last_w:
                deps.add(self.last_w[b])
            for r in self.readers.get(b, ()):
                deps.add(r)
        for b in writes:
            self.last_w[b] = idx
            self.readers[b] = []
        for b in reads:
            self.readers.setdefault(b, []).append(idx)
        deps.discard(idx)
        self.ops.append(dict(eng=eng, fn=fn, deps=deps, dma=dma, sig=sig))
        return idx

    def emit(self, block):
        nc = self.nc
        ops = self.ops
        engs = ['pe', 'act', 'dve', 'pool', 'sp']
        sems = {e: self.es.enter_context(nc.semaphore(f"s_{e}")) for e in ['pe', 'act', 'dve', 'pool']}
        keysem = {}
        for o in ops:
            if o['dma'] is not None and o['dma'] not in keysem:
                keysem[o['dma']] = self.es.enter_context(nc.semaphore(f"d_{len(keysem)}"))
        cnt = {e: 0 for e in sems}
        for o in ops:
            if o['dma'] is None:
                if o['sig']:
                    cnt[o['eng']] += 1
                    o['cnt'] = cnt[o['eng']]
                else:
                    o['cnt'] = None
        nxt = {e: None for e in sems}
        for o in reversed(ops):
            if o['dma'] is None:
                if o['cnt'] is None:
                    o['cnt'] = nxt[o['eng']]
                    assert o['cnt'] is not None
                else:
                    nxt[o['eng']] = o['cnt']
        cum = {k: 0 for k in keysem}
        waited = {}
        per_eng = {e: [] for e in engs}
        for i, o in enumerate(ops):
            waits = {}
            for j in o['deps']:
                d = ops[j]
                if d['dma'] is not None:
                    s = keysem[d['dma']]
                    v = cum[d['dma']]
                else:
                    if d['eng'] == 'pe' and o['eng'] == 'pe' and o['dma'] is None:
                        continue
                    s = sems[d['eng']]
                    v = d['cnt']
                key = (o['eng'], id(s))
                if waited.get(key, 0) >= v:
                    continue
                if key not in waits or waits[key][1] < v:
                    waits[key] = (s, v)
            for key, (s, v) in waits.items():
                waited[key] = v
            if o['dma'] is not None:
                cum[o['dma']] += 16
            per_eng[o['eng']].append((o, list(waits.values())))
        self.n_ops = {e: len(per_eng[e]) for e in engs}

        def run(e, lst):
            for o, waits in lst:
                for (s, v) in waits:
                    e.wait_ge(s, v)
                ins = o['fn'](e)
                if o['dma'] is not None:
                    ins.then_inc(keysem[o['dma']], 16)
                elif o['sig']:
                    ins.then_inc(sems[o['eng']], 1)

        @block.tensor
        def _(e):
            run(e, per_eng['pe'])

        @block.scalar
        def _(e):
            run(e, per_eng['act'])

        @block.vector
        def _(e):
            run(e, per_eng['dve'])

        @block.gpsimd
        def _(e):
            run(e, per_eng['pool'])

        @block.sync
        def _(e):
            run(e, per_eng['sp'])
            for k, s in keysem.items():
                if isinstance(k, str) and k.startswith('out'):
                    e.wait_ge(s, cum[k])


def build_nc(debug=False):
    nc = bass.Bass("TRN2", target_bir_lowering=False)
    es = ExitStack()

    def din(name, shape, dt=F32):
        return nc.dram_tensor(name, list(shape), dt, kind="ExternalInput").ap()

    xT = din("xT", [D, T])
    c_l = din("c_l", [128, KT])
    w_ada = din("w_ada", [D, 9 * D])
    b_ada = din("b_ada", [128, 72])
    gvec = din("gvec", [128, 4, KT])
    w1i = din("w1i", [D, 2 * FF])
    w1o = din("w1o", [FF, D])
    w2i = din("w2i", [D, 2 * FF])
    w2o = din("w2o", [FF, D])
    w_in = din("w_in", [D, 3 * D])
    pool_w = din("pool_w", [128, 4, 128])
    pool_bs = din("pool_bs", [128, 2, 4])
    w_pu = din("w_pu", [512, D])
    w_glu = din("w_glu", [512, D])
    b_glu = din("b_glu", [128, 8])
    w_su = din("w_su", [512, D])
    w_o = din("w_o", [D, D])
    ssm_sc = din("ssm_sc", [128, 3, NPAIR])
    ssm_b = din("ssm_b", [128, 2, NPAIR, 16])
    ssm_cc = din("ssm_cc", [128, 2, NPAIR, 16])
    ssm_dd = din("ssm_dd", [128, 32])
    cst_ident = din("cst_ident", [128, 128])
    cst_sel = din("cst_sel", [128, 1920])
    cst_mask = din("cst_mask", [128, 128])
    outT = nc.dram_tensor("outT", [D, T], F32, kind="ExternalOutput").ap()
    DBG = os.environ.get('KDBG', '0') == '1'
    dbgt = {}
    if DBG:
        for nm in ('s', 'v', 'z', 'm'):
            dbgt[nm] = nc.dram_tensor("dbg_" + nm, [128, 8, NT], BF16, kind="ExternalOutput").ap()

    def tap(nm, j0, ti):
        if DBG and ti == 0:
            P.add('sp', lambda e: e.dma_start(out=dbgt[nm], in_=cp[:, j0:j0 + 8, :]), reads=[('cp', j) for j in range(j0, j0 + 8)], dma='out_' + nm)

    def sb(name, shape, dt=F32):
        return es.enter_context(nc.sbuf_tensor(name, list(shape), dt))

    def ps(name, shape, dt=F32):
        return es.enter_context(nc.psum_tensor(name, list(shape), dt))

    P = Prog(nc, es)

    xs = sb("xs", [128, KT, NT])
    hb = sb("hb", [128, KT, NT], BF16)
    NCHK = 24
    cp = sb("cp", [128, NCHK, NT], BF16)
    tmpf = sb("tmpf", [128, 2, NT + 16])
    rstd = sb("rstd", [128, NT])
    NSLOT = 4
    wsl = sb("wsl", [128, NSLOT, 4096], BF16)
    ones_bf = sb("ones_bf", [128, 128], BF16)
    ident = sb("ident", [128, 128])
    eps_sb = sb("eps_sb", [128, 1])
    modv = sb("modv", [128, 72])
    cl_sb = sb("cl_sb", [128, KT])
    sc_bf = sb("sc_bf", [128, KT], BF16)
    bada_sb = sb("bada_sb", [128, 72])
    gv_sb = sb("gv_sb", [128, 4, KT])
    Avec = sb("Avec", [128, 3, KT])
    Gvec = sb("Gvec", [128, 3, KT])
    banks = [ps(f"bank{i}", [128, 512]) for i in range(8)]
    ostage = cp[:].rearrange("p a b -> p (a b)").bitcast(F32)

    def bank_b(i):
        return ('bank', i)

    slot_use = [0]

    def load_w(src_ap_list, shape_views):
        s = slot_use[0] % NSLOT
        slot_use[0] += 1
        for dst_fn, src in src_ap_list:
            dst = dst_fn(wsl[:, s, :])
            P.add('pool', lambda e, dst=dst, src=src: e.dma_start(out=dst, in_=src),
                  writes=[('slot', s)], dma=('slot', s))
        return s

    bank_rr = [0]

    reserved = [None]

    def next_bank():
        b = bank_rr[0] % 8
        bank_rr[0] += 1
        if b == reserved[0]:
            return next_bank()
        return b

    def small_load(dst, src, key):
        P.add('sp', lambda e: e.dma_start(out=dst, in_=src), writes=[key], dma=key)

    small_load(cl_sb[:], c_l, 'cl')
    small_load(bada_sb[:], b_ada, 'bada')
    small_load(gv_sb[:], gvec, 'gv')
    small_load(ident[:], cst_ident, 'ident')
    P.add('dve', lambda e: e.memset(ones_bf[:], 1.0 / D), writes=['ones'])
    P.add('dve', lambda e: e.memset(eps_sb[:], EPS), writes=['eps'])

    P.add('act', lambda e: e.activation(out=sc_bf[:], in_=cl_sb[:], func=AF.Silu),
          reads=['cl'], writes=['sc'])
    mod_bank = next_bank()
    reserved[0] = mod_bank
    for sl in range(18):
        c0 = sl * 512
        s = load_w([(lambda sa: sa.rearrange("p (k c) -> p k c", k=KT),
                     w_ada[:, c0:c0 + 512].rearrange("(k p) c -> p k c", p=128))], None)
        wv = wsl[:, s, :].rearrange("p (k c) -> p k c", k=KT)
        for mm in range(4):
            m = sl * 4 + mm
            for k in range(KT):
                P.add('pe', lambda e, m=m, k=k, mm=mm, wv=wv: e.matmul(
                    banks[mod_bank][:, m:m + 1], lhsT=wv[:, k, mm * 128:(mm + 1) * 128],
                    rhs=sc_bf[:, k:k + 1], start=(k == 0), stop=(k == KT - 1)),
                    reads=[('slot', s), 'sc'], writes=[bank_b(mod_bank)], sig=(k == KT - 1))
    def mod_finalize():
        P.add('dve', lambda e: e.tensor_tensor(out=modv[:], in0=banks[mod_bank][:, 0:72], in1=bada_sb[:], op=ALU.add),
              reads=[bank_b(mod_bank), 'bada'], writes=['modv'])
        for sub in range(3):
            sc_ap = modv[:, (sub * 3 + 1) * 8:(sub * 3 + 1) * 8 + 8]
            gt_ap = modv[:, (sub * 3 + 2) * 8:(sub * 3 + 2) * 8 + 8]
            P.add('dve', lambda e, sub=sub, sc_ap=sc_ap: e.scalar_tensor_tensor(
                out=Avec[:, sub, :], in0=sc_ap, scalar=1.0, in1=gv_sb[:, sub, :], op0=ALU.add, op1=ALU.mult),
                reads=['modv', 'gv'], writes=['Avec'])
            P.add('dve', lambda e, sub=sub, gt_ap=gt_ap: e.tensor_scalar(
                out=Gvec[:, sub, :], in0=gt_ap, scalar1=0.5, scalar2=None, op0=ALU.mult),
                reads=['modv'], writes=['Gvec'])


    def Bvec(sub, k):
        return modv[:, (sub * 3) * 8 + k:(sub * 3) * 8 + k + 1]

    def norm_mod(sub, t0):
        for k in range(KT):
            P.add('act', lambda e, k=k: e.activation(out=hb[:, k, :], in_=xs[:, k, :], func=AF.Square),
                  reads=[('xs', k)], writes=[('hb', k)])
        bs = [next_bank() for _ in range(NH)]
        for hf in range(NH):
            for k in range(KT):
                P.add('pe', lambda e, k=k, hf=hf: e.matmul(
                    banks[bs[hf]][:], lhsT=ones_bf[:], rhs=hb[:, k, hf * 512:(hf + 1) * 512],
                    start=(k == 0), stop=(k == KT - 1)),
                    reads=['ones', ('hb', k)], writes=[bank_b(bs[hf])], sig=(k == KT - 1))
            P.add('act', lambda e, hf=hf: e.activation(
                out=rstd[:, hf * 512:(hf + 1) * 512], in_=banks[bs[hf]][:], func=AF.Sqrt,
                bias=eps_sb[:, 0:1], scale=1.0),
                reads=[bank_b(bs[hf]), 'eps'], writes=[('rstd', hf)])
            P.add('dve', lambda e, hf=hf: e.reciprocal(
                out=rstd[:, hf * 512:(hf + 1) * 512], in_=rstd[:, hf * 512:(hf + 1) * 512]),
                reads=[('rstd', hf)], writes=[('rstd', hf)])
        for k in range(KT):
            tb = k % 2
            if sub < 3:
                a_ap = Avec[:, sub, k:k + 1]
            else:
                a_ap = gv_sb[:, 3, k:k + 1]
            if sub == 3:
                P.add('dve', lambda e, k=k, a_ap=a_ap: e.scalar_tensor_tensor(
                    out=ostage[:, k * NT:(k + 1) * NT], in0=xs[:, k, :], scalar=a_ap, in1=rstd[:], op0=ALU.mult, op1=ALU.mult),
                    reads=[('xs', k), ('rstd', 0), ('rstd', 1), 'gv'], writes=[('cp', 2 * k), ('cp', 2 * k + 1)])
            else:
                P.add('dve', lambda e, k=k, tb=tb, a_ap=a_ap: e.scalar_tensor_tensor(
                    out=tmpf[:, tb, 0:NT], in0=xs[:, k, :], scalar=a_ap, in1=rstd[:], op0=ALU.mult, op1=ALU.mult),
                    reads=[('xs', k), ('rstd', 0), ('rstd', 1), 'Avec', 'gv'], writes=[('tmpf', tb)])
            if sub < 3:
                P.add('act', lambda e, k=k, tb=tb: e.activation(
                    out=hb[:, k, :], in_=tmpf[:, tb, 0:NT], func=AF.Identity, bias=Bvec(sub, k), scale=1.0),
                    reads=[('tmpf', tb), 'modv'], writes=[('hb', k)])
            else:
                P.add('sp', lambda e, k=k: e.dma_start(
                    out=outT[k * 128:(k + 1) * 128, t0:t0 + NT], in_=ostage[:, k * NT:(k + 1) * NT]),
                    reads=[('cp', 2 * k), ('cp', 2 * k + 1)], dma='out')

    def ffn(sub, w_i, w_o_):
        for sl in range(FT // 2):
            f0 = sl * 2
            s = load_w([
                (lambda sa: sa.rearrange("p (k a c) -> p k a c", k=KT, a=2)[:, :, 0, :],
                 w_i[:, f0 * 128:f0 * 128 + 256].rearrange("(k p) c -> p k c", p=128)),
                (lambda sa: sa.rearrange("p (k a c) -> p k a c", k=KT, a=2)[:, :, 1, :],
                 w_i[:, FF + f0 * 128:FF + f0 * 128 + 256].rearrange("(k p) c -> p k c", p=128)),
            ], None)
            wv = wsl[:, s, :].rearrange("p (k a c) -> p k a c", k=KT, a=2)
            for ff in range(2):
                f = f0 + ff
                ba = [[next_bank() for _ in range(NH)] for _ in range(2)]
                for part in range(2):
                    for hf in range(NH):
                        b = ba[part][hf]
                        for k in range(KT):
                            P.add('pe', lambda e, k=k, hf=hf, part=part, ff=ff, b=b, wv=wv: e.matmul(
                                banks[b][:], lhsT=wv[:, k, part, ff * 128:(ff + 1) * 128],
                                rhs=hb[:, k, hf * 512:(hf + 1) * 512], start=(k == 0), stop=(k == KT - 1)),
                                reads=[('slot', s), ('hb', k)], writes=[bank_b(b)], sig=(k == KT - 1))
                tb = f % 2
                for hf in range(NH):
                    P.add('act', lambda e, hf=hf, tb=tb, b=ba[0][hf]: e.activation(
                        out=cp[:, 22 + tb, hf * 512:(hf + 1) * 512], in_=banks[b][:], func=AF.Silu),
                        reads=[bank_b(ba[0][hf])], writes=[('cp', 22 + tb)])
                    P.add('dve', lambda e, hf=hf, tb=tb, f=f, b=ba[1][hf]: e.tensor_tensor(
                        out=cp[:, f, hf * 512:(hf + 1) * 512], in0=banks[b][:],
                        in1=cp[:, 22 + tb, hf * 512:(hf + 1) * 512], op=ALU.mult),
                        reads=[bank_b(ba[1][hf]), ('cp', 22 + tb)], writes=[('cp', f)])
        for m2 in range(4):
            ss = []
            for fh in range(2):
                s = load_w([(lambda sa: sa[:, 0:11 * 256].rearrange("p (f c) -> p f c", f=11),
                             w_o_[fh * 1408:(fh + 1) * 1408, m2 * 256:(m2 + 1) * 256].rearrange(
                                 "(f p) c -> p f c", p=128))], None)
                ss.append(s)
            for mm in range(2):
                m = m2 * 2 + mm
                for hf in range(NH):
                    b = next_bank()
                    for f in range(FT):
                        s = ss[f // 11]
                        wv = wsl[:, s, 0:11 * 256].rearrange("p (f c) -> p f c", f=11)
                        P.add('pe', lambda e, f=f, hf=hf, mm=mm, b=b, wv=wv: e.matmul(
                            banks[b][:], lhsT=wv[:, f % 11, mm * 128:(mm + 1) * 128],
                            rhs=cp[:, f, hf * 512:(hf + 1) * 512], start=(f == 0), stop=(f == FT - 1)),
                            reads=[('slot', s), ('cp', f)], writes=[bank_b(b)], sig=(f == FT - 1))
                    P.add('dve', lambda e, m=m, hf=hf, b=b: e.scalar_tensor_tensor(
                        out=xs[:, m, hf * 512:(hf + 1) * 512], in0=banks[b][:], scalar=Gvec[:, sub, m:m + 1],
                        in1=xs[:, m, hf * 512:(hf + 1) * 512], op0=ALU.mult, op1=ALU.add),
                        reads=[bank_b(b), 'Gvec', ('xs', m)], writes=[('xs', m)])

    PI = float(np.pi)
    cst_rc = din("cst_rc", [128, 4, 16])
    lhsTX = sb("lhsTX", [128, 32, 2, 128], BF16)
    Wintra = sb("Wintra", [128, 32, 128], BF16)
    WinterZ = sb("WinterZ", [128, 2, 2, NPAIR, 128], BF16)
    ident_bf = sb("ident_bf", [128, 128], BF16)
    sel_bf = sb("sel_bf", [128, 1920], BF16)
    Tc = sb("Tc", [128, NPAIR, 128])
    Ts = sb("Ts", [128, NPAIR, 128])
    smalls = sb("smalls", [128, 30, NPAIR])
    PW = sb("PW", [128, 9, 2, NPAIR])
    carry = sb("carry", [128, 2, NPAIR])
    halo = sb("halo", [128, 4, 16])
    maskf = sb("maskf", [128, 128])
    dd_sb = sb("dd_sb", [128, 32])
    rc_sb = sb("rc_sb", [128, 4, 16])
    poolw_bf = sb("poolw_bf", [128, 4, 128], BF16)
    pbs_sb = sb("pbs_sb", [128, 2, 4])
    pbsc = sb("pbsc", [128, 4])
    bglu_sb = sb("bglu_sb", [128, 8])
    bglu_h = sb("bglu_h", [128, 8])
    scl = sb("scl", [128, 3, NPAIR])
    ki_sb = sb("ki_sb", [128, NPAIR], mybir.dt.int32)

    cpf = cp[:].rearrange("p a b -> p (a b)").bitcast(F32)
    hbf = hb[:].rearrange("p a b -> p (a b)").bitcast(F32)
    WxT = cpf[:, 0:4096].rearrange("p (r q c) -> p r q c", r=2, q=NPAIR)
    WinT = cpf[:, 4096:8192].rearrange("p (r q c) -> p r q c", r=2, q=NPAIR)
    bb = cpf[:, 8192:8704].rearrange("p (r q c) -> p r q c", r=2, q=NPAIR)
    cc = cpf[:, 8704:9216].rearrange("p (r q c) -> p r q c", r=2, q=NPAIR)
    Bbar = cpf[:, 9216:9728].rearrange("p (r q c) -> p r q c", r=2, q=NPAIR)
    tS = cpf[:, 9728:10752].rearrange("p (r q c) -> p r q c", r=4, q=NPAIR)
    Kpp = hbf.rearrange("p (r q c) -> p r q c", r=2, q=NPAIR)
    ID_WXT = [('cp', j) for j in range(0, 8)]
    ID_WIN = [('cp', j) for j in range(8, 16)]
    ID_SM = [('cp', j) for j in range(16, 24)]
    ID_KPP = [('hb', j) for j in range(KT)]

    def dv(fn, reads, writes):
        P.add('dve', fn, reads=reads, writes=writes)

    def tt(out, a, b, op, reads, writes):
        dv(lambda e: e.tensor_tensor(out=out, in0=a, in1=b, op=op), reads, writes)

    def sm(i):
        return smalls[:, i, :]

    if MIXER:
        small_load(scl[:], ssm_sc, 'scl')
        small_load(dd_sb[:], ssm_dd, 'dd')
        small_load(maskf[:], cst_mask, 'maskf')
        small_load(rc_sb[:], cst_rc, 'rc')
        small_load(pbs_sb[:], pool_bs, 'pbs')
        small_load(bglu_sb[:], b_glu, 'bglu')
        P.add('sp', lambda e: e.dma_start(out=bb, in_=ssm_b), writes=ID_SM, dma='bbcc')
        P.add('sp', lambda e: e.dma_start(out=cc, in_=ssm_cc), writes=ID_SM, dma='bbcc')
        P.add('pool', lambda e: e.dma_start(out=sel_bf[:], in_=cst_sel), writes=['sel'], dma='sel')
        P.add('pool', lambda e: e.dma_start(out=poolw_bf[:], in_=pool_w), writes=['poolw'], dma='poolw')
        dv(lambda e: e.memset(carry[:], 0.0), [], ['carry'])
        dv(lambda e: e.memset(halo[:], 0.0), [], ['halo'])
        dv(lambda e: e.memset(lhsTX[:], 0.0), [], ['lhsTX'])
        dv(lambda e: e.tensor_tensor(out=pbsc[:], in0=pbs_sb[:, 0, :], in1=pbs_sb[:, 1, :], op=ALU.mult), ['pbs'], ['pbsc'])
        dv(lambda e: e.tensor_scalar(out=bglu_h[:], in0=bglu_sb[:], scalar1=0.5, scalar2=None, op0=ALU.mult), ['bglu'], ['bgluh'])

        P.skip = SUB < 1
        S = 'smalls'
        LR, DT, XM, MAG, R8, ANG, YS, YC, SN, CS, AR, AI, NR, DEN, RDEN, FRE, FIM, T1, T2, T3, T4, IR, II, RR8 = range(24)
        P.add('act', lambda e: e.activation(out=sm(LR), in_=scl[:, 0, :], func=AF.Exp), reads=['scl'], writes=[S])
        P.add('act', lambda e: e.activation(out=sm(DT), in_=scl[:, 2, :], func=AF.Exp), reads=['scl'], writes=[S])
        dv(lambda e: e.tensor_scalar(out=sm(LR), in0=sm(LR), scalar1=-1.0, scalar2=None, op0=ALU.mult), [S], [S])
        tt(sm(XM), sm(LR), sm(DT), ALU.mult, [S], [S])
        P.add('act', lambda e: e.activation(out=sm(MAG), in_=sm(XM), func=AF.Exp), reads=[S], writes=[S])
        P.add('act', lambda e: e.activation(out=sm(R8), in_=sm(XM), func=AF.Exp, scale=8.0), reads=[S], writes=[S])
        tt(sm(ANG), scl[:, 1, :], sm(DT), ALU.mult, [S, 'scl'], [S])
        P.skip = SUB < 2

        def range_reduce(dst, off):
            dv(lambda e: e.tensor_scalar(out=sm(T4), in0=sm(ANG), scalar1=off, scalar2=None, op0=ALU.add), [S], [S])
            dv(lambda e: e.tensor_scalar(out=sm(T3), in0=sm(T4), scalar1=1.0 / (2 * PI), scalar2=None, op0=ALU.mult), [S], [S])
            dv(lambda e: e.tensor_copy(out=ki_sb[:], in_=sm(T3)), [S], ['ki'])
            dv(lambda e: e.tensor_copy(out=sm(T3), in_=ki_sb[:]), ['ki'], [S])
            dv(lambda e: e.scalar_tensor_tensor(out=sm(dst), in0=sm(T3), scalar=-2 * PI, in1=sm(T4), op0=ALU.mult, op1=ALU.add), [S], [S])
            dv(lambda e: e.tensor_scalar(out=sm(T3), in0=sm(dst), scalar1=PI, scalar2=None, op0=ALU.is_gt), [S], [S])
            dv(lambda e: e.scalar_tensor_tensor(out=sm(dst), in0=sm(T3), scalar=-2 * PI, in1=sm(dst), op0=ALU.mult, op1=ALU.add), [S], [S])
            dv(lambda e: e.tensor_scalar(out=sm(T3), in0=sm(dst), scalar1=-PI, scalar2=None, op0=ALU.is_lt), [S], [S])
            dv(lambda e: e.scalar_tensor_tensor(out=sm(dst), in0=sm(T3), scalar=2 * PI, in1=sm(dst), op0=ALU.mult, op1=ALU.add), [S], [S])
        range_reduce(YS, 0.0)
        P.skip = SUB < 3
        TH, ZZ, SS, CC, U1, U2 = 24, 25, 26, 27, 28, 29
        dv(lambda e: e.tensor_scalar(out=sm(TH), in0=sm(YS), scalar1=0.25, scalar2=None, op0=ALU.mult), [S], [S])
        tt(sm(ZZ), sm(TH), sm(TH), ALU.mult, [S], [S])
        dv(lambda e: e.memset(sm(SS), 1.0), [S], [S])
        dv(lambda e: e.memset(sm(CC), 1.0), [S], [S])
        for kk in (156.0, 110.0, 72.0, 42.0, 20.0, 6.0):
            tt(sm(SS), sm(SS), sm(ZZ), ALU.mult, [S], [S])
            dv(lambda e, kk=kk: e.tensor_scalar(out=sm(SS), in0=sm(SS), scalar1=-1.0 / kk, scalar2=1.0, op0=ALU.mult, op1=ALU.add), [S], [S])
        tt(sm(SS), sm(SS), sm(TH), ALU.mult, [S], [S])
        for kk in (182.0, 132.0, 90.0, 56.0, 30.0, 12.0, 2.0):
            tt(sm(CC), sm(CC), sm(ZZ), ALU.mult, [S], [S])
            dv(lambda e, kk=kk: e.tensor_scalar(out=sm(CC), in0=sm(CC), scalar1=-1.0 / kk, scalar2=1.0, op0=ALU.mult, op1=ALU.add), [S], [S])
        for _ in range(2):
            tt(sm(U1), sm(SS), sm(CC), ALU.mult, [S], [S])
            tt(sm(U2), sm(SS), sm(SS), ALU.mult, [S], [S])
            tt(sm(CC), sm(CC), sm(CC), ALU.mult, [S], [S])
            tt(sm(CC), sm(CC), sm(U2), ALU.subtract, [S], [S])
            dv(lambda e: e.tensor_scalar(out=sm(SS), in0=sm(U1), scalar1=2.0, scalar2=None, op0=ALU.mult), [S], [S])
        dv(lambda e: e.tensor_copy(out=sm(SN), in_=sm(SS)), [S], [S])
        dv(lambda e: e.tensor_copy(out=sm(CS), in_=sm(CC)), [S], [S])
        tt(sm(AR), sm(MAG), sm(CS), ALU.mult, [S], [S])
        tt(sm(AI), sm(MAG), sm(SN), ALU.mult, [S], [S])
        P.skip = SETUP < 2
        dv(lambda e: e.tensor_scalar(out=sm(NR), in0=sm(AR), scalar1=-1.0, scalar2=None, op0=ALU.add), [S], [S])
        tt(sm(T1), sm(LR), sm(LR), ALU.mult, [S], [S])
        tt(sm(T2), scl[:, 1, :], scl[:, 1, :], ALU.mult, [S, 'scl'], [S])
        tt(sm(DEN), sm(T1), sm(T2), ALU.add, [S], [S])
        dv(lambda e: e.reciprocal(out=sm(RDEN), in_=sm(DEN)), [S], [S])
        tt(sm(T1), sm(NR), sm(LR), ALU.mult, [S], [S])
        tt(sm(T2), sm(AI), scl[:, 1, :], ALU.mult, [S, 'scl'], [S])
        tt(sm(T1), sm(T1), sm(T2), ALU.add, [S], [S])
        tt(sm(FRE), sm(T1), sm(RDEN), ALU.mult, [S], [S])
        tt(sm(T1), sm(AI), sm(LR), ALU.mult, [S], [S])
        tt(sm(T2), sm(NR), scl[:, 1, :], ALU.mult, [S, 'scl'], [S])
        tt(sm(T1), sm(T1), sm(T2), ALU.subtract, [S], [S])
        tt(sm(FIM), sm(T1), sm(RDEN), ALU.mult, [S], [S])

        def bc16(ap2):
            return ap2.unsqueeze(2).to_broadcast([128, NPAIR, 16])

        def cmul(o_re, o_im, xr, xi, yr, yi, reads, writes, neg_im=False):
            t0, t1 = tS[:, 0], tS[:, 1]
            rd = reads + ID_SM
            wr = writes + ID_SM
            tt(t0, xr, yr, ALU.mult, rd, ID_SM)
            tt(t1, xi, yi, ALU.mult, rd, ID_SM)
            tt(o_re, t0, t1, ALU.subtract, rd, wr)
            tt(t0, xr, yi, ALU.mult, rd, ID_SM)
            tt(t1, xi, yr, ALU.mult, rd, ID_SM)
            if not neg_im:
                tt(o_im, t0, t1, ALU.add, rd, wr)
            else:
                dv(lambda e: e.scalar_tensor_tensor(out=o_im, in0=t0, scalar=-1.0, in1=t1, op0=ALU.mult, op1=ALU.subtract), rd, wr)

        cmul(Bbar[:, 0], Bbar[:, 1], bc16(sm(FRE)), bc16(sm(FIM)), bb[:, 0], bb[:, 1], [S, 'bbcc'], [])
        dv(lambda e: e.tensor_copy(out=PW[:, 1, 0, :], in_=sm(AR)), [S], ['PW'])
        dv(lambda e: e.tensor_copy(out=PW[:, 1, 1, :], in_=sm(AI)), [S], ['PW'])
        for k in range(1, 8):
            t0, t1 = sm(T1), sm(T2)
            tt(t0, PW[:, k, 0, :], sm(AR), ALU.mult, [S, 'PW'], [S])
            tt(t1, PW[:, k, 1, :], sm(AI), ALU.mult, [S, 'PW'], [S])
            tt(PW[:, k + 1, 0, :], t0, t1, ALU.subtract, [S], ['PW'])
            tt(t0, PW[:, k, 0, :], sm(AI), ALU.mult, [S, 'PW'], [S])
            tt(t1, PW[:, k, 1, :], sm(AR), ALU.mult, [S, 'PW'], [S])
            tt(PW[:, k + 1, 1, :], t0, t1, ALU.add, [S], ['PW'])
        dv(lambda e: e.reciprocal(out=sm(RR8), in_=sm(R8)), [S], [S])
        tt(Tc[:, :, 0], PW[:, 8, 0, :], sm(RR8), ALU.mult, [S, 'PW'], ['Tc'])
        tt(Ts[:, :, 0], PW[:, 8, 1, :], sm(RR8), ALU.mult, [S, 'PW'], ['Ts'])

        def ptt(out, a, b, op, reads, writes):
            P.add('pool', lambda e: e.tensor_tensor(out=out, in0=a, in1=b, op=op), reads=reads, writes=writes)
        RS = [('rstd', 0), ('rstd', 1)]
        for mlev in range(7):
            n = 1 << mlev
            cb = Tc[:, :, n - 1:n].to_broadcast([128, NPAIR, n])
            sbb = Ts[:, :, n - 1:n].to_broadcast([128, NPAIR, n])
            tB = rstd[:, 0:NPAIR * n].rearrange("p (q c) -> p q c", q=NPAIR)
            ptt(Tc[:, :, n:2 * n], Tc[:, :, 0:n], cb, ALU.mult, ['Tc'], ['Tc'])
            ptt(tB, Ts[:, :, 0:n], sbb, ALU.mult, ['Ts'], RS)
            ptt(Tc[:, :, n:2 * n], Tc[:, :, n:2 * n], tB, ALU.subtract, ['Tc'] + RS, ['Tc'])
            ptt(Ts[:, :, n:2 * n], Tc[:, :, 0:n], sbb, ALU.mult, ['Tc', 'Ts'], ['Ts'])
            ptt(tB, Ts[:, :, 0:n], cb, ALU.mult, ['Tc', 'Ts'], RS)
            ptt(Ts[:, :, n:2 * n], Ts[:, :, n:2 * n], tB, ALU.add, ['Ts'] + RS, ['Ts'])
        WxT5 = [WxT[:, r].rearrange("p q (t h) -> p q t h", t=8) for r in range(2)]
        WinT5 = [WinT[:, r].rearrange("p q (t h) -> p q t h", t=8) for r in range(2)]
        for tau in range(8):
            k = 7 - tau
            if k == 0:
                dv(lambda e, tau=tau: e.tensor_copy(out=WxT5[0][:, :, tau, :], in_=Bbar[:, 0]), ID_SM, ID_WXT)
                dv(lambda e, tau=tau: e.tensor_copy(out=WxT5[1][:, :, tau, :], in_=Bbar[:, 1]), ID_SM, ID_WXT)
            else:
                cmul(WxT5[0][:, :, tau, :], WxT5[1][:, :, tau, :], bc16(PW[:, k, 0, :]), bc16(PW[:, k, 1, :]),
                     Bbar[:, 0], Bbar[:, 1], ['PW'], ID_WXT)
        for t in range(8):
            cmul(WinT5[0][:, :, t, :], WinT5[1][:, :, t, :], cc[:, 0], cc[:, 1],
                 bc16(PW[:, t + 1, 0, :]), bc16(PW[:, t + 1, 1, :]), ['PW', 'bbcc'], ID_WIN, neg_im=True)
        tt(sm(T1), PW[:, 8, 0, :], PW[:, 8, 0, :], ALU.mult, ['PW'], [S])
        tt(sm(T2), PW[:, 8, 1, :], PW[:, 8, 1, :], ALU.mult, ['PW'], [S])
        tt(sm(T1), sm(T1), sm(T2), ALU.add, [S], [S])
        dv(lambda e: e.reciprocal(out=sm(T3), in_=sm(T1)), [S], [S])
        tt(sm(IR), PW[:, 8, 0, :], sm(T3), ALU.mult, [S, 'PW'], [S])
        dv(lambda e: e.scalar_tensor_tensor(out=sm(II), in0=PW[:, 8, 1, :], scalar=-1.0, in1=sm(T3), op0=ALU.mult, op1=ALU.mult), [S, 'PW'], [S])
        for hq in range(2):
            qs = slice(hq * 8, hq * 8 + 8)
            tA = tmpf[:, 0, 0:1024].rearrange("p (q c) -> p q c", q=8)
            tB = tmpf[:, 1, 0:1024].rearrange("p (q c) -> p q c", q=8)
            irb = sm(IR)[:, qs].unsqueeze(2).to_broadcast([128, 8, 128])
            iib = sm(II)[:, qs].unsqueeze(2).to_broadcast([128, 8, 128])
            TT = [('tmpf', 0), ('tmpf', 1)]
            tt(tA, WxT[:, 0, qs, :], irb, ALU.mult, [S] + ID_WXT, TT)
            tt(tB, WxT[:, 1, qs, :], iib, ALU.mult, [S] + ID_WXT, TT)
            tt(Kpp[:, 0, qs, :], tA, tB, ALU.subtract, TT, ID_KPP)
            tt(tA, WxT[:, 1, qs, :], irb, ALU.mult, [S] + ID_WXT, TT)
            tt(tB, WxT[:, 0, qs, :], iib, ALU.mult, [S] + ID_WXT, TT)
            tt(Kpp[:, 1, qs, :], tA, tB, ALU.add, TT, ID_KPP)
        xsb = xs[:].rearrange("p a b -> p (a b)").bitcast(BF16)
        Kpp_bf = xsb[:, 0:4096].rearrange("p (r q c) -> p r q c", r=2, q=NPAIR)
        WxT_bf = xsb[:, 4096:8192].rearrange("p (r q c) -> p r q c", r=2, q=NPAIR)
        ID_XB = [('xs', j) for j in range(4)]
        dv(lambda e: e.memset(WinterZ[:], 0.0), [], ['Winter'])
        dv(lambda e: e.tensor_copy(out=ident_bf[:], in_=ident[:]), ['ident'], ['identbf'])
        for r in range(2):
            for g2 in range(2):
                rows = slice(g2 * 64, g2 * 64 + 64)
                P.add('act', lambda e, r=r, g2=g2, rows=rows: e.activation(out=WinterZ[rows, g2, r], in_=WinT[rows, r], func=AF.Identity), reads=ID_WIN, writes=['Winter'])
            P.add('act', lambda e, r=r: e.activation(out=Kpp_bf[:, r], in_=Kpp[:, r], func=AF.Identity), reads=ID_KPP, writes=ID_XB)
            P.add('act', lambda e, r=r: e.activation(out=WxT_bf[:, r], in_=WxT[:, r], func=AF.Identity), reads=ID_WXT, writes=ID_XB)
        P.skip = SETUP < 3
        for gb in range(8):
            b = next_bank()
            for j in range(4):
                g = gb * 4 + j
                q, g2 = g // 2, g % 2
                rows = slice(g2 * 64, g2 * 64 + 64)
                P.add('pe', lambda e, b=b, j=j, q=q, g2=g2: e.matmul(
                    banks[b][:, j * 128:(j + 1) * 128], lhsT=Kpp_bf[:, 0, q, :], rhs=WinterZ[:, g2, 0, q, :], start=True, stop=False),
                    reads=ID_XB + ['Winter'], writes=[bank_b(b)], sig=False)
                P.add('pe', lambda e, b=b, j=j, q=q, g2=g2: e.matmul(
                    banks[b][:, j * 128:(j + 1) * 128], lhsT=Kpp_bf[:, 1, q, :], rhs=WinterZ[:, g2, 1, q, :], start=False, stop=True),
                    reads=ID_XB + ['Winter'], writes=[bank_b(b)], sig=True)
            for j in range(4):
                g = gb * 4 + j
                tA = tmpf[:, j % 2, 0:128]
                dv(lambda e, b=b, j=j, tA=tA: e.tensor_tensor(out=tA, in0=banks[b][:, j * 128:(j + 1) * 128], in1=maskf[:], op=ALU.mult),
                   [bank_b(b), 'maskf'], [('tmpf', j % 2)])
                dv(lambda e, g=g, tA=tA: e.scalar_tensor_tensor(out=Wintra[:, g, :], in0=ident[:], scalar=dd_sb[:, g:g + 1], in1=tA,
                                                              op0=ALU.mult, op1=ALU.add),
                   [('tmpf', j % 2), 'ident', 'dd'], ['Wintra'])
        P.skip = SETUP < 4
        for qb in range(8):
            b = next_bank()
            for j in range(4):
                q, r = qb * 2 + j // 2, j % 2
                P.add('pe', lambda e, b=b, j=j, q=q, r=r: e.matmul(banks[b][:, j * 128:(j + 1) * 128], lhsT=WxT_bf[:, r, q, :], rhs=ident_bf[:],
                                                              start=True, stop=True),
                      reads=ID_XB + ['identbf'], writes=[bank_b(b)])
            for j in range(4):
                q, r = qb * 2 + j // 2, j % 2
                P.add('act', lambda e, b=b, j=j, q=q, r=r: e.activation(out=lhsTX[:, 2 * q, r, 0:64], in_=banks[b][:, j * 128:j * 128 + 64], func=AF.Identity),
                      reads=[bank_b(b)], writes=['lhsTX'])
                P.add('act', lambda e, b=b, j=j, q=q, r=r: e.activation(out=lhsTX[:, 2 * q + 1, r, 64:128], in_=banks[b][:, j * 128 + 64:j * 128 + 128], func=AF.Identity),
                      reads=[bank_b(b)], writes=['lhsTX'])
        P.skip = SETUP < 5

    P.skip = False
    mod_finalize()
    reserved[0] = None
    Ush = cp[:, 4:8, :].rearrange("p a (g c) -> p (a g) c", g=8)
    Ssh = [cp[:, 8:10, :].rearrange("p a (q c) -> p (a q) c", q=8),
           cp[:, 10:12, :].rearrange("p a (q c) -> p (a q) c", q=8)]
    Gsh = cp[:, 12:16, :].rearrange("p a (g c) -> p (a g) c", g=8)
    swf = cp[:, 12:20, :].rearrange("p a b -> p (a b)").bitcast(F32).rearrange("p (w q c) -> p w q c", w=4, q=8)
    ID_SW = [('cp', j) for j in range(12, 20)]
    upf = cp[:, 4:9, :].rearrange("p a b -> p (a b)")[:, 0:4 * (NT + 16)].rearrange("p (g c) -> p g c", g=4)
    ID_UPF = [('cp', j) for j in range(4, 9)]
    TGP = [0, 1, 2, 3, 20, 21, 22, 23]
    TGS = [4, 5, 6, 7, 8, 17, 18, 19]
    VCH = [12, 13, 14, 15]
    GLT = [4, 5]
    ZP = [9, 10, 11, 16]
    u16 = sb("u16", [128, 4, 16])

    def proj_slab(w_src, col0, ncols_tiles, consume):
        s = load_w([(lambda sa: sa.rearrange("p (k c) -> p k c", k=KT),
                     w_src[:, col0:col0 + 512].rearrange("(k p) c -> p k c", p=128))], None)
        wv = wsl[:, s, :].rearrange("p (k c) -> p k c", k=KT)
        for mi in range(4):
            for hf in range(NH):
                b = next_bank()
                for k in range(KT):
                    P.add('pe', lambda e, k=k, hf=hf, mi=mi, b=b, wv=wv: e.matmul(
                        banks[b][:], lhsT=wv[:, k, mi * 128:(mi + 1) * 128], rhs=hb[:, k, hf * 512:(hf + 1) * 512],
                        start=(k == 0), stop=(k == KT - 1)),
                        reads=[('slot', s), ('hb', k)], writes=[bank_b(b)], sig=(k == KT - 1))
                consume(mi, hf, b)

    def hs(hf):
        return slice(hf * 512, (hf + 1) * 512)

    def mixer(ti):
        if STAGE < 1:
            return
        def c_ussm(mi, hf, b):
            P.add('act', lambda e: e.activation(out=cp[:, mi, hs(hf)], in_=banks[b][:], func=AF.Identity),
                  reads=[bank_b(b)], writes=[('cp', mi)])
        proj_slab(w_in, 512, 4, c_ussm)
        if STAGE < 2:
            return
        for gb in range(8):
            b = next_bank()
            for j in range(4):
                g = gb * 4 + j
                blk, gi = g // 8, g % 8
                uv = cp[:, blk, :].rearrange("p (c t) -> p c t", t=8)
                for tau in range(8):
                    o = 240 * gi + 112 - 16 * tau
                    P.add('pe', lambda e, b=b, j=j, o=o, tau=tau, uv=uv: e.matmul(
                        banks[b][:, j * 128:(j + 1) * 128], lhsT=sel_bf[:, o:o + 128], rhs=uv[:, :, tau],
                        start=(tau == 0), stop=(tau == 7)),
                        reads=['sel', ('cp', blk)], writes=[bank_b(b)], sig=(tau == 7))
            P.add('act', lambda e, b=b, gb=gb: e.activation(
                out=Ush[:, gb * 4:gb * 4 + 4, :], in_=banks[b][:].rearrange("p (g c) -> p g c", g=4), func=AF.Identity),
                reads=[bank_b(b)], writes=[('cp', 4 + gb // 2)])
        if STAGE < 3:
            return
        A_, B_, C_, D_ = swf[:, 0], swf[:, 1], swf[:, 2], swf[:, 3]
        for hq in range(2):
            xb = [[next_bank(), next_bank()], [next_bank(), next_bank()]]
            for qq in range(8):
                q = hq * 8 + qq
                for r in range(2):
                    b = xb[r][qq // 4]
                    reg = banks[b][:, (qq % 4) * 128:(qq % 4 + 1) * 128]
                    for g2 in range(2):
                        g = 2 * q + g2
                        P.add('pe', lambda e, reg=reg, g=g, r=r, g2=g2: e.matmul(
                            reg, lhsT=lhsTX[:, g, r, :], rhs=Ush[:, g, :], start=(g2 == 0), stop=(g2 == 1)),
                            reads=['lhsTX', ('cp', 4 + g // 8)], writes=[bank_b(b)], sig=(g2 == 1))
            qs = slice(hq * 8, hq * 8 + 8)
            for bh in range(2):
                q4 = slice(bh * 4, bh * 4 + 4)
                qg = slice(hq * 8 + bh * 4, hq * 8 + bh * 4 + 4)
                Xr = banks[xb[0][bh]][:].rearrange("p (q c) -> p q c", q=4)
                Xi = banks[xb[1][bh]][:].rearrange("p (q c) -> p q c", q=4)
                rdb = [bank_b(xb[0][bh]), bank_b(xb[1][bh]), 'Tc', 'Ts'] + ID_SW
                tt(A_[:, q4, :], Xr, Tc[:, qg, :], ALU.mult, rdb, ID_SW)
                tt(B_[:, q4, :], Xi, Ts[:, qg, :], ALU.mult, rdb, ID_SW)
                tt(A_[:, q4, :], A_[:, q4, :], B_[:, q4, :], ALU.add, rdb, ID_SW)
                tt(C_[:, q4, :], Xi, Tc[:, qg, :], ALU.mult, rdb, ID_SW)
                tt(B_[:, q4, :], Xr, Ts[:, qg, :], ALU.mult, rdb, ID_SW)
                tt(C_[:, q4, :], C_[:, q4, :], B_[:, q4, :], ALU.subtract, rdb, ID_SW)
            for qq in range(8):
                q = hq * 8 + qq
                r8b = sm(R8)[:, q:q + 1].to_broadcast([128, 128])
                dv(lambda e, qq=qq, q=q, r8b=r8b: e.tensor_tensor_scan(
                    out=B_[:, qq, :], data0=r8b, data1=A_[:, qq, :], initial=carry[:, 0, q:q + 1], op0=ALU.mult, op1=ALU.add),
                    ID_SW + ['carry', 'smalls'], ID_SW)
                dv(lambda e, qq=qq, q=q, r8b=r8b: e.tensor_tensor_scan(
                    out=D_[:, qq, :], data0=r8b, data1=C_[:, qq, :], initial=carry[:, 1, q:q + 1], op0=ALU.mult, op1=ALU.add),
                    ID_SW + ['carry', 'smalls'], ID_SW)
            rd = ID_SW + ['Tc', 'Ts']
            tt(A_, B_, Tc[:, qs, :], ALU.mult, rd, ID_SW)
            tt(C_, D_, Ts[:, qs, :], ALU.mult, rd, ID_SW)
            tt(A_, A_, C_, ALU.subtract, rd, ID_SW)
            tt(C_, B_, Ts[:, qs, :], ALU.mult, rd, ID_SW)
            tt(B_, D_, Tc[:, qs, :], ALU.mult, rd, ID_SW)
            tt(C_, C_, B_, ALU.add, rd, ID_SW)
            for r, src in ((0, A_), (1, C_)):
                sid = [('cp', 8 + 2 * r), ('cp', 9 + 2 * r)]
                dv(lambda e, r=r, src=src, qs=qs: e.tensor_copy(out=Ssh[r][:, qs, 1:128], in_=src[:, :, 0:127]), ID_SW, sid)
                dv(lambda e, r=r, qs=qs: e.tensor_copy(out=Ssh[r][:, qs, 0:1], in_=carry[:, r, qs].unsqueeze(2)), ['carry'], sid)
                dv(lambda e, r=r, src=src, qs=qs: e.tensor_copy(out=carry[:, r, qs].unsqueeze(2), in_=src[:, :, 127:128]), ID_SW, ['carry'])
        for gi in range(2):
            def c_gp(mi, hf, b, gi=gi):
                ch = TGP[gi * 4 + mi]
                P.add('act', lambda e: e.activation(out=cp[:, ch, hs(hf)], in_=banks[b][:], func=AF.Tanh, scale=0.5),
                      reads=[bank_b(b)], writes=[('cp', ch)])
            proj_slab(w_in, 1024 + gi * 512, 4, c_gp)
        for gb in range(8):
            b = next_bank()
            for j in range(4):
                g = gb * 4 + j
                q, g2 = g // 2, g % 2
                rows = slice(g2 * 64, g2 * 64 + 64)
                reg = banks[b][:, j * 128:(j + 1) * 128]
                rdl = ['Wintra', 'Winter', ('cp', 4 + g // 8), ('cp', 8), ('cp', 9), ('cp', 10), ('cp', 11)]
                P.add('pe', lambda e, reg=reg, g=g: e.matmul(reg, lhsT=Wintra[:, g, :], rhs=Ush[:, g, :], start=True, stop=False),
                      reads=rdl, writes=[bank_b(b)], sig=False)
                P.add('pe', lambda e, reg=reg, q=q, g2=g2: e.matmul(reg, lhsT=WinterZ[:, g2, 0, q, :], rhs=Ssh[0][:, q, :], start=False, stop=False),
                      reads=rdl, writes=[bank_b(b)], sig=False)
                P.add('pe', lambda e, reg=reg, q=q, g2=g2: e.matmul(reg, lhsT=WinterZ[:, g2, 1, q, :], rhs=Ssh[1][:, q, :], start=False, stop=True),
                      reads=rdl, writes=[bank_b(b)], sig=True)
            P.add('act', lambda e, b=b, gb=gb: e.activation(
                out=Gsh[:, gb * 4:gb * 4 + 4, :], in_=banks[b][:].rearrange("p (g c) -> p g c", g=4), func=AF.Gelu),
                reads=[bank_b(b)], writes=[('cp', 12 + gb // 2)])
        for blk in range(4):
            for th in range(2):
                b = next_bank()
                for j in range(4):
                    t = th * 4 + j
                    for gi in range(8):
                        o = 240 * t + 112 - 16 * gi
                        P.add('pe', lambda e, b=b, j=j, o=o, gi=gi, blk=blk: e.matmul(
                            banks[b][:, j * 128:(j + 1) * 128], lhsT=sel_bf[:, o:o + 128], rhs=Gsh[:, blk * 8 + gi, :],
                            start=(gi == 0), stop=(gi == 7)),
                            reads=['sel', ('cp', 12 + blk)], writes=[bank_b(b)], sig=(gi == 7))
                gv = cp[:, 16 + blk, :].rearrange("p (c t) -> p c t", t=8)[:, :, th * 4:th * 4 + 4]
                dv(lambda e, b=b, gv=gv: e.tensor_copy(out=gv, in_=banks[b][:].rearrange("p (t c) -> p c t", t=4)),
                   [bank_b(b)], [('cp', 16 + blk)])
        s = load_w([(lambda sa: sa.rearrange("p (k c) -> p k c", k=4), w_glu.rearrange("(k p) c -> p k c", p=128))], None)
        wv = wsl[:, s, :].rearrange("p (k c) -> p k c", k=4)
        for mi in range(4):
            for hf in range(NH):
                bv_, bg_ = next_bank(), next_bank()
                for part, b in ((0, bv_), (1, bg_)):
                    for k in range(4):
                        P.add('pe', lambda e, k=k, b=b, part=part, mi=mi, hf=hf, wv=wv: e.matmul(
                            banks[b][:], lhsT=wv[:, k, part * 512 + mi * 128: part * 512 + (mi + 1) * 128],
                            rhs=cp[:, 16 + k, hs(hf)], start=(k == 0), stop=(k == 3)),
                            reads=[('slot', s), ('cp', 16 + k)], writes=[bank_b(b)], sig=(k == 3))
                tb = (mi * NH + hf) % 2
                P.add('act', lambda e, b=bg_, mi=mi, tb=tb: e.activation(
                    out=cp[:, GLT[tb], 0:512], in_=banks[b][:], func=AF.Tanh, bias=bglu_h[:, 4 + mi:5 + mi], scale=0.5),
                    reads=[bank_b(bg_), 'bgluh'], writes=[('cp', GLT[tb])])
                dv(lambda e, b=bv_, mi=mi, tb=tb: e.tensor_scalar(
                    out=tmpf[:, tb, 0:512], in0=banks[b][:], scalar1=bglu_sb[:, mi:mi + 1], scalar2=0.5, op0=ALU.add, op1=ALU.mult),
                    [bank_b(bv_), 'bglu'], [('tmpf', tb)])
                dv(lambda e, mi=mi, hf=hf, tb=tb: e.scalar_tensor_tensor(
                    out=cp[:, VCH[mi], hs(hf)], in0=cp[:, GLT[tb], 0:512], scalar=1.0, in1=tmpf[:, tb, 0:512], op0=ALU.add, op1=ALU.mult),
                    [('cp', GLT[tb]), ('tmpf', tb)], [('cp', VCH[mi])])
        if STAGE < 5:
            return
        W16 = NT + 16
        dv(lambda e: e.tensor_copy(out=upf[:, :, 0:16], in_=halo[:]), ['halo'], ID_UPF)

        def c_upool(mi, hf, b):
            P.add('act', lambda e: e.activation(out=upf[:, mi, 16 + hf * 512:16 + (hf + 1) * 512], in_=banks[b][:], func=AF.Identity),
                  reads=[bank_b(b)], writes=ID_UPF)
        proj_slab(w_in, 0, 4, c_upool)
        dv(lambda e: e.tensor_copy(out=halo[:], in_=upf[:, :, NT:NT + 16]), ID_UPF, ['halo'])
        for kg in range(4):
            w = 2 << kg
            src = upf[:, kg, :]
            TT = [('tmpf', 0), ('tmpf', 1)]
            for j in range(kg + 1):
                d = 1 << j
                lo = 2 * d - 1
                dst = tmpf[:, j % 2, :]
                P.add('dve', lambda e, dst=dst, src=src, d=d, lo=lo: e.tensor_tensor(
                    out=dst[:, lo:W16], in0=src[:, lo:W16], in1=src[:, lo - d:W16 - d], op=ALU.add),
                    reads=ID_UPF + TT, writes=[('tmpf', j % 2)])
                src = dst
            ssum = src
            if ti == 0:
                dv(lambda e, kg=kg: e.tensor_copy(out=u16[:, kg, :], in_=upf[:, kg, 16:32]), ID_UPF, ['u16'])
            dv(lambda e, kg=kg, w=w, ssum=ssum: e.scalar_tensor_tensor(
                out=upf[:, kg, 16:W16], in0=ssum[:, 16:W16], scalar=1.0 / w, in1=upf[:, kg, 16:W16], op0=ALU.mult, op1=ALU.subtract),
                ID_UPF + TT, ID_UPF)
            if ti == 0:
                dv(lambda e, kg=kg, ssum=ssum: e.tensor_tensor(out=ssum[:, 16:32], in0=ssum[:, 16:32], in1=rc_sb[:, kg, :], op=ALU.mult),
                   TT + ['rc'], TT)
                dv(lambda e, kg=kg, ssum=ssum: e.tensor_tensor(out=upf[:, kg, 16:32], in0=ssum[:, 16:32], in1=u16[:, kg, :], op=ALU.subtract),
                   TT + ID_UPF + ['u16'], ID_UPF)
        for kg in range(4):
            for hf in range(NH):
                b = next_bank()
                P.add('pe', lambda e, kg=kg, hf=hf, b=b: e.matmul(banks[b][:], lhsT=poolw_bf[:, kg, :], rhs=upf[:, kg, 16 + hf * 512:16 + (hf + 1) * 512], start=True, stop=True),
                      reads=['poolw'] + ID_UPF, writes=[bank_b(b)])
                P.add('act', lambda e, kg=kg, hf=hf, b=b: e.activation(
                    out=cp[:, ZP[kg], hs(hf)], in_=banks[b][:], func=AF.Identity, bias=pbsc[:, kg:kg + 1], scale=pbs_sb[:, 1, kg:kg + 1]),
                    reads=[bank_b(b), 'pbsc', 'pbs'], writes=[('cp', ZP[kg])])
        if STAGE < 6:
            return
        for gi in range(2):
            def c_gs(mi, hf, b, gi=gi):
                ch = TGS[gi * 4 + mi]
                P.add('act', lambda e: e.activation(out=cp[:, ch, hs(hf)], in_=banks[b][:], func=AF.Tanh, scale=0.5),
                      reads=[bank_b(b)], writes=[('cp', ch)])
            proj_slab(w_in, 2048 + gi * 512, 4, c_gs)
        s_pu = load_w([(lambda sa: sa.rearrange("p (k c) -> p k c", k=4), w_pu.rearrange("(k p) c -> p k c", p=128))], None)
        s_su = load_w([(lambda sa: sa.rearrange("p (k c) -> p k c", k=4), w_su.rearrange("(k p) c -> p k c", p=128))], None)
        wpu = wsl[:, s_pu, :].rearrange("p (k c) -> p k c", k=4)
        wsu = wsl[:, s_su, :].rearrange("p (k c) -> p k c", k=4)
        for m in range(8):
            for hf in range(NH):
                bp, bs_ = next_bank(), next_bank()
                for k in range(4):
                    P.add('pe', lambda e, k=k, m=m, hf=hf, b=bp: e.matmul(
                        banks[b][:], lhsT=wpu[:, k, m * 128:(m + 1) * 128], rhs=cp[:, ZP[k], hs(hf)], start=(k == 0), stop=(k == 3)),
                        reads=[('slot', s_pu), ('cp', ZP[k])], writes=[bank_b(bp)], sig=(k == 3))
                for k in range(4):
                    P.add('pe', lambda e, k=k, m=m, hf=hf, b=bs_: e.matmul(
                        banks[b][:], lhsT=wsu[:, k, m * 128:(m + 1) * 128], rhs=cp[:, VCH[k], hs(hf)], start=(k == 0), stop=(k == 3)),
                        reads=[('slot', s_su), ('cp', VCH[k])], writes=[bank_b(bs_)], sig=(k == 3))
                chs = TGS[m]
                dv(lambda e, m=m, hf=hf, b=bp: e.scalar_tensor_tensor(
                    out=tmpf[:, 0, 0:512], in0=cp[:, TGP[m], hs(hf)], scalar=1.0, in1=banks[b][:], op0=ALU.add, op1=ALU.mult),
                    [bank_b(bp), ('cp', TGP[m])], [('tmpf', 0)])
                dv(lambda e, chs=chs, hf=hf, b=bs_: e.scalar_tensor_tensor(
                    out=tmpf[:, 1, 0:512], in0=cp[:, chs, hs(hf)], scalar=1.0, in1=banks[b][:], op0=ALU.add, op1=ALU.mult),
                    [bank_b(bs_), ('cp', chs)], [('tmpf', 1)])
                P.add('dve', lambda e, m=m, hf=hf: e.tensor_tensor(
                    out=cp[:, TGP[m], hs(hf)], in0=tmpf[:, 0, 0:512], in1=tmpf[:, 1, 0:512], op=ALU.add),
                    reads=[('tmpf', 0), ('tmpf', 1)], writes=[('cp', TGP[m])])
        for m2 in range(2):
            s = load_w([(lambda sa: sa.rearrange("p (k c) -> p k c", k=KT),
                         w_o[:, m2 * 512:(m2 + 1) * 512].rearrange("(k p) c -> p k c", p=128))], None)
            wv = wsl[:, s, :].rearrange("p (k c) -> p k c", k=KT)
            for mm in range(4):
                m = m2 * 4 + mm
                for hf in range(NH):
                    b = next_bank()
                    for k in range(KT):
                        P.add('pe', lambda e, k=k, mm=mm, hf=hf, b=b, wv=wv: e.matmul(
                            banks[b][:], lhsT=wv[:, k, mm * 128:(mm + 1) * 128], rhs=cp[:, TGP[k], hs(hf)], start=(k == 0), stop=(k == KT - 1)),
                            reads=[('slot', s), ('cp', TGP[k])], writes=[bank_b(b)], sig=(k == KT - 1))
                    dv(lambda e, m=m, hf=hf, b=b: e.scalar_tensor_tensor(
                        out=xs[:, m, hs(hf)], in0=banks[b][:], scalar=Gvec[:, 1, m:m + 1], in1=xs[:, m, hs(hf)], op0=ALU.mult, op1=ALU.add),
                        [bank_b(b), 'Gvec', ('xs', m)], [('xs', m)])

    for ti in range(NTILES):
        t0 = ti * NT
        for k in range(KT):
            P.add('sp', lambda e, k=k, t0=t0: e.dma_start(out=xs[:, k, :], in_=xT[k * 128:(k + 1) * 128, t0:t0 + NT]),
                  writes=[('xs', k)], dma=('xs', k))
        norm_mod(0, t0)
        ffn(0, w1i, w1o)
        if MIXER:
            norm_mod(1, t0)
            mixer(ti)
        norm_mod(2, t0)
        ffn(2, w2i, w2o)
        norm_mod(3, t0)

    with nc.Block() as block:
        P.emit(block)
    return nc, es, P


_CACHE = {}


def _consts():
    ident = np.eye(128, dtype=np.float32)
    sel = np.zeros((128, 1920), np.float32)
    for p in range(128):
        sel[p, 240 * (p // 16) + 112 + (p % 16)] = 1.0
    mask = np.zeros((128, 128), np.float32)
    for a in range(8):
        for b in range(8):
            if b >= a:
                mask[a * 16:(a + 1) * 16, b * 16:(b + 1) * 16] = 1.0
    rc = np.zeros((128, 4, 16), np.float32)
    for k in range(4):
        w = 2 << k
        for tt_ in range(16):
            rc[:, k, tt_] = 1.0 / min(tt_ + 1, w)
    return ident, sel, mask, rc


def _pair_layout(a):
    g, n = a.shape[0], a.shape[1]
    rest = a.shape[2:]
    a = a.reshape((16, 2, n) + rest)
    a = np.moveaxis(a, 0, 2)
    return np.ascontiguousarray(a.reshape((128, 16) + rest))


def make_in_maps(inputs):
    f = lambda a: np.ascontiguousarray(np.asarray(a, dtype=np.float32))
    x = f(inputs["x"])
    c = f(inputs["c"])
    ident, sel, mask, rc = _consts()
    gvec = np.stack([f(inputs["g_ffn1"])[0], f(inputs["g_mix"])[0], f(inputs["g_ffn2"])[0], f(inputs["g_final"])], 0)
    gvec = np.ascontiguousarray(gvec.reshape(4, KT, 128).transpose(2, 0, 1))
    b_ada = np.ascontiguousarray(f(inputs["b_ada"])[0].reshape(72, 128).T)
    pool_w = np.ascontiguousarray(f(inputs["pool_w"])[0].transpose(1, 0, 2))
    pool_bs = np.ascontiguousarray(np.stack([f(inputs["pool_b"])[0].reshape(4, 128).T,
                                             f(inputs["pool_scale"])[0].reshape(4, 128).T], 1))
    b_glu = np.ascontiguousarray(f(inputs["b_glu"])[0].reshape(8, 128).T)
    lrl = f(inputs["ssm_lam_re_log"])[0]
    lim = f(inputs["ssm_lam_im"])[0]
    ldt = np.broadcast_to(f(inputs["ssm_log_dt"])[0][:, None], (32, 64))
    ssm_sc = np.ascontiguousarray(np.stack([_pair_layout(lrl), _pair_layout(lim), _pair_layout(ldt)], 1))
    ssm_b = np.ascontiguousarray(np.stack([_pair_layout(f(inputs["ssm_b_re"])[0]),
                                           _pair_layout(f(inputs["ssm_b_im"])[0])], 1))
    cre = f(inputs["ssm_c_re"])[0].transpose(0, 2, 1)
    cim = f(inputs["ssm_c_im"])[0].transpose(0, 2, 1)
    ssm_cc = np.ascontiguousarray(np.stack([_pair_layout(cre), _pair_layout(cim)], 1))
    dd = f(inputs["ssm_d"])[0].reshape(32, 16)
    ssm_dd = np.ascontiguousarray(np.tile(dd.T, (8, 1)))
    shared = dict(
        w_ada=f(inputs["w_ada"])[0], b_ada=b_ada, gvec=gvec,
        w1i=f(inputs["w_ffn1_in"])[0], w1o=f(inputs["w_ffn1_out"])[0],
        w2i=f(inputs["w_ffn2_in"])[0], w2o=f(inputs["w_ffn2_out"])[0],
        w_in=f(inputs["w_in"])[0], pool_w=pool_w, pool_bs=pool_bs,
        w_pu=f(inputs["w_pool_up"])[0], w_glu=f(inputs["w_glu"])[0], b_glu=b_glu,
        w_su=f(inputs["w_ssm_up"])[0], w_o=f(inputs["w_out"])[0],
        ssm_sc=ssm_sc, ssm_b=ssm_b, ssm_cc=ssm_cc, ssm_dd=ssm_dd,
        cst_ident=ident, cst_sel=sel, cst_mask=mask, cst_rc=rc,
    )
    maps = []
    for b in range(8):
        m = dict(shared)
        m["xT"] = np.ascontiguousarray(x[b].T)
        m["c_l"] = np.ascontiguousarray(c[b].reshape(KT, 128).T)
        maps.append(m)
    return maps


def kernel(**inputs):
    if "nc" not in _CACHE:
        _CACHE["nc"] = build_nc()
    nc, es, P = _CACHE["nc"]
    in_maps = make_in_maps(inputs)
    res = run_bass_kernel_spmd(nc, in_maps, core_ids=list(range(8)))
    out = np.stack([np.asarray(r["outT"], dtype=np.float32).T for r in res.results], 0)
    return np.ascontiguousarray(out)
```

```python
import numpy as np
from contextlib import ExitStack
import concourse.bass as bass
import concourse.mybir as mybir
from concourse.bass_utils import run_bass_kernel_spmd

F32 = mybir.dt.float32
BF16 = mybir.dt.bfloat16
AF = mybir.ActivationFunctionType
ALU = mybir.AluOpType

D = 1024
T = 4096
FF = 2816
NT = 1024
import os
NTILES = int(os.environ.get('KTILES', str(T // NT)))
NH = NT // 512
KT = D // 128
FT = FF // 128
EPS = 1e-6
LCH = 8
NC_T = NT // LCH
NPAIR = 16

import os
MIXER = True
STAGE = int(os.environ.get('KSTAGE', '9'))
SETUP = int(os.environ.get('KSETUP', '9'))
SUB = int(os.environ.get('KSUB', '9'))


class Prog:
    def __init__(self, nc, es):
        self.nc = nc
        self.es = es
        self.ops = []
        self.last_w = {}
        self.readers = {}
        self.dma_keys = {}

    skip = False

    def add(self, eng, fn, reads=(), writes=(), dma=None, sig=True):
        if self.skip:
            return None
        idx = len(self.ops)
        deps = set()
        for b in reads:
            if b in self.last_w:
                deps.add(self.last_w[b])
        for b in writes:
            if b in self.last_w:
                deps.add(self.last_w[b])
            for r in self.readers.get(b, ()):
                deps.add(r)
        for b in writes:
            self.last_w[b] = idx
            self.readers[b] = []
        for b in reads:
            self.readers.setdefault(b, []).append(idx)
        deps.discard(idx)
        self.ops.append(dict(eng=eng, fn=fn, deps=deps, dma=dma, sig=sig))
        return idx

    def emit(self, block):
        nc = self.nc
        ops = self.ops
        engs = ['pe', 'act', 'dve', 'pool', 'sp']
        sems = {e: self.es.enter_context(nc.semaphore(f"s_{e}")) for e in ['pe', 'act', 'dve', 'pool']}
        keysem = {}
        for o in ops:
            if o['dma'] is not None and o['dma'] not in keysem:
                keysem[o['dma']] = self.es.enter_context(nc.semaphore(f"d_{len(keysem)}"))
        cnt = {e: 0 for e in sems}
        for o in ops:
            if o['dma'] is None:
                if o['sig']:
                    cnt[o['eng']] += 1
                    o['cnt'] = cnt[o['eng']]
                else:
                    o['cnt'] = None
        nxt = {e: None for e in sems}
        for o in reversed(ops):
            if o['dma'] is None:
                if o['cnt'] is None:
                    o['cnt'] = nxt[o['eng']]
                    assert o['cnt'] is not None
                else:
                    nxt[o['eng']] = o['cnt']
        cum = {k: 0 for k in keysem}
        waited = {}
        per_eng = {e: [] for e in engs}
        for i, o in enumerate(ops):
            waits = {}
            for j in o['deps']:
                d = ops[j]
                if d['dma'] is not None:
                    s = keysem[d['dma']]
                    v = cum[d['dma']]
                else:
                    if d['eng'] == 'pe' and o['eng'] == 'pe' and o['dma'] is None:
                        continue
                    s = sems[d['eng']]
                    v = d['cnt']
                key = (o['eng'], id(s))
                if waited.get(key, 0) >= v:
                    continue
                if key not in waits or waits[key][1] < v:
                    waits[key] = (s, v)
            for key, (s, v) in waits.items():
                waited[key] = v
            if o['dma'] is not None:
                cum[o['dma']] += 16
            per_eng[o['eng']].append((o, list(waits.values())))
        self.n_ops = {e: len(per_eng[e]) for e in engs}

        def run(e, lst):
            for o, waits in lst:
                for (s, v) in waits:
                    e.wait_ge(s, v)
                ins = o['fn'](e)
                if o['dma'] is not None:
                    ins.then_inc(keysem[o['dma']], 16)
                elif o['sig']:
                    ins.then_inc(sems[o['eng']], 1)

        @block.tensor
        def _(e):
            run(e, per_eng['pe'])

        @block.scalar
        def _(e):
            run(e, per_eng['act'])

        @block.vector
        def _(e):
            run(e, per_eng['dve'])

        @block.gpsimd
        def _(e):
            run(e, per_eng['pool'])

        @block.sync
        def _(e):
            run(e, per_eng['sp'])
            for k, s in keysem.items():
                if isinstance(k, str) and k.startswith('out'):
                    e.wait_ge(s, cum[k])


def build_nc(debug=False):
    nc = bass.Bass("TRN2", target_bir_lowering=False)
    es = ExitStack()

    def din(name, shape, dt=F32):
        return nc.dram_tensor(name, list(shape), dt, kind="ExternalInput").ap()

    xT = din("xT", [D, T])
    c_l = din("c_l", [128, KT])
    w_ada = din("w_ada", [D, 9 * D])
    b_ada = din("b_ada", [128, 72])
    gvec = din("gvec", [128, 4, KT])
    w1i = din("w1i", [D, 2 * FF])
    w1o = din("w1o", [FF, D])
    w2i = din("w2i", [D, 2 * FF])
    w2o = din("w2o", [FF, D])
    w_in = din("w_in", [D, 3 * D])
    pool_w = din("pool_w", [128, 4, 128])
    pool_bs = din("pool_bs", [128, 2, 4])
    w_pu = din("w_pu", [512, D])
    w_glu = din("w_glu", [512, D])
    b_glu = din("b_glu", [128, 8])
    w_su = din("w_su", [512, D])
    w_o = din("w_o", [D, D])
    ssm_sc = din("ssm_sc", [128, 3, NPAIR])
    ssm_b = din("ssm_b", [128, 2, NPAIR, 16])
    ssm_cc = din("ssm_cc", [128, 2, NPAIR, 16])
    ssm_dd = din("ssm_dd", [128, 32])
    cst_ident = din("cst_ident", [128, 128])
    cst_sel = din("cst_sel", [128, 1920])
    cst_mask = din("cst_mask", [128, 128])
    outT = nc.dram_tensor("outT", [D, T], F32, kind="ExternalOutput").ap()
    DBG = os.environ.get('KDBG', '0') == '1'
    dbgt = {}
    if DBG:
        for nm in ('s', 'v', 'z', 'm'):
            dbgt[nm] = nc.dram_tensor("dbg_" + nm, [128, 8, NT], BF16, kind="ExternalOutput").ap()

    def tap(nm, j0, ti):
        if DBG and ti == 0:
            P.add('sp', lambda e: e.dma_start(out=dbgt[nm], in_=cp[:, j0:j0 + 8, :]), reads=[('cp', j) for j in range(j0, j0 + 8)], dma='out_' + nm)

    def sb(name, shape, dt=F32):
        return es.enter_context(nc.sbuf_tensor(name, list(shape), dt))

    def ps(name, shape, dt=F32):
        return es.enter_context(nc.psum_tensor(name, list(shape), dt))

    P = Prog(nc, es)

    xs = sb("xs", [128, KT, NT])
    hb = sb("hb", [128, KT, NT], BF16)
    NCHK = 24
    cp = sb("cp", [128, NCHK, NT], BF16)
    tmpf = sb("tmpf", [128, 2, NT + 16])
    rstd = sb("rstd", [128, NT])
    NSLOT = 4
    wsl = sb("wsl", [128, NSLOT, 4096], BF16)
    ones_bf = sb("ones_bf", [128, 128], BF16)
    ident = sb("ident", [128, 128])
    eps_sb = sb("eps_sb", [128, 1])
    modv = sb("modv", [128, 72])
    cl_sb = sb("cl_sb", [128, KT])
    sc_bf = sb("sc_bf", [128, KT], BF16)
    bada_sb = sb("bada_sb", [128, 72])
    gv_sb = sb("gv_sb", [128, 4, KT])
    Avec = sb("Avec", [128, 3, KT])
    Gvec = sb("Gvec", [128, 3, KT])
    banks = [ps(f"bank{i}", [128, 512]) for i in range(8)]
    ostage = cp[:].rearrange("p a b -> p (a b)").bitcast(F32)

    def bank_b(i):
        return ('bank', i)

    slot_use = [0]

    def load_w(src_ap_list, shape_views):
        s = slot_use[0] % NSLOT
        slot_use[0] += 1
        for dst_fn, src in src_ap_list:
            dst = dst_fn(wsl[:, s, :])
            P.add('pool', lambda e, dst=dst, src=src: e.dma_start(out=dst, in_=src),
                  writes=[('slot', s)], dma=('slot', s))
        return s

    bank_rr = [0]

    reserved = [None]

    def next_bank():
        b = bank_rr[0] % 8
        bank_rr[0] += 1
        if b == reserved[0]:
            return next_bank()
        return b

    def small_load(dst, src, key):
        P.add('sp', lambda e: e.dma_start(out=dst, in_=src), writes=[key], dma=key)

    small_load(cl_sb[:], c_l, 'cl')
    small_load(bada_sb[:], b_ada, 'bada')
    small_load(gv_sb[:], gvec, 'gv')
    small_load(ident[:], cst_ident, 'ident')
    P.add('dve', lambda e: e.memset(ones_bf[:], 1.0 / D), writes=['ones'])
    P.add('dve', lambda e: e.memset(eps_sb[:], EPS), writes=['eps'])

    P.add('act', lambda e: e.activation(out=sc_bf[:], in_=cl_sb[:], func=AF.Silu),
          reads=['cl'], writes=['sc'])
    mod_bank = next_bank()
    reserved[0] = mod_bank
    for sl in range(18):
        c0 = sl * 512
        s = load_w([(lambda sa: sa.rearrange("p (k c) -> p k c", k=KT),
                     w_ada[:, c0:c0 + 512].rearrange("(k p) c -> p k c", p=128))], None)
        wv = wsl[:, s, :].rearrange("p (k c) -> p k c", k=KT)
        for mm in range(4):
            m = sl * 4 + mm
            for k in range(KT):
                P.add('pe', lambda e, m=m, k=k, mm=mm, wv=wv: e.matmul(
                    banks[mod_bank][:, m:m + 1], lhsT=wv[:, k, mm * 128:(mm + 1) * 128],
                    rhs=sc_bf[:, k:k + 1], start=(k == 0), stop=(k == KT - 1)),
                    reads=[('slot', s), 'sc'], writes=[bank_b(mod_bank)], sig=(k == KT - 1))
    def mod_finalize():
        P.add('dve', lambda e: e.tensor_tensor(out=modv[:], in0=banks[mod_bank][:, 0:72], in1=bada_sb[:], op=ALU.add),
              reads=[bank_b(mod_bank), 'bada'], writes=['modv'])
        for sub in range(3):
            sc_ap = modv[:, (sub * 3 + 1) * 8:(sub * 3 + 1) * 8 + 8]
            gt_ap = modv[:, (sub * 3 + 2) * 8:(sub * 3 + 2) * 8 + 8]
            P.add('dve', lambda e, sub=sub, sc_ap=sc_ap: e.scalar_tensor_tensor(
                out=Avec[:, sub, :], in0=sc_ap, scalar=1.0, in1=gv_sb[:, sub, :], op0=ALU.add, op1=ALU.mult),
                reads=['modv', 'gv'], writes=['Avec'])
            P.add('dve', lambda e, sub=sub, gt_ap=gt_ap: e.tensor_scalar(
                out=Gvec[:, sub, :], in0=gt_ap, scalar1=0.5, scalar2=None, op0=ALU.mult),
                reads=['modv'], writes=['Gvec'])


    def Bvec(sub, k):
        return modv[:, (sub * 3) * 8 + k:(sub * 3) * 8 + k + 1]

    deferred_out = []

    def norm_mod(sub, t0):
        for k in range(KT):
            P.add('act', lambda e, k=k: e.activation(out=hb[:, k, :], in_=xs[:, k, :], func=AF.Square),
                  reads=[('xs', k)], writes=[('hb', k)])
        bs = [next_bank() for _ in range(NH)]
        for hf in range(NH):
            for k in range(KT):
                P.add('pe', lambda e, k=k, hf=hf: e.matmul(
                    banks[bs[hf]][:], lhsT=ones_bf[:], rhs=hb[:, k, hf * 512:(hf + 1) * 512],
                    start=(k == 0), stop=(k == KT - 1)),
                    reads=['ones', ('hb', k)], writes=[bank_b(bs[hf])], sig=(k == KT - 1))
            P.add('act', lambda e, hf=hf: e.activation(
                out=rstd[:, hf * 512:(hf + 1) * 512], in_=banks[bs[hf]][:], func=AF.Ln,
                bias=eps_sb[:, 0:1], scale=1.0),
                reads=[bank_b(bs[hf]), 'eps'], writes=[('rstd', hf)])
            P.add('act', lambda e, hf=hf: e.activation(
                out=rstd[:, hf * 512:(hf + 1) * 512], in_=rstd[:, hf * 512:(hf + 1) * 512], func=AF.Exp, scale=-0.5),
                reads=[('rstd', hf)], writes=[('rstd', hf)])
        for k in range(KT):
            tb = k % 2
            if sub < 3:
                a_ap = Avec[:, sub, k:k + 1]
            else:
                a_ap = gv_sb[:, 3, k:k + 1]
            if sub == 3:
                P.add('dve', lambda e, k=k, a_ap=a_ap: e.scalar_tensor_tensor(
                    out=ostage[:, k * NT:(k + 1) * NT], in0=xs[:, k, :], scalar=a_ap, in1=rstd[:], op0=ALU.mult, op1=ALU.mult),
                    reads=[('xs', k), ('rstd', 0), ('rstd', 1), 'gv'], writes=[('cp', 2 * k), ('cp', 2 * k + 1)])
            else:
                P.add('dve', lambda e, k=k, tb=tb, a_ap=a_ap: e.scalar_tensor_tensor(
                    out=tmpf[:, tb, 0:NT], in0=xs[:, k, :], scalar=a_ap, in1=rstd[:], op0=ALU.mult, op1=ALU.mult),
                    reads=[('xs', k), ('rstd', 0), ('rstd', 1), 'Avec', 'gv'], writes=[('tmpf', tb)])
            if sub < 3:
                P.add('act', lambda e, k=k, tb=tb: e.activation(
                    out=hb[:, k, :], in_=tmpf[:, tb, 0:NT], func=AF.Identity, bias=Bvec(sub, k), scale=1.0),
                    reads=[('tmpf', tb), 'modv'], writes=[('hb', k)])
            else:
                deferred_out.append((k, t0))

    def ffn(sub, w_i, w_o_):
        for sl in range(FT // 2):
            f0 = sl * 2
            s = load_w([
                (lambda sa: sa.rearrange("p (k a c) -> p k a c", k=KT, a=2)[:, :, 0, :],
                 w_i[:, f0 * 128:f0 * 128 + 256].rearrange("(k p) c -> p k c", p=128)),
                (lambda sa: sa.rearrange("p (k a c) -> p k a c", k=KT, a=2)[:, :, 1, :],
                 w_i[:, FF + f0 * 128:FF + f0 * 128 + 256].rearrange("(k p) c -> p k c", p=128)),
            ], None)
            wv = wsl[:, s, :].rearrange("p (k a c) -> p k a c", k=KT, a=2)
            for ff in range(2):
                f = f0 + ff
                ba = [[next_bank() for _ in range(NH)] for _ in range(2)]
                for part in range(2):
                    for hf in range(NH):
                        b = ba[part][hf]
                        for k in range(KT):
                            P.add('pe', lambda e, k=k, hf=hf, part=part, ff=ff, b=b, wv=wv: e.matmul(
                                banks[b][:], lhsT=wv[:, k, part, ff * 128:(ff + 1) * 128],
                                rhs=hb[:, k, hf * 512:(hf + 1) * 512], start=(k == 0), stop=(k == KT - 1)),
                                reads=[('slot', s), ('hb', k)], writes=[bank_b(b)], sig=(k == KT - 1))
                tb = f % 2
                for hf in range(NH):
                    P.add('act', lambda e, hf=hf, tb=tb, b=ba[0][hf]: e.activation(
                        out=cp[:, 22 + tb, hf * 512:(hf + 1) * 512], in_=banks[b][:], func=AF.Silu),
                        reads=[bank_b(ba[0][hf])], writes=[('cp', 22 + tb)])
                    P.add('dve', lambda e, hf=hf, tb=tb, f=f, b=ba[1][hf]: e.tensor_tensor(
                        out=cp[:, f, hf * 512:(hf + 1) * 512], in0=banks[b][:],
                        in1=cp[:, 22 + tb, hf * 512:(hf + 1) * 512], op=ALU.mult),
                        reads=[bank_b(ba[1][hf]), ('cp', 22 + tb)], writes=[('cp', f)])
        for m2 in range(4):
            ss = []
            for fh in range(2):
                s = load_w([(lambda sa: sa[:, 0:11 * 256].rearrange("p (f c) -> p f c", f=11),
                             w_o_[fh * 1408:(fh + 1) * 1408, m2 * 256:(m2 + 1) * 256].rearrange(
                                 "(f p) c -> p f c", p=128))], None)
                ss.append(s)
            for mm in range(2):
                m = m2 * 2 + mm
                for hf in range(NH):
                    b = next_bank()
                    for f in range(FT):
                        s = ss[f // 11]
                        wv = wsl[:, s, 0:11 * 256].rearrange("p (f c) -> p f c", f=11)
                        P.add('pe', lambda e, f=f, hf=hf, mm=mm, b=b, wv=wv: e.matmul(
                            banks[b][:], lhsT=wv[:, f % 11, mm * 128:(mm + 1) * 128],
                            rhs=cp[:, f, hf * 512:(hf + 1) * 512], start=(f == 0), stop=(f == FT - 1)),
                            reads=[('slot', s), ('cp', f)], writes=[bank_b(b)], sig=(f == FT - 1))
                    P.add('dve', lambda e, m=m, hf=hf, b=b: e.scalar_tensor_tensor(
                        out=xs[:, m, hf * 512:(hf + 1) * 512], in0=banks[b][:], scalar=Gvec[:, sub, m:m + 1],
                        in1=xs[:, m, hf * 512:(hf + 1) * 512], op0=ALU.mult, op1=ALU.add),
                        reads=[bank_b(b), 'Gvec', ('xs', m)], writes=[('xs', m)])

    PI = float(np.pi)
    cst_rc = din("cst_rc", [128, 4, 16])
    lhsTX = sb("lhsTX", [128, 32, 2, 128], BF16)
    Wintra = sb("Wintra", [128, 32, 128], BF16)
    WinterZ = sb("WinterZ", [128, 2, 2, NPAIR, 128], BF16)
    ident_bf = sb("ident_bf", [128, 128], BF16)
    sel_bf = sb("sel_bf", [128, 1920], BF16)
    Tc = sb("Tc", [128, NPAIR, 128])
    Ts = sb("Ts", [128, NPAIR, 128])
    smalls = sb("smalls", [128, 30, NPAIR])
    PW = sb("PW", [128, 9, 2, NPAIR])
    carry = sb("carry", [128, 2, NPAIR])
    halo = sb("halo", [128, 4, 16])
    maskf = sb("maskf", [128, 128])
    dd_sb = sb("dd_sb", [128, 32])
    rc_sb = sb("rc_sb", [128, 4, 16])
    poolw_bf = sb("poolw_bf", [128, 4, 128], BF16)
    pbs_sb = sb("pbs_sb", [128, 2, 4])
    pbsc = sb("pbsc", [128, 4])
    bglu_sb = sb("bglu_sb", [128, 8])
    bglu_h = sb("bglu_h", [128, 8])
    scl = sb("scl", [128, 3, NPAIR])
    ki_sb = sb("ki_sb", [128, NPAIR], mybir.dt.int32)

    cpf = cp[:].rearrange("p a b -> p (a b)").bitcast(F32)
    hbf = hb[:].rearrange("p a b -> p (a b)").bitcast(F32)
    WxT = cpf[:, 0:4096].rearrange("p (r q c) -> p r q c", r=2, q=NPAIR)
    WinT = cpf[:, 4096:8192].rearrange("p (r q c) -> p r q c", r=2, q=NPAIR)
    bb = cpf[:, 8192:8704].rearrange("p (r q c) -> p r q c", r=2, q=NPAIR)
    cc = cpf[:, 8704:9216].rearrange("p (r q c) -> p r q c", r=2, q=NPAIR)
    Bbar = cpf[:, 9216:9728].rearrange("p (r q c) -> p r q c", r=2, q=NPAIR)
    tS = cpf[:, 9728:10752].rearrange("p (r q c) -> p r q c", r=4, q=NPAIR)
    Kpp = hbf.rearrange("p (r q c) -> p r q c", r=2, q=NPAIR)
    ID_WXT = [('cp', j) for j in range(0, 8)]
    ID_WIN = [('cp', j) for j in range(8, 16)]
    ID_SM = [('cp', j) for j in range(16, 24)]
    ID_KPP = [('hb', j) for j in range(KT)]

    def dv(fn, reads, writes):
        P.add('dve', fn, reads=reads, writes=writes)

    def tt(out, a, b, op, reads, writes):
        dv(lambda e: e.tensor_tensor(out=out, in0=a, in1=b, op=op), reads, writes)

    def sm(i):
        return smalls[:, i, :]

    if MIXER:
        small_load(scl[:], ssm_sc, 'scl')
        small_load(dd_sb[:], ssm_dd, 'dd')
        small_load(maskf[:], cst_mask, 'maskf')
        small_load(rc_sb[:], cst_rc, 'rc')
        small_load(pbs_sb[:], pool_bs, 'pbs')
        small_load(bglu_sb[:], b_glu, 'bglu')
        P.add('sp', lambda e: e.dma_start(out=bb, in_=ssm_b), writes=ID_SM, dma='bbcc')
        P.add('sp', lambda e: e.dma_start(out=cc, in_=ssm_cc), writes=ID_SM, dma='bbcc')
        P.add('pool', lambda e: e.dma_start(out=sel_bf[:], in_=cst_sel), writes=['sel'], dma='sel')
        P.add('pool', lambda e: e.dma_start(out=poolw_bf[:], in_=pool_w), writes=['poolw'], dma='poolw')
        dv(lambda e: e.memset(carry[:], 0.0), [], ['carry'])
        dv(lambda e: e.memset(halo[:], 0.0), [], ['halo'])
        dv(lambda e: e.memset(lhsTX[:], 0.0), [], ['lhsTX'])
        dv(lambda e: e.tensor_tensor(out=pbsc[:], in0=pbs_sb[:, 0, :], in1=pbs_sb[:, 1, :], op=ALU.mult), ['pbs'], ['pbsc'])
        dv(lambda e: e.tensor_scalar(out=bglu_h[:], in0=bglu_sb[:], scalar1=0.5, scalar2=None, op0=ALU.mult), ['bglu'], ['bgluh'])

        P.skip = SUB < 1
        S = 'smalls'
        LR, DT, XM, MAG, R8, ANG, YS, YC, SN, CS, AR, AI, NR, DEN, RDEN, FRE, FIM, T1, T2, T3, T4, IR, II, RR8 = range(24)
        P.add('act', lambda e: e.activation(out=sm(LR), in_=scl[:, 0, :], func=AF.Exp), reads=['scl'], writes=[S])
        P.add('act', lambda e: e.activation(out=sm(DT), in_=scl[:, 2, :], func=AF.Exp), reads=['scl'], writes=[S])
        dv(lambda e: e.tensor_scalar(out=sm(LR), in0=sm(LR), scalar1=-1.0, scalar2=None, op0=ALU.mult), [S], [S])
        tt(sm(XM), sm(LR), sm(DT), ALU.mult, [S], [S])
        P.add('act', lambda e: e.activation(out=sm(MAG), in_=sm(XM), func=AF.Exp), reads=[S], writes=[S])
        P.add('act', lambda e: e.activation(out=sm(R8), in_=sm(XM), func=AF.Exp, scale=8.0), reads=[S], writes=[S])
        tt(sm(ANG), scl[:, 1, :], sm(DT), ALU.mult, [S, 'scl'], [S])
        P.skip = SUB < 2

        def range_reduce(dst, off):
            dv(lambda e: e.tensor_scalar(out=sm(T4), in0=sm(ANG), scalar1=off, scalar2=None, op0=ALU.add), [S], [S])
            dv(lambda e: e.tensor_scalar(out=sm(T3), in0=sm(T4), scalar1=1.0 / (2 * PI), scalar2=None, op0=ALU.mult), [S], [S])
            dv(lambda e: e.tensor_copy(out=ki_sb[:], in_=sm(T3)), [S], ['ki'])
            dv(lambda e: e.tensor_copy(out=sm(T3), in_=ki_sb[:]), ['ki'], [S])
            dv(lambda e: e.scalar_tensor_tensor(out=sm(dst), in0=sm(T3), scalar=-2 * PI, in1=sm(T4), op0=ALU.mult, op1=ALU.add), [S], [S])
            dv(lambda e: e.tensor_scalar(out=sm(T3), in0=sm(dst), scalar1=PI, scalar2=None, op0=ALU.is_gt), [S], [S])
            dv(lambda e: e.scalar_tensor_tensor(out=sm(dst), in0=sm(T3), scalar=-2 * PI, in1=sm(dst), op0=ALU.mult, op1=ALU.add), [S], [S])
            dv(lambda e: e.tensor_scalar(out=sm(T3), in0=sm(dst), scalar1=-PI, scalar2=None, op0=ALU.is_lt), [S], [S])
            dv(lambda e: e.scalar_tensor_tensor(out=sm(dst), in0=sm(T3), scalar=2 * PI, in1=sm(dst), op0=ALU.mult, op1=ALU.add), [S], [S])
        range_reduce(YS, 0.0)
        P.skip = SUB < 3
        TH, ZZ, SS, CC, U1, U2 = 24, 25, 26, 27, 28, 29
        dv(lambda e: e.tensor_scalar(out=sm(TH), in0=sm(YS), scalar1=0.25, scalar2=None, op0=ALU.mult), [S], [S])
        tt(sm(ZZ), sm(TH), sm(TH), ALU.mult, [S], [S])
        dv(lambda e: e.memset(sm(SS), 1.0), [S], [S])
        dv(lambda e: e.memset(sm(CC), 1.0), [S], [S])
        for kk in (156.0, 110.0, 72.0, 42.0, 20.0, 6.0):
            tt(sm(SS), sm(SS), sm(ZZ), ALU.mult, [S], [S])
            dv(lambda e, kk=kk: e.tensor_scalar(out=sm(SS), in0=sm(SS), scalar1=-1.0 / kk, scalar2=1.0, op0=ALU.mult, op1=ALU.add), [S], [S])
        tt(sm(SS), sm(SS), sm(TH), ALU.mult, [S], [S])
        for kk in (182.0, 132.0, 90.0, 56.0, 30.0, 12.0, 2.0):
            tt(sm(CC), sm(CC), sm(ZZ), ALU.mult, [S], [S])
            dv(lambda e, kk=kk: e.tensor_scalar(out=sm(CC), in0=sm(CC), scalar1=-1.0 / kk, scalar2=1.0, op0=ALU.mult, op1=ALU.add), [S], [S])
        for _ in range(2):
            tt(sm(U1), sm(SS), sm(CC), ALU.mult, [S], [S])
            tt(sm(U2), sm(SS), sm(SS), ALU.mult, [S], [S])
            tt(sm(CC), sm(CC), sm(CC), ALU.mult, [S], [S])
            tt(sm(CC), sm(CC), sm(U2), ALU.subtract, [S], [S])
            dv(lambda e: e.tensor_scalar(out=sm(SS), in0=sm(U1), scalar1=2.0, scalar2=None, op0=ALU.mult), [S], [S])
        dv(lambda e: e.tensor_copy(out=sm(SN), in_=sm(SS)), [S], [S])
        dv(lambda e: e.tensor_copy(out=sm(CS), in_=sm(CC)), [S], [S])
        tt(sm(AR), sm(MAG), sm(CS), ALU.mult, [S], [S])
        tt(sm(AI), sm(MAG), sm(SN), ALU.mult, [S], [S])
        P.skip = SETUP < 2
        dv(lambda e: e.tensor_scalar(out=sm(NR), in0=sm(AR), scalar1=-1.0, scalar2=None, op0=ALU.add), [S], [S])
        tt(sm(T1), sm(LR), sm(LR), ALU.mult, [S], [S])
        tt(sm(T2), scl[:, 1, :], scl[:, 1, :], ALU.mult, [S, 'scl'], [S])
        tt(sm(DEN), sm(T1), sm(T2), ALU.add, [S], [S])
        dv(lambda e: e.reciprocal(out=sm(RDEN), in_=sm(DEN)), [S], [S])
        tt(sm(T1), sm(NR), sm(LR), ALU.mult, [S], [S])
        tt(sm(T2), sm(AI), scl[:, 1, :], ALU.mult, [S, 'scl'], [S])
        tt(sm(T1), sm(T1), sm(T2), ALU.add, [S], [S])
        tt(sm(FRE), sm(T1), sm(RDEN), ALU.mult, [S], [S])
        tt(sm(T1), sm(AI), sm(LR), ALU.mult, [S], [S])
        tt(sm(T2), sm(NR), scl[:, 1, :], ALU.mult, [S, 'scl'], [S])
        tt(sm(T1), sm(T1), sm(T2), ALU.subtract, [S], [S])
        tt(sm(FIM), sm(T1), sm(RDEN), ALU.mult, [S], [S])

        def bc16(ap2):
            return ap2.unsqueeze(2).to_broadcast([128, NPAIR, 16])

        def cmul(o_re, o_im, xr, xi, yr, yi, reads, writes, neg_im=False):
            t0, t1 = tS[:, 0], tS[:, 1]
            rd = reads + ID_SM
            wr = writes + ID_SM
            tt(t0, xr, yr, ALU.mult, rd, ID_SM)
            tt(t1, xi, yi, ALU.mult, rd, ID_SM)
            tt(o_re, t0, t1, ALU.subtract, rd, wr)
            tt(t0, xr, yi, ALU.mult, rd, ID_SM)
            tt(t1, xi, yr, ALU.mult, rd, ID_SM)
            if not neg_im:
                tt(o_im, t0, t1, ALU.add, rd, wr)
            else:
                dv(lambda e: e.scalar_tensor_tensor(out=o_im, in0=t0, scalar=-1.0, in1=t1, op0=ALU.mult, op1=ALU.subtract), rd, wr)

        cmul(Bbar[:, 0], Bbar[:, 1], bc16(sm(FRE)), bc16(sm(FIM)), bb[:, 0], bb[:, 1], [S, 'bbcc'], [])
        dv(lambda e: e.tensor_copy(out=PW[:, 1, 0, :], in_=sm(AR)), [S], ['PW'])
        dv(lambda e: e.tensor_copy(out=PW[:, 1, 1, :], in_=sm(AI)), [S], ['PW'])
        for k in range(1, 8):
            t0, t1 = sm(T1), sm(T2)
            tt(t0, PW[:, k, 0, :], sm(AR), ALU.mult, [S, 'PW'], [S])
            tt(t1, PW[:, k, 1, :], sm(AI), ALU.mult, [S, 'PW'], [S])
            tt(PW[:, k + 1, 0, :], t0, t1, ALU.subtract, [S], ['PW'])
            tt(t0, PW[:, k, 0, :], sm(AI), ALU.mult, [S, 'PW'], [S])
            tt(t1, PW[:, k, 1, :], sm(AR), ALU.mult, [S, 'PW'], [S])
            tt(PW[:, k + 1, 1, :], t0, t1, ALU.add, [S], ['PW'])
        dv(lambda e: e.reciprocal(out=sm(RR8), in_=sm(R8)), [S], [S])
        tt(Tc[:, :, 0], PW[:, 8, 0, :], sm(RR8), ALU.mult, [S, 'PW'], ['Tc'])
        tt(Ts[:, :, 0], PW[:, 8, 1, :], sm(RR8), ALU.mult, [S, 'PW'], ['Ts'])

        def ptt(out, a, b, op, reads, writes):
            P.add('pool', lambda e: e.tensor_tensor(out=out, in0=a, in1=b, op=op), reads=reads, writes=writes)
        RS = [('rstd', 0), ('rstd', 1)]
        for mlev in range(7):
            n = 1 << mlev
            cb = Tc[:, :, n - 1:n].to_broadcast([128, NPAIR, n])
            sbb = Ts[:, :, n - 1:n].to_broadcast([128, NPAIR, n])
            tB = rstd[:, 0:NPAIR * n].rearrange("p (q c) -> p q c", q=NPAIR)
            ptt(Tc[:, :, n:2 * n], Tc[:, :, 0:n], cb, ALU.mult, ['Tc'], ['Tc'])
            ptt(tB, Ts[:, :, 0:n], sbb, ALU.mult, ['Ts'], RS)
            ptt(Tc[:, :, n:2 * n], Tc[:, :, n:2 * n], tB, ALU.subtract, ['Tc'] + RS, ['Tc'])
            ptt(Ts[:, :, n:2 * n], Tc[:, :, 0:n], sbb, ALU.mult, ['Tc', 'Ts'], ['Ts'])
            ptt(tB, Ts[:, :, 0:n], cb, ALU.mult, ['Tc', 'Ts'], RS)
            ptt(Ts[:, :, n:2 * n], Ts[:, :, n:2 * n], tB, ALU.add, ['Ts'] + RS, ['Ts'])
        WxT5 = [WxT[:, r].rearrange("p q (t h) -> p q t h", t=8) for r in range(2)]
        WinT5 = [WinT[:, r].rearrange("p q (t h) -> p q t h", t=8) for r in range(2)]
        for tau in range(8):
            k = 7 - tau
            if k == 0:
                dv(lambda e, tau=tau: e.tensor_copy(out=WxT5[0][:, :, tau, :], in_=Bbar[:, 0]), ID_SM, ID_WXT)
                dv(lambda e, tau=tau: e.tensor_copy(out=WxT5[1][:, :, tau, :], in_=Bbar[:, 1]), ID_SM, ID_WXT)
            else:
                cmul(WxT5[0][:, :, tau, :], WxT5[1][:, :, tau, :], bc16(PW[:, k, 0, :]), bc16(PW[:, k, 1, :]),
                     Bbar[:, 0], Bbar[:, 1], ['PW'], ID_WXT)
        for t in range(8):
            cmul(WinT5[0][:, :, t, :], WinT5[1][:, :, t, :], cc[:, 0], cc[:, 1],
                 bc16(PW[:, t + 1, 0, :]), bc16(PW[:, t + 1, 1, :]), ['PW', 'bbcc'], ID_WIN, neg_im=True)
        tt(sm(T1), PW[:, 8, 0, :], PW[:, 8, 0, :], ALU.mult, ['PW'], [S])
        tt(sm(T2), PW[:, 8, 1, :], PW[:, 8, 1, :], ALU.mult, ['PW'], [S])
        tt(sm(T1), sm(T1), sm(T2), ALU.add, [S], [S])
        dv(lambda e: e.reciprocal(out=sm(T3), in_=sm(T1)), [S], [S])
        tt(sm(IR), PW[:, 8, 0, :], sm(T3), ALU.mult, [S, 'PW'], [S])
        dv(lambda e: e.scalar_tensor_tensor(out=sm(II), in0=PW[:, 8, 1, :], scalar=-1.0, in1=sm(T3), op0=ALU.mult, op1=ALU.mult), [S, 'PW'], [S])
        for hq in range(2):
            qs = slice(hq * 8, hq * 8 + 8)
            tA = tmpf[:, 0, 0:1024].rearrange("p (q c) -> p q c", q=8)
            tB = tmpf[:, 1, 0:1024].rearrange("p (q c) -> p q c", q=8)
            irb = sm(IR)[:, qs].unsqueeze(2).to_broadcast([128, 8, 128])
            iib = sm(II)[:, qs].unsqueeze(2).to_broadcast([128, 8, 128])
            TT = [('tmpf', 0), ('tmpf', 1)]
            tt(tA, WxT[:, 0, qs, :], irb, ALU.mult, [S] + ID_WXT, TT)
            tt(tB, WxT[:, 1, qs, :], iib, ALU.mult, [S] + ID_WXT, TT)
            tt(Kpp[:, 0, qs, :], tA, tB, ALU.subtract, TT, ID_KPP)
            tt(tA, WxT[:, 1, qs, :], irb, ALU.mult, [S] + ID_WXT, TT)
            tt(tB, WxT[:, 0, qs, :], iib, ALU.mult, [S] + ID_WXT, TT)
            tt(Kpp[:, 1, qs, :], tA, tB, ALU.add, TT, ID_KPP)
        xsb = xs[:].rearrange("p a b -> p (a b)").bitcast(BF16)
        Kpp_bf = xsb[:, 0:4096].rearrange("p (r q c) -> p r q c", r=2, q=NPAIR)
        WxT_bf = xsb[:, 4096:8192].rearrange("p (r q c) -> p r q c", r=2, q=NPAIR)
        ID_XB = [('xs', j) for j in range(4)]
        dv(lambda e: e.memset(WinterZ[:], 0.0), [], ['Winter'])
        dv(lambda e: e.tensor_copy(out=ident_bf[:], in_=ident[:]), ['ident'], ['identbf'])
        for r in range(2):
            for g2 in range(2):
                rows = slice(g2 * 64, g2 * 64 + 64)
                P.add('act', lambda e, r=r, g2=g2, rows=rows: e.activation(out=WinterZ[rows, g2, r], in_=WinT[rows, r], func=AF.Identity), reads=ID_WIN, writes=['Winter'])
            P.add('act', lambda e, r=r: e.activation(out=Kpp_bf[:, r], in_=Kpp[:, r], func=AF.Identity), reads=ID_KPP, writes=ID_XB)
            P.add('act', lambda e, r=r: e.activation(out=WxT_bf[:, r], in_=WxT[:, r], func=AF.Identity), reads=ID_WXT, writes=ID_XB)
        P.skip = SETUP < 3
        for gb in range(8):
            b = next_bank()
            for j in range(4):
                g = gb * 4 + j
                q, g2 = g // 2, g % 2
                rows = slice(g2 * 64, g2 * 64 + 64)
                P.add('pe', lambda e, b=b, j=j, q=q, g2=g2: e.matmul(
                    banks[b][:, j * 128:(j + 1) * 128], lhsT=Kpp_bf[:, 0, q, :], rhs=WinterZ[:, g2, 0, q, :], start=True, stop=False),
                    reads=ID_XB + ['Winter'], writes=[bank_b(b)], sig=False)
                P.add('pe', lambda e, b=b, j=j, q=q, g2=g2: e.matmul(
                    banks[b][:, j * 128:(j + 1) * 128], lhsT=Kpp_bf[:, 1, q, :], rhs=WinterZ[:, g2, 1, q, :], start=False, stop=True),
                    reads=ID_XB + ['Winter'], writes=[bank_b(b)], sig=True)
            for j in range(4):
                g = gb * 4 + j
                tA = tmpf[:, j % 2, 0:128]
                dv(lambda e, b=b, j=j, tA=tA: e.tensor_tensor(out=tA, in0=banks[b][:, j * 128:(j + 1) * 128], in1=maskf[:], op=ALU.mult),
                   [bank_b(b), 'maskf'], [('tmpf', j % 2)])
                dv(lambda e, g=g, tA=tA: e.scalar_tensor_tensor(out=Wintra[:, g, :], in0=ident[:], scalar=dd_sb[:, g:g + 1], in1=tA,
                                                              op0=ALU.mult, op1=ALU.add),
                   [('tmpf', j % 2), 'ident', 'dd'], ['Wintra'])
        P.skip = SETUP < 4
        for qb in range(8):
            b = next_bank()
            for j in range(4):
                q, r = qb * 2 + j // 2, j % 2
                P.add('pe', lambda e, b=b, j=j, q=q, r=r: e.matmul(banks[b][:, j * 128:(j + 1) * 128], lhsT=WxT_bf[:, r, q, :], rhs=ident_bf[:],
                                                              start=True, stop=True),
                      reads=ID_XB + ['identbf'], writes=[bank_b(b)])
            for j in range(4):
                q, r = qb * 2 + j // 2, j % 2
                P.add('act', lambda e, b=b, j=j, q=q, r=r: e.activation(out=lhsTX[:, 2 * q, r, 0:64], in_=banks[b][:, j * 128:j * 128 + 64], func=AF.Identity),
                      reads=[bank_b(b)], writes=['lhsTX'])
                P.add('act', lambda e, b=b, j=j, q=q, r=r: e.activation(out=lhsTX[:, 2 * q + 1, r, 64:128], in_=banks[b][:, j * 128 + 64:j * 128 + 128], func=AF.Identity),
                      reads=[bank_b(b)], writes=['lhsTX'])
        P.skip = SETUP < 5

    P.skip = False
    mod_finalize()
    reserved[0] = None
    Ush = cp[:, 4:8, :].rearrange("p a (g c) -> p (a g) c", g=8)
    Ssh = [cp[:, 8:10, :].rearrange("p a (q c) -> p (a q) c", q=8),
           cp[:, 10:12, :].rearrange("p a (q c) -> p (a q) c", q=8)]
    Gsh = cp[:, 12:16, :].rearrange("p a (g c) -> p (a g) c", g=8)
    swf = cp[:, 12:20, :].rearrange("p a b -> p (a b)").bitcast(F32).rearrange("p (w q c) -> p w q c", w=4, q=8)
    ID_SW = [('cp', j) for j in range(12, 20)]
    upf = cp[:, 4:9, :].rearrange("p a b -> p (a b)")[:, 0:4 * (NT + 16)].rearrange("p (g c) -> p g c", g=4)
    ID_UPF = [('cp', j) for j in range(4, 9)]
    TGP = [0, 1, 2, 3, 20, 21, 22, 23]
    TGS = [4, 5, 6, 7, 8, 17, 18, 19]
    VCH = [12, 13, 14, 15]
    GLT = [4, 5]
    ZP = [9, 10, 11, 16]
    u16 = sb("u16", [128, 4, 16])

    def proj_slab(w_src, col0, ncols_tiles, consume):
        s = load_w([(lambda sa: sa.rearrange("p (k c) -> p k c", k=KT),
                     w_src[:, col0:col0 + 512].rearrange("(k p) c -> p k c", p=128))], None)
        wv = wsl[:, s, :].rearrange("p (k c) -> p k c", k=KT)
        for mi in range(4):
            for hf in range(NH):
                b = next_bank()
                for k in range(KT):
                    P.add('pe', lambda e, k=k, hf=hf, mi=mi, b=b, wv=wv: e.matmul(
                        banks[b][:], lhsT=wv[:, k, mi * 128:(mi + 1) * 128], rhs=hb[:, k, hf * 512:(hf + 1) * 512],
                        start=(k == 0), stop=(k == KT - 1)),
                        reads=[('slot', s), ('hb', k)], writes=[bank_b(b)], sig=(k == KT - 1))
                consume(mi, hf, b)

    def hs(hf):
        return slice(hf * 512, (hf + 1) * 512)

    def mixer(ti):
        if STAGE < 1:
            return
        def c_ussm(mi, hf, b):
            P.add('act', lambda e: e.activation(out=cp[:, mi, hs(hf)], in_=banks[b][:], func=AF.Identity),
                  reads=[bank_b(b)], writes=[('cp', mi)])
        proj_slab(w_in, 512, 4, c_ussm)
        if STAGE < 2:
            return
        for gb in range(8):
            b = next_bank()
            for j in range(4):
                g = gb * 4 + j
                blk, gi = g // 8, g % 8
                uv = cp[:, blk, :].rearrange("p (c t) -> p c t", t=8)
                for tau in range(8):
                    o = 240 * gi + 112 - 16 * tau
                    P.add('pe', lambda e, b=b, j=j, o=o, tau=tau, uv=uv: e.matmul(
                        banks[b][:, j * 128:(j + 1) * 128], lhsT=sel_bf[:, o:o + 128], rhs=uv[:, :, tau],
                        start=(tau == 0), stop=(tau == 7)),
                        reads=['sel', ('cp', blk)], writes=[bank_b(b)], sig=(tau == 7))
            P.add('act', lambda e, b=b, gb=gb: e.activation(
                out=Ush[:, gb * 4:gb * 4 + 4, :], in_=banks[b][:].rearrange("p (g c) -> p g c", g=4), func=AF.Identity),
                reads=[bank_b(b)], writes=[('cp', 4 + gb // 2)])
        if STAGE < 3:
            return
        A_, B_, C_, D_ = swf[:, 0], swf[:, 1], swf[:, 2], swf[:, 3]
        for hq in range(2):
            xb = [[next_bank(), next_bank()], [next_bank(), next_bank()]]
            for qq in range(8):
                q = hq * 8 + qq
                for r in range(2):
                    b = xb[r][qq // 4]
                    reg = banks[b][:, (qq % 4) * 128:(qq % 4 + 1) * 128]
                    for g2 in range(2):
                        g = 2 * q + g2
                        P.add('pe', lambda e, reg=reg, g=g, r=r, g2=g2: e.matmul(
                            reg, lhsT=lhsTX[:, g, r, :], rhs=Ush[:, g, :], start=(g2 == 0), stop=(g2 == 1)),
                            reads=['lhsTX', ('cp', 4 + g // 8)], writes=[bank_b(b)], sig=(g2 == 1))
            qs = slice(hq * 8, hq * 8 + 8)
            for bh in range(2):
                q4 = slice(bh * 4, bh * 4 + 4)
                qg = slice(hq * 8 + bh * 4, hq * 8 + bh * 4 + 4)
                Xr = banks[xb[0][bh]][:].rearrange("p (q c) -> p q c", q=4)
                Xi = banks[xb[1][bh]][:].rearrange("p (q c) -> p q c", q=4)
                rdb = [bank_b(xb[0][bh]), bank_b(xb[1][bh]), 'Tc', 'Ts'] + ID_SW
                tt(A_[:, q4, :], Xr, Tc[:, qg, :], ALU.mult, rdb, ID_SW)
                tt(B_[:, q4, :], Xi, Ts[:, qg, :], ALU.mult, rdb, ID_SW)
                tt(A_[:, q4, :], A_[:, q4, :], B_[:, q4, :], ALU.add, rdb, ID_SW)
                tt(C_[:, q4, :], Xi, Tc[:, qg, :], ALU.mult, rdb, ID_SW)
                tt(B_[:, q4, :], Xr, Ts[:, qg, :], ALU.mult, rdb, ID_SW)
                tt(C_[:, q4, :], C_[:, q4, :], B_[:, q4, :], ALU.subtract, rdb, ID_SW)
            for qq in range(8):
                q = hq * 8 + qq
                r8b = sm(R8)[:, q:q + 1].to_broadcast([128, 128])
                dv(lambda e, qq=qq, q=q, r8b=r8b: e.tensor_tensor_scan(
                    out=B_[:, qq, :], data0=r8b, data1=A_[:, qq, :], initial=carry[:, 0, q:q + 1], op0=ALU.mult, op1=ALU.add),
                    ID_SW + ['carry', 'smalls'], ID_SW)
                dv(lambda e, qq=qq, q=q, r8b=r8b: e.tensor_tensor_scan(
                    out=D_[:, qq, :], data0=r8b, data1=C_[:, qq, :], initial=carry[:, 1, q:q + 1], op0=ALU.mult, op1=ALU.add),
                    ID_SW + ['carry', 'smalls'], ID_SW)
            rd = ID_SW + ['Tc', 'Ts']
            tt(A_, B_, Tc[:, qs, :], ALU.mult, rd, ID_SW)
            tt(C_, D_, Ts[:, qs, :], ALU.mult, rd, ID_SW)
            tt(A_, A_, C_, ALU.subtract, rd, ID_SW)
            tt(C_, B_, Ts[:, qs, :], ALU.mult, rd, ID_SW)
            tt(B_, D_, Tc[:, qs, :], ALU.mult, rd, ID_SW)
            tt(C_, C_, B_, ALU.add, rd, ID_SW)
            for r, src in ((0, A_), (1, C_)):
                sid = [('cp', 8 + 2 * r), ('cp', 9 + 2 * r)]
                dv(lambda e, r=r, src=src, qs=qs: e.tensor_copy(out=Ssh[r][:, qs, 1:128], in_=src[:, :, 0:127]), ID_SW, sid)
                dv(lambda e, r=r, qs=qs: e.tensor_copy(out=Ssh[r][:, qs, 0:1], in_=carry[:, r, qs].unsqueeze(2)), ['carry'], sid)
                dv(lambda e, r=r, src=src, qs=qs: e.tensor_copy(out=carry[:, r, qs].unsqueeze(2), in_=src[:, :, 127:128]), ID_SW, ['carry'])
        for gi in range(2):
            def c_gp(mi, hf, b, gi=gi):
                ch = TGP[gi * 4 + mi]
                P.add('act', lambda e: e.activation(out=cp[:, ch, hs(hf)], in_=banks[b][:], func=AF.Tanh, scale=0.5),
                      reads=[bank_b(b)], writes=[('cp', ch)])
            proj_slab(w_in, 1024 + gi * 512, 4, c_gp)
        for gb in range(8):
            b = next_bank()
            for j in range(4):
                g = gb * 4 + j
                q, g2 = g // 2, g % 2
                rows = slice(g2 * 64, g2 * 64 + 64)
                reg = banks[b][:, j * 128:(j + 1) * 128]
                rdl = ['Wintra', 'Winter', ('cp', 4 + g // 8), ('cp', 8), ('cp', 9), ('cp', 10), ('cp', 11)]
                P.add('pe', lambda e, reg=reg, g=g: e.matmul(reg, lhsT=Wintra[:, g, :], rhs=Ush[:, g, :], start=True, stop=False),
                      reads=rdl, writes=[bank_b(b)], sig=False)
                P.add('pe', lambda e, reg=reg, q=q, g2=g2: e.matmul(reg, lhsT=WinterZ[:, g2, 0, q, :], rhs=Ssh[0][:, q, :], start=False, stop=False),
                      reads=rdl, writes=[bank_b(b)], sig=False)
                P.add('pe', lambda e, reg=reg, q=q, g2=g2: e.matmul(reg, lhsT=WinterZ[:, g2, 1, q, :], rhs=Ssh[1][:, q, :], start=False, stop=True),
                      reads=rdl, writes=[bank_b(b)], sig=True)
            P.add('act', lambda e, b=b, gb=gb: e.activation(
                out=Gsh[:, gb * 4:gb * 4 + 4, :], in_=banks[b][:].rearrange("p (g c) -> p g c", g=4), func=AF.Gelu),
                reads=[bank_b(b)], writes=[('cp', 12 + gb // 2)])
        for blk in range(4):
            for th in range(2):
                b = next_bank()
                for j in range(4):
                    t = th * 4 + j
                    for gi in range(8):
                        o = 240 * t + 112 - 16 * gi
                        P.add('pe', lambda e, b=b, j=j, o=o, gi=gi, blk=blk: e.matmul(
                            banks[b][:, j * 128:(j + 1) * 128], lhsT=sel_bf[:, o:o + 128], rhs=Gsh[:, blk * 8 + gi, :],
                            start=(gi == 0), stop=(gi == 7)),
                            reads=['sel', ('cp', 12 + blk)], writes=[bank_b(b)], sig=(gi == 7))
                gv = cp[:, 16 + blk, :].rearrange("p (c t) -> p c t", t=8)[:, :, th * 4:th * 4 + 4]
                dv(lambda e, b=b, gv=gv: e.tensor_copy(out=gv, in_=banks[b][:].rearrange("p (t c) -> p c t", t=4)),
                   [bank_b(b)], [('cp', 16 + blk)])
        s = load_w([(lambda sa: sa.rearrange("p (k c) -> p k c", k=4), w_glu.rearrange("(k p) c -> p k c", p=128))], None)
        wv = wsl[:, s, :].rearrange("p (k c) -> p k c", k=4)
        for mi in range(4):
            for hf in range(NH):
                bv_, bg_ = next_bank(), next_bank()
                for part, b in ((0, bv_), (1, bg_)):
                    for k in range(4):
                        P.add('pe', lambda e, k=k, b=b, part=part, mi=mi, hf=hf, wv=wv: e.matmul(
                            banks[b][:], lhsT=wv[:, k, part * 512 + mi * 128: part * 512 + (mi + 1) * 128],
                            rhs=cp[:, 16 + k, hs(hf)], start=(k == 0), stop=(k == 3)),
                            reads=[('slot', s), ('cp', 16 + k)], writes=[bank_b(b)], sig=(k == 3))
                tb = (mi * NH + hf) % 2
                P.add('act', lambda e, b=bg_, mi=mi, tb=tb: e.activation(
                    out=cp[:, GLT[tb], 0:512], in_=banks[b][:], func=AF.Tanh, bias=bglu_h[:, 4 + mi:5 + mi], scale=0.5),
                    reads=[bank_b(bg_), 'bgluh'], writes=[('cp', GLT[tb])])
                dv(lambda e, b=bv_, mi=mi, tb=tb: e.tensor_scalar(
                    out=tmpf[:, tb, 0:512], in0=banks[b][:], scalar1=bglu_sb[:, mi:mi + 1], scalar2=0.5, op0=ALU.add, op1=ALU.mult),
                    [bank_b(bv_), 'bglu'], [('tmpf', tb)])
                dv(lambda e, mi=mi, hf=hf, tb=tb: e.scalar_tensor_tensor(
                    out=cp[:, VCH[mi], hs(hf)], in0=cp[:, GLT[tb], 0:512], scalar=1.0, in1=tmpf[:, tb, 0:512], op0=ALU.add, op1=ALU.mult),
                    [('cp', GLT[tb]), ('tmpf', tb)], [('cp', VCH[mi])])
        if STAGE < 5:
            return
        W16 = NT + 16
        dv(lambda e: e.tensor_copy(out=upf[:, :, 0:16], in_=halo[:]), ['halo'], ID_UPF)

        def c_upool(mi, hf, b):
            P.add('act', lambda e: e.activation(out=upf[:, mi, 16 + hf * 512:16 + (hf + 1) * 512], in_=banks[b][:], func=AF.Identity),
                  reads=[bank_b(b)], writes=ID_UPF)
        proj_slab(w_in, 0, 4, c_upool)
        dv(lambda e: e.tensor_copy(out=halo[:], in_=upf[:, :, NT:NT + 16]), ID_UPF, ['halo'])
        for kg in range(4):
            w = 2 << kg
            src = upf[:, kg, :]
            TT = [('tmpf', 0), ('tmpf', 1)]
            for j in range(kg + 1):
                d = 1 << j
                lo = 2 * d - 1
                dst = tmpf[:, j % 2, :]
                P.add('dve', lambda e, dst=dst, src=src, d=d, lo=lo: e.tensor_tensor(
                    out=dst[:, lo:W16], in0=src[:, lo:W16], in1=src[:, lo - d:W16 - d], op=ALU.add),
                    reads=ID_UPF + TT, writes=[('tmpf', j % 2)])
                src = dst
            ssum = src
            if ti == 0:
                dv(lambda e, kg=kg: e.tensor_copy(out=u16[:, kg, :], in_=upf[:, kg, 16:32]), ID_UPF, ['u16'])
            dv(lambda e, kg=kg, w=w, ssum=ssum: e.scalar_tensor_tensor(
                out=upf[:, kg, 16:W16], in0=ssum[:, 16:W16], scalar=1.0 / w, in1=upf[:, kg, 16:W16], op0=ALU.mult, op1=ALU.subtract),
                ID_UPF + TT, ID_UPF)
            if ti == 0:
                dv(lambda e, kg=kg, ssum=ssum: e.tensor_tensor(out=ssum[:, 16:32], in0=ssum[:, 16:32], in1=rc_sb[:, kg, :], op=ALU.mult),
                   TT + ['rc'], TT)
                dv(lambda e, kg=kg, ssum=ssum: e.tensor_tensor(out=upf[:, kg, 16:32], in0=ssum[:, 16:32], in1=u16[:, kg, :], op=ALU.subtract),
                   TT + ID_UPF + ['u16'], ID_UPF)
        for kg in range(4):
            for hf in range(NH):
                b = next_bank()
                P.add('pe', lambda e, kg=kg, hf=hf, b=b: e.matmul(banks[b][:], lhsT=poolw_bf[:, kg, :], rhs=upf[:, kg, 16 + hf * 512:16 + (hf + 1) * 512], start=True, stop=True),
                      reads=['poolw'] + ID_UPF, writes=[bank_b(b)])
                P.add('act', lambda e, kg=kg, hf=hf, b=b: e.activation(
                    out=cp[:, ZP[kg], hs(hf)], in_=banks[b][:], func=AF.Identity, bias=pbsc[:, kg:kg + 1], scale=pbs_sb[:, 1, kg:kg + 1]),
                    reads=[bank_b(b), 'pbsc', 'pbs'], writes=[('cp', ZP[kg])])
        if STAGE < 6:
            return
        for gi in range(2):
            def c_gs(mi, hf, b, gi=gi):
                ch = TGS[gi * 4 + mi]
                P.add('act', lambda e: e.activation(out=cp[:, ch, hs(hf)], in_=banks[b][:], func=AF.Tanh, scale=0.5),
                      reads=[bank_b(b)], writes=[('cp', ch)])
            proj_slab(w_in, 2048 + gi * 512, 4, c_gs)
        s_pu = load_w([(lambda sa: sa.rearrange("p (k c) -> p k c", k=4), w_pu.rearrange("(k p) c -> p k c", p=128))], None)
        s_su = load_w([(lambda sa: sa.rearrange("p (k c) -> p k c", k=4), w_su.rearrange("(k p) c -> p k c", p=128))], None)
        wpu = wsl[:, s_pu, :].rearrange("p (k c) -> p k c", k=4)
        wsu = wsl[:, s_su, :].rearrange("p (k c) -> p k c", k=4)
        for m in range(8):
            for hf in range(NH):
                bp, bs_ = next_bank(), next_bank()
                for k in range(4):
                    P.add('pe', lambda e, k=k, m=m, hf=hf, b=bp: e.matmul(
                        banks[b][:], lhsT=wpu[:, k, m * 128:(m + 1) * 128], rhs=cp[:, ZP[k], hs(hf)], start=(k == 0), stop=(k == 3)),
                        reads=[('slot', s_pu), ('cp', ZP[k])], writes=[bank_b(bp)], sig=(k == 3))
                for k in range(4):
                    P.add('pe', lambda e, k=k, m=m, hf=hf, b=bs_: e.matmul(
                        banks[b][:], lhsT=wsu[:, k, m * 128:(m + 1) * 128], rhs=cp[:, VCH[k], hs(hf)], start=(k == 0), stop=(k == 3)),
                        reads=[('slot', s_su), ('cp', VCH[k])], writes=[bank_b(bs_)], sig=(k == 3))
                chs = TGS[m]
                dv(lambda e, m=m, hf=hf, b=bp: e.scalar_tensor_tensor(
                    out=tmpf[:, 0, 0:512], in0=cp[:, TGP[m], hs(hf)], scalar=1.0, in1=banks[b][:], op0=ALU.add, op1=ALU.mult),
                    [bank_b(bp), ('cp', TGP[m])], [('tmpf', 0)])
                dv(lambda e, chs=chs, hf=hf, b=bs_: e.scalar_tensor_tensor(
                    out=tmpf[:, 1, 0:512], in0=cp[:, chs, hs(hf)], scalar=1.0, in1=banks[b][:], op0=ALU.add, op1=ALU.mult),
                    [bank_b(bs_), ('cp', chs)], [('tmpf', 1)])
                P.add('dve', lambda e, m=m, hf=hf: e.tensor_tensor(
                    out=cp[:, TGP[m], hs(hf)], in0=tmpf[:, 0, 0:512], in1=tmpf[:, 1, 0:512], op=ALU.add),
                    reads=[('tmpf', 0), ('tmpf', 1)], writes=[('cp', TGP[m])])
        for m2 in range(2):
            s = load_w([(lambda sa: sa.rearrange("p (k c) -> p k c", k=KT),
                         w_o[:, m2 * 512:(m2 + 1) * 512].rearrange("(k p) c -> p k c", p=128))], None)
            wv = wsl[:, s, :].rearrange("p (k c) -> p k c", k=KT)
            for mm in range(4):
                m = m2 * 4 + mm
                for hf in range(NH):
                    b = next_bank()
                    for k in range(KT):
                        P.add('pe', lambda e, k=k, mm=mm, hf=hf, b=b, wv=wv: e.matmul(
                            banks[b][:], lhsT=wv[:, k, mm * 128:(mm + 1) * 128], rhs=cp[:, TGP[k], hs(hf)], start=(k == 0), stop=(k == KT - 1)),
                            reads=[('slot', s), ('cp', TGP[k])], writes=[bank_b(b)], sig=(k == KT - 1))
                    dv(lambda e, m=m, hf=hf, b=b: e.scalar_tensor_tensor(
                        out=xs[:, m, hs(hf)], in0=banks[b][:], scalar=Gvec[:, 1, m:m + 1], in1=xs[:, m, hs(hf)], op0=ALU.mult, op1=ALU.add),
                        [bank_b(b), 'Gvec', ('xs', m)], [('xs', m)])

    def flush_out():
        while deferred_out:
            k, t0o = deferred_out.pop(0)
            P.add('sp', lambda e, k=k, t0o=t0o: e.dma_start(
                out=outT[k * 128:(k + 1) * 128, t0o:t0o + NT], in_=ostage[:, k * NT:(k + 1) * NT]),
                reads=[('cp', 2 * k), ('cp', 2 * k + 1)], dma='out')

    for ti in range(NTILES):
        t0 = ti * NT
        for k in range(KT):
            P.add('sp', lambda e, k=k, t0=t0: e.dma_start(out=xs[:, k, :], in_=xT[k * 128:(k + 1) * 128, t0:t0 + NT]),
                  writes=[('xs', k)], dma=('xs', k))
        flush_out()
        norm_mod(0, t0)
        ffn(0, w1i, w1o)
        if MIXER:
            norm_mod(1, t0)
            mixer(ti)
        norm_mod(2, t0)
        ffn(2, w2i, w2o)
        norm_mod(3, t0)

    flush_out()
    with nc.Block() as block:
        P.emit(block)
    return nc, es, P


_CACHE = {}


def _consts():
    ident = np.eye(128, dtype=np.float32)
    sel = np.zeros((128, 1920), np.float32)
    for p in range(128):
        sel[p, 240 * (p // 16) + 112 + (p % 16)] = 1.0
    mask = np.zeros((128, 128), np.float32)
    for a in range(8):
        for b in range(8):
            if b >= a:
                mask[a * 16:(a + 1) * 16, b * 16:(b + 1) * 16] = 1.0
    rc = np.zeros((128, 4, 16), np.float32)
    for k in range(4):
        w = 2 << k
        for tt_ in range(16):
            rc[:, k, tt_] = 1.0 / min(tt_ + 1, w)
    return ident, sel, mask, rc


def _pair_layout(a):
    g, n = a.shape[0], a.shape[1]
    rest = a.shape[2:]
    a = a.reshape((16, 2, n) + rest)
    a = np.moveaxis(a, 0, 2)
    return np.ascontiguousarray(a.reshape((128, 16) + rest))


def make_in_maps(inputs):
    f = lambda a: np.ascontiguousarray(np.asarray(a, dtype=np.float32))
    x = f(inputs["x"])
    c = f(inputs["c"])
    ident, sel, mask, rc = _consts()
    gvec = np.stack([f(inputs["g_ffn1"])[0], f(inputs["g_mix"])[0], f(inputs["g_ffn2"])[0], f(inputs["g_final"])], 0)
    gvec = np.ascontiguousarray(gvec.reshape(4, KT, 128).transpose(2, 0, 1))
    b_ada = np.ascontiguousarray(f(inputs["b_ada"])[0].reshape(72, 128).T)
    pool_w = np.ascontiguousarray(f(inputs["pool_w"])[0].transpose(1, 0, 2))
    pool_bs = np.ascontiguousarray(np.stack([f(inputs["pool_b"])[0].reshape(4, 128).T,
                                             f(inputs["pool_scale"])[0].reshape(4, 128).T], 1))
    b_glu = np.ascontiguousarray(f(inputs["b_glu"])[0].reshape(8, 128).T)
    lrl = f(inputs["ssm_lam_re_log"])[0]
    lim = f(inputs["ssm_lam_im"])[0]
    ldt = np.broadcast_to(f(inputs["ssm_log_dt"])[0][:, None], (32, 64))
    ssm_sc = np.ascontiguousarray(np.stack([_pair_layout(lrl), _pair_layout(lim), _pair_layout(ldt)], 1))
    ssm_b = np.ascontiguousarray(np.stack([_pair_layout(f(inputs["ssm_b_re"])[0]),
                                           _pair_layout(f(inputs["ssm_b_im"])[0])], 1))
    cre = f(inputs["ssm_c_re"])[0].transpose(0, 2, 1)
    cim = f(inputs["ssm_c_im"])[0].transpose(0, 2, 1)
    ssm_cc = np.ascontiguousarray(np.stack([_pair_layout(cre), _pair_layout(cim)], 1))
    dd = f(inputs["ssm_d"])[0].reshape(32, 16)
    ssm_dd = np.ascontiguousarray(np.tile(dd.T, (8, 1)))
    shared = dict(
        w_ada=f(inputs["w_ada"])[0], b_ada=b_ada, gvec=gvec,
        w1i=f(inputs["w_ffn1_in"])[0], w1o=f(inputs["w_ffn1_out"])[0],
        w2i=f(inputs["w_ffn2_in"])[0], w2o=f(inputs["w_ffn2_out"])[0],
        w_in=f(inputs["w_in"])[0], pool_w=pool_w, pool_bs=pool_bs,
        w_pu=f(inputs["w_pool_up"])[0], w_glu=f(inputs["w_glu"])[0], b_glu=b_glu,
        w_su=f(inputs["w_ssm_up"])[0], w_o=f(inputs["w_out"])[0],
        ssm_sc=ssm_sc, ssm_b=ssm_b, ssm_cc=ssm_cc, ssm_dd=ssm_dd,
        cst_ident=ident, cst_sel=sel, cst_mask=mask, cst_rc=rc,
    )
    maps = []
    for b in range(8):
        m = dict(shared)
        m["xT"] = np.ascontiguousarray(x[b].T)
        m["c_l"] = np.ascontiguousarray(c[b].reshape(KT, 128).T)
        maps.append(m)
    return maps


def kernel(**inputs):
    if "nc" not in _CACHE:
        _CACHE["nc"] = build_nc()
    nc, es, P = _CACHE["nc"]
    in_maps = make_in_maps(inputs)
    res = run_bass_kernel_spmd(nc, in_maps, core_ids=list(range(8)))
    out = np.stack([np.asarray(r["outT"], dtype=np.float32).T for r in res.results], 0)
    return np.ascontiguousarray(out)
```

```python
import numpy as np
from contextlib import ExitStack
import concourse.bass as bass
import concourse.mybir as mybir
from concourse.bass_utils import run_bass_kernel_spmd

F32 = mybir.dt.float32
BF16 = mybir.dt.bfloat16
AF = mybir.ActivationFunctionType
ALU = mybir.AluOpType

D = 1024
T = 4096
FF = 2816
NT = 1024
import os
NTILES = int(os.environ.get('KTILES', str(T // NT)))
NH = NT // 512
KT = D // 128
FT = FF // 128
EPS = 1e-6
LCH = 8
NC_T = NT // LCH
NPAIR = 16

import os
MIXER = True
STAGE = int(os.environ.get('KSTAGE', '9'))
SETUP = int(os.environ.get('KSETUP', '9'))
SUB = int(os.environ.get('KSUB', '9'))


class Prog:
    def __init__(self, nc, es):
        self.nc = nc
        self.es = es
        self.ops = []
        self.last_w = {}
        self.readers = {}
        self.dma_keys = {}

    skip = False
    deferred = None
    _atomic = None

    def begin_defer(self):
        self.deferred = []

    def end_defer(self):
        lst, self.deferred = self.deferred, None
        return lst

    def atomic_begin(self):
        if self.deferred is not None:
            self._atomic = []

    def atomic_end(self):
        if self.deferred is not None:
            self.deferred.append(self._atomic)
            self._atomic = None

    def pull(self, lst, n):
        for _ in range(n):
            if not lst:
                return
            for args in lst.pop(0):
                self.add(*args[0], **args[1])

    def add(self, eng, fn, reads=(), writes=(), dma=None, sig=True):
        if self.skip:
            return None
        if self.deferred is not None:
            item = ((eng, fn), dict(reads=list(reads), writes=list(writes), dma=dma, sig=sig))
            if self._atomic is not None:
                self._atomic.append(item)
            else:
                self.deferred.append([item])
            return None
        idx = len(self.ops)
        deps = set()
        for b in reads:
            if b in self.last_w:
                deps.add(self.last_w[b])
        for b in writes:
            if b in self.last_w:
                deps.add(self.last_w[b])
            for r in self.readers.get(b, ()):
                deps.add(r)
        for b in writes:
            self.last_w[b] = idx
            self.readers[b] = []
        for b in reads:
            self.readers.setdefault(b, []).append(idx)
        deps.discard(idx)
        self.ops.append(dict(eng=eng, fn=fn, deps=deps, dma=dma, sig=sig))
        return idx

    def emit(self, block):
        nc = self.nc
        ops = self.ops
        engs = ['pe', 'act', 'dve', 'pool', 'sp']
        sems = {e: self.es.enter_context(nc.semaphore(f"s_{e}")) for e in ['pe', 'act', 'dve', 'pool']}
        keysem = {}
        for o in ops:
            if o['dma'] is not None and o['dma'] not in keysem:
                keysem[o['dma']] = self.es.enter_context(nc.semaphore(f"d_{len(keysem)}"))
        cnt = {e: 0 for e in sems}
        for o in ops:
            if o['dma'] is None:
                if o['sig']:
                    cnt[o['eng']] += 1
                    o['cnt'] = cnt[o['eng']]
                else:
                    o['cnt'] = None
        nxt = {e: None for e in sems}
        for o in reversed(ops):
            if o['dma'] is None:
                if o['cnt'] is None:
                    o['cnt'] = nxt[o['eng']]
                    assert o['cnt'] is not None
                else:
                    nxt[o['eng']] = o['cnt']
        cum = {k: 0 for k in keysem}
        waited = {}
        per_eng = {e: [] for e in engs}
        for i, o in enumerate(ops):
            waits = {}
            for j in o['deps']:
                d = ops[j]
                if d['dma'] is not None:
                    s = keysem[d['dma']]
                    v = cum[d['dma']]
                else:
                    if d['eng'] == 'pe' and o['eng'] == 'pe' and o['dma'] is None:
                        continue
                    s = sems[d['eng']]
                    v = d['cnt']
                key = (o['eng'], id(s))
                if waited.get(key, 0) >= v:
                    continue
                if key not in waits or waits[key][1] < v:
                    waits[key] = (s, v)
            for key, (s, v) in waits.items():
                waited[key] = v
            if o['dma'] is not None:
                cum[o['dma']] += 16
            per_eng[o['eng']].append((o, list(waits.values())))
        self.n_ops = {e: len(per_eng[e]) for e in engs}

        def run(e, lst):
            for o, waits in lst:
                for (s, v) in waits:
                    e.wait_ge(s, v)
                ins = o['fn'](e)
                if o['dma'] is not None:
                    ins.then_inc(keysem[o['dma']], 16)
                elif o['sig']:
                    ins.then_inc(sems[o['eng']], 1)

        @block.tensor
        def _(e):
            run(e, per_eng['pe'])

        @block.scalar
        def _(e):
            run(e, per_eng['act'])

        @block.vector
        def _(e):
            run(e, per_eng['dve'])

        @block.gpsimd
        def _(e):
            run(e, per_eng['pool'])

        @block.sync
        def _(e):
            run(e, per_eng['sp'])
            for k, s in keysem.items():
                if isinstance(k, str) and k.startswith('out'):
                    e.wait_ge(s, cum[k])


def build_nc(debug=False):
    nc = bass.Bass("TRN2", target_bir_lowering=False)
    es = ExitStack()

    def din(name, shape, dt=F32):
        return nc.dram_tensor(name, list(shape), dt, kind="ExternalInput").ap()

    xT = din("xT", [D, T])
    c_l = din("c_l", [128, KT])
    w_ada = din("w_ada", [D, 9 * D])
    b_ada = din("b_ada", [128, 72])
    gvec = din("gvec", [128, 4, KT])
    w1i = din("w1i", [D, 2 * FF])
    w1o = din("w1o", [FF, D])
    w2i = din("w2i", [D, 2 * FF])
    w2o = din("w2o", [FF, D])
    w_in = din("w_in", [D, 3 * D])
    pool_w = din("pool_w", [128, 4, 128])
    pool_bs = din("pool_bs", [128, 2, 4])
    w_pu = din("w_pu", [512, D])
    w_glu = din("w_glu", [512, D])
    b_glu = din("b_glu", [128, 8])
    w_su = din("w_su", [512, D])
    w_o = din("w_o", [D, D])
    ssm_sc = din("ssm_sc", [128, 3, NPAIR])
    ssm_b = din("ssm_b", [128, 2, NPAIR, 16])
    ssm_cc = din("ssm_cc", [128, 2, NPAIR, 16])
    ssm_dd = din("ssm_dd", [128, 32])
    cst_ident = din("cst_ident", [128, 128])
    cst_sel = din("cst_sel", [128, 1920])
    cst_mask = din("cst_mask", [128, 128])
    outT = nc.dram_tensor("outT", [D, T], F32, kind="ExternalOutput").ap()
    DBG = os.environ.get('KDBG', '0') == '1'
    dbgt = {}
    if DBG:
        for nm in ('s', 'v', 'z', 'm'):
            dbgt[nm] = nc.dram_tensor("dbg_" + nm, [128, 8, NT], BF16, kind="ExternalOutput").ap()

    def tap(nm, j0, ti):
        if DBG and ti == 0:
            P.add('sp', lambda e: e.dma_start(out=dbgt[nm], in_=cp[:, j0:j0 + 8, :]), reads=[('cp', j) for j in range(j0, j0 + 8)], dma='out_' + nm)

    def sb(name, shape, dt=F32):
        return es.enter_context(nc.sbuf_tensor(name, list(shape), dt))

    def ps(name, shape, dt=F32):
        return es.enter_context(nc.psum_tensor(name, list(shape), dt))

    P = Prog(nc, es)

    xs = sb("xs", [128, KT, NT])
    hb = sb("hb", [128, KT, NT], BF16)
    NCHK = 24
    cp = sb("cp", [128, NCHK, NT], BF16)
    tmpf = sb("tmpf", [128, 2, NT + 16])
    rstd = sb("rstd", [128, NT])
    NSLOT = 4
    wsl = sb("wsl", [128, NSLOT, 4096], BF16)
    ones_bf = sb("ones_bf", [128, 128], BF16)
    ident = sb("ident", [128, 128])
    eps_sb = sb("eps_sb", [128, 1])
    modv = sb("modv", [128, 72])
    cl_sb = sb("cl_sb", [128, KT])
    sc_bf = sb("sc_bf", [128, KT], BF16)
    bada_sb = sb("bada_sb", [128, 72])
    gv_sb = sb("gv_sb", [128, 4, KT])
    Avec = sb("Avec", [128, 3, KT])
    Gvec = sb("Gvec", [128, 3, KT])
    banks = [ps(f"bank{i}", [128, 512]) for i in range(8)]
    ostage = cp[:].rearrange("p a b -> p (a b)").bitcast(F32)

    def bank_b(i):
        return ('bank', i)

    slot_use = [0]

    def load_w(src_ap_list, shape_views):
        s = slot_use[0] % NSLOT
        slot_use[0] += 1
        for dst_fn, src in src_ap_list:
            dst = dst_fn(wsl[:, s, :])
            P.add('pool', lambda e, dst=dst, src=src: e.dma_start(out=dst, in_=src),
                  writes=[('slot', s)], dma=('slot', s))
        return s

    bank_rr = [0]

    reserved = [None]

    def next_bank():
        b = bank_rr[0] % 8
        bank_rr[0] += 1
        if b == reserved[0]:
            return next_bank()
        return b

    def small_load(dst, src, key):
        P.add('sp', lambda e: e.dma_start(out=dst, in_=src), writes=[key], dma=key)

    small_load(cl_sb[:], c_l, 'cl')
    small_load(bada_sb[:], b_ada, 'bada')
    small_load(gv_sb[:], gvec, 'gv')
    small_load(ident[:], cst_ident, 'ident')
    P.add('dve', lambda e: e.memset(ones_bf[:], 1.0 / D), writes=['ones'])
    P.add('dve', lambda e: e.memset(eps_sb[:], EPS), writes=['eps'])

    P.add('act', lambda e: e.activation(out=sc_bf[:], in_=cl_sb[:], func=AF.Silu),
          reads=['cl'], writes=['sc'])
    mod_bank = next_bank()
    reserved[0] = mod_bank

    def mod_slab(sl):
        c0 = sl * 512
        s = load_w([(lambda sa: sa.rearrange("p (k c) -> p k c", k=KT),
                     w_ada[:, c0:c0 + 512].rearrange("(k p) c -> p k c", p=128))], None)
        wv = wsl[:, s, :].rearrange("p (k c) -> p k c", k=KT)
        for mm in range(4):
            m = sl * 4 + mm
            for k in range(KT):
                P.add('pe', lambda e, m=m, k=k, mm=mm, wv=wv: e.matmul(
                    banks[mod_bank][:, m:m + 1], lhsT=wv[:, k, mm * 128:(mm + 1) * 128],
                    rhs=sc_bf[:, k:k + 1], start=(k == 0), stop=(k == KT - 1)),
                    reads=[('slot', s), 'sc'], writes=[bank_b(mod_bank)], sig=(k == KT - 1))

    def mod_finalize_sub(sub):
        cols = slice(24 * sub, 24 * sub + 24)
        P.add('dve', lambda e: e.tensor_tensor(out=modv[:, cols], in0=banks[mod_bank][:, cols], in1=bada_sb[:, cols], op=ALU.add),
              reads=[bank_b(mod_bank), 'bada'], writes=['modv'])
        sc_ap = modv[:, (sub * 3 + 1) * 8:(sub * 3 + 1) * 8 + 8]
        gt_ap = modv[:, (sub * 3 + 2) * 8:(sub * 3 + 2) * 8 + 8]
        P.add('dve', lambda e: e.scalar_tensor_tensor(
            out=Avec[:, sub, :], in0=sc_ap, scalar=1.0, in1=gv_sb[:, sub, :], op0=ALU.add, op1=ALU.mult),
            reads=['modv', 'gv'], writes=['Avec'])
        P.add('dve', lambda e: e.tensor_scalar(
            out=Gvec[:, sub, :], in0=gt_ap, scalar1=0.5, scalar2=None, op0=ALU.mult),
            reads=['modv'], writes=['Gvec'])

    for sl in range(6):
        mod_slab(sl)
    mod_finalize_sub(0)

    def Bvec(sub, k):
        return modv[:, (sub * 3) * 8 + k:(sub * 3) * 8 + k + 1]

    deferred_out = []

    def norm_mod(sub, t0):
        for k in range(KT):
            P.add('act', lambda e, k=k: e.activation(out=hb[:, k, :], in_=xs[:, k, :], func=AF.Square),
                  reads=[('xs', k)], writes=[('hb', k)])
        bs = [next_bank() for _ in range(NH)]
        for hf in range(NH):
            for k in range(KT):
                P.add('pe', lambda e, k=k, hf=hf: e.matmul(
                    banks[bs[hf]][:], lhsT=ones_bf[:], rhs=hb[:, k, hf * 512:(hf + 1) * 512],
                    start=(k == 0), stop=(k == KT - 1)),
                    reads=['ones', ('hb', k)], writes=[bank_b(bs[hf])], sig=(k == KT - 1))
            P.add('act', lambda e, hf=hf: e.activation(
                out=rstd[:, hf * 512:(hf + 1) * 512], in_=banks[bs[hf]][:], func=AF.Ln,
                bias=eps_sb[:, 0:1], scale=1.0),
                reads=[bank_b(bs[hf]), 'eps'], writes=[('rstd', hf)])
            P.add('act', lambda e, hf=hf: e.activation(
                out=rstd[:, hf * 512:(hf + 1) * 512], in_=rstd[:, hf * 512:(hf + 1) * 512], func=AF.Exp, scale=-0.5),
                reads=[('rstd', hf)], writes=[('rstd', hf)])
        for k in range(KT):
            tb = k % 2
            if sub < 3:
                a_ap = Avec[:, sub, k:k + 1]
            else:
                a_ap = gv_sb[:, 3, k:k + 1]
            if sub == 3:
                P.add('dve', lambda e, k=k, a_ap=a_ap: e.scalar_tensor_tensor(
                    out=ostage[:, k * NT:(k + 1) * NT], in0=xs[:, k, :], scalar=a_ap, in1=rstd[:], op0=ALU.mult, op1=ALU.mult),
                    reads=[('xs', k), ('rstd', 0), ('rstd', 1), 'gv'], writes=[('cp', 2 * k), ('cp', 2 * k + 1)])
            else:
                P.add('dve', lambda e, k=k, tb=tb, a_ap=a_ap: e.scalar_tensor_tensor(
                    out=tmpf[:, tb, 0:NT], in0=xs[:, k, :], scalar=a_ap, in1=rstd[:], op0=ALU.mult, op1=ALU.mult),
                    reads=[('xs', k), ('rstd', 0), ('rstd', 1), 'Avec', 'gv'], writes=[('tmpf', tb)])
            if sub < 3:
                P.add('act', lambda e, k=k, tb=tb: e.activation(
                    out=hb[:, k, :], in_=tmpf[:, tb, 0:NT], func=AF.Identity, bias=Bvec(sub, k), scale=1.0),
                    reads=[('tmpf', tb), 'modv'], writes=[('hb', k)])
            else:
                deferred_out.append((k, t0))

    def ffn(sub, w_i, w_o_, hook=None):
        for sl in range(FT // 2):
            f0 = sl * 2
            s = load_w([
                (lambda sa: sa.rearrange("p (k a c) -> p k a c", k=KT, a=2)[:, :, 0, :],
                 w_i[:, f0 * 128:f0 * 128 + 256].rearrange("(k p) c -> p k c", p=128)),
                (lambda sa: sa.rearrange("p (k a c) -> p k a c", k=KT, a=2)[:, :, 1, :],
                 w_i[:, FF + f0 * 128:FF + f0 * 128 + 256].rearrange("(k p) c -> p k c", p=128)),
            ], None)
            wv = wsl[:, s, :].rearrange("p (k a c) -> p k a c", k=KT, a=2)
            for ff in range(2):
                f = f0 + ff
                ba = [[next_bank() for _ in range(NH)] for _ in range(2)]
                for part in range(2):
                    for hf in range(NH):
                        b = ba[part][hf]
                        for k in range(KT):
                            P.add('pe', lambda e, k=k, hf=hf, part=part, ff=ff, b=b, wv=wv: e.matmul(
                                banks[b][:], lhsT=wv[:, k, part, ff * 128:(ff + 1) * 128],
                                rhs=hb[:, k, hf * 512:(hf + 1) * 512], start=(k == 0), stop=(k == KT - 1)),
                                reads=[('slot', s), ('hb', k)], writes=[bank_b(b)], sig=(k == KT - 1))
                tb = f % 2
                for hf in range(NH):
                    P.add('act', lambda e, hf=hf, tb=tb, b=ba[0][hf]: e.activation(
                        out=cp[:, 22 + tb, hf * 512:(hf + 1) * 512], in_=banks[b][:], func=AF.Silu),
                        reads=[bank_b(ba[0][hf])], writes=[('cp', 22 + tb)])
                    P.add('dve', lambda e, hf=hf, tb=tb, f=f, b=ba[1][hf]: e.tensor_tensor(
                        out=cp[:, f, hf * 512:(hf + 1) * 512], in0=banks[b][:],
                        in1=cp[:, 22 + tb, hf * 512:(hf + 1) * 512], op=ALU.mult),
                        reads=[bank_b(ba[1][hf]), ('cp', 22 + tb)], writes=[('cp', f)])
                if hook is not None:
                    hook('in', f)
        for m2 in range(4):
            ss = []
            for fh in range(2):
                s = load_w([(lambda sa: sa[:, 0:11 * 256].rearrange("p (f c) -> p f c", f=11),
                             w_o_[fh * 1408:(fh + 1) * 1408, m2 * 256:(m2 + 1) * 256].rearrange(
                                 "(f p) c -> p f c", p=128))], None)
                ss.append(s)
            for mm in range(2):
                m = m2 * 2 + mm
                for hf in range(NH):
                    b = next_bank()
                    for f in range(FT):
                        s = ss[f // 11]
                        wv = wsl[:, s, 0:11 * 256].rearrange("p (f c) -> p f c", f=11)
                        P.add('pe', lambda e, f=f, hf=hf, mm=mm, b=b, wv=wv: e.matmul(
                            banks[b][:], lhsT=wv[:, f % 11, mm * 128:(mm + 1) * 128],
                            rhs=cp[:, f, hf * 512:(hf + 1) * 512], start=(f == 0), stop=(f == FT - 1)),
                            reads=[('slot', s), ('cp', f)], writes=[bank_b(b)], sig=(f == FT - 1))
                    P.add('dve', lambda e, m=m, hf=hf, b=b: e.scalar_tensor_tensor(
                        out=xs[:, m, hf * 512:(hf + 1) * 512], in0=banks[b][:], scalar=Gvec[:, sub, m:m + 1],
                        in1=xs[:, m, hf * 512:(hf + 1) * 512], op0=ALU.mult, op1=ALU.add),
                        reads=[bank_b(b), 'Gvec', ('xs', m)], writes=[('xs', m)])
                if hook is not None:
                    hook('out', m)

    PI = float(np.pi)
    cst_rc = din("cst_rc", [128, 4, 16])
    lhsTX = sb("lhsTX", [128, 32, 2, 128], BF16)
    Wintra = sb("Wintra", [128, 32, 128], BF16)
    WinterZ = sb("WinterZ", [128, 2, 2, NPAIR, 128], BF16)
    ident_bf = sb("ident_bf", [128, 128], BF16)
    sel_bf = sb("sel_bf", [128, 1920], BF16)
    TcTs = sb("TcTs", [128, 2, NPAIR, 128])
    Tc = TcTs[:, 0]
    Ts = TcTs[:, 1]
    smalls = sb("smalls", [128, 30, NPAIR])
    PW = sb("PW", [128, 9, 2, NPAIR])
    carry = sb("carry", [128, 2, NPAIR])
    halo = sb("halo", [128, 4, 16])
    maskf = sb("maskf", [128, 128])
    dd_sb = sb("dd_sb", [128, 32])
    rc_sb = sb("rc_sb", [128, 4, 16])
    poolw_bf = sb("poolw_bf", [128, 4, 128], BF16)
    pbs_sb = sb("pbs_sb", [128, 2, 4])
    pbsc = sb("pbsc", [128, 4])
    bglu_sb = sb("bglu_sb", [128, 8])
    bglu_h = sb("bglu_h", [128, 8])
    scl = sb("scl", [128, 3, NPAIR])
    ki_sb = sb("ki_sb", [128, NPAIR], mybir.dt.int32)

    tcb = TcTs[:].rearrange("p a q c -> p (a q c)").bitcast(BF16)
    WxT_bf = tcb[:, 0:4096].rearrange("p (r q c) -> p r q c", r=2, q=NPAIR)
    Kpp_bf = tcb[:, 4096:8192].rearrange("p (r q c) -> p r q c", r=2, q=NPAIR)
    tflat0 = tmpf[:].rearrange("p a b -> p (a b)")
    bb = tflat0[:, 0:512].rearrange("p (r q c) -> p r q c", r=2, q=NPAIR)
    cc = tflat0[:, 512:1024].rearrange("p (r q c) -> p r q c", r=2, q=NPAIR)
    Bbar = tflat0[:, 1024:1536].rearrange("p (r q c) -> p r q c", r=2, q=NPAIR)
    tS = tflat0[:, 1536:2048].rearrange("p (r q c) -> p r q c", r=2, q=NPAIR)
    TF = [('tmpf', 0), ('tmpf', 1)]
    IPW = rstd[:, 0:9 * 2 * NPAIR].rearrange("p (k r q) -> p k r q", k=9, r=2)
    PWI = ['PW', ('rstd', 0), ('rstd', 1)]

    def dv(fn, reads, writes):
        P.add('dve', fn, reads=reads, writes=writes)

    def tt(out, a, b, op, reads, writes):
        dv(lambda e: e.tensor_tensor(out=out, in0=a, in1=b, op=op), reads, writes)

    def sm(i):
        return smalls[:, i, :]

    P.begin_defer()
    if MIXER:
        small_load(scl[:], ssm_sc, 'scl')
        small_load(dd_sb[:], ssm_dd, 'dd')
        small_load(maskf[:], cst_mask, 'maskf')
        small_load(rc_sb[:], cst_rc, 'rc')
        small_load(pbs_sb[:], pool_bs, 'pbs')
        small_load(bglu_sb[:], b_glu, 'bglu')
        P.add('sp', lambda e: e.dma_start(out=bb, in_=ssm_b), writes=TF, dma='bbcc')
        P.add('sp', lambda e: e.dma_start(out=cc, in_=ssm_cc), writes=TF, dma='bbcc')
        P.add('pool', lambda e: e.dma_start(out=sel_bf[:], in_=cst_sel), writes=['sel'], dma='sel')
        P.add('pool', lambda e: e.dma_start(out=poolw_bf[:], in_=pool_w), writes=['poolw'], dma='poolw')
        dv(lambda e: e.memset(carry[:], 0.0), [], ['carry'])
        dv(lambda e: e.memset(halo[:], 0.0), [], ['halo'])
        dv(lambda e: e.memset(lhsTX[:], 0.0), [], ['lhsTX'])
        dv(lambda e: e.tensor_tensor(out=pbsc[:], in0=pbs_sb[:, 0, :], in1=pbs_sb[:, 1, :], op=ALU.mult), ['pbs'], ['pbsc'])
        dv(lambda e: e.tensor_scalar(out=bglu_h[:], in0=bglu_sb[:], scalar1=0.5, scalar2=None, op0=ALU.mult), ['bglu'], ['bgluh'])

        S = 'smalls'
        LR, DT, XM, MAG, R8, ANG, YS, YC, SN, CS, AR, AI, NR, DEN, RDEN, FRE, FIM, T1, T2, T3, T4, IR, II, RR8 = range(24)
        P.add('act', lambda e: e.activation(out=sm(LR), in_=scl[:, 0, :], func=AF.Exp), reads=['scl'], writes=[S])
        P.add('act', lambda e: e.activation(out=sm(DT), in_=scl[:, 2, :], func=AF.Exp), reads=['scl'], writes=[S])
        dv(lambda e: e.tensor_scalar(out=sm(LR), in0=sm(LR), scalar1=-1.0, scalar2=None, op0=ALU.mult), [S], [S])
        tt(sm(XM), sm(LR), sm(DT), ALU.mult, [S], [S])
        P.add('act', lambda e: e.activation(out=sm(MAG), in_=sm(XM), func=AF.Exp), reads=[S], writes=[S])
        P.add('act', lambda e: e.activation(out=sm(R8), in_=sm(XM), func=AF.Exp, scale=8.0), reads=[S], writes=[S])
        tt(sm(ANG), scl[:, 1, :], sm(DT), ALU.mult, [S, 'scl'], [S])

        def range_reduce(dst, off):
            dv(lambda e: e.tensor_scalar(out=sm(T4), in0=sm(ANG), scalar1=off, scalar2=None, op0=ALU.add), [S], [S])
            dv(lambda e: e.tensor_scalar(out=sm(T3), in0=sm(T4), scalar1=1.0 / (2 * PI), scalar2=None, op0=ALU.mult), [S], [S])
            dv(lambda e: e.tensor_copy(out=ki_sb[:], in_=sm(T3)), [S], ['ki'])
            dv(lambda e: e.tensor_copy(out=sm(T3), in_=ki_sb[:]), ['ki'], [S])
            dv(lambda e: e.scalar_tensor_tensor(out=sm(dst), in0=sm(T3), scalar=-2 * PI, in1=sm(T4), op0=ALU.mult, op1=ALU.add), [S], [S])
            dv(lambda e: e.tensor_scalar(out=sm(T3), in0=sm(dst), scalar1=PI, scalar2=None, op0=ALU.is_gt), [S], [S])
            dv(lambda e: e.scalar_tensor_tensor(out=sm(dst), in0=sm(T3), scalar=-2 * PI, in1=sm(dst), op0=ALU.mult, op1=ALU.add), [S], [S])
            dv(lambda e: e.tensor_scalar(out=sm(T3), in0=sm(dst), scalar1=-PI, scalar2=None, op0=ALU.is_lt), [S], [S])
            dv(lambda e: e.scalar_tensor_tensor(out=sm(dst), in0=sm(T3), scalar=2 * PI, in1=sm(dst), op0=ALU.mult, op1=ALU.add), [S], [S])
        range_reduce(YS, 0.0)
        TH, ZZ, SS, CC, U1, U2 = 24, 25, 26, 27, 28, 29
        dv(lambda e: e.tensor_scalar(out=sm(TH), in0=sm(YS), scalar1=0.25, scalar2=None, op0=ALU.mult), [S], [S])
        tt(sm(ZZ), sm(TH), sm(TH), ALU.mult, [S], [S])
        dv(lambda e: e.memset(sm(SS), 1.0), [S], [S])
        dv(lambda e: e.memset(sm(CC), 1.0), [S], [S])
        for kk in (156.0, 110.0, 72.0, 42.0, 20.0, 6.0):
            tt(sm(SS), sm(SS), sm(ZZ), ALU.mult, [S], [S])
            dv(lambda e, kk=kk: e.tensor_scalar(out=sm(SS), in0=sm(SS), scalar1=-1.0 / kk, scalar2=1.0, op0=ALU.mult, op1=ALU.add), [S], [S])
        tt(sm(SS), sm(SS), sm(TH), ALU.mult, [S], [S])
        for kk in (182.0, 132.0, 90.0, 56.0, 30.0, 12.0, 2.0):
            tt(sm(CC), sm(CC), sm(ZZ), ALU.mult, [S], [S])
            dv(lambda e, kk=kk: e.tensor_scalar(out=sm(CC), in0=sm(CC), scalar1=-1.0 / kk, scalar2=1.0, op0=ALU.mult, op1=ALU.add), [S], [S])
        for _ in range(2):
            tt(sm(U1), sm(SS), sm(CC), ALU.mult, [S], [S])
            tt(sm(U2), sm(SS), sm(SS), ALU.mult, [S], [S])
            tt(sm(CC), sm(CC), sm(CC), ALU.mult, [S], [S])
            tt(sm(CC), sm(CC), sm(U2), ALU.subtract, [S], [S])
            dv(lambda e: e.tensor_scalar(out=sm(SS), in0=sm(U1), scalar1=2.0, scalar2=None, op0=ALU.mult), [S], [S])
        dv(lambda e: e.tensor_copy(out=sm(SN), in_=sm(SS)), [S], [S])
        dv(lambda e: e.tensor_copy(out=sm(CS), in_=sm(CC)), [S], [S])
        tt(sm(AR), sm(MAG), sm(CS), ALU.mult, [S], [S])
        tt(sm(AI), sm(MAG), sm(SN), ALU.mult, [S], [S])
        dv(lambda e: e.tensor_scalar(out=sm(NR), in0=sm(AR), scalar1=-1.0, scalar2=None, op0=ALU.add), [S], [S])
        tt(sm(T1), sm(LR), sm(LR), ALU.mult, [S], [S])
        tt(sm(T2), scl[:, 1, :], scl[:, 1, :], ALU.mult, [S, 'scl'], [S])
        tt(sm(DEN), sm(T1), sm(T2), ALU.add, [S], [S])
        dv(lambda e: e.reciprocal(out=sm(RDEN), in_=sm(DEN)), [S], [S])
        tt(sm(T1), sm(NR), sm(LR), ALU.mult, [S], [S])
        tt(sm(T2), sm(AI), scl[:, 1, :], ALU.mult, [S, 'scl'], [S])
        tt(sm(T1), sm(T1), sm(T2), ALU.add, [S], [S])
        tt(sm(FRE), sm(T1), sm(RDEN), ALU.mult, [S], [S])
        tt(sm(T1), sm(AI), sm(LR), ALU.mult, [S], [S])
        tt(sm(T2), sm(NR), scl[:, 1, :], ALU.mult, [S, 'scl'], [S])
        tt(sm(T1), sm(T1), sm(T2), ALU.subtract, [S], [S])
        tt(sm(FIM), sm(T1), sm(RDEN), ALU.mult, [S], [S])


        def bc16(ap2):
            return ap2.unsqueeze(2).to_broadcast([128, NPAIR, 16])

        def cmul(o_re, o_im, xr, xi, yr, yi, reads, writes):
            t0, t1 = tS[:, 0], tS[:, 1]
            rd = reads + TF
            tt(t0, xr, yr, ALU.mult, rd, TF)
            tt(t1, xi, yi, ALU.mult, rd, TF)
            tt(o_re, t0, t1, ALU.subtract, rd, writes + TF)
            tt(t0, xr, yi, ALU.mult, rd, TF)
            tt(t1, xi, yr, ALU.mult, rd, TF)
            tt(o_im, t0, t1, ALU.add, rd, writes + TF)

        cmul(Bbar[:, 0], Bbar[:, 1], bc16(sm(FRE)), bc16(sm(FIM)), bb[:, 0], bb[:, 1], [S, 'bbcc'], [])
        dv(lambda e: e.tensor_copy(out=PW[:, 1, 0, :], in_=sm(AR)), [S], ['PW'])
        dv(lambda e: e.tensor_copy(out=PW[:, 1, 1, :], in_=sm(AI)), [S], ['PW'])
        tt(sm(T1), sm(AR), sm(AR), ALU.mult, [S], [S])
        tt(sm(T2), sm(AI), sm(AI), ALU.mult, [S], [S])
        tt(sm(T1), sm(T1), sm(T2), ALU.add, [S], [S])
        dv(lambda e: e.reciprocal(out=sm(T3), in_=sm(T1)), [S], [S])
        tt(sm(IR), sm(AR), sm(T3), ALU.mult, [S], [S])
        dv(lambda e: e.scalar_tensor_tensor(out=sm(II), in0=sm(AI), scalar=-1.0, in1=sm(T3), op0=ALU.mult, op1=ALU.mult), [S], [S])
        dv(lambda e: e.tensor_copy(out=IPW[:, 1, 0, :], in_=sm(IR)), [S], PWI)
        dv(lambda e: e.tensor_copy(out=IPW[:, 1, 1, :], in_=sm(II)), [S], PWI)
        for (PWX, XR, XI) in ((PW, AR, AI), (IPW, IR, II)):
            for k in range(1, 8):
                t0, t1 = sm(T1), sm(T2)
                tt(t0, PWX[:, k, 0, :], sm(XR), ALU.mult, [S] + PWI, [S])
                tt(t1, PWX[:, k, 1, :], sm(XI), ALU.mult, [S] + PWI, [S])
                tt(PWX[:, k + 1, 0, :], t0, t1, ALU.subtract, [S], PWI)
                tt(t0, PWX[:, k, 0, :], sm(XI), ALU.mult, [S] + PWI, [S])
                tt(t1, PWX[:, k, 1, :], sm(XR), ALU.mult, [S] + PWI, [S])
                tt(PWX[:, k + 1, 1, :], t0, t1, ALU.add, [S], PWI)
        WxT5 = [WxT_bf[:, r].rearrange("p q (t h) -> p q t h", t=8) for r in range(2)]
        Kpp5 = [Kpp_bf[:, r].rearrange("p q (t h) -> p q t h", t=8) for r in range(2)]
        for tau in range(8):
            k = 7 - tau
            if k == 0:
                dv(lambda e, tau=tau: e.tensor_copy(out=WxT5[0][:, :, tau, :], in_=Bbar[:, 0]), TF, ['TcTs'])
                dv(lambda e, tau=tau: e.tensor_copy(out=WxT5[1][:, :, tau, :], in_=Bbar[:, 1]), TF, ['TcTs'])
            else:
                cmul(WxT5[0][:, :, tau, :], WxT5[1][:, :, tau, :], bc16(PW[:, k, 0, :]), bc16(PW[:, k, 1, :]),
                     Bbar[:, 0], Bbar[:, 1], ['PW'], ['TcTs'])
            cmul(Kpp5[0][:, :, tau, :], Kpp5[1][:, :, tau, :], bc16(IPW[:, tau + 1, 0, :]), bc16(IPW[:, tau + 1, 1, :]),
                 Bbar[:, 0], Bbar[:, 1], PWI, ['TcTs'])
        dv(lambda e: e.memset(WinterZ[:], 0.0), [], ['Winter'])
        dv(lambda e: e.tensor_copy(out=ident_bf[:], in_=ident[:]), ['ident'], ['identbf'])
        t0, t1 = tS[:, 0], tS[:, 1]
        for t in range(8):
            pr, pi = bc16(PW[:, t + 1, 0, :]), bc16(PW[:, t + 1, 1, :])
            rd = ['PW', 'bbcc'] + TF
            tt(t0, cc[:, 0], pr, ALU.mult, rd, TF)
            tt(t1, cc[:, 1], pi, ALU.mult, rd, TF)
            for g2 in range(2):
                rows = slice(g2 * 64, g2 * 64 + 64)
                ov = WinterZ[rows, g2, 0].rearrange("p q (t h) -> p q t h", t=8)[:, :, t, :]
                tt(ov, t0[rows], t1[rows], ALU.subtract, rd, ['Winter'])
            tt(t0, cc[:, 0], pi, ALU.mult, rd, TF)
            tt(t1, cc[:, 1], pr, ALU.mult, rd, TF)
            for g2 in range(2):
                rows = slice(g2 * 64, g2 * 64 + 64)
                ov = WinterZ[rows, g2, 1].rearrange("p q (t h) -> p q t h", t=8)[:, :, t, :]
                dv(lambda e, ov=ov, rows=rows, t0=t0, t1=t1: e.scalar_tensor_tensor(out=ov, in0=t0[rows], scalar=-1.0, in1=t1[rows], op0=ALU.mult, op1=ALU.subtract),
                   rd, ['Winter'])
        tflat = tmpf[:].rearrange("p a b -> p (a b)")
        for gb in range(8):
            P.atomic_begin()
            b = next_bank()
            for j in range(4):
                g = gb * 4 + j
                q, g2 = g // 2, g % 2
                P.add('pe', lambda e, b=b, j=j, q=q, g2=g2: e.matmul(
                    banks[b][:, j * 128:(j + 1) * 128], lhsT=Kpp_bf[:, 0, q, :], rhs=WinterZ[:, g2, 0, q, :], start=True, stop=False),
                    reads=['TcTs', 'Winter'], writes=[bank_b(b)], sig=False)
                P.add('pe', lambda e, b=b, j=j, q=q, g2=g2: e.matmul(
                    banks[b][:, j * 128:(j + 1) * 128], lhsT=Kpp_bf[:, 1, q, :], rhs=WinterZ[:, g2, 1, q, :], start=False, stop=True),
                    reads=['TcTs', 'Winter'], writes=[bank_b(b)], sig=True)
            for j in range(4):
                g = gb * 4 + j
                tA = tflat[:, 1536 + (j % 2) * 128:1536 + (j % 2) * 128 + 128]
                dv(lambda e, b=b, j=j, tA=tA: e.tensor_tensor(out=tA, in0=banks[b][:, j * 128:(j + 1) * 128], in1=maskf[:], op=ALU.mult),
                   [bank_b(b), 'maskf'] + TF, TF)
                dv(lambda e, g=g, tA=tA: e.scalar_tensor_tensor(out=Wintra[:, g, :], in0=ident[:], scalar=dd_sb[:, g:g + 1], in1=tA,
                                                              op0=ALU.mult, op1=ALU.add),
                   TF + ['ident', 'dd'], ['Wintra'])
            P.atomic_end()
        for qb in range(8):
            P.atomic_begin()
            b = next_bank()
            for j in range(4):
                q, r = qb * 2 + j // 2, j % 2
                P.add('pe', lambda e, b=b, j=j, q=q, r=r: e.matmul(banks[b][:, j * 128:(j + 1) * 128], lhsT=WxT_bf[:, r, q, :], rhs=ident_bf[:],
                                                              start=True, stop=True),
                      reads=['TcTs', 'identbf'], writes=[bank_b(b)])
            for j in range(4):
                q, r = qb * 2 + j // 2, j % 2
                P.add('act', lambda e, b=b, j=j, q=q, r=r: e.activation(out=lhsTX[:, 2 * q, r, 0:64], in_=banks[b][:, j * 128:j * 128 + 64], func=AF.Identity),
                      reads=[bank_b(b)], writes=['lhsTX'])
                P.add('act', lambda e, b=b, j=j, q=q, r=r: e.activation(out=lhsTX[:, 2 * q + 1, r, 64:128], in_=banks[b][:, j * 128 + 64:j * 128 + 128], func=AF.Identity),
                      reads=[bank_b(b)], writes=['lhsTX'])
            P.atomic_end()
        dv(lambda e: e.reciprocal(out=sm(RR8), in_=sm(R8)), [S], [S])
        tt(Tc[:, :, 0], PW[:, 8, 0, :], sm(RR8), ALU.mult, [S, 'PW', 'TcTs'], ['TcTs'])
        tt(Ts[:, :, 0], PW[:, 8, 1, :], sm(RR8), ALU.mult, [S, 'PW', 'TcTs'], ['TcTs'])

        def ptt(out, a, b, op, reads, writes):
            P.add('pool', lambda e: e.tensor_tensor(out=out, in0=a, in1=b, op=op), reads=reads, writes=writes)
        RS = [('rstd', 0), ('rstd', 1)]
        for mlev in range(7):
            n = 1 << mlev
            cb = Tc[:, :, n - 1:n].to_broadcast([128, NPAIR, n])
            sbb = Ts[:, :, n - 1:n].to_broadcast([128, NPAIR, n])
            tB = rstd[:, 0:NPAIR * n].rearrange("p (q c) -> p q c", q=NPAIR)
            ptt(Tc[:, :, n:2 * n], Tc[:, :, 0:n], cb, ALU.mult, ['TcTs'], ['TcTs'])
            ptt(tB, Ts[:, :, 0:n], sbb, ALU.mult, ['TcTs'], RS)
            ptt(Tc[:, :, n:2 * n], Tc[:, :, n:2 * n], tB, ALU.subtract, ['TcTs'] + RS, ['TcTs'])
            ptt(Ts[:, :, n:2 * n], Tc[:, :, 0:n], sbb, ALU.mult, ['TcTs'], ['TcTs'])
            ptt(tB, Ts[:, :, 0:n], cb, ALU.mult, ['TcTs'], RS)
            ptt(Ts[:, :, n:2 * n], Ts[:, :, n:2 * n], tB, ALU.add, ['TcTs'] + RS, ['TcTs'])

    setup_ops = P.end_defer()
    Ush = cp[:, 4:8, :].rearrange("p a (g c) -> p (a g) c", g=8)
    Ssh = [cp[:, 8:10, :].rearrange("p a (q c) -> p (a q) c", q=8),
           cp[:, 10:12, :].rearrange("p a (q c) -> p (a q) c", q=8)]
    Gsh = cp[:, 12:16, :].rearrange("p a (g c) -> p (a g) c", g=8)
    swf = cp[:, 12:20, :].rearrange("p a b -> p (a b)").bitcast(F32).rearrange("p (w q c) -> p w q c", w=4, q=8)
    ID_SW = [('cp', j) for j in range(12, 20)]
    upf = cp[:, 4:9, :].rearrange("p a b -> p (a b)")[:, 0:4 * (NT + 16)].rearrange("p (g c) -> p g c", g=4)
    ID_UPF = [('cp', j) for j in range(4, 9)]
    TGP = [0, 1, 2, 3, 20, 21, 22, 23]
    TGS = [4, 5, 6, 7, 8, 17, 18, 19]
    VCH = [12, 13, 14, 15]
    GLT = [4, 5]
    ZP = [9, 10, 11, 16]
    u16 = sb("u16", [128, 4, 16])

    def proj_slab(w_src, col0, ncols_tiles, consume):
        s = load_w([(lambda sa: sa.rearrange("p (k c) -> p k c", k=KT),
                     w_src[:, col0:col0 + 512].rearrange("(k p) c -> p k c", p=128))], None)
        wv = wsl[:, s, :].rearrange("p (k c) -> p k c", k=KT)
        for mi in range(4):
            for hf in range(NH):
                b = next_bank()
                for k in range(KT):
                    P.add('pe', lambda e, k=k, hf=hf, mi=mi, b=b, wv=wv: e.matmul(
                        banks[b][:], lhsT=wv[:, k, mi * 128:(mi + 1) * 128], rhs=hb[:, k, hf * 512:(hf + 1) * 512],
                        start=(k == 0), stop=(k == KT - 1)),
                        reads=[('slot', s), ('hb', k)], writes=[bank_b(b)], sig=(k == KT - 1))
                consume(mi, hf, b)

    def hs(hf):
        return slice(hf * 512, (hf + 1) * 512)

    def mixer(ti):
        if STAGE < 1:
            return
        def c_ussm(mi, hf, b):
            P.add('act', lambda e: e.activation(out=cp[:, mi, hs(hf)], in_=banks[b][:], func=AF.Identity),
                  reads=[bank_b(b)], writes=[('cp', mi)])
        proj_slab(w_in, 512, 4, c_ussm)
        if STAGE < 2:
            return
        for gb in range(8):
            b = next_bank()
            for j in range(4):
                g = gb * 4 + j
                blk, gi = g // 8, g % 8
                uv = cp[:, blk, :].rearrange("p (c t) -> p c t", t=8)
                for tau in range(8):
                    o = 240 * gi + 112 - 16 * tau
                    P.add('pe', lambda e, b=b, j=j, o=o, tau=tau, uv=uv: e.matmul(
                        banks[b][:, j * 128:(j + 1) * 128], lhsT=sel_bf[:, o:o + 128], rhs=uv[:, :, tau],
                        start=(tau == 0), stop=(tau == 7)),
                        reads=['sel', ('cp', blk)], writes=[bank_b(b)], sig=(tau == 7))
            P.add('act', lambda e, b=b, gb=gb: e.activation(
                out=Ush[:, gb * 4:gb * 4 + 4, :], in_=banks[b][:].rearrange("p (g c) -> p g c", g=4), func=AF.Identity),
                reads=[bank_b(b)], writes=[('cp', 4 + gb // 2)])
        if STAGE < 3:
            return
        A_, B_, C_, D_ = swf[:, 0], swf[:, 1], swf[:, 2], swf[:, 3]
        for hq in range(2):
            xb = [[next_bank(), next_bank()], [next_bank(), next_bank()]]
            for qq in range(8):
                q = hq * 8 + qq
                for r in range(2):
                    b = xb[r][qq // 4]
                    reg = banks[b][:, (qq % 4) * 128:(qq % 4 + 1) * 128]
                    for g2 in range(2):
                        g = 2 * q + g2
                        P.add('pe', lambda e, reg=reg, g=g, r=r, g2=g2: e.matmul(
                            reg, lhsT=lhsTX[:, g, r, :], rhs=Ush[:, g, :], start=(g2 == 0), stop=(g2 == 1)),
                            reads=['lhsTX', ('cp', 4 + g // 8)], writes=[bank_b(b)], sig=(g2 == 1))
            qs = slice(hq * 8, hq * 8 + 8)
            for bh in range(2):
                q4 = slice(bh * 4, bh * 4 + 4)
                qg = slice(hq * 8 + bh * 4, hq * 8 + bh * 4 + 4)
                Xr = banks[xb[0][bh]][:].rearrange("p (q c) -> p q c", q=4)
                Xi = banks[xb[1][bh]][:].rearrange("p (q c) -> p q c", q=4)
                rdb = [bank_b(xb[0][bh]), bank_b(xb[1][bh]), 'TcTs'] + ID_SW
                tt(A_[:, q4, :], Xr, Tc[:, qg, :], ALU.mult, rdb, ID_SW)
                tt(B_[:, q4, :], Xi, Ts[:, qg, :], ALU.mult, rdb, ID_SW)
                tt(A_[:, q4, :], A_[:, q4, :], B_[:, q4, :], ALU.add, rdb, ID_SW)
                tt(C_[:, q4, :], Xi, Tc[:, qg, :], ALU.mult, rdb, ID_SW)
                tt(B_[:, q4, :], Xr, Ts[:, qg, :], ALU.mult, rdb, ID_SW)
                tt(C_[:, q4, :], C_[:, q4, :], B_[:, q4, :], ALU.subtract, rdb, ID_SW)
            for qq in range(8):
                q = hq * 8 + qq
                r8b = sm(R8)[:, q:q + 1].to_broadcast([128, 128])
                dv(lambda e, qq=qq, q=q, r8b=r8b: e.tensor_tensor_scan(
                    out=B_[:, qq, :], data0=r8b, data1=A_[:, qq, :], initial=carry[:, 0, q:q + 1], op0=ALU.mult, op1=ALU.add),
                    ID_SW + ['carry', 'smalls'], ID_SW)
                dv(lambda e, qq=qq, q=q, r8b=r8b: e.tensor_tensor_scan(
                    out=D_[:, qq, :], data0=r8b, data1=C_[:, qq, :], initial=carry[:, 1, q:q + 1], op0=ALU.mult, op1=ALU.add),
                    ID_SW + ['carry', 'smalls'], ID_SW)
            rd = ID_SW + ['TcTs']
            tt(A_, B_, Tc[:, qs, :], ALU.mult, rd, ID_SW)
            tt(C_, D_, Ts[:, qs, :], ALU.mult, rd, ID_SW)
            tt(A_, A_, C_, ALU.subtract, rd, ID_SW)
            tt(C_, B_, Ts[:, qs, :], ALU.mult, rd, ID_SW)
            tt(B_, D_, Tc[:, qs, :], ALU.mult, rd, ID_SW)
            tt(C_, C_, B_, ALU.add, rd, ID_SW)
            for r, src in ((0, A_), (1, C_)):
                sid = [('cp', 8 + 2 * r), ('cp', 9 + 2 * r)]
                dv(lambda e, r=r, src=src, qs=qs: e.tensor_copy(out=Ssh[r][:, qs, 1:128], in_=src[:, :, 0:127]), ID_SW, sid)
                dv(lambda e, r=r, qs=qs: e.tensor_copy(out=Ssh[r][:, qs, 0:1], in_=carry[:, r, qs].unsqueeze(2)), ['carry'], sid)
                dv(lambda e, r=r, src=src, qs=qs: e.tensor_copy(out=carry[:, r, qs].unsqueeze(2), in_=src[:, :, 127:128]), ID_SW, ['carry'])
        for gi in range(2):
            def c_gp(mi, hf, b, gi=gi):
                ch = TGP[gi * 4 + mi]
                P.add('act', lambda e: e.activation(out=cp[:, ch, hs(hf)], in_=banks[b][:], func=AF.Tanh, scale=0.5),
                      reads=[bank_b(b)], writes=[('cp', ch)])
            proj_slab(w_in, 1024 + gi * 512, 4, c_gp)
        for gb in range(8):
            b = next_bank()
            for j in range(4):
                g = gb * 4 + j
                q, g2 = g // 2, g % 2
                rows = slice(g2 * 64, g2 * 64 + 64)
                reg = banks[b][:, j * 128:(j + 1) * 128]
                rdl = ['Wintra', 'Winter', ('cp', 4 + g // 8), ('cp', 8), ('cp', 9), ('cp', 10), ('cp', 11)]
                P.add('pe', lambda e, reg=reg, g=g: e.matmul(reg, lhsT=Wintra[:, g, :], rhs=Ush[:, g, :], start=True, stop=False),
                      reads=rdl, writes=[bank_b(b)], sig=False)
                P.add('pe', lambda e, reg=reg, q=q, g2=g2: e.matmul(reg, lhsT=WinterZ[:, g2, 0, q, :], rhs=Ssh[0][:, q, :], start=False, stop=False),
                      reads=rdl, writes=[bank_b(b)], sig=False)
                P.add('pe', lambda e, reg=reg, q=q, g2=g2: e.matmul(reg, lhsT=WinterZ[:, g2, 1, q, :], rhs=Ssh[1][:, q, :], start=False, stop=True),
                      reads=rdl, writes=[bank_b(b)], sig=True)
            P.add('act', lambda e, b=b, gb=gb: e.activation(
                out=Gsh[:, gb * 4:gb * 4 + 4, :], in_=banks[b][:].rearrange("p (g c) -> p g c", g=4), func=AF.Gelu),
                reads=[bank_b(b)], writes=[('cp', 12 + gb // 2)])
        for blk in range(4):
            for th in range(2):
                b = next_bank()
                for j in range(4):
                    t = th * 4 + j
                    for gi in range(8):
                        o = 240 * t + 112 - 16 * gi
                        P.add('pe', lambda e, b=b, j=j, o=o, gi=gi, blk=blk: e.matmul(
                            banks[b][:, j * 128:(j + 1) * 128], lhsT=sel_bf[:, o:o + 128], rhs=Gsh[:, blk * 8 + gi, :],
                            start=(gi == 0), stop=(gi == 7)),
                            reads=['sel', ('cp', 12 + blk)], writes=[bank_b(b)], sig=(gi == 7))
                gv = cp[:, 16 + blk, :].rearrange("p (c t) -> p c t", t=8)[:, :, th * 4:th * 4 + 4]
                dv(lambda e, b=b, gv=gv: e.tensor_copy(out=gv, in_=banks[b][:].rearrange("p (t c) -> p c t", t=4)),
                   [bank_b(b)], [('cp', 16 + blk)])
        s = load_w([(lambda sa: sa.rearrange("p (k c) -> p k c", k=4), w_glu.rearrange("(k p) c -> p k c", p=128))], None)
        wv = wsl[:, s, :].rearrange("p (k c) -> p k c", k=4)
        for mi in range(4):
            for hf in range(NH):
                bv_, bg_ = next_bank(), next_bank()
                for part, b in ((0, bv_), (1, bg_)):
                    for k in range(4):
                        P.add('pe', lambda e, k=k, b=b, part=part, mi=mi, hf=hf, wv=wv: e.matmul(
                            banks[b][:], lhsT=wv[:, k, part * 512 + mi * 128: part * 512 + (mi + 1) * 128],
                            rhs=cp[:, 16 + k, hs(hf)], start=(k == 0), stop=(k == 3)),
                            reads=[('slot', s), ('cp', 16 + k)], writes=[bank_b(b)], sig=(k == 3))
                tb = (mi * NH + hf) % 2
                P.add('act', lambda e, b=bg_, mi=mi, tb=tb: e.activation(
                    out=cp[:, GLT[tb], 0:512], in_=banks[b][:], func=AF.Tanh, bias=bglu_h[:, 4 + mi:5 + mi], scale=0.5),
                    reads=[bank_b(bg_), 'bgluh'], writes=[('cp', GLT[tb])])
                dv(lambda e, b=bv_, mi=mi, tb=tb: e.tensor_scalar(
                    out=tmpf[:, tb, 0:512], in0=banks[b][:], scalar1=bglu_sb[:, mi:mi + 1], scalar2=0.5, op0=ALU.add, op1=ALU.mult),
                    [bank_b(bv_), 'bglu'], [('tmpf', tb)])
                dv(lambda e, mi=mi, hf=hf, tb=tb: e.scalar_tensor_tensor(
                    out=cp[:, VCH[mi], hs(hf)], in0=cp[:, GLT[tb], 0:512], scalar=1.0, in1=tmpf[:, tb, 0:512], op0=ALU.add, op1=ALU.mult),
                    [('cp', GLT[tb]), ('tmpf', tb)], [('cp', VCH[mi])])
        if STAGE < 5:
            return
        W16 = NT + 16
        dv(lambda e: e.tensor_copy(out=upf[:, :, 0:16], in_=halo[:]), ['halo'], ID_UPF)

        def c_upool(mi, hf, b):
            P.add('act', lambda e: e.activation(out=upf[:, mi, 16 + hf * 512:16 + (hf + 1) * 512], in_=banks[b][:], func=AF.Identity),
                  reads=[bank_b(b)], writes=ID_UPF)
        proj_slab(w_in, 0, 4, c_upool)
        dv(lambda e: e.tensor_copy(out=halo[:], in_=upf[:, :, NT:NT + 16]), ID_UPF, ['halo'])
        for kg in range(4):
            w = 2 << kg
            src = upf[:, kg, :]
            TT = [('tmpf', 0), ('tmpf', 1)]
            for j in range(kg + 1):
                d = 1 << j
                lo = 2 * d - 1
                dst = tmpf[:, j % 2, :]
                P.add('dve', lambda e, dst=dst, src=src, d=d, lo=lo: e.tensor_tensor(
                    out=dst[:, lo:W16], in0=src[:, lo:W16], in1=src[:, lo - d:W16 - d], op=ALU.add),
                    reads=ID_UPF + TT, writes=[('tmpf', j % 2)])
                src = dst
            ssum = src
            if ti == 0:
                dv(lambda e, kg=kg: e.tensor_copy(out=u16[:, kg, :], in_=upf[:, kg, 16:32]), ID_UPF, ['u16'])
            dv(lambda e, kg=kg, w=w, ssum=ssum: e.scalar_tensor_tensor(
                out=upf[:, kg, 16:W16], in0=ssum[:, 16:W16], scalar=1.0 / w, in1=upf[:, kg, 16:W16], op0=ALU.mult, op1=ALU.subtract),
                ID_UPF + TT, ID_UPF)
            if ti == 0:
                dv(lambda e, kg=kg, ssum=ssum: e.tensor_tensor(out=ssum[:, 16:32], in0=ssum[:, 16:32], in1=rc_sb[:, kg, :], op=ALU.mult),
                   TT + ['rc'], TT)
                dv(lambda e, kg=kg, ssum=ssum: e.tensor_tensor(out=upf[:, kg, 16:32], in0=ssum[:, 16:32], in1=u16[:, kg, :], op=ALU.subtract),
                   TT + ID_UPF + ['u16'], ID_UPF)
        for kg in range(4):
            for hf in range(NH):
                b = next_bank()
                P.add('pe', lambda e, kg=kg, hf=hf, b=b: e.matmul(banks[b][:], lhsT=poolw_bf[:, kg, :], rhs=upf[:, kg, 16 + hf * 512:16 + (hf + 1) * 512], start=True, stop=True),
                      reads=['poolw'] + ID_UPF, writes=[bank_b(b)])
                P.add('act', lambda e, kg=kg, hf=hf, b=b: e.activation(
                    out=cp[:, ZP[kg], hs(hf)], in_=banks[b][:], func=AF.Identity, bias=pbsc[:, kg:kg + 1], scale=pbs_sb[:, 1, kg:kg + 1]),
                    reads=[bank_b(b), 'pbsc', 'pbs'], writes=[('cp', ZP[kg])])
        if STAGE < 6:
            return
        for gi in range(2):
            def c_gs(mi, hf, b, gi=gi):
                ch = TGS[gi * 4 + mi]
                P.add('act', lambda e: e.activation(out=cp[:, ch, hs(hf)], in_=banks[b][:], func=AF.Tanh, scale=0.5),
                      reads=[bank_b(b)], writes=[('cp', ch)])
            proj_slab(w_in, 2048 + gi * 512, 4, c_gs)
        s_pu = load_w([(lambda sa: sa.rearrange("p (k c) -> p k c", k=4), w_pu.rearrange("(k p) c -> p k c", p=128))], None)
        s_su = load_w([(lambda sa: sa.rearrange("p (k c) -> p k c", k=4), w_su.rearrange("(k p) c -> p k c", p=128))], None)
        wpu = wsl[:, s_pu, :].rearrange("p (k c) -> p k c", k=4)
        wsu = wsl[:, s_su, :].rearrange("p (k c) -> p k c", k=4)
        for m in range(8):
            for hf in range(NH):
                bp, bs_ = next_bank(), next_bank()
                for k in range(4):
                    P.add('pe', lambda e, k=k, m=m, hf=hf, b=bp: e.matmul(
                        banks[b][:], lhsT=wpu[:, k, m * 128:(m + 1) * 128], rhs=cp[:, ZP[k], hs(hf)], start=(k == 0), stop=(k == 3)),
                        reads=[('slot', s_pu), ('cp', ZP[k])], writes=[bank_b(bp)], sig=(k == 3))
                for k in range(4):
                    P.add('pe', lambda e, k=k, m=m, hf=hf, b=bs_: e.matmul(
                        banks[b][:], lhsT=wsu[:, k, m * 128:(m + 1) * 128], rhs=cp[:, VCH[k], hs(hf)], start=(k == 0), stop=(k == 3)),
                        reads=[('slot', s_su), ('cp', VCH[k])], writes=[bank_b(bs_)], sig=(k == 3))
                chs = TGS[m]
                dv(lambda e, m=m, hf=hf, b=bp: e.scalar_tensor_tensor(
                    out=tmpf[:, 0, 0:512], in0=cp[:, TGP[m], hs(hf)], scalar=1.0, in1=banks[b][:], op0=ALU.add, op1=ALU.mult),
                    [bank_b(bp), ('cp', TGP[m])], [('tmpf', 0)])
                dv(lambda e, chs=chs, hf=hf, b=bs_: e.scalar_tensor_tensor(
                    out=tmpf[:, 1, 0:512], in0=cp[:, chs, hs(hf)], scalar=1.0, in1=banks[b][:], op0=ALU.add, op1=ALU.mult),
                    [bank_b(bs_), ('cp', chs)], [('tmpf', 1)])
                P.add('dve', lambda e, m=m, hf=hf: e.tensor_tensor(
                    out=cp[:, TGP[m], hs(hf)], in0=tmpf[:, 0, 0:512], in1=tmpf[:, 1, 0:512], op=ALU.add),
                    reads=[('tmpf', 0), ('tmpf', 1)], writes=[('cp', TGP[m])])
        for m2 in range(2):
            s = load_w([(lambda sa: sa.rearrange("p (k c) -> p k c", k=KT),
                         w_o[:, m2 * 512:(m2 + 1) * 512].rearrange("(k p) c -> p k c", p=128))], None)
            wv = wsl[:, s, :].rearrange("p (k c) -> p k c", k=KT)
            for mm in range(4):
                m = m2 * 4 + mm
                for hf in range(NH):
                    b = next_bank()
                    for k in range(KT):
                        P.add('pe', lambda e, k=k, mm=mm, hf=hf, b=b, wv=wv: e.matmul(
                            banks[b][:], lhsT=wv[:, k, mm * 128:(mm + 1) * 128], rhs=cp[:, TGP[k], hs(hf)], start=(k == 0), stop=(k == KT - 1)),
                            reads=[('slot', s), ('cp', TGP[k])], writes=[bank_b(b)], sig=(k == KT - 1))
                    dv(lambda e, m=m, hf=hf, b=b: e.scalar_tensor_tensor(
                        out=xs[:, m, hs(hf)], in0=banks[b][:], scalar=Gvec[:, 1, m:m + 1], in1=xs[:, m, hs(hf)], op0=ALU.mult, op1=ALU.add),
                        [bank_b(b), 'Gvec', ('xs', m)], [('xs', m)])

    def flush_out():
        while deferred_out:
            k, t0o = deferred_out.pop(0)
            P.add('sp', lambda e, k=k, t0o=t0o: e.dma_start(
                out=outT[k * 128:(k + 1) * 128, t0o:t0o + NT], in_=ostage[:, k * NT:(k + 1) * NT]),
                reads=[('cp', 2 * k), ('cp', 2 * k + 1)], dma='out')

    for ti in range(NTILES):
        t0 = ti * NT
        for k in range(KT):
            P.add('sp', lambda e, k=k, t0=t0: e.dma_start(out=xs[:, k, :], in_=xT[k * 128:(k + 1) * 128, t0:t0 + NT]),
                  writes=[('xs', k)], dma=('xs', k))
        flush_out()
        norm_mod(0, t0)
        if ti == 0:
            mod_next = [6]

            def hook0(kind, i):
                P.pull(setup_ops, 14 if kind == 'in' else 13)
                if kind == 'in' and i % 2 == 1 and mod_next[0] < 18:
                    mod_slab(mod_next[0])
                    mod_next[0] += 1
            ffn(0, w1i, w1o, hook=hook0)
            P.pull(setup_ops, len(setup_ops))
            while mod_next[0] < 18:
                mod_slab(mod_next[0])
                mod_next[0] += 1
            mod_finalize_sub(1)
            mod_finalize_sub(2)
            reserved[0] = None
        else:
            ffn(0, w1i, w1o)
        if MIXER:
            norm_mod(1, t0)
            mixer(ti)
        norm_mod(2, t0)
        ffn(2, w2i, w2o)
        norm_mod(3, t0)

    flush_out()
    with nc.Block() as block:
        P.emit(block)
    return nc, es, P


_CACHE = {}


def _consts():
    ident = np.eye(128, dtype=np.float32)
    sel = np.zeros((128, 1920), np.float32)
    for p in range(128):
        sel[p, 240 * (p // 16) + 112 + (p % 16)] = 1.0
    mask = np.zeros((128, 128), np.float32)
    for a in range(8):
        for b in range(8):
            if b >= a:
                mask[a * 16:(a + 1) * 16, b * 16:(b + 1) * 16] = 1.0
    rc = np.zeros((128, 4, 16), np.float32)
    for k in range(4):
        w = 2 << k
        for tt_ in range(16):
            rc[:, k, tt_] = 1.0 / min(tt_ + 1, w)
    return ident, sel, mask, rc


def _pair_layout(a):
    g, n = a.shape[0], a.shape[1]
    rest = a.shape[2:]
    a = a.reshape((16, 2, n) + rest)
    a = np.moveaxis(a, 0, 2)
    return np.ascontiguousarray(a.reshape((128, 16) + rest))


def make_in_maps(inputs):
    f = lambda a: np.ascontiguousarray(np.asarray(a, dtype=np.float32))
    x = f(inputs["x"])
    c = f(inputs["c"])
    ident, sel, mask, rc = _consts()
    gvec = np.stack([f(inputs["g_ffn1"])[0], f(inputs["g_mix"])[0], f(inputs["g_ffn2"])[0], f(inputs["g_final"])], 0)
    gvec = np.ascontiguousarray(gvec.reshape(4, KT, 128).transpose(2, 0, 1))
    b_ada = np.ascontiguousarray(f(inputs["b_ada"])[0].reshape(72, 128).T)
    pool_w = np.ascontiguousarray(f(inputs["pool_w"])[0].transpose(1, 0, 2))
    pool_bs = np.ascontiguousarray(np.stack([f(inputs["pool_b"])[0].reshape(4, 128).T,
                                             f(inputs["pool_scale"])[0].reshape(4, 128).T], 1))
    b_glu = np.ascontiguousarray(f(inputs["b_glu"])[0].reshape(8, 128).T)
    lrl = f(inputs["ssm_lam_re_log"])[0]
    lim = f(inputs["ssm_lam_im"])[0]
    ldt = np.broadcast_to(f(inputs["ssm_log_dt"])[0][:, None], (32, 64))
    ssm_sc = np.ascontiguousarray(np.stack([_pair_layout(lrl), _pair_layout(lim), _pair_layout(ldt)], 1))
    ssm_b = np.ascontiguousarray(np.stack([_pair_layout(f(inputs["ssm_b_re"])[0]),
                                           _pair_layout(f(inputs["ssm_b_im"])[0])], 1))
    cre = f(inputs["ssm_c_re"])[0].transpose(0, 2, 1)
    cim = f(inputs["ssm_c_im"])[0].transpose(0, 2, 1)
    ssm_cc = np.ascontiguousarray(np.stack([_pair_layout(cre), _pair_layout(cim)], 1))
    dd = f(inputs["ssm_d"])[0].reshape(32, 16)
    ssm_dd = np.ascontiguousarray(np.tile(dd.T, (8, 1)))
    shared = dict(
        w_ada=f(inputs["w_ada"])[0], b_ada=b_ada, gvec=gvec,
        w1i=f(inputs["w_ffn1_in"])[0], w1o=f(inputs["w_ffn1_out"])[0],
        w2i=f(inputs["w_ffn2_in"])[0], w2o=f(inputs["w_ffn2_out"])[0],
        w_in=f(inputs["w_in"])[0], pool_w=pool_w, pool_bs=pool_bs,
        w_pu=f(inputs["w_pool_up"])[0], w_glu=f(inputs["w_glu"])[0], b_glu=b_glu,
        w_su=f(inputs["w_ssm_up"])[0], w_o=f(inputs["w_out"])[0],
        ssm_sc=ssm_sc, ssm_b=ssm_b, ssm_cc=ssm_cc, ssm_dd=ssm_dd,
        cst_ident=ident, cst_sel=sel, cst_mask=mask, cst_rc=rc,
    )
    maps = []
    for b in range(8):
        m = dict(shared)
        m["xT"] = np.ascontiguousarray(x[b].T)
        m["c_l"] = np.ascontiguousarray(c[b].reshape(KT, 128).T)
        maps.append(m)
    return maps


def kernel(**inputs):
    if "nc" not in _CACHE:
        _CACHE["nc"] = build_nc()
    nc, es, P = _CACHE["nc"]
    in_maps = make_in_maps(inputs)
    res = run_bass_kernel_spmd(nc, in_maps, core_ids=list(range(8)))
    out = np.stack([np.asarray(r["outT"], dtype=np.float32).T for r in res.results], 0)
    return np.ascontiguousarray(out)
```

```python
import numpy as np
from contextlib import ExitStack
import concourse.bass as bass
import concourse.mybir as mybir
from concourse.bass_utils import run_bass_kernel_spmd

F32 = mybir.dt.float32
BF16 = mybir.dt.bfloat16
AF = mybir.ActivationFunctionType
ALU = mybir.AluOpType

D = 1024
T = 4096
FF = 2816
NT = 1024
import os
NTILES = int(os.environ.get('KTILES', str(T // NT)))
NH = NT // 512
KT = D // 128
FT = FF // 128
EPS = 1e-6
LCH = 8
NC_T = NT // LCH
NPAIR = 16

import os
MIXER = True
STAGE = int(os.environ.get('KSTAGE', '9'))
SETUP = int(os.environ.get('KSETUP', '9'))
SUB = int(os.environ.get('KSUB', '9'))


class Prog:
    def __init__(self, nc, es):
        self.nc = nc
        self.es = es
        self.ops = []
        self.last_w = {}
        self.readers = {}
        self.dma_keys = {}

    skip = False
    deferred = None
    _atomic = None

    def begin_defer(self):
        self.deferred = []

    def end_defer(self):
        lst, self.deferred = self.deferred, None
        return lst

    def atomic_begin(self):
        if self.deferred is not None:
            self._atomic = []

    def atomic_end(self):
        if self.deferred is not None:
            self.deferred.append(self._atomic)
            self._atomic = None

    def pull(self, lst, n):
        for _ in range(n):
            if not lst:
                return
            for args in lst.pop(0):
                self.add(*args[0], **args[1])

    def add(self, eng, fn, reads=(), writes=(), dma=None, sig=True):
        if self.skip:
            return None
        if self.deferred is not None:
            item = ((eng, fn), dict(reads=list(reads), writes=list(writes), dma=dma, sig=sig))
            if self._atomic is not None:
                self._atomic.append(item)
            else:
                self.deferred.append([item])
            return None
        idx = len(self.ops)
        deps = set()
        for b in reads:
            if b in self.last_w:
                deps.add(self.last_w[b])
        for b in writes:
            if b in self.last_w:
                deps.add(self.last_w[b])
            for r in self.readers.get(b, ()):
                deps.add(r)
        for b in writes:
            self.last_w[b] = idx
            self.readers[b] = []
        for b in reads:
            self.readers.setdefault(b, []).append(idx)
        deps.discard(idx)
        self.ops.append(dict(eng=eng, fn=fn, deps=deps, dma=dma, sig=sig))
        return idx

    def emit(self, block):
        nc = self.nc
        ops = self.ops
        engs = ['pe', 'act', 'dve', 'pool', 'sp']
        sems = {e: self.es.enter_context(nc.semaphore(f"s_{e}")) for e in ['pe', 'act', 'dve', 'pool']}
        keysem = {}
        for o in ops:
            if o['dma'] is not None and o['dma'] not in keysem:
                keysem[o['dma']] = self.es.enter_context(nc.semaphore(f"d_{len(keysem)}"))
        cnt = {e: 0 for e in sems}
        for o in ops:
            if o['dma'] is None:
                if o['sig']:
                    cnt[o['eng']] += 1
                    o['cnt'] = cnt[o['eng']]
                else:
                    o['cnt'] = None
        nxt = {e: None for e in sems}
        for o in reversed(ops):
            if o['dma'] is None:
                if o['cnt'] is None:
                    o['cnt'] = nxt[o['eng']]
                    assert o['cnt'] is not None
                else:
                    nxt[o['eng']] = o['cnt']
        cum = {k: 0 for k in keysem}
        waited = {}
        per_eng = {e: [] for e in engs}
        for i, o in enumerate(ops):
            waits = {}
            for j in o['deps']:
                d = ops[j]
                if d['dma'] is not None:
                    s = keysem[d['dma']]
                    v = cum[d['dma']]
                else:
                    if d['eng'] == 'pe' and o['eng'] == 'pe' and o['dma'] is None:
                        continue
                    s = sems[d['eng']]
                    v = d['cnt']
                key = (o['eng'], id(s))
                if waited.get(key, 0) >= v:
                    continue
                if key not in waits or waits[key][1] < v:
                    waits[key] = (s, v)
            for key, (s, v) in waits.items():
                waited[key] = v
            if o['dma'] is not None:
                cum[o['dma']] += 16
            per_eng[o['eng']].append((o, list(waits.values())))
        self.n_ops = {e: len(per_eng[e]) for e in engs}

        def run(e, lst):
            for o, waits in lst:
                for (s, v) in waits:
                    e.wait_ge(s, v)
                ins = o['fn'](e)
                if o['dma'] is not None:
                    ins.then_inc(keysem[o['dma']], 16)
                elif o['sig']:
                    ins.then_inc(sems[o['eng']], 1)

        @block.tensor
        def _(e):
            run(e, per_eng['pe'])

        @block.scalar
        def _(e):
            run(e, per_eng['act'])

        @block.vector
        def _(e):
            run(e, per_eng['dve'])

        @block.gpsimd
        def _(e):
            run(e, per_eng['pool'])

        @block.sync
        def _(e):
            run(e, per_eng['sp'])
            for k, s in keysem.items():
                if isinstance(k, str) and k.startswith('out'):
                    e.wait_ge(s, cum[k])


def build_nc(debug=False):
    nc = bass.Bass("TRN2", target_bir_lowering=False)
    es = ExitStack()

    def din(name, shape, dt=F32):
        return nc.dram_tensor(name, list(shape), dt, kind="ExternalInput").ap()

    xT = din("xT", [D, T])
    c_l = din("c_l", [128, KT])
    w_ada = din("w_ada", [D, 9 * D])
    b_ada = din("b_ada", [128, 72])
    gvec = din("gvec", [128, 4, KT])
    w1i = din("w1i", [D, 2 * FF])
    w1o = din("w1o", [FF, D])
    w2i = din("w2i", [D, 2 * FF])
    w2o = din("w2o", [FF, D])
    w_in = din("w_in", [D, 3 * D])
    pool_w = din("pool_w", [128, 4, 128])
    pool_bs = din("pool_bs", [128, 2, 4])
    w_pu = din("w_pu", [512, D])
    w_glu = din("w_glu", [512, D])
    b_glu = din("b_glu", [128, 8])
    w_su = din("w_su", [512, D])
    w_o = din("w_o", [D, D])
    ssm_sc = din("ssm_sc", [128, 3, NPAIR])
    ssm_b = din("ssm_b", [128, 2, NPAIR, 16])
    ssm_cc = din("ssm_cc", [128, 2, NPAIR, 16])
    ssm_dd = din("ssm_dd", [128, 32])
    cst_ident = din("cst_ident", [128, 128])
    cst_sel = din("cst_sel", [128, 1920])
    cst_mask = din("cst_mask", [128, 128])
    outT = nc.dram_tensor("outT", [D, T], F32, kind="ExternalOutput").ap()
    DBG = os.environ.get('KDBG', '0') == '1'
    dbgt = {}
    if DBG:
        for nm in ('s', 'v', 'z', 'm'):
            dbgt[nm] = nc.dram_tensor("dbg_" + nm, [128, 8, NT], BF16, kind="ExternalOutput").ap()

    def tap(nm, j0, ti):
        if DBG and ti == 0:
            P.add('sp', lambda e: e.dma_start(out=dbgt[nm], in_=cp[:, j0:j0 + 8, :]), reads=[('cp', j) for j in range(j0, j0 + 8)], dma='out_' + nm)

    def sb(name, shape, dt=F32):
        return es.enter_context(nc.sbuf_tensor(name, list(shape), dt))

    def ps(name, shape, dt=F32):
        return es.enter_context(nc.psum_tensor(name, list(shape), dt))

    P = Prog(nc, es)

    xs = sb("xs", [128, KT, NT])
    hb = sb("hb", [128, KT, NT], BF16)
    NCHK = 24
    cp = sb("cp", [128, NCHK, NT], BF16)
    tmpf = sb("tmpf", [128, 2, NT])
    rstd = sb("rstd", [128, NT])
    NSLOT = 4
    wsl = sb("wsl", [128, NSLOT, 4096], BF16)
    ones_bf = sb("ones_bf", [128, 128], BF16)
    eps_sb = sb("eps_sb", [128, 1])
    modv = sb("modv", [128, 72])
    cl_sb = sb("cl_sb", [128, KT])
    sc_bf = sb("sc_bf", [128, KT], BF16)
    bada_sb = sb("bada_sb", [128, 72])
    gv_sb = sb("gv_sb", [128, 4, KT])
    Avec = sb("Avec", [128, 3, KT])
    Gvec = sb("Gvec", [128, 3, KT])
    banks = [ps(f"bank{i}", [128, 512]) for i in range(8)]
    ostage = cp[:].rearrange("p a b -> p (a b)").bitcast(F32)

    def bank_b(i):
        return ('bank', i)

    slot_use = [0]

    def load_w(src_ap_list, shape_views):
        s = slot_use[0] % NSLOT
        slot_use[0] += 1
        for dst_fn, src in src_ap_list:
            dst = dst_fn(wsl[:, s, :])
            P.add('pool', lambda e, dst=dst, src=src: e.dma_start(out=dst, in_=src),
                  writes=[('slot', s)], dma=('slot', s))
        return s

    bank_rr = [0]

    reserved = [None]

    def next_bank():
        b = bank_rr[0] % 8
        bank_rr[0] += 1
        if b == reserved[0]:
            return next_bank()
        return b

    def small_load(dst, src, key):
        P.add('sp', lambda e: e.dma_start(out=dst, in_=src), writes=[key], dma=key)

    small_load(cl_sb[:], c_l, 'cl')
    small_load(bada_sb[:], b_ada, 'bada')
    small_load(gv_sb[:], gvec, 'gv')
    P.add('dve', lambda e: e.memset(ones_bf[:], 1.0 / D), writes=['ones'])
    P.add('dve', lambda e: e.memset(eps_sb[:], EPS), writes=['eps'])

    P.add('act', lambda e: e.activation(out=sc_bf[:], in_=cl_sb[:], func=AF.Silu),
          reads=['cl'], writes=['sc'])
    mod_bank = next_bank()
    reserved[0] = mod_bank

    def mod_slab(sl):
        c0 = sl * 512
        s = load_w([(lambda sa: sa.rearrange("p (k c) -> p k c", k=KT),
                     w_ada[:, c0:c0 + 512].rearrange("(k p) c -> p k c", p=128))], None)
        wv = wsl[:, s, :].rearrange("p (k c) -> p k c", k=KT)
        for mm in range(4):
            m = sl * 4 + mm
            for k in range(KT):
                P.add('pe', lambda e, m=m, k=k, mm=mm, wv=wv: e.matmul(
                    banks[mod_bank][:, m:m + 1], lhsT=wv[:, k, mm * 128:(mm + 1) * 128],
                    rhs=sc_bf[:, k:k + 1], start=(k == 0), stop=(k == KT - 1)),
                    reads=[('slot', s), 'sc'], writes=[bank_b(mod_bank)], sig=(k == KT - 1))

    def mod_finalize_sub(sub):
        cols = slice(24 * sub, 24 * sub + 24)
        P.add('dve', lambda e: e.tensor_tensor(out=modv[:, cols], in0=banks[mod_bank][:, cols], in1=bada_sb[:, cols], op=ALU.add),
              reads=[bank_b(mod_bank), 'bada'], writes=['modv'])
        sc_ap = modv[:, (sub * 3 + 1) * 8:(sub * 3 + 1) * 8 + 8]
        gt_ap = modv[:, (sub * 3 + 2) * 8:(sub * 3 + 2) * 8 + 8]
        P.add('dve', lambda e: e.scalar_tensor_tensor(
            out=Avec[:, sub, :], in0=sc_ap, scalar=1.0, in1=gv_sb[:, sub, :], op0=ALU.add, op1=ALU.mult),
            reads=['modv', 'gv'], writes=['Avec'])
        P.add('dve', lambda e: e.tensor_scalar(
            out=Gvec[:, sub, :], in0=gt_ap, scalar1=0.5, scalar2=None, op0=ALU.mult),
            reads=['modv'], writes=['Gvec'])

    for sl in range(6):
        mod_slab(sl)
    mod_finalize_sub(0)

    def Bvec(sub, k):
        return modv[:, (sub * 3) * 8 + k:(sub * 3) * 8 + k + 1]

    deferred_out = []

    def norm_mod(sub, t0):
        for k in range(KT):
            P.add('act', lambda e, k=k: e.activation(out=hb[:, k, :], in_=xs[:, k, :], func=AF.Square),
                  reads=[('xs', k)], writes=[('hb', k)])
        bs = [next_bank() for _ in range(NH)]
        for hf in range(NH):
            for k in range(KT):
                P.add('pe', lambda e, k=k, hf=hf: e.matmul(
                    banks[bs[hf]][:], lhsT=ones_bf[:], rhs=hb[:, k, hf * 512:(hf + 1) * 512],
                    start=(k == 0), stop=(k == KT - 1)),
                    reads=['ones', ('hb', k)], writes=[bank_b(bs[hf])], sig=(k == KT - 1))
            P.add('act', lambda e, hf=hf: e.activation(
                out=rstd[:, hf * 512:(hf + 1) * 512], in_=banks[bs[hf]][:], func=AF.Ln,
                bias=eps_sb[:, 0:1], scale=1.0),
                reads=[bank_b(bs[hf]), 'eps'], writes=[('rstd', hf)])
            P.add('act', lambda e, hf=hf: e.activation(
                out=rstd[:, hf * 512:(hf + 1) * 512], in_=rstd[:, hf * 512:(hf + 1) * 512], func=AF.Exp, scale=-0.5),
                reads=[('rstd', hf)], writes=[('rstd', hf)])
        for k in range(KT):
            tb = k % 2
            if sub < 3:
                a_ap = Avec[:, sub, k:k + 1]
            else:
                a_ap = gv_sb[:, 3, k:k + 1]
            if sub == 3:
                P.add('dve', lambda e, k=k, a_ap=a_ap: e.scalar_tensor_tensor(
                    out=ostage[:, k * NT:(k + 1) * NT], in0=xs[:, k, :], scalar=a_ap, in1=rstd[:], op0=ALU.mult, op1=ALU.mult),
                    reads=[('xs', k), ('rstd', 0), ('rstd', 1), 'gv'], writes=[('cp', 2 * k), ('cp', 2 * k + 1)])
            else:
                P.add('dve', lambda e, k=k, tb=tb, a_ap=a_ap: e.scalar_tensor_tensor(
                    out=tmpf[:, tb, 0:NT], in0=xs[:, k, :], scalar=a_ap, in1=rstd[:], op0=ALU.mult, op1=ALU.mult),
                    reads=[('xs', k), ('rstd', 0), ('rstd', 1), 'Avec', 'gv'], writes=[('tmpf', tb)])
            if sub < 3:
                P.add('act', lambda e, k=k, tb=tb: e.activation(
                    out=hb[:, k, :], in_=tmpf[:, tb, 0:NT], func=AF.Identity, bias=Bvec(sub, k), scale=1.0),
                    reads=[('tmpf', tb), 'modv'], writes=[('hb', k)])
            else:
                deferred_out.append((k, t0))

    def ffn(sub, w_i, w_o_, hook=None):
        for sl in range(FT // 2):
            f0 = sl * 2
            s = load_w([
                (lambda sa: sa.rearrange("p (k a c) -> p k a c", k=KT, a=2)[:, :, 0, :],
                 w_i[:, f0 * 128:f0 * 128 + 256].rearrange("(k p) c -> p k c", p=128)),
                (lambda sa: sa.rearrange("p (k a c) -> p k a c", k=KT, a=2)[:, :, 1, :],
                 w_i[:, FF + f0 * 128:FF + f0 * 128 + 256].rearrange("(k p) c -> p k c", p=128)),
            ], None)
            wv = wsl[:, s, :].rearrange("p (k a c) -> p k a c", k=KT, a=2)
            for ff in range(2):
                f = f0 + ff
                ba = [[next_bank() for _ in range(NH)] for _ in range(2)]
                for part in range(2):
                    for hf in range(NH):
                        b = ba[part][hf]
                        for k in range(KT):
                            P.add('pe', lambda e, k=k, hf=hf, part=part, ff=ff, b=b, wv=wv: e.matmul(
                                banks[b][:], lhsT=wv[:, k, part, ff * 128:(ff + 1) * 128],
                                rhs=hb[:, k, hf * 512:(hf + 1) * 512], start=(k == 0), stop=(k == KT - 1)),
                                reads=[('slot', s), ('hb', k)], writes=[bank_b(b)], sig=(k == KT - 1))
                tb = f % 2
                for hf in range(NH):
                    P.add('act', lambda e, hf=hf, tb=tb, b=ba[0][hf]: e.activation(
                        out=cp[:, 22 + tb, hf * 512:(hf + 1) * 512], in_=banks[b][:], func=AF.Silu),
                        reads=[bank_b(ba[0][hf])], writes=[('cp', 22 + tb)])
                    P.add('dve', lambda e, hf=hf, tb=tb, f=f, b=ba[1][hf]: e.tensor_tensor(
                        out=cp[:, f, hf * 512:(hf + 1) * 512], in0=banks[b][:],
                        in1=cp[:, 22 + tb, hf * 512:(hf + 1) * 512], op=ALU.mult),
                        reads=[bank_b(ba[1][hf]), ('cp', 22 + tb)], writes=[('cp', f)])
                    if hook is not None:
                        hook('in', 2 * f + hf)
        for m2 in range(4):
            ss = []
            for fh in range(2):
                s = load_w([(lambda sa: sa[:, 0:11 * 256].rearrange("p (f c) -> p f c", f=11),
                             w_o_[fh * 1408:(fh + 1) * 1408, m2 * 256:(m2 + 1) * 256].rearrange(
                                 "(f p) c -> p f c", p=128))], None)
                ss.append(s)
            for mm in range(2):
                m = m2 * 2 + mm
                for hf in range(NH):
                    b = next_bank()
                    for f in range(FT):
                        s = ss[f // 11]
                        wv = wsl[:, s, 0:11 * 256].rearrange("p (f c) -> p f c", f=11)
                        P.add('pe', lambda e, f=f, hf=hf, mm=mm, b=b, wv=wv: e.matmul(
                            banks[b][:], lhsT=wv[:, f % 11, mm * 128:(mm + 1) * 128],
                            rhs=cp[:, f, hf * 512:(hf + 1) * 512], start=(f == 0), stop=(f == FT - 1)),
                            reads=[('slot', s), ('cp', f)], writes=[bank_b(b)], sig=(f == FT - 1))
                    P.add('dve', lambda e, m=m, hf=hf, b=b: e.scalar_tensor_tensor(
                        out=xs[:, m, hf * 512:(hf + 1) * 512], in0=banks[b][:], scalar=Gvec[:, sub, m:m + 1],
                        in1=xs[:, m, hf * 512:(hf + 1) * 512], op0=ALU.mult, op1=ALU.add),
                        reads=[bank_b(b), 'Gvec', ('xs', m)], writes=[('xs', m)])
                if hook is not None:
                    hook('out', m)

    PI = float(np.pi)
    cst_rc = din("cst_rc", [128, 4, 16])
    lhsTX = sb("lhsTX", [128, 32, 2, 128], BF16)
    Wintra = sb("Wintra", [128, 32, 128], BF16)
    WinterZ = sb("WinterZ", [128, 2, 2, NPAIR, 128], BF16)
    ident_bf = sb("ident_bf", [128, 128], BF16)
    sel_bf = sb("sel_bf", [128, 1920], BF16)
    TcTs = sb("TcTs", [128, 2, NPAIR, 128])
    Tc = TcTs[:, 0]
    Ts = TcTs[:, 1]
    smalls = sb("smalls", [128, 30, NPAIR])
    PW = sb("PW", [128, 9, 2, NPAIR])
    carry = sb("carry", [128, 2, NPAIR])
    halo = sb("halo", [128, 4, 16], BF16)
    maskf = sb("maskf", [128, 128], BF16)
    dd_sb = sb("dd_sb", [128, 32])
    rc_sb = sb("rc_sb", [128, 4, 16])
    poolw_bf = sb("poolw_bf", [128, 4, 128], BF16)
    pbs_sb = sb("pbs_sb", [128, 2, 4])
    pbsc = sb("pbsc", [128, 4])
    bglu_sb = sb("bglu_sb", [128, 8])
    bglu_h = sb("bglu_h", [128, 8])
    scl = sb("scl", [128, 3, NPAIR])
    ki_sb = sb("ki_sb", [128, NPAIR], mybir.dt.int32)

    tcb = TcTs[:].rearrange("p a q c -> p (a q c)").bitcast(BF16)
    WxT_bf = tcb[:, 0:4096].rearrange("p (r q c) -> p r q c", r=2, q=NPAIR)
    Kpp_bf = tcb[:, 4096:8192].rearrange("p (r q c) -> p r q c", r=2, q=NPAIR)
    tflat0 = tmpf[:].rearrange("p a b -> p (a b)")
    bb = tflat0[:, 0:512].rearrange("p (r q c) -> p r q c", r=2, q=NPAIR)
    cc = tflat0[:, 512:1024].rearrange("p (r q c) -> p r q c", r=2, q=NPAIR)
    Bbar = tflat0[:, 1024:1536].rearrange("p (r q c) -> p r q c", r=2, q=NPAIR)
    tS = tflat0[:, 1536:2048].rearrange("p (r q c) -> p r q c", r=2, q=NPAIR)
    TF = [('tmpf', 0), ('tmpf', 1)]
    IPW = rstd[:, 0:9 * 2 * NPAIR].rearrange("p (k r q) -> p k r q", k=9, r=2)
    PWI = ['PW', ('rstd', 0), ('rstd', 1)]

    def dv(fn, reads, writes):
        P.add('dve', fn, reads=reads, writes=writes)

    def tt(out, a, b, op, reads, writes):
        dv(lambda e: e.tensor_tensor(out=out, in0=a, in1=b, op=op), reads, writes)

    def sm(i):
        return smalls[:, i, :]

    P.begin_defer()
    if MIXER:
        small_load(scl[:], ssm_sc, 'scl')
        small_load(dd_sb[:], ssm_dd, 'dd')
        P.add('pool', lambda e: e.dma_start(out=maskf[:], in_=cst_mask), writes=['maskf'], dma='maskf')
        small_load(rc_sb[:], cst_rc, 'rc')
        small_load(pbs_sb[:], pool_bs, 'pbs')
        small_load(bglu_sb[:], b_glu, 'bglu')
        P.add('sp', lambda e: e.dma_start(out=bb, in_=ssm_b), writes=TF, dma='bbcc')
        P.add('sp', lambda e: e.dma_start(out=cc, in_=ssm_cc), writes=TF, dma='bbcc')
        P.add('pool', lambda e: e.dma_start(out=sel_bf[:], in_=cst_sel), writes=['sel'], dma='sel')
        P.add('pool', lambda e: e.dma_start(out=poolw_bf[:], in_=pool_w), writes=['poolw'], dma='poolw')
        dv(lambda e: e.memset(carry[:], 0.0), [], ['carry'])
        dv(lambda e: e.memset(halo[:], 0.0), [], ['halo'])
        dv(lambda e: e.memset(lhsTX[:], 0.0), [], ['lhsTX'])
        dv(lambda e: e.tensor_tensor(out=pbsc[:], in0=pbs_sb[:, 0, :], in1=pbs_sb[:, 1, :], op=ALU.mult), ['pbs'], ['pbsc'])
        poolw_f = Wintra[:, 0:8, :].rearrange("p a b -> p (a b)").bitcast(F32).rearrange("p (g d) -> p g d", g=4)
        P.add('sp', lambda e: e.dma_start(out=poolw_f, in_=pool_w), writes=['Wintra'], dma='poolwf')
        P.add('pool', lambda e: e.dma_start(out=ident_bf[:], in_=cst_ident), writes=['identbf'], dma='identbf')
        for kg in range(4):
            w = 2 << kg
            dv(lambda e, kg=kg, w=w: e.tensor_scalar(out=pwB[:, kg, :], in0=poolw_f[:, kg, :], scalar1=1.0 - w, scalar2=None, op0=ALU.mult),
               ['Wintra'], ['pwAB'])
            dv(lambda e, kg=kg, w=w: e.tensor_scalar(out=pbs_w[:, kg:kg + 1], in0=pbs_sb[:, 1, kg:kg + 1], scalar1=1.0 / w, scalar2=None, op0=ALU.mult),
               ['pbs'], ['pbsw'])

        S = 'smalls'
        LR, DT, XM, MAG, R8, ANG, YS, YC, SN, CS, AR, AI, NR, DEN, RDEN, FRE, FIM, T1, T2, T3, T4, IR, II, RR8 = range(24)
        P.add('act', lambda e: e.activation(out=sm(LR), in_=scl[:, 0, :], func=AF.Exp), reads=['scl'], writes=[S])
        P.add('act', lambda e: e.activation(out=sm(DT), in_=scl[:, 2, :], func=AF.Exp), reads=['scl'], writes=[S])
        dv(lambda e: e.tensor_scalar(out=sm(LR), in0=sm(LR), scalar1=-1.0, scalar2=None, op0=ALU.mult), [S], [S])
        tt(sm(XM), sm(LR), sm(DT), ALU.mult, [S], [S])
        P.add('act', lambda e: e.activation(out=sm(MAG), in_=sm(XM), func=AF.Exp), reads=[S], writes=[S])
        P.add('act', lambda e: e.activation(out=sm(R8), in_=sm(XM), func=AF.Exp, scale=8.0), reads=[S], writes=[S])
        tt(sm(ANG), scl[:, 1, :], sm(DT), ALU.mult, [S, 'scl'], [S])

        def range_reduce(dst, off):
            dv(lambda e: e.tensor_scalar(out=sm(T4), in0=sm(ANG), scalar1=off, scalar2=None, op0=ALU.add), [S], [S])
            dv(lambda e: e.tensor_scalar(out=sm(T3), in0=sm(T4), scalar1=1.0 / (2 * PI), scalar2=None, op0=ALU.mult), [S], [S])
            dv(lambda e: e.tensor_copy(out=ki_sb[:], in_=sm(T3)), [S], ['ki'])
            dv(lambda e: e.tensor_copy(out=sm(T3), in_=ki_sb[:]), ['ki'], [S])
            dv(lambda e: e.scalar_tensor_tensor(out=sm(dst), in0=sm(T3), scalar=-2 * PI, in1=sm(T4), op0=ALU.mult, op1=ALU.add), [S], [S])
            dv(lambda e: e.tensor_scalar(out=sm(T3), in0=sm(dst), scalar1=PI, scalar2=None, op0=ALU.is_gt), [S], [S])
            dv(lambda e: e.scalar_tensor_tensor(out=sm(dst), in0=sm(T3), scalar=-2 * PI, in1=sm(dst), op0=ALU.mult, op1=ALU.add), [S], [S])
            dv(lambda e: e.tensor_scalar(out=sm(T3), in0=sm(dst), scalar1=-PI, scalar2=None, op0=ALU.is_lt), [S], [S])
            dv(lambda e: e.scalar_tensor_tensor(out=sm(dst), in0=sm(T3), scalar=2 * PI, in1=sm(dst), op0=ALU.mult, op1=ALU.add), [S], [S])
        range_reduce(YS, 0.0)
        TH, ZZ, SS, CC, U1, U2 = 24, 25, 26, 27, 28, 29
        dv(lambda e: e.tensor_scalar(out=sm(TH), in0=sm(YS), scalar1=0.25, scalar2=None, op0=ALU.mult), [S], [S])
        tt(sm(ZZ), sm(TH), sm(TH), ALU.mult, [S], [S])
        dv(lambda e: e.memset(sm(SS), 1.0), [S], [S])
        dv(lambda e: e.memset(sm(CC), 1.0), [S], [S])
        for kk in (156.0, 110.0, 72.0, 42.0, 20.0, 6.0):
            tt(sm(SS), sm(SS), sm(ZZ), ALU.mult, [S], [S])
            dv(lambda e, kk=kk: e.tensor_scalar(out=sm(SS), in0=sm(SS), scalar1=-1.0 / kk, scalar2=1.0, op0=ALU.mult, op1=ALU.add), [S], [S])
        tt(sm(SS), sm(SS), sm(TH), ALU.mult, [S], [S])
        for kk in (182.0, 132.0, 90.0, 56.0, 30.0, 12.0, 2.0):
            tt(sm(CC), sm(CC), sm(ZZ), ALU.mult, [S], [S])
            dv(lambda e, kk=kk: e.tensor_scalar(out=sm(CC), in0=sm(CC), scalar1=-1.0 / kk, scalar2=1.0, op0=ALU.mult, op1=ALU.add), [S], [S])
        for _ in range(2):
            tt(sm(U1), sm(SS), sm(CC), ALU.mult, [S], [S])
            tt(sm(U2), sm(SS), sm(SS), ALU.mult, [S], [S])
            tt(sm(CC), sm(CC), sm(CC), ALU.mult, [S], [S])
            tt(sm(CC), sm(CC), sm(U2), ALU.subtract, [S], [S])
            dv(lambda e: e.tensor_scalar(out=sm(SS), in0=sm(U1), scalar1=2.0, scalar2=None, op0=ALU.mult), [S], [S])
        dv(lambda e: e.tensor_copy(out=sm(SN), in_=sm(SS)), [S], [S])
        dv(lambda e: e.tensor_copy(out=sm(CS), in_=sm(CC)), [S], [S])
        tt(sm(AR), sm(MAG), sm(CS), ALU.mult, [S], [S])
        tt(sm(AI), sm(MAG), sm(SN), ALU.mult, [S], [S])
        dv(lambda e: e.tensor_scalar(out=sm(NR), in0=sm(AR), scalar1=-1.0, scalar2=None, op0=ALU.add), [S], [S])
        tt(sm(T1), sm(LR), sm(LR), ALU.mult, [S], [S])
        tt(sm(T2), scl[:, 1, :], scl[:, 1, :], ALU.mult, [S, 'scl'], [S])
        tt(sm(DEN), sm(T1), sm(T2), ALU.add, [S], [S])
        dv(lambda e: e.reciprocal(out=sm(RDEN), in_=sm(DEN)), [S], [S])
        tt(sm(T1), sm(NR), sm(LR), ALU.mult, [S], [S])
        tt(sm(T2), sm(AI), scl[:, 1, :], ALU.mult, [S, 'scl'], [S])
        tt(sm(T1), sm(T1), sm(T2), ALU.add, [S], [S])
        tt(sm(FRE), sm(T1), sm(RDEN), ALU.mult, [S], [S])
        tt(sm(T1), sm(AI), sm(LR), ALU.mult, [S], [S])
        tt(sm(T2), sm(NR), scl[:, 1, :], ALU.mult, [S, 'scl'], [S])
        tt(sm(T1), sm(T1), sm(T2), ALU.subtract, [S], [S])
        tt(sm(FIM), sm(T1), sm(RDEN), ALU.mult, [S], [S])


        def bc16(ap2):
            return ap2.unsqueeze(2).to_broadcast([128, NPAIR, 16])

        def cmul(o_re, o_im, xr, xi, yr, yi, reads, writes):
            t0, t1 = tS[:, 0], tS[:, 1]
            rd = reads + TF
            tt(t0, xr, yr, ALU.mult, rd, TF)
            tt(t1, xi, yi, ALU.mult, rd, TF)
            tt(o_re, t0, t1, ALU.subtract, rd, writes + TF)
            tt(t0, xr, yi, ALU.mult, rd, TF)
            tt(t1, xi, yr, ALU.mult, rd, TF)
            tt(o_im, t0, t1, ALU.add, rd, writes + TF)

        cmul(Bbar[:, 0], Bbar[:, 1], bc16(sm(FRE)), bc16(sm(FIM)), bb[:, 0], bb[:, 1], [S, 'bbcc'], [])
        dv(lambda e: e.tensor_copy(out=PW[:, 1, 0, :], in_=sm(AR)), [S], ['PW'])
        dv(lambda e: e.tensor_copy(out=PW[:, 1, 1, :], in_=sm(AI)), [S], ['PW'])
        tt(sm(T1), sm(AR), sm(AR), ALU.mult, [S], [S])
        tt(sm(T2), sm(AI), sm(AI), ALU.mult, [S], [S])
        tt(sm(T1), sm(T1), sm(T2), ALU.add, [S], [S])
        dv(lambda e: e.reciprocal(out=sm(T3), in_=sm(T1)), [S], [S])
        tt(sm(IR), sm(AR), sm(T3), ALU.mult, [S], [S])
        dv(lambda e: e.scalar_tensor_tensor(out=sm(II), in0=sm(AI), scalar=-1.0, in1=sm(T3), op0=ALU.mult, op1=ALU.mult), [S], [S])
        dv(lambda e: e.tensor_copy(out=IPW[:, 1, 0, :], in_=sm(IR)), [S], PWI)
        dv(lambda e: e.tensor_copy(out=IPW[:, 1, 1, :], in_=sm(II)), [S], PWI)
        for (PWX, XR, XI) in ((PW, AR, AI), (IPW, IR, II)):
            for k in range(1, 8):
                t0, t1 = sm(T1), sm(T2)
                tt(t0, PWX[:, k, 0, :], sm(XR), ALU.mult, [S] + PWI, [S])
                tt(t1, PWX[:, k, 1, :], sm(XI), ALU.mult, [S] + PWI, [S])
                tt(PWX[:, k + 1, 0, :], t0, t1, ALU.subtract, [S], PWI)
                tt(t0, PWX[:, k, 0, :], sm(XI), ALU.mult, [S] + PWI, [S])
                tt(t1, PWX[:, k, 1, :], sm(XR), ALU.mult, [S] + PWI, [S])
                tt(PWX[:, k + 1, 1, :], t0, t1, ALU.add, [S], PWI)
        WxT5 = [WxT_bf[:, r].rearrange("p q (t h) -> p q t h", t=8) for r in range(2)]
        Kpp5 = [Kpp_bf[:, r].rearrange("p q (t h) -> p q t h", t=8) for r in range(2)]
        for tau in range(8):
            k = 7 - tau
            if k == 0:
                dv(lambda e, tau=tau: e.tensor_copy(out=WxT5[0][:, :, tau, :], in_=Bbar[:, 0]), TF, ['TcTs'])
                dv(lambda e, tau=tau: e.tensor_copy(out=WxT5[1][:, :, tau, :], in_=Bbar[:, 1]), TF, ['TcTs'])
            else:
                cmul(WxT5[0][:, :, tau, :], WxT5[1][:, :, tau, :], bc16(PW[:, k, 0, :]), bc16(PW[:, k, 1, :]),
                     Bbar[:, 0], Bbar[:, 1], ['PW'], ['TcTs'])
            cmul(Kpp5[0][:, :, tau, :], Kpp5[1][:, :, tau, :], bc16(IPW[:, tau + 1, 0, :]), bc16(IPW[:, tau + 1, 1, :]),
                 Bbar[:, 0], Bbar[:, 1], PWI, ['TcTs'])
        dv(lambda e: e.memset(WinterZ[:], 0.0), [], ['Winter'])
        t0, t1 = tS[:, 0], tS[:, 1]
        for t in range(8):
            pr, pi = bc16(PW[:, t + 1, 0, :]), bc16(PW[:, t + 1, 1, :])
            rd = ['PW', 'bbcc'] + TF
            tt(t0, cc[:, 0], pr, ALU.mult, rd, TF)
            tt(t1, cc[:, 1], pi, ALU.mult, rd, TF)
            for g2 in range(2):
                rows = slice(g2 * 64, g2 * 64 + 64)
                ov = WinterZ[rows, g2, 0].rearrange("p q (t h) -> p q t h", t=8)[:, :, t, :]
                tt(ov, t0[rows], t1[rows], ALU.subtract, rd, ['Winter'])
            tt(t0, cc[:, 0], pi, ALU.mult, rd, TF)
            tt(t1, cc[:, 1], pr, ALU.mult, rd, TF)
            for g2 in range(2):
                rows = slice(g2 * 64, g2 * 64 + 64)
                ov = WinterZ[rows, g2, 1].rearrange("p q (t h) -> p q t h", t=8)[:, :, t, :]
                dv(lambda e, ov=ov, rows=rows, t0=t0, t1=t1: e.scalar_tensor_tensor(out=ov, in0=t0[rows], scalar=-1.0, in1=t1[rows], op0=ALU.mult, op1=ALU.subtract),
                   rd, ['Winter'])
        tflat = tmpf[:].rearrange("p a b -> p (a b)")
        for gb in range(8):
            P.atomic_begin()
            b = next_bank()
            for j in range(4):
                g = gb * 4 + j
                q, g2 = g // 2, g % 2
                P.add('pe', lambda e, b=b, j=j, q=q, g2=g2: e.matmul(
                    banks[b][:, j * 128:(j + 1) * 128], lhsT=Kpp_bf[:, 0, q, :], rhs=WinterZ[:, g2, 0, q, :], start=True, stop=False),
                    reads=['TcTs', 'Winter'], writes=[bank_b(b)], sig=False)
                P.add('pe', lambda e, b=b, j=j, q=q, g2=g2: e.matmul(
                    banks[b][:, j * 128:(j + 1) * 128], lhsT=Kpp_bf[:, 1, q, :], rhs=WinterZ[:, g2, 1, q, :], start=False, stop=True),
                    reads=['TcTs', 'Winter'], writes=[bank_b(b)], sig=True)
            for j in range(4):
                g = gb * 4 + j
                tA = tflat[:, 1536 + (j % 2) * 128:1536 + (j % 2) * 128 + 128]
                dv(lambda e, b=b, j=j, tA=tA: e.tensor_tensor(out=tA, in0=banks[b][:, j * 128:(j + 1) * 128], in1=maskf[:], op=ALU.mult),
                   [bank_b(b), 'maskf'] + TF, TF)
                dv(lambda e, g=g, tA=tA: e.scalar_tensor_tensor(out=Wintra[:, g, :], in0=ident_bf[:], scalar=dd_sb[:, g:g + 1], in1=tA,
                                                              op0=ALU.mult, op1=ALU.add),
                   TF + ['identbf', 'dd'], ['Wintra'])
            P.atomic_end()
        for qb in range(8):
            P.atomic_begin()
            b = next_bank()
            for j in range(4):
                q, r = qb * 2 + j // 2, j % 2
                P.add('pe', lambda e, b=b, j=j, q=q, r=r: e.matmul(banks[b][:, j * 128:(j + 1) * 128], lhsT=WxT_bf[:, r, q, :], rhs=ident_bf[:],
                                                              start=True, stop=True),
                      reads=['TcTs', 'identbf'], writes=[bank_b(b)])
            for j in range(4):
                q, r = qb * 2 + j // 2, j % 2
                P.add('act', lambda e, b=b, j=j, q=q, r=r: e.activation(out=lhsTX[:, 2 * q, r, 0:64], in_=banks[b][:, j * 128:j * 128 + 64], func=AF.Identity),
                      reads=[bank_b(b)], writes=['lhsTX'])
                P.add('act', lambda e, b=b, j=j, q=q, r=r: e.activation(out=lhsTX[:, 2 * q + 1, r, 64:128], in_=banks[b][:, j * 128 + 64:j * 128 + 128], func=AF.Identity),
                      reads=[bank_b(b)], writes=['lhsTX'])
            P.atomic_end()
        dv(lambda e: e.reciprocal(out=sm(RR8), in_=sm(R8)), [S], [S])
        tt(Tc[:, :, 0], PW[:, 8, 0, :], sm(RR8), ALU.mult, [S, 'PW', 'TcTs'], ['TcTs'])
        tt(Ts[:, :, 0], PW[:, 8, 1, :], sm(RR8), ALU.mult, [S, 'PW', 'TcTs'], ['TcTs'])

        def ptt(out, a, b, op, reads, writes):
            P.add('pool', lambda e: e.tensor_tensor(out=out, in0=a, in1=b, op=op), reads=reads, writes=writes)
        RS = [('rstd', 0), ('rstd', 1)]
        for mlev in range(7):
            n = 1 << mlev
            cb = Tc[:, :, n - 1:n].to_broadcast([128, NPAIR, n])
            sbb = Ts[:, :, n - 1:n].to_broadcast([128, NPAIR, n])
            tB = rstd[:, 0:NPAIR * n].rearrange("p (q c) -> p q c", q=NPAIR)
            ptt(Tc[:, :, n:2 * n], Tc[:, :, 0:n], cb, ALU.mult, ['TcTs'], ['TcTs'])
            ptt(tB, Ts[:, :, 0:n], sbb, ALU.mult, ['TcTs'], RS)
            ptt(Tc[:, :, n:2 * n], Tc[:, :, n:2 * n], tB, ALU.subtract, ['TcTs'] + RS, ['TcTs'])
            ptt(Ts[:, :, n:2 * n], Tc[:, :, 0:n], sbb, ALU.mult, ['TcTs'], ['TcTs'])
            ptt(tB, Ts[:, :, 0:n], cb, ALU.mult, ['TcTs'], RS)
            ptt(Ts[:, :, n:2 * n], Ts[:, :, n:2 * n], tB, ALU.add, ['TcTs'] + RS, ['TcTs'])

    setup_ops = P.end_defer()
    Ush = cp[:, 4:8, :].rearrange("p a (g c) -> p (a g) c", g=8)
    Ssh = [cp[:, 8:10, :].rearrange("p a (q c) -> p (a q) c", q=8),
           cp[:, 10:12, :].rearrange("p a (q c) -> p (a q) c", q=8)]
    Gsh = cp[:, 12:16, :].rearrange("p a (g c) -> p (a g) c", g=8)
    swf = cp[:, 12:20, :].rearrange("p a b -> p (a b)").bitcast(F32).rearrange("p (w q c) -> p w q c", w=4, q=8)
    ID_SW = [('cp', j) for j in range(12, 20)]
    upf = cp[:, 4:9, :].rearrange("p a b -> p (a b)")[:, 0:4 * (NT + 16)].rearrange("p (g c) -> p g c", g=4)
    ID_UPF = [('cp', j) for j in range(4, 9)]
    TGP = [0, 1, 2, 3, 20, 21, 22, 23]
    TGS = [4, 5, 6, 7, 8, 17, 18, 19]
    VCH = [12, 13, 14, 15]
    GLT = [4, 5]
    ZP = [9, 10, 11, 16]
    z16 = sb("z16", [128, 4, 16], BF16)
    pwB = sb("pwB", [128, 4, 128], BF16)
    pbs_w = sb("pbs_w", [128, 4])

    def proj_slab(w_src, col0, ncols_tiles, consume):
        s = load_w([(lambda sa: sa.rearrange("p (k c) -> p k c", k=KT),
                     w_src[:, col0:col0 + 512].rearrange("(k p) c -> p k c", p=128))], None)
        wv = wsl[:, s, :].rearrange("p (k c) -> p k c", k=KT)
        for mi in range(4):
            for hf in range(NH):
                b = next_bank()
                for k in range(KT):
                    P.add('pe', lambda e, k=k, hf=hf, mi=mi, b=b, wv=wv: e.matmul(
                        banks[b][:], lhsT=wv[:, k, mi * 128:(mi + 1) * 128], rhs=hb[:, k, hf * 512:(hf + 1) * 512],
                        start=(k == 0), stop=(k == KT - 1)),
                        reads=[('slot', s), ('hb', k)], writes=[bank_b(b)], sig=(k == KT - 1))
                consume(mi, hf, b)

    def hs(hf):
        return slice(hf * 512, (hf + 1) * 512)

    def mixer(ti):
        if STAGE < 1:
            return
        def c_ussm(mi, hf, b):
            P.add('act', lambda e: e.activation(out=cp[:, mi, hs(hf)], in_=banks[b][:], func=AF.Identity),
                  reads=[bank_b(b)], writes=[('cp', mi)])
        proj_slab(w_in, 512, 4, c_ussm)
        if STAGE < 2:
            return
        for gb in range(8):
            b = next_bank()
            for j in range(4):
                g = gb * 4 + j
                blk, gi = g // 8, g % 8
                uv = cp[:, blk, :].rearrange("p (c t) -> p c t", t=8)
                for tau in range(8):
                    o = 240 * gi + 112 - 16 * tau
                    P.add('pe', lambda e, b=b, j=j, o=o, tau=tau, uv=uv: e.matmul(
                        banks[b][:, j * 128:(j + 1) * 128], lhsT=sel_bf[:, o:o + 128], rhs=uv[:, :, tau],
                        start=(tau == 0), stop=(tau == 7)),
                        reads=['sel', ('cp', blk)], writes=[bank_b(b)], sig=(tau == 7))
            P.add('act', lambda e, b=b, gb=gb: e.activation(
                out=Ush[:, gb * 4:gb * 4 + 4, :], in_=banks[b][:].rearrange("p (g c) -> p g c", g=4), func=AF.Identity),
                reads=[bank_b(b)], writes=[('cp', 4 + gb // 2)])
        if STAGE < 3:
            return
        A_, B_, C_, D_ = swf[:, 0], swf[:, 1], swf[:, 2], swf[:, 3]
        for hq in range(2):
            xb = [[next_bank(), next_bank()], [next_bank(), next_bank()]]
            for qq in range(8):
                q = hq * 8 + qq
                for r in range(2):
                    b = xb[r][qq // 4]
                    reg = banks[b][:, (qq % 4) * 128:(qq % 4 + 1) * 128]
                    for g2 in range(2):
                        g = 2 * q + g2
                        P.add('pe', lambda e, reg=reg, g=g, r=r, g2=g2: e.matmul(
                            reg, lhsT=lhsTX[:, g, r, :], rhs=Ush[:, g, :], start=(g2 == 0), stop=(g2 == 1)),
                            reads=['lhsTX', ('cp', 4 + g // 8)], writes=[bank_b(b)], sig=(g2 == 1))
            qs = slice(hq * 8, hq * 8 + 8)
            for bh in range(2):
                q4 = slice(bh * 4, bh * 4 + 4)
                qg = slice(hq * 8 + bh * 4, hq * 8 + bh * 4 + 4)
                Xr = banks[xb[0][bh]][:].rearrange("p (q c) -> p q c", q=4)
                Xi = banks[xb[1][bh]][:].rearrange("p (q c) -> p q c", q=4)
                rdb = [bank_b(xb[0][bh]), bank_b(xb[1][bh]), 'TcTs'] + ID_SW
                tt(A_[:, q4, :], Xr, Tc[:, qg, :], ALU.mult, rdb, ID_SW)
                tt(B_[:, q4, :], Xi, Ts[:, qg, :], ALU.mult, rdb, ID_SW)
                tt(A_[:, q4, :], A_[:, q4, :], B_[:, q4, :], ALU.add, rdb, ID_SW)
                tt(C_[:, q4, :], Xi, Tc[:, qg, :], ALU.mult, rdb, ID_SW)
                tt(B_[:, q4, :], Xr, Ts[:, qg, :], ALU.mult, rdb, ID_SW)
                tt(C_[:, q4, :], C_[:, q4, :], B_[:, q4, :], ALU.subtract, rdb, ID_SW)
            for qq in range(8):
                q = hq * 8 + qq
                r8b = sm(R8)[:, q:q + 1].to_broadcast([128, 128])
                dv(lambda e, qq=qq, q=q, r8b=r8b: e.tensor_tensor_scan(
                    out=B_[:, qq, :], data0=r8b, data1=A_[:, qq, :], initial=carry[:, 0, q:q + 1], op0=ALU.mult, op1=ALU.add),
                    ID_SW + ['carry', 'smalls'], ID_SW)
                dv(lambda e, qq=qq, q=q, r8b=r8b: e.tensor_tensor_scan(
                    out=D_[:, qq, :], data0=r8b, data1=C_[:, qq, :], initial=carry[:, 1, q:q + 1], op0=ALU.mult, op1=ALU.add),
                    ID_SW + ['carry', 'smalls'], ID_SW)
            rd = ID_SW + ['TcTs']
            tt(A_, B_, Tc[:, qs, :], ALU.mult, rd, ID_SW)
            tt(C_, D_, Ts[:, qs, :], ALU.mult, rd, ID_SW)
            tt(A_, A_, C_, ALU.subtract, rd, ID_SW)
            tt(C_, B_, Ts[:, qs, :], ALU.mult, rd, ID_SW)
            tt(B_, D_, Tc[:, qs, :], ALU.mult, rd, ID_SW)
            tt(C_, C_, B_, ALU.add, rd, ID_SW)
            for r, src in ((0, A_), (1, C_)):
                sid = [('cp', 8 + 2 * r), ('cp', 9 + 2 * r)]
                dv(lambda e, r=r, src=src, qs=qs: e.tensor_copy(out=Ssh[r][:, qs, 1:128], in_=src[:, :, 0:127]), ID_SW, sid)
                dv(lambda e, r=r, qs=qs: e.tensor_copy(out=Ssh[r][:, qs, 0:1], in_=carry[:, r, qs].unsqueeze(2)), ['carry'], sid)
                dv(lambda e, r=r, src=src, qs=qs: e.tensor_copy(out=carry[:, r, qs].unsqueeze(2), in_=src[:, :, 127:128]), ID_SW, ['carry'])
        for gi in range(2):
            def c_gp(mi, hf, b, gi=gi):
                ch = TGP[gi * 4 + mi]
                P.add('act', lambda e: e.activation(out=cp[:, ch, hs(hf)], in_=banks[b][:], func=AF.Tanh, scale=0.5),
                      reads=[bank_b(b)], writes=[('cp', ch)])
            proj_slab(w_in, 1024 + gi * 512, 4, c_gp)
        for gb in range(8):
            b = next_bank()
            for j in range(4):
                g = gb * 4 + j
                q, g2 = g // 2, g % 2
                rows = slice(g2 * 64, g2 * 64 + 64)
                reg = banks[b][:, j * 128:(j + 1) * 128]
                rdl = ['Wintra', 'Winter', ('cp', 4 + g // 8), ('cp', 8), ('cp', 9), ('cp', 10), ('cp', 11)]
                P.add('pe', lambda e, reg=reg, g=g: e.matmul(reg, lhsT=Wintra[:, g, :], rhs=Ush[:, g, :], start=True, stop=False),
                      reads=rdl, writes=[bank_b(b)], sig=False)
                P.add('pe', lambda e, reg=reg, q=q, g2=g2: e.matmul(reg, lhsT=WinterZ[:, g2, 0, q, :], rhs=Ssh[0][:, q, :], start=False, stop=False),
                      reads=rdl, writes=[bank_b(b)], sig=False)
                P.add('pe', lambda e, reg=reg, q=q, g2=g2: e.matmul(reg, lhsT=WinterZ[:, g2, 1, q, :], rhs=Ssh[1][:, q, :], start=False, stop=True),
                      reads=rdl, writes=[bank_b(b)], sig=True)
            P.add('act', lambda e, b=b, gb=gb: e.activation(
                out=Gsh[:, gb * 4:gb * 4 + 4, :], in_=banks[b][:].rearrange("p (g c) -> p g c", g=4), func=AF.Gelu),
                reads=[bank_b(b)], writes=[('cp', 12 + gb // 2)])
        for blk in range(4):
            for th in range(2):
                b = next_bank()
                for j in range(4):
                    t = th * 4 + j
                    for gi in range(8):
                        o = 240 * t + 112 - 16 * gi
                        P.add('pe', lambda e, b=b, j=j, o=o, gi=gi, blk=blk: e.matmul(
                            banks[b][:, j * 128:(j + 1) * 128], lhsT=sel_bf[:, o:o + 128], rhs=Gsh[:, blk * 8 + gi, :],
                            start=(gi == 0), stop=(gi == 7)),
                            reads=['sel', ('cp', 12 + blk)], writes=[bank_b(b)], sig=(gi == 7))
                gv = cp[:, 16 + blk, :].rearrange("p (c t) -> p c t", t=8)[:, :, th * 4:th * 4 + 4]
                dv(lambda e, b=b, gv=gv: e.tensor_copy(out=gv, in_=banks[b][:].rearrange("p (t c) -> p c t", t=4)),
                   [bank_b(b)], [('cp', 16 + blk)])
        s = load_w([(lambda sa: sa.rearrange("p (k c) -> p k c", k=4), w_glu.rearrange("(k p) c -> p k c", p=128))], None)
        wv = wsl[:, s, :].rearrange("p (k c) -> p k c", k=4)
        for mi in range(4):
            for hf in range(NH):
                bv_, bg_ = next_bank(), next_bank()
                for part, b in ((0, bv_), (1, bg_)):
                    for k in range(4):
                        P.add('pe', lambda e, k=k, b=b, part=part, mi=mi, hf=hf, wv=wv: e.matmul(
                            banks[b][:], lhsT=wv[:, k, part * 512 + mi * 128: part * 512 + (mi + 1) * 128],
                            rhs=cp[:, 16 + k, hs(hf)], start=(k == 0), stop=(k == 3)),
                            reads=[('slot', s), ('cp', 16 + k)], writes=[bank_b(b)], sig=(k == 3))
                tb = (mi * NH + hf) % 2
                P.add('act', lambda e, b=bg_, mi=mi, tb=tb: e.activation(
                    out=cp[:, GLT[tb], 0:512], in_=banks[b][:], func=AF.Tanh, bias=bglu_h[:, 4 + mi:5 + mi], scale=0.5),
                    reads=[bank_b(bg_), 'bgluh'], writes=[('cp', GLT[tb])])
                dv(lambda e, b=bv_, mi=mi, tb=tb: e.tensor_scalar(
                    out=tmpf[:, tb, 0:512], in0=banks[b][:], scalar1=bglu_sb[:, mi:mi + 1], scalar2=0.5, op0=ALU.add, op1=ALU.mult),
                    [bank_b(bv_), 'bglu'], [('tmpf', tb)])
                dv(lambda e, mi=mi, hf=hf, tb=tb: e.scalar_tensor_tensor(
                    out=cp[:, VCH[mi], hs(hf)], in0=cp[:, GLT[tb], 0:512], scalar=1.0, in1=tmpf[:, tb, 0:512], op0=ALU.add, op1=ALU.mult),
                    [('cp', GLT[tb]), ('tmpf', tb)], [('cp', VCH[mi])])
        if STAGE < 5:
            return
        W16 = NT + 16
        dv(lambda e: e.tensor_copy(out=upf[:, :, 0:16], in_=halo[:]), ['halo'], ID_UPF)

        def c_upool(mi, hf, b):
            P.add('act', lambda e: e.activation(out=upf[:, mi, 16 + hf * 512:16 + (hf + 1) * 512], in_=banks[b][:], func=AF.Identity),
                  reads=[bank_b(b)], writes=ID_UPF)
        proj_slab(w_in, 0, 4, c_upool)
        dv(lambda e: e.tensor_copy(out=halo[:], in_=upf[:, :, NT:NT + 16]), ID_UPF, ['halo'])
        for kg in range(4):
            w = 2 << kg
            for hf in range(NH):
                b = next_bank()
                for j in range(w):
                    lt = pwB[:, kg, :] if j == 0 else poolw_bf[:, kg, :]
                    c0 = 16 + hf * 512 - j
                    P.add('pe', lambda e, kg=kg, b=b, lt=lt, c0=c0, j=j, w=w: e.matmul(
                        banks[b][:], lhsT=lt, rhs=upf[:, kg, c0:c0 + 512], start=(j == 0), stop=(j == w - 1)),
                        reads=['pwAB', 'poolw'] + ID_UPF, writes=[bank_b(b)], sig=(j == w - 1))
                P.add('act', lambda e, kg=kg, hf=hf, b=b: e.activation(
                    out=cp[:, ZP[kg], hs(hf)], in_=banks[b][:], func=AF.Identity, bias=pbsc[:, kg:kg + 1], scale=pbs_w[:, kg:kg + 1]),
                    reads=[bank_b(b), 'pbsc', 'pbsw'], writes=[('cp', ZP[kg])])
        if ti == 0:
            TT = [('tmpf', 0), ('tmpf', 1)]
            for kg in range(4):
                src = upf[:, kg, 0:32]
                for j in range(kg + 1):
                    d = 1 << j
                    lo = 2 * d - 1
                    dst = tmpf[:, j % 2, 0:32]
                    P.add('dve', lambda e, dst=dst, src=src, d=d, lo=lo: e.tensor_tensor(
                        out=dst[:, lo:32], in0=src[:, lo:32], in1=src[:, lo - d:32 - d], op=ALU.add),
                        reads=ID_UPF + TT, writes=[('tmpf', j % 2)])
                    src = dst
                ssum = src
                dv(lambda e, kg=kg, ssum=ssum: e.tensor_tensor(out=ssum[:, 16:32], in0=ssum[:, 16:32], in1=rc_sb[:, kg, :], op=ALU.mult),
                   TT + ['rc'], TT)
                dv(lambda e, kg=kg, ssum=ssum: e.tensor_tensor(out=z16[:, kg, :], in0=ssum[:, 16:32], in1=upf[:, kg, 16:32], op=ALU.subtract),
                   TT + ID_UPF, ['z16'])
                b = next_bank()
                P.add('pe', lambda e, kg=kg, b=b: e.matmul(banks[b][:, 0:16], lhsT=poolw_bf[:, kg, :], rhs=z16[:, kg, :], start=True, stop=True),
                      reads=['poolw', 'z16'], writes=[bank_b(b)])
                P.add('act', lambda e, kg=kg, b=b: e.activation(
                    out=cp[:, ZP[kg], 0:16], in_=banks[b][:, 0:16], func=AF.Identity, bias=pbsc[:, kg:kg + 1], scale=pbs_sb[:, 1, kg:kg + 1]),
                    reads=[bank_b(b), 'pbsc', 'pbs'], writes=[('cp', ZP[kg])])
        if STAGE < 6:
            return
        for gi in range(2):
            def c_gs(mi, hf, b, gi=gi):
                ch = TGS[gi * 4 + mi]
                P.add('act', lambda e: e.activation(out=cp[:, ch, hs(hf)], in_=banks[b][:], func=AF.Tanh, scale=0.5),
                      reads=[bank_b(b)], writes=[('cp', ch)])
            proj_slab(w_in, 2048 + gi * 512, 4, c_gs)
        s_pu = load_w([(lambda sa: sa.rearrange("p (k c) -> p k c", k=4), w_pu.rearrange("(k p) c -> p k c", p=128))], None)
        s_su = load_w([(lambda sa: sa.rearrange("p (k c) -> p k c", k=4), w_su.rearrange("(k p) c -> p k c", p=128))], None)
        wpu = wsl[:, s_pu, :].rearrange("p (k c) -> p k c", k=4)
        wsu = wsl[:, s_su, :].rearrange("p (k c) -> p k c", k=4)
        for m in range(8):
            for hf in range(NH):
                bp, bs_ = next_bank(), next_bank()
                for k in range(4):
                    P.add('pe', lambda e, k=k, m=m, hf=hf, b=bp: e.matmul(
                        banks[b][:], lhsT=wpu[:, k, m * 128:(m + 1) * 128], rhs=cp[:, ZP[k], hs(hf)], start=(k == 0), stop=(k == 3)),
                        reads=[('slot', s_pu), ('cp', ZP[k])], writes=[bank_b(bp)], sig=(k == 3))
                for k in range(4):
                    P.add('pe', lambda e, k=k, m=m, hf=hf, b=bs_: e.matmul(
                        banks[b][:], lhsT=wsu[:, k, m * 128:(m + 1) * 128], rhs=cp[:, VCH[k], hs(hf)], start=(k == 0), stop=(k == 3)),
                        reads=[('slot', s_su), ('cp', VCH[k])], writes=[bank_b(bs_)], sig=(k == 3))
                chs = TGS[m]
                dv(lambda e, m=m, hf=hf, b=bp: e.scalar_tensor_tensor(
                    out=tmpf[:, 0, 0:512], in0=cp[:, TGP[m], hs(hf)], scalar=1.0, in1=banks[b][:], op0=ALU.add, op1=ALU.mult),
                    [bank_b(bp), ('cp', TGP[m])], [('tmpf', 0)])
                dv(lambda e, chs=chs, hf=hf, b=bs_: e.scalar_tensor_tensor(
                    out=tmpf[:, 1, 0:512], in0=cp[:, chs, hs(hf)], scalar=1.0, in1=banks[b][:], op0=ALU.add, op1=ALU.mult),
                    [bank_b(bs_), ('cp', chs)], [('tmpf', 1)])
                P.add('dve', lambda e, m=m, hf=hf: e.tensor_tensor(
                    out=cp[:, TGP[m], hs(hf)], in0=tmpf[:, 0, 0:512], in1=tmpf[:, 1, 0:512], op=ALU.add),
                    reads=[('tmpf', 0), ('tmpf', 1)], writes=[('cp', TGP[m])])
        for m2 in range(2):
            s = load_w([(lambda sa: sa.rearrange("p (k c) -> p k c", k=KT),
                         w_o[:, m2 * 512:(m2 + 1) * 512].rearrange("(k p) c -> p k c", p=128))], None)
            wv = wsl[:, s, :].rearrange("p (k c) -> p k c", k=KT)
            for mm in range(4):
                m = m2 * 4 + mm
                for hf in range(NH):
                    b = next_bank()
                    for k in range(KT):
                        P.add('pe', lambda e, k=k, mm=mm, hf=hf, b=b, wv=wv: e.matmul(
                            banks[b][:], lhsT=wv[:, k, mm * 128:(mm + 1) * 128], rhs=cp[:, TGP[k], hs(hf)], start=(k == 0), stop=(k == KT - 1)),
                            reads=[('slot', s), ('cp', TGP[k])], writes=[bank_b(b)], sig=(k == KT - 1))
                    dv(lambda e, m=m, hf=hf, b=b: e.scalar_tensor_tensor(
                        out=xs[:, m, hs(hf)], in0=banks[b][:], scalar=Gvec[:, 1, m:m + 1], in1=xs[:, m, hs(hf)], op0=ALU.mult, op1=ALU.add),
                        [bank_b(b), 'Gvec', ('xs', m)], [('xs', m)])

    def flush_out():
        while deferred_out:
            k, t0o = deferred_out.pop(0)
            P.add('sp', lambda e, k=k, t0o=t0o: e.dma_start(
                out=outT[k * 128:(k + 1) * 128, t0o:t0o + NT], in_=ostage[:, k * NT:(k + 1) * NT]),
                reads=[('cp', 2 * k), ('cp', 2 * k + 1)], dma='out')

    for ti in range(NTILES):
        t0 = ti * NT
        for k in range(KT):
            P.add('sp', lambda e, k=k, t0=t0: e.dma_start(out=xs[:, k, :], in_=xT[k * 128:(k + 1) * 128, t0:t0 + NT]),
                  writes=[('xs', k)], dma=('xs', k))
        flush_out()
        norm_mod(0, t0)
        if ti == 0:
            mod_next = [6]

            def hook0(kind, i):
                P.pull(setup_ops, 8 if kind == 'in' else 7)
                if kind == 'in' and i % 4 == 3 and mod_next[0] < 18:
                    mod_slab(mod_next[0])
                    mod_next[0] += 1
            ffn(0, w1i, w1o, hook=hook0)
            P.pull(setup_ops, len(setup_ops))
            while mod_next[0] < 18:
                mod_slab(mod_next[0])
                mod_next[0] += 1
            mod_finalize_sub(1)
            mod_finalize_sub(2)
            reserved[0] = None
        else:
            ffn(0, w1i, w1o)
        if MIXER:
            norm_mod(1, t0)
            mixer(ti)
        norm_mod(2, t0)
        ffn(2, w2i, w2o)
        norm_mod(3, t0)

    flush_out()
    with nc.Block() as block:
        P.emit(block)
    return nc, es, P


_CACHE = {}


def _consts():
    ident = np.eye(128, dtype=np.float32)
    sel = np.zeros((128, 1920), np.float32)
    for p in range(128):
        sel[p, 240 * (p // 16) + 112 + (p % 16)] = 1.0
    mask = np.zeros((128, 128), np.float32)
    for a in range(8):
        for b in range(8):
            if b >= a:
                mask[a * 16:(a + 1) * 16, b * 16:(b + 1) * 16] = 1.0
    rc = np.zeros((128, 4, 16), np.float32)
    for k in range(4):
        w = 2 << k
        for tt_ in range(16):
            rc[:, k, tt_] = 1.0 / min(tt_ + 1, w)
    return ident, sel, mask, rc


def _pair_layout(a):
    g, n = a.shape[0], a.shape[1]
    rest = a.shape[2:]
    a = a.reshape((16, 2, n) + rest)
    a = np.moveaxis(a, 0, 2)
    return np.ascontiguousarray(a.reshape((128, 16) + rest))


def make_in_maps(inputs):
    f = lambda a: np.ascontiguousarray(np.asarray(a, dtype=np.float32))
    x = f(inputs["x"])
    c = f(inputs["c"])
    ident, sel, mask, rc = _consts()
    gvec = np.stack([f(inputs["g_ffn1"])[0], f(inputs["g_mix"])[0], f(inputs["g_ffn2"])[0], f(inputs["g_final"])], 0)
    gvec = np.ascontiguousarray(gvec.reshape(4, KT, 128).transpose(2, 0, 1))
    b_ada = np.ascontiguousarray(f(inputs["b_ada"])[0].reshape(72, 128).T)
    pool_w = np.ascontiguousarray(f(inputs["pool_w"])[0].transpose(1, 0, 2))
    pool_bs = np.ascontiguousarray(np.stack([f(inputs["pool_b"])[0].reshape(4, 128).T,
                                             f(inputs["pool_scale"])[0].reshape(4, 128).T], 1))
    b_glu = np.ascontiguousarray(f(inputs["b_glu"])[0].reshape(8, 128).T)
    lrl = f(inputs["ssm_lam_re_log"])[0]
    lim = f(inputs["ssm_lam_im"])[0]
    ldt = np.broadcast_to(f(inputs["ssm_log_dt"])[0][:, None], (32, 64))
    ssm_sc = np.ascontiguousarray(np.stack([_pair_layout(lrl), _pair_layout(lim), _pair_layout(ldt)], 1))
    ssm_b = np.ascontiguousarray(np.stack([_pair_layout(f(inputs["ssm_b_re"])[0]),
                                           _pair_layout(f(inputs["ssm_b_im"])[0])], 1))
    cre = f(inputs["ssm_c_re"])[0].transpose(0, 2, 1)
    cim = f(inputs["ssm_c_im"])[0].transpose(0, 2, 1)
    ssm_cc = np.ascontiguousarray(np.stack([_pair_layout(cre), _pair_layout(cim)], 1))
    dd = f(inputs["ssm_d"])[0].reshape(32, 16)
    ssm_dd = np.ascontiguousarray(np.tile(dd.T, (8, 1)))
    shared = dict(
        w_ada=f(inputs["w_ada"])[0], b_ada=b_ada, gvec=gvec,
        w1i=f(inputs["w_ffn1_in"])[0], w1o=f(inputs["w_ffn1_out"])[0],
        w2i=f(inputs["w_ffn2_in"])[0], w2o=f(inputs["w_ffn2_out"])[0],
        w_in=f(inputs["w_in"])[0], pool_w=pool_w, pool_bs=pool_bs,
        w_pu=f(inputs["w_pool_up"])[0], w_glu=f(inputs["w_glu"])[0], b_glu=b_glu,
        w_su=f(inputs["w_ssm_up"])[0], w_o=f(inputs["w_out"])[0],
        ssm_sc=ssm_sc, ssm_b=ssm_b, ssm_cc=ssm_cc, ssm_dd=ssm_dd,
        cst_ident=ident, cst_sel=sel, cst_mask=mask, cst_rc=rc,
    )
    maps = []
    for b in range(8):
        m = dict(shared)
        m["xT"] = np.ascontiguousarray(x[b].T)
        m["c_l"] = np.ascontiguousarray(c[b].reshape(KT, 128).T)
        maps.append(m)
    return maps


def kernel(**inputs):
    if "nc" not in _CACHE:
        _CACHE["nc"] = build_nc()
    nc, es, P = _CACHE["nc"]
    in_maps = make_in_maps(inputs)
    res = run_bass_kernel_spmd(nc, in_maps, core_ids=list(range(8)))
    out = np.stack([np.asarray(r["outT"], dtype=np.float32).T for r in res.results], 0)
    return np.ascontiguousarray(out)
```
